# Optimizing a Trainium2 kernel written in Bass

```python
import math
import jax, jax.numpy as jnp
from jax import lax
import numpy as np

D_MODEL = 2048
BATCH = 4
SEQ = 4096
DEPTH = 4

N_BRANCH = 3
BRANCH_WIDTH = D_MODEL // 2
S5_GROUP = 16
S5_GROUPS = BRANCH_WIDTH // S5_GROUP
S5_STATE = 64
S5_DT_MIN = 1e-3
S5_DT_MAX = 1e-1
S5_MIN_DECAY = 1e-4
GLA_HEADS = 4
GLA_DV = BRANCH_WIDTH // GLA_HEADS
GLA_DK = GLA_DV // 2
GLA_KEY = GLA_HEADS * GLA_DK
GLA_GATE_RANK = 16
GLA_GATE_TAU = 16.0
HGRN_EXPAND = 128
HGRN_HEADS = BRANCH_WIDTH // HGRN_EXPAND
HGRN_DV = BRANCH_WIDTH // HGRN_HEADS
HGRN_KEY = HGRN_HEADS * HGRN_EXPAND
CHUNK = 64
SUB_CHUNK = 16
MLP_HIDDEN = 4 * D_MODEL
DN_ALPHA = (2 * DEPTH) ** 0.25
DN_BETA = (8 * DEPTH) ** -0.25
LN_EPS = 1e-5
RMS_EPS = 1e-6

IN_WIDTHS = (BRANCH_WIDTH,
             GLA_KEY, GLA_KEY, BRANCH_WIDTH, GLA_GATE_RANK, BRANCH_WIDTH,
             HGRN_KEY, HGRN_KEY, BRANCH_WIDTH, BRANCH_WIDTH,
             N_BRANCH * D_MODEL)
IN_TOTAL = sum(IN_WIDTHS)

kernel_name = 'hybrid_s5_gla_hgrn2_deepnorm'


def layer_norm(x, g, b):
    xf = x.astype(jnp.float32)
    mu = jnp.mean(xf, axis=-1, keepdims=True)
    var = jnp.mean(jnp.square(xf - mu), axis=-1, keepdims=True)
    return ((xf - mu) * lax.rsqrt(var + LN_EPS) * g + b).astype(x.dtype)


def head_rms_norm(o, w):
    of = o.astype(jnp.float32)
    y = of * lax.rsqrt(jnp.mean(jnp.square(of), axis=-1, keepdims=True) + RMS_EPS) * w
    return y.astype(o.dtype)


def split_columns(h):
    bounds, acc = [], 0
    for w in IN_WIDTHS[:-1]:
        acc += w
        bounds.append(acc)
    return jnp.split(h, bounds, axis=-1)


def chunk_gated_linear_attention(q, k, v, log_g):
    bsz, seq, nh, dk = q.shape
    dv = v.shape[-1]
    n_chunks = seq // CHUNK
    n_sub = CHUNK // SUB_CHUNK

    def to_chunks(t):
        return jnp.moveaxis(t.astype(jnp.float32).reshape(bsz, n_chunks, CHUNK, nh, t.shape[-1]), 1, 0)

    later = (jnp.arange(n_sub)[:, None] > jnp.arange(n_sub)[None, :])[:, :, None, None, None]
    causal = (jnp.arange(SUB_CHUNK)[:, None] >= jnp.arange(SUB_CHUNK)[None, :])[:, :, None, None]

    def step(state, inp):
        qc, kc, vc, gc = inp
        gcum = jnp.cumsum(gc, axis=1)
        g_last = gcum[:, -1]
        o = jnp.einsum('blhk,bhkv->blhv', qc * jnp.exp(gcum), state)
        qs = qc.reshape(bsz, n_sub, SUB_CHUNK, nh, dk)
        ks = kc.reshape(bsz, n_sub, SUB_CHUNK, nh, dk)
        vs = vc.reshape(bsz, n_sub, SUB_CHUNK, nh, dv)
        gs = gcum.reshape(bsz, n_sub, SUB_CHUNK, nh, dk)
        g_start = jnp.concatenate([jnp.zeros_like(gs[:, :1, 0]), gs[:, :-1, -1]], axis=1)
        q_ref = qs * jnp.exp(gs - g_start[:, :, None])
        e_off = jnp.where(later, g_start[:, :, None, None] - gs[:, None], -jnp.inf)
        s_off = jnp.einsum('bpihk,bprjhk->bhpirj', q_ref, ks[:, None] * jnp.exp(e_off))
        o_off = jnp.einsum('bhpirj,brjhv->bpihv', s_off, vs)
        e_diag = jnp.where(causal, gs[:, :, :, None] - gs[:, :, None], -jnp.inf)
        s_diag = jnp.einsum('bpihk,bpijhk,bpjhk->bhpij', qs, jnp.exp(e_diag), ks)
        o_diag = jnp.einsum('bhpij,bpjhv->bpihv', s_diag, vs)
        o = o + (o_off + o_diag).reshape(bsz, CHUNK, nh, dv)
        new_state = jnp.exp(g_last)[..., None] * state + jnp.einsum(
            'blhk,blhv->bhkv', kc * jnp.exp(g_last[:, None] - gcum), vc)
        return new_state, o

    init = jnp.zeros((bsz, nh, dk, dv), jnp.float32)
    _, o = lax.scan(step, init, (to_chunks(q), to_chunks(k), to_chunks(v), to_chunks(log_g)))
    return jnp.moveaxis(o, 0, 1).reshape(bsz, seq, nh, dv).astype(v.dtype)


def _complex_scan_combine(left, right):
    a1r, a1i, b1r, b1i = left
    a2r, a2i, b2r, b2i = right
    return (a2r * a1r - a2i * a1i,
            a2r * a1i + a2i * a1r,
            a2r * b1r - a2i * b1i + b2r,
            a2r * b1i + a2i * b1r + b2i)


def s5_branch(u, lam_re, lam_im, log_dt, b_re, b_im, c_re, c_im, d_skip, w_glu, b_glu):
    bsz, seq, _ = u.shape
    f32 = jnp.float32
    ug = u.astype(f32).reshape(bsz, seq, S5_GROUPS, S5_GROUP)
    lr = jnp.minimum(lam_re.astype(f32), -S5_MIN_DECAY)
    li = lam_im.astype(f32)
    dt = jnp.exp(log_dt.astype(f32))[:, None]
    mag = jnp.exp(lr * dt)
    abar_re, abar_im = mag * jnp.cos(li * dt), mag * jnp.sin(li * dt)
    den = lr * lr + li * li
    fac_re = ((abar_re - 1.0) * lr + abar_im * li) / den
    fac_im = (abar_im * lr - (abar_re - 1.0) * li) / den
    br, bi = b_re.astype(f32), b_im.astype(f32)
    bbar_re = fac_re[..., None] * br - fac_im[..., None] * bi
    bbar_im = fac_re[..., None] * bi + fac_im[..., None] * br
    bu_re = jnp.einsum('bsgc,gpc->bsgp', ug, bbar_re)
    bu_im = jnp.einsum('bsgc,gpc->bsgp', ug, bbar_im)
    a_re = jnp.broadcast_to(abar_re[None, None], (1, seq, S5_GROUPS, S5_STATE))
    a_im = jnp.broadcast_to(abar_im[None, None], (1, seq, S5_GROUPS, S5_STATE))
    _, _, s_re, s_im = lax.associative_scan(_complex_scan_combine, (a_re, a_im, bu_re, bu_im), axis=1)
    y = (jnp.einsum('bsgp,gcp->bsgc', s_re, c_re.astype(f32))
         - jnp.einsum('bsgp,gcp->bsgc', s_im, c_im.astype(f32))
         + d_skip.astype(f32).reshape(S5_GROUPS, S5_GROUP) * ug)
    z = jax.nn.gelu(y.reshape(bsz, seq, BRANCH_WIDTH).astype(u.dtype))
    return z * jax.nn.sigmoid(z @ w_glu + b_glu)


def gla_branch(q, k, v, g_low, gate, w_gate, b_gate, norm_w):
    bsz, seq, _ = q.shape
    shp_k = (bsz, seq, GLA_HEADS, GLA_DK)
    shp_v = (bsz, seq, GLA_HEADS, GLA_DV)
    log_a = jax.nn.log_sigmoid((g_low @ w_gate + b_gate).astype(jnp.float32)) / GLA_GATE_TAU
    o = chunk_gated_linear_attention((q * GLA_DK ** -0.5).reshape(shp_k), k.reshape(shp_k),
                                     v.reshape(shp_v), log_a.reshape(shp_k))
    o = head_rms_norm(o, norm_w) * jax.nn.silu(gate).reshape(shp_v)
    return o.reshape(bsz, seq, BRANCH_WIDTH)


def hgrn2_branch(q, f_logit, i, gate, lb, norm_w):
    bsz, seq, _ = q.shape
    shp_k = (bsz, seq, HGRN_HEADS, HGRN_EXPAND)
    shp_v = (bsz, seq, HGRN_HEADS, HGRN_DV)
    f = (lb + (1.0 - lb) * jax.nn.sigmoid(f_logit.astype(jnp.float32))).reshape(shp_k)
    o = chunk_gated_linear_attention(jax.nn.silu(q).reshape(shp_k), 1.0 - f,
                                     i.reshape(shp_v), jnp.log(f))
    o = head_rms_norm(o * jax.nn.sigmoid(gate).reshape(shp_v), norm_w)
    return o.reshape(bsz, seq, BRANCH_WIDTH)


def hybrid_mixer(x, w_in, s5_lam_re, s5_lam_im, s5_log_dt, s5_b_re, s5_b_im, s5_c_re, s5_c_im,
                 s5_d, s5_w_glu, s5_b_glu, gla_w_gate, gla_b_gate, gla_norm_w, lb, hgrn_norm_w,
                 w_up, w_out):
    bsz, seq, _ = x.shape
    h = jnp.einsum('bsd,dn->bsn', x, w_in)
    (u_a, q_b, k_b, v_b, glow_b, gate_b, q_c, f_c, i_c, gate_c, merge_gates) = split_columns(h)
    y_a = s5_branch(u_a, s5_lam_re, s5_lam_im, s5_log_dt, s5_b_re, s5_b_im, s5_c_re, s5_c_im,
                    s5_d, s5_w_glu, s5_b_glu)
    y_b = gla_branch(q_b, k_b, v_b, glow_b, gate_b, gla_w_gate, gla_b_gate, gla_norm_w)
    y_c = hgrn2_branch(q_c, f_c, i_c, gate_c, lb, hgrn_norm_w)
    ys = jnp.stack([y_a, y_b, y_c], axis=2)
    up = jnp.einsum('bsnw,nwd->bsnd', ys, w_up)
    g = jax.nn.sigmoid(merge_gates.reshape(bsz, seq, N_BRANCH, D_MODEL))
    merged = jnp.sum(g * up, axis=2)
    return merged @ w_out


def squared_relu_mlp(x, w1, w2):
    return jnp.square(jax.nn.relu(x @ w1)) @ w2


def setup_inputs(seed: int = 0) -> dict:
    key = jax.random.key(seed)
    ks = jax.random.split(key, 25)
    f32 = jnp.float32
    L, D, W, G, P, C = DEPTH, D_MODEL, BRANCH_WIDTH, S5_GROUPS, S5_STATE, S5_GROUP

    def nrm(k, shape, scale):
        return jax.random.normal(k, shape, f32) * scale

    return {
        'x': nrm(ks[0], (BATCH, SEQ, D), 1.0),
        'w_in': nrm(ks[1], (L, D, IN_TOTAL), D ** -0.5),
        's5_lam_re': -0.5 + nrm(ks[2], (L, G, P), 0.01),
        's5_lam_im': jnp.pi * jnp.arange(P, dtype=f32) + nrm(ks[3], (L, G, P), 0.01),
        's5_log_dt': jax.random.uniform(ks[4], (L, G), f32, math.log(S5_DT_MIN), math.log(S5_DT_MAX)),
        's5_b_re': nrm(ks[5], (L, G, P, C), C ** -0.5),
        's5_b_im': nrm(ks[6], (L, G, P, C), C ** -0.5),
        's5_c_re': nrm(ks[7], (L, G, C, P), P ** -0.5),
        's5_c_im': nrm(ks[8], (L, G, C, P), P ** -0.5),
        's5_d': nrm(ks[9], (L, W), 1.0),
        's5_w_glu': nrm(ks[10], (L, W, W), W ** -0.5),
        's5_b_glu': nrm(ks[11], (L, W), 0.01),
        'gla_w_gate': nrm(ks[12], (L, GLA_GATE_RANK, GLA_KEY), GLA_GATE_RANK ** -0.5),
        'gla_b_gate': nrm(ks[13], (L, GLA_KEY), 0.01),
        'gla_norm_w': 1.0 + nrm(ks[14], (L, GLA_DV), 0.02),
        'hgrn_lb_logits': nrm(ks[15], (L, HGRN_KEY), 0.1),
        'hgrn_norm_w': 1.0 + nrm(ks[16], (L, HGRN_DV), 0.02),
        'w_up': nrm(ks[17], (L, N_BRANCH, W, D), W ** -0.5),
        'w_out': nrm(ks[18], (L, D, D), D ** -0.5 * DN_BETA),
        'ln1_g': 1.0 + nrm(ks[19], (L, D), 0.02),
        'ln1_b': nrm(ks[20], (L, D), 0.01),
        'ln2_g': 1.0 + nrm(ks[21], (L, D), 0.02),
        'ln2_b': nrm(ks[22], (L, D), 0.01),
        'w_mlp_in': nrm(ks[23], (L, D, MLP_HIDDEN), D ** -0.5),
        'w_mlp_out': nrm(ks[24], (L, MLP_HIDDEN, D), MLP_HIDDEN ** -0.5 * DN_BETA),
    }


def reference(x, w_in, s5_lam_re, s5_lam_im, s5_log_dt, s5_b_re, s5_b_im, s5_c_re, s5_c_im, s5_d,
              s5_w_glu, s5_b_glu, gla_w_gate, gla_b_gate, gla_norm_w, hgrn_lb_logits, hgrn_norm_w,
              w_up, w_out, ln1_g, ln1_b, ln2_g, ln2_b, w_mlp_in, w_mlp_out):
    p = jax.nn.softmax(hgrn_lb_logits.astype(jnp.float32), axis=0)
    lb = jnp.cumsum(p, axis=0) - p[0]
    for l in range(DEPTH):
        mix = hybrid_mixer(x, w_in[l], s5_lam_re[l], s5_lam_im[l], s5_log_dt[l], s5_b_re[l], s5_b_im[l],
                           s5_c_re[l], s5_c_im[l], s5_d[l], s5_w_glu[l], s5_b_glu[l], gla_w_gate[l],
                           gla_b_gate[l], gla_norm_w[l], lb[l], hgrn_norm_w[l], w_up[l], w_out[l])
        x = layer_norm(DN_ALPHA * x + mix, ln1_g[l], ln1_b[l])
        x = layer_norm(DN_ALPHA * x + squared_relu_mlp(x, w_mlp_in[l], w_mlp_out[l]), ln2_g[l], ln2_b[l])
    return x
```

```python
import contextlib
import math
import types
import numpy as np
import concourse.bass as bass
import concourse.mybir as mybir
from concourse.bass_utils import run_bass_kernel_spmd

F32 = mybir.dt.float32
BF16 = mybir.dt.bfloat16
AF = mybir.ActivationFunctionType
ALU = mybir.AluOpType

D = 2048
W = 1024
NIN = 14352
HID = 8192
DEPTH = 4
ALPHA = (2 * DEPTH) ** 0.25
TWO_PI = 2.0 * math.pi
O_UA, O_QB, O_KB, O_VB, O_GL, O_GB, O_QC, O_FC, O_IC, O_GC, O_MG = (
    0, 1024, 1536, 2048, 3072, 3088, 4112, 5136, 6160, 7184, 8208)


def _freeze(fn):
    if fn.__closure__ is None:
        return fn
    cells = []
    for c in fn.__closure__:
        try:
            cells.append(types.CellType(c.cell_contents))
        except ValueError:
            cells.append(c)
    g = types.FunctionType(fn.__code__, fn.__globals__, fn.__name__, fn.__defaults__, tuple(cells))
    g.__kwdefaults__ = fn.__kwdefaults__
    return g


class Res:
    __slots__ = ("w", "r")

    def __init__(self):
        self.w = None
        self.r = {}


class Prog:
    KQ = 8
    ENGS = ("pe", "act", "dve", "pool", "sp")

    def __init__(self, nc, st):
        self.nc = nc
        self.eng = {}
        hs = dict(pe=nc.tensor, act=nc.scalar, dve=nc.vector, pool=nc.gpsimd, sp=nc.sync)
        for name in self.ENGS:
            sem = st.enter_context(nc.semaphore("sem_" + name))
            self.eng[name] = dict(h=hs[name], sem=sem, n=0, prog=[], waited={})
        self.dq = {}
        for q in ("sp", "pool", "act"):
            sems = [st.enter_context(nc.semaphore(f"dq_{q}_{i}")) for i in range(self.KQ)]
            self.dq[q] = dict(sems=sems, n=0)
        self.dma_uid = 0
        self.stopped = False
        self.nphase = 0
        self.max_phase = 10 ** 9

    def _waits(self, eng, reads, writes):
        E = self.eng[eng]
        deps = []
        for r in reads:
            if r.w is not None:
                deps.append(r.w)
        for w in writes:
            if w.w is not None:
                deps.append(w.w)
            deps.extend(w.r.values())
        waits = []
        for (sem, val, key, src) in deps:
            if src == "pe" and eng == "pe":
                continue
            if E["waited"].get(key, 0) >= val:
                continue
            E["waited"][key] = val
            waits.append((sem, val))
        return waits

    def _record(self, tok, reads, writes, rkey):
        for r in reads:
            r.r[rkey] = tok
        for w in writes:
            w.w = tok
            w.r = {}

    def op(self, eng, fn, reads=(), writes=()):
        if self.stopped:
            return None
        E = self.eng[eng]
        fn = _freeze(fn)
        waits = self._waits(eng, reads, writes)
        E["n"] += 1
        sem = E["sem"]
        tok = (sem, E["n"], "e_" + eng, eng)

        def emit(h, fn=fn, waits=waits, sem=sem):
            for s, v in waits:
                h.wait_ge(s, v)
            fn(h).then_inc(sem, 1)

        E["prog"].append(emit)
        self._record(tok, reads, writes, eng)
        return tok

    def dma(self, q, out, in_, reads=(), writes=()):
        if self.stopped:
            return None
        E = self.eng[q]
        Dq = self.dq[q]
        waits = self._waits(q, reads, writes)
        n = Dq["n"]
        Dq["n"] += 1
        s = Dq["sems"][n % self.KQ]
        val = 16 * (n // self.KQ + 1)
        key = f"dq_{q}_{n % self.KQ}"
        if n >= self.KQ and E["waited"].get(key, 0) < val - 16:
            E["waited"][key] = val - 16
            waits.append((s, val - 16))
        tok = (s, val, key, "dma")

        def emit(h, waits=waits, s=s, out=out, in_=in_):
            for ss, v in waits:
                h.wait_ge(ss, v)
            h.dma_start(out=out, in_=in_).then_inc(s, 16)

        E["prog"].append(emit)
        self.dma_uid += 1
        self._record(tok, reads, writes, "dma%d" % self.dma_uid)
        return tok

    def barrier(self):
        if self.stopped:
            return
        self.nphase += 1
        if self.nphase >= self.max_phase:
            self._barrier()
            self.stopped = True
            return
        self._barrier()

    def _barrier(self):
        targets = []
        for name in self.ENGS:
            X = self.eng[name]
            if X["n"] > 0:
                targets.append((X["sem"], X["n"], "e_" + name))
        for q, Dq in self.dq.items():
            n = Dq["n"]
            for i in range(min(n, self.KQ)):
                cnt = (n - 1 - i) // self.KQ + 1
                targets.append((Dq["sems"][i], 16 * cnt, f"dq_{q}_{i}"))
        for name in self.ENGS:
            E = self.eng[name]
            waits = []
            for (sem, val, key) in targets:
                if key == "e_" + name and name == "pe":
                    continue
                if E["waited"].get(key, 0) >= val:
                    continue
                E["waited"][key] = val
                waits.append((sem, val))

            def emit(h, waits=waits):
                for s, v in waits:
                    h.wait_ge(s, v)

            E["prog"].append(emit)

    def emit(self):
        nc = self.nc
        with nc.Block() as block:
            @block.tensor
            def _(h):
                for f in self.eng["pe"]["prog"]:
                    f(h)

            @block.scalar
            def _(h):
                for f in self.eng["act"]["prog"]:
                    f(h)

            @block.vector
            def _(h):
                for f in self.eng["dve"]["prog"]:
                    f(h)

            @block.gpsimd
            def _(h):
                for f in self.eng["pool"]["prog"]:
                    f(h)

            @block.sync
            def _(h):
                for f in self.eng["sp"]["prog"]:
                    f(h)


class Tile:
    def __init__(self, t):
        self.t = t
        self.res = Res()

    def __getitem__(self, k):
        return self.t[k]


class Arena:
    def __init__(self, nc, limit):
        self.nc = nc
        self.off = 16384
        self.limit = limit
        self.cnt = 0
        self.base = 16384

    def reset(self, to=None):
        self.off = self.base if to is None else to

    def alloc(self, shape, dtype):
        nbytes = int(np.prod(shape[1:])) * (4 if dtype == F32 else 2)
        nbytes = (nbytes + 63) // 64 * 64
        assert self.off + nbytes <= self.limit, (self.off, nbytes, self.limit)
        self.cnt += 1
        t = self.nc.alloc_sbuf_tensor_at("sb%d" % self.cnt, list(shape), dtype, offset=self.off)
        self.off += nbytes
        return Tile(t)


class K:
    pass


def build(T, L, TSPAN=1024, max_phase=10 ** 9):
    nc = bass.Bass("TRN2", target_bir_lowering=False)
    st = contextlib.ExitStack()
    P = Prog(nc, st)
    P.max_phase = max_phase
    TS = min(TSPAN, T)
    NSP = T // TS

    def din(name, shape, dt=F32):
        return nc.dram_tensor(name, list(shape), dt, kind="ExternalInput").ap()

    def dscr(name, shape, dt):
        import os
        kind = "ExternalOutput" if os.environ.get("KDEBUG") else "Internal"
        return nc.dram_tensor(name, list(shape), dt, kind=kind).ap()

    x_in = din("x", [T, D])
    w_in = din("w_in", [L, D, NIN])
    w_glu = din("w_glu", [L, W, W])
    w_up = din("w_up", [L, 3 * W, D])
    w_out = din("w_out", [L, D, D])
    w_m1 = din("w_m1", [L, D, HID])
    w_m2 = din("w_m2", [L, HID, D])
    lnp = din("lnp", [L, 4, D])
    s5p = din("s5p", [L, 128, 3, 32])
    s5b = din("s5b", [L, 2, 128, 32, 128])
    s5c = din("s5c", [L, 2, 128, 32, 128])
    s5v = din("s5v", [L, 128, 2, 8])
    glaw = din("glaw", [L, 16, 512])
    glav = din("glav", [L, 128, 6])
    hlb = din("hlb", [128, 8, DEPTH])
    hnw = din("hnw", [128, L])
    out = nc.dram_tensor("out", [T, D], F32, kind="ExternalOutput").ap()

    xF = dscr("xF", [T, D], F32)
    zF = dscr("zF", [T, D], F32)
    xT = dscr("xT", [D, T], BF16)
    hU = dscr("hU", [W, T], BF16)
    hQB = dscr("hQB", [512, T], BF16)
    hKB = dscr("hKB", [512, T], BF16)
    hGL = dscr("hGL", [16, T], F32)
    hGB = dscr("hGB", [W, T], BF16)
    hVB = dscr("hVB", [T, W], BF16)
    hQC = dscr("hQC", [W, T], BF16)
    hFC = dscr("hFC", [W, T], F32)
    hIC = dscr("hIC", [T, W], BF16)
    hGC = dscr("hGC", [W, T], BF16)
    hMG = dscr("hMG", [3 * D, T], BF16)
    zT = dscr("zT", [W, T], BF16)
    yT = dscr("yT", [3 * W, T], BF16)
    mT = dscr("mT", [D, T], BF16)
    hT = dscr("hT", [HID, T], BF16)

    SB_LIMIT = 192 * 1024
    ar = Arena(nc, SB_LIMIT)
    psum = nc.alloc_psum_tensor("psum", [128, 8, 512], F32)
    banks = [Res() for _ in range(8)]
    bank_i = [0]

    def next_bank():
        b = bank_i[0] % 8
        bank_i[0] += 1
        return b

    sub_i = {}

    def next_bank_in(lo, hi):
        i = sub_i.get((lo, hi), 0)
        sub_i[(lo, hi)] = i + 1
        return lo + i % (hi - lo)

    ident = ar.alloc([128, 128], BF16)
    ones_f = ar.alloc([128, 128], F32)
    mask = ar.alloc([128, 128], F32)
    ones_bf = ar.alloc([128, 128], BF16)
    ones_bf2 = ar.alloc([128, 128], BF16)
    cneg_pi = ar.alloc([128, 1], F32)
    ceps5 = ar.alloc([128, 1], F32)
    ceps6 = ar.alloc([128, 1], F32)
    lb_all = ar.alloc([128, 8, DEPTH], F32)
    hnw_t = ar.alloc([128, L], F32)
    ar.base = ar.off

    def c_setup():
        P.op("pool", lambda h: h.memset(ones_f[:], 1.0), writes=[ones_f.res])
        P.op("pool", lambda h: h.memset(cneg_pi[:], -math.pi), writes=[cneg_pi.res])
        P.op("pool", lambda h: h.memset(ceps5[:], 1e-5), writes=[ceps5.res])
        P.op("pool", lambda h: h.memset(ceps6[:], 1e-6), writes=[ceps6.res])
        P.op("pool", lambda h: h.memset(ones_bf[:], 1.0 / 128.0), writes=[ones_bf.res])
        P.op("pool", lambda h: h.memset(ones_bf2[:], 1.0 / 256.0), writes=[ones_bf2.res])
        P.op("pool", lambda h: h.affine_select(out=ident[:], in_=ones_f[:], pattern=[[-1, 128]], base=0,
                                               channel_multiplier=1, compare_op=ALU.is_equal, fill=0.0),
             reads=[ones_f.res], writes=[ident.res])
        P.op("pool", lambda h: h.affine_select(out=mask[:], in_=ones_f[:], pattern=[[1, 128]], base=0,
                                               channel_multiplier=-1, compare_op=ALU.is_ge, fill=0.0),
             reads=[ones_f.res], writes=[mask.res])
        P.op("pool", lambda h: h.memset(mask[0:64, 64:128], 0.0), writes=[mask.res])
        P.dma("sp", lb_all[:], hlb, writes=[lb_all.res])
        P.dma("sp", hnw_t[:], hnw, writes=[hnw_t.res])
        P.op("act", lambda h: h.activation(out=lb_all[:], in_=lb_all[:], func=AF.Exp),
             reads=[lb_all.res], writes=[lb_all.res])
        ssum = ar.alloc([128, 8, 1], F32)
        P.op("dve", lambda h: h.tensor_add(out=ssum[:], in0=lb_all[:, :, 0:1], in1=lb_all[:, :, 1:2]),
             reads=[lb_all.res], writes=[ssum.res])
        for j in (2, 3):
            P.op("dve", lambda h, j=j: h.tensor_add(out=ssum[:], in0=ssum[:], in1=lb_all[:, :, j:j + 1]),
                 reads=[lb_all.res, ssum.res], writes=[ssum.res])
        P.op("dve", lambda h: h.reciprocal(out=ssum[:], in_=ssum[:]), reads=[ssum.res], writes=[ssum.res])
        P.op("dve", lambda h: h.tensor_tensor(out=lb_all[:], in0=lb_all[:], in1=ssum[:].to_broadcast([128, 8, DEPTH]),
                                              op=ALU.mult), reads=[lb_all.res, ssum.res], writes=[lb_all.res])
        P.op("dve", lambda h: h.tensor_add(out=lb_all[:, :, 2:3], in0=lb_all[:, :, 2:3], in1=lb_all[:, :, 1:2]),
             reads=[lb_all.res], writes=[lb_all.res])
        P.op("dve", lambda h: h.tensor_add(out=lb_all[:, :, 3:4], in0=lb_all[:, :, 3:4], in1=lb_all[:, :, 2:3]),
             reads=[lb_all.res], writes=[lb_all.res])
        P.op("dve", lambda h: h.memset(lb_all[:, :, 0:1], 0.0), writes=[lb_all.res])
        P.barrier()

    def ps_ap(b, rows=128, n=512):
        return psum[0:rows, b, 0:n]

    class Rot:
        def __init__(self, tiles):
            self.tiles = tiles
            self.i = 0

        def next(self):
            t = self.tiles[self.i % len(self.tiles)]
            self.i += 1
            return t

    def mm_group(bank, out_ap, pairs, extra_reads, start=True, stop=True):
        def fn(h):
            ins = None
            n = len(pairs)
            for i, (l, r) in enumerate(pairs):
                ins = h.matmul(out_ap, l, r, start=(start and i == 0), stop=(stop and i == n - 1))
            return ins
        return P.op("pe", fn, reads=extra_reads, writes=[banks[bank]])

    def gemm_phase(act_src, Kd, w_src, cols, mode, epi, TP, kgroups=None, kbp=None):
        ar.reset()
        KB = Kd // 128
        nparts = (1 if KB <= 24 else KB // 16) if kbp is None else KB // kbp
        KBP = KB // nparts
        actA = ar.alloc([128, KB * TP], BF16)
        wb = Rot([ar.alloc([128, KBP, 512], BF16) for _ in range(2)])
        aux = dict(
            sf=Rot([ar.alloc([128, 512], F32) for _ in range(4)]),
            sb=Rot([ar.alloc([128, 512], BF16) for _ in range(4)]),
            xf=Rot([ar.alloc([128, 3 if mode == "fm" else 1, 512], F32) for _ in range(2)]),
            gb=Rot([ar.alloc([128, 3 if mode == "fm" else 1, 512 if mode == "fm" else 8], BF16) for _ in range(2)]),
        )
        NQ = TP // 512
        actq = [Res() for _ in range(NQ)]
        if kgroups is None:
            kgroups = [(0, KBP)]
        actv = actA[:].rearrange("p (k t) -> p k t", t=TP)
        for tp in range(T // TP):
            for q in range(NQ):
                P.dma("sp", actv[:, :, q * 512:(q + 1) * 512],
                      act_src[:, tp * TP + q * 512:tp * TP + (q + 1) * 512].rearrange("(k p) t -> p k t", p=128),
                      writes=[actq[q]])
            loads = [(ci, pa) for ci in range(len(cols)) for pa in range(nparts)]
            wtiles = {}

            def load(idx):
                ci, pa = loads[idx]
                off, width, tag = cols[ci]
                wt = wb.next()
                P.dma("pool", wt[:, :, 0:width],
                      w_src[pa * KBP * 128:(pa + 1) * KBP * 128, off:off + width].rearrange("(k p) n -> p k n", p=128),
                      writes=[wt.res])
                wtiles[idx] = wt

            load(0)
            for idx in range(len(loads)):
                if idx + 1 < len(loads):
                    load(idx + 1)
                ci, pa = loads[idx]
                off, width, tag = cols[ci]
                wt = wtiles.pop(idx)
                if mode == "fm":
                    assert nparts == 1
                    for nb in range((width + 127) // 128):
                        rows = min(128, width - nb * 128)
                        for tt in range(TP // 512):
                            bl = []
                            for (k0, k1) in kgroups:
                                b = next_bank()
                                pairs = [(wt[:, kb, nb * 128:nb * 128 + rows], actv[:, kb, tt * 512:(tt + 1) * 512])
                                         for kb in range(k0, k1)]
                                mm_group(b, ps_ap(b, rows), pairs, [wt.res, actq[tt]])
                                bl.append(b)
                            epi(bl, tag, off + nb * 128, rows, tp * TP + tt * 512, aux)
                else:
                    nt = TP // 128
                    if nparts == 1:
                        for t1 in range(nt):
                            b = next_bank()
                            pairs = [(actv[:, kb, t1 * 128:(t1 + 1) * 128], wt[:, kb, 0:width]) for kb in range(KBP)]
                            mm_group(b, ps_ap(b, 128, width), pairs, [wt.res, actq[t1 // 4]])
                            epi([b], tag, off, width, tp * TP + t1 * 128, aux)
                    else:
                        assert nt <= 8
                        if pa == 0:
                            cur = [next_bank() for _ in range(nt)]
                            wtiles["cur"] = cur
                        cur = wtiles["cur"]
                        for t1 in range(nt):
                            b = cur[t1]
                            pairs = [(actv[:, pa * KBP + kb, t1 * 128:(t1 + 1) * 128], wt[:, kb, 0:width])
                                     for kb in range(KBP)]
                            mm_group(b, ps_ap(b, 128, width), pairs, [wt.res, actq[t1 // 4]],
                                     start=(pa == 0), stop=(pa == nparts - 1))
                            if pa == nparts - 1:
                                epi([b], tag, off, width, tp * TP + t1 * 128, aux)
        P.barrier()

    def evac_store(b, rows, n, func, scale, dt, dest, aux, bias=None, eng=None):
        stg = (aux["sb"] if dt == BF16 else aux["sf"]).next()
        src = ps_ap(b, rows, n)
        if func is None and (eng or "dve") == "dve":
            P.op("dve", lambda h: h.tensor_scalar(out=stg[0:rows, 0:n], in0=src, scalar1=float(scale), scalar2=None,
                                                  op0=ALU.mult), reads=[banks[b]], writes=[stg.res])
        else:
            f = AF.Copy if func is None else func
            if bias is None:
                P.op("act", lambda h: h.activation(out=stg[0:rows, 0:n], in_=src, func=f, scale=float(scale)),
                     reads=[banks[b]], writes=[stg.res])
            else:
                P.op("act", lambda h: h.activation(out=stg[0:rows, 0:n], in_=src, func=f, scale=float(scale),
                                                   bias=bias), reads=[banks[b]], writes=[stg.res])
        P.dma("sp", dest, stg[0:rows, 0:n], reads=[stg.res])

    def ln_phase(src, norm, l, which, dstF, dstOut):
        ar.reset()
        if norm:
            g_bc = ar.alloc([128, D], F32)
            b_bc = ar.alloc([128, D], F32)
            P.dma("sp", g_bc[:], lnp[l, 2 * which, :].partition_broadcast(128), writes=[g_bc.res])
            P.dma("sp", b_bc[:], lnp[l, 2 * which + 1, :].partition_broadcast(128), writes=[b_bc.res])
        zt = Rot([ar.alloc([128, D], F32) for _ in range(4)])
        jk = Rot([ar.alloc([128, D], F32) for _ in range(2)])
        xb = Rot([ar.alloc([128, D], BF16) for _ in range(3)])
        xtt = Rot([ar.alloc([128, 16, 128], BF16) for _ in range(3)])
        stat = Rot([ar.alloc([128, 8], F32) for _ in range(4)])
        for tt in range(T // 128):
            z = zt.next()
            P.dma("sp", z[:], src[tt * 128:(tt + 1) * 128, :], writes=[z.res])
            xbt = xb.next()
            if norm:
                s = stat.next()
                j = jk.next()
                P.op("act", lambda h: h.activation(out=j[:], in_=z[:], func=AF.Copy, accum_out=s[:, 0:1]),
                     reads=[z.res], writes=[j.res, s.res])
                P.op("act", lambda h: h.activation(out=j[:], in_=z[:], func=AF.Square, accum_out=s[:, 1:2]),
                     reads=[z.res], writes=[j.res, s.res])
                P.op("dve", lambda h: h.tensor_scalar(out=s[:, 2:3], in0=s[:, 0:1], scalar1=1.0 / D, scalar2=None,
                                                      op0=ALU.mult), reads=[s.res], writes=[s.res])
                P.op("dve", lambda h: h.tensor_tensor(out=s[:, 3:4], in0=s[:, 2:3], in1=s[:, 2:3], op=ALU.mult),
                     reads=[s.res], writes=[s.res])
                P.op("dve", lambda h: h.scalar_tensor_tensor(out=s[:, 4:5], in0=s[:, 1:2], scalar=1.0 / D,
                                                             in1=s[:, 3:4], op0=ALU.mult, op1=ALU.subtract),
                     reads=[s.res], writes=[s.res])
                P.op("act", lambda h: h.activation(out=s[:, 5:6], in_=s[:, 4:5], func=AF.Ln, bias=ceps5[:, 0:1]),
                     reads=[s.res, ceps5.res], writes=[s.res])
                P.op("act", lambda h: h.activation(out=s[:, 5:6], in_=s[:, 5:6], func=AF.Exp, scale=-0.5),
                     reads=[s.res], writes=[s.res])
                P.op("dve", lambda h: h.tensor_scalar(out=z[:], in0=z[:], scalar1=s[:, 2:3], scalar2=s[:, 5:6],
                                                      op0=ALU.subtract, op1=ALU.mult),
                     reads=[z.res, s.res], writes=[z.res])
                P.op("pool", lambda h: h.tensor_tensor(out=z[:], in0=z[:], in1=g_bc[:], op=ALU.mult),
                     reads=[z.res, g_bc.res], writes=[z.res])
                P.op("pool", lambda h: h.tensor_tensor(out=z[:], in0=z[:], in1=b_bc[:], op=ALU.add),
                     reads=[z.res, b_bc.res], writes=[z.res])
                P.dma("sp", dstF[tt * 128:(tt + 1) * 128, :], z[:], reads=[z.res])
                if dstOut is not None:
                    P.dma("sp", dstOut[tt * 128:(tt + 1) * 128, :], z[:], reads=[z.res])
            P.op("act", lambda h: h.activation(out=xbt[:], in_=z[:], func=AF.Copy), reads=[z.res], writes=[xbt.res])
            xt_ = xtt.next()
            for half in range(2):
                b = next_bank()
                pv = psum[:, b, :].bitcast(BF16).rearrange("p (k t) -> p k t", t=128)

                def fn(h, b=b, pv=pv, half=half, xbt=xbt):
                    ins = None
                    for k in range(8):
                        kk = half * 8 + k
                        ins = h.transpose(pv[:, k, :], xbt[:, kk * 128:(kk + 1) * 128], ident[:])
                    return ins
                P.op("pe", fn, reads=[xbt.res, ident.res], writes=[banks[b]])
                eng = "dve" if half == 0 else "act"
                if eng == "dve":
                    P.op("dve", lambda h, pv=pv, half=half, xt_=xt_: h.tensor_copy(out=xt_[:, half * 8:half * 8 + 8, :], in_=pv),
                         reads=[banks[b]], writes=[xt_.res])
                else:
                    P.op("act", lambda h, pv=pv, half=half, xt_=xt_: h.activation(out=xt_[:, half * 8:half * 8 + 8, :], in_=pv, func=AF.Copy),
                         reads=[banks[b]], writes=[xt_.res])
            P.dma("sp", xT[:, tt * 128:(tt + 1) * 128].rearrange("(k p) t -> p k t", p=128), xt_[:], reads=[xt_.res])
        P.barrier()

    def gla_like(l, kind):
        ar.reset()
        NC = TS // 64
        NPC = TS // 128
        nh = 4 if kind == "gla" else 8
        nvb = 2 if kind == "gla" else 1
        DV = 128 * nvb
        gs = (-1.0 / 16.0) if kind == "gla" else 1.0
        mask0 = ar.alloc([128, TS], F32)
        P.op("pool", lambda h: h.memset(mask0[:], 1.0), writes=[mask0.res])
        P.op("pool", lambda h: h.memset(mask0[:].rearrange("p (c i) -> p c i", i=64)[:, :, 0:1], 0.0), writes=[mask0.res])
        gsets = []
        for _ in range(2):
            gsets.append((ar.alloc([128, TS], F32), ar.alloc([128, TS], F32), ar.alloc([128, TS], F32), ar.alloc([128, TS], F32),
                          ar.alloc([128, TS], F32), ar.alloc([128, TS], BF16), ar.alloc([128, TS], BF16), ar.alloc([128, TS], BF16),
                          ar.alloc([128, TS], BF16), ar.alloc([128, TS], BF16), ar.alloc([128, NC], F32),
                          ar.alloc([128, NPC, DV], BF16), ar.alloc([128, NPC, 128], BF16)))
        git = [0]
        KV = ar.alloc([128, NC, DV], F32)
        Sall = ar.alloc([128, NC + 1, DV], F32)
        Sbf = ar.alloc([128, NC, DV], BF16)
        AT = Rot([ar.alloc([128, 128], BF16) for _ in range(3)])
        og = Rot([ar.alloc([128, nvb, 512], F32) for _ in range(2)])
        sq = Rot([ar.alloc([128, nvb, 512], BF16) for _ in range(2)])
        gt = Rot([ar.alloc([128, nvb, 512], BF16) for _ in range(2)])
        rstd = Rot([ar.alloc([128, 512], F32) for _ in range(2)])
        yb = Rot([ar.alloc([128, nvb, 512], BF16) for _ in range(2)])
        glt = ar.alloc([16, TS], F32)
        wg = ar.alloc([16, 512], F32)
        gv = ar.alloc([128, 6], F32)
        nbg = ar.alloc([128, 4], F32)
        if kind == "gla":
            P.dma("sp", wg[:], glaw[l], writes=[wg.res])
            P.dma("sp", gv[:], glav[l], writes=[gv.res])
            P.op("dve", lambda h: h.tensor_scalar(out=nbg[:], in0=gv[:, 0:4], scalar1=-1.0, scalar2=None, op0=ALU.mult),
                 reads=[gv.res], writes=[nbg.res])
        for hh in range(nh):
            P.op("dve", lambda h: h.memset(Sall[:, 0, :], 0.0), writes=[Sall.res])
            for sp in range(NSP):
                t0 = sp * TS
                fr, gg, G, tmp, tmp2, qb, kb_, qt, kt, kpT, egl, vtok, kptok = gsets[git[0] % 2]
                git[0] += 1
                if sp > 0:
                    P.op("dve", lambda h: h.tensor_copy(out=Sall[:, 0, :], in_=Sall[:, NC, :]),
                         reads=[Sall.res], writes=[Sall.res])
                if kind == "gla":
                    if hh == 0:
                        pass
                    P.dma("sp", glt[:], hGL[:, t0:t0 + TS], writes=[glt.res])
                    for tt in range(TS // 512):
                        b = next_bank()
                        mm_group(b, ps_ap(b), [(wg[:, hh * 128:(hh + 1) * 128], glt[:, tt * 512:(tt + 1) * 512])],
                                 [wg.res, glt.res])
                        P.op("act", lambda h, b=b, tt=tt: h.activation(out=fr[:, tt * 512:(tt + 1) * 512], in_=ps_ap(b),
                                                                       func=AF.Exp, scale=-1.0, bias=nbg[:, hh:hh + 1]),
                             reads=[banks[b], nbg.res], writes=[fr.res])
                    P.op("act", lambda h: h.activation(out=gg[:], in_=fr[:], func=AF.Ln, bias=1.0, scale=1.0),
                         reads=[fr.res], writes=[gg.res])
                    P.dma("sp", qb[:], hQB[hh * 128:(hh + 1) * 128, t0:t0 + TS], writes=[qb.res])
                    P.dma("sp", kb_[:], hKB[hh * 128:(hh + 1) * 128, t0:t0 + TS], writes=[kb_.res])
                    kk = kb_
                    P.dma("sp", vtok[:], hVB[t0:t0 + TS, hh * DV:(hh + 1) * DV].rearrange("(c p) v -> p c v", p=128),
                          writes=[vtok.res])
                else:
                    P.dma("sp", fr[:], hFC[hh * 128:(hh + 1) * 128, t0:t0 + TS], writes=[fr.res])
                    P.op("act", lambda h: h.activation(out=fr[:], in_=fr[:], func=AF.Sigmoid), reads=[fr.res], writes=[fr.res])
                    oml = ar_small["oml"]
                    P.op("dve", lambda h: h.tensor_scalar(out=fr[:], in0=fr[:], scalar1=oml[:, hh:hh + 1],
                                                          scalar2=lb_all[:, hh, l:l + 1], op0=ALU.mult, op1=ALU.add),
                         reads=[fr.res, oml.res, lb_all.res], writes=[fr.res])
                    P.op("act", lambda h: h.activation(out=gg[:], in_=fr[:], func=AF.Ln), reads=[fr.res], writes=[gg.res])
                    P.op("dve", lambda h: h.tensor_scalar(out=fr[:], in0=fr[:], scalar1=-1.0, scalar2=1.0,
                                                          op0=ALU.mult, op1=ALU.add), reads=[fr.res], writes=[fr.res])
                    kk = fr
                    P.dma("sp", qb[:], hQC[hh * 128:(hh + 1) * 128, t0:t0 + TS], writes=[qb.res])
                    P.dma("sp", vtok[:], hIC[t0:t0 + TS, hh * DV:(hh + 1) * DV].rearrange("(c p) v -> p c v", p=128),
                          writes=[vtok.res])
                P.op("dve", lambda h: h.tensor_tensor_scan(out=G[:], data0=mask0[:], data1=gg[:], initial=0.0,
                                                           op0=ALU.mult, op1=ALU.add),
                     reads=[mask0.res, gg.res], writes=[G.res])
                Gv = G[:].rearrange("p (c i) -> p c i", i=64)
                P.op("act", lambda h: h.activation(out=tmp[:], in_=G[:], func=AF.Exp, scale=gs), reads=[G.res], writes=[tmp.res])
                P.op("pool", lambda h: h.tensor_tensor(out=qt[:], in0=qb[:], in1=tmp[:], op=ALU.mult),
                     reads=[qb.res, tmp.res], writes=[qt.res])
                P.op("act", lambda h: h.activation(out=tmp2[:], in_=G[:], func=AF.Exp, scale=-gs), reads=[G.res], writes=[tmp2.res])
                P.op("dve", lambda h, kk=kk: h.tensor_tensor(out=kt[:], in0=kk[:], in1=tmp2[:], op=ALU.mult),
                     reads=[kk.res, tmp2.res], writes=[kt.res])
                P.op("dve", lambda h: h.tensor_tensor(out=tmp[:].rearrange("p (c i) -> p c i", i=64),
                                                      in0=Gv[:, :, 63:64].to_broadcast([128, NC, 64]), in1=Gv,
                                                      op=ALU.subtract), reads=[G.res, tmp.res, qt.res], writes=[tmp.res])
                P.op("act", lambda h: h.activation(out=tmp[:], in_=tmp[:], func=AF.Exp, scale=gs), reads=[tmp.res], writes=[tmp.res])
                P.op("pool", lambda h, kk=kk: h.tensor_tensor(out=kpT[:], in0=kk[:], in1=tmp[:], op=ALU.mult),
                     reads=[kk.res, tmp.res], writes=[kpT.res])
                P.op("act", lambda h: h.activation(out=egl[:], in_=Gv[:, :, 63], func=AF.Exp, scale=gs),
                     reads=[G.res], writes=[egl.res])
                for g8 in range((NPC + 7) // 8):
                    b = next_bank()
                    pv = psum[:, b, :].bitcast(BF16).rearrange("p (k t) -> p k t", t=128)
                    n8 = min(8, NPC - g8 * 8)

                    def fn(h, pv=pv, g8=g8, n8=n8):
                        ins = None
                        for k in range(n8):
                            pc = g8 * 8 + k
                            ins = h.transpose(pv[:, k, :], kpT[:, pc * 128:(pc + 1) * 128], ident[:])
                        return ins
                    P.op("pe", fn, reads=[kpT.res, ident.res], writes=[banks[b]])
                    P.op("act", lambda h, pv=pv, g8=g8, n8=n8: h.activation(out=kptok[:, g8 * 8:g8 * 8 + n8, :], in_=pv[:, 0:n8, :],
                                                                            func=AF.Copy), reads=[banks[b]], writes=[kptok.res])
                per_bank = 512 // DV
                KVv = KV[:].rearrange("p (c two) v -> p c two v", two=2)
                for p0 in range(0, NPC, per_bank):
                    bA, bB = next_bank(), next_bank()
                    pA = psum[:, bA, :].rearrange("p (c v) -> p c v", v=DV)
                    pB = psum[:, bB, :].rearrange("p (c v) -> p c v", v=DV)

                    def fn(h, p0=p0, pA=pA, pB=pB):
                        ins = None
                        for i in range(per_bank):
                            pc = p0 + i
                            h.matmul(pA[:, i, :], kptok[0:64, pc, :], vtok[0:64, pc, :], start=True, stop=True)
                            ins = h.matmul(pB[:, i, :], kptok[64:128, pc, :], vtok[64:128, pc, :], start=True, stop=True)
                        return ins
                    P.op("pe", fn, reads=[kptok.res, vtok.res], writes=[banks[bA], banks[bB]])
                    P.op("dve", lambda h, p0=p0, pA=pA: h.tensor_copy(out=KVv[:, p0:p0 + per_bank, 0, :], in_=pA),
                         reads=[banks[bA]], writes=[KV.res])
                    P.op("act", lambda h, p0=p0, pB=pB: h.activation(out=KVv[:, p0:p0 + per_bank, 1, :], in_=pB, func=AF.Copy),
                         reads=[banks[bB]], writes=[KV.res])
                for c in range(NC):
                    P.op("dve", lambda h, c=c: h.scalar_tensor_tensor(out=Sall[:, c + 1, :], in0=Sall[:, c, :], scalar=egl[:, c:c + 1],
                                                                     in1=KV[:, c, :], op0=ALU.mult, op1=ALU.add),
                         reads=[egl.res, KV.res, Sall.res], writes=[Sall.res])
                P.op("act", lambda h: h.activation(out=Sbf[:], in_=Sall[:, 0:NC, :], func=AF.Copy),
                     reads=[Sall.res], writes=[Sbf.res])
                for tt in range(TS // 512):
                    ob = [next_bank() for _ in range(nvb)]
                    for p4 in range(4):
                        pc = tt * 4 + p4
                        bs = next_bank()
                        mm_group(bs, psum[:, bs, 0:128], [(kt[:, pc * 128:(pc + 1) * 128], qt[:, pc * 128:(pc + 1) * 128])],
                                 [kt.res, qt.res])
                        at = AT.next()
                        P.op("dve", lambda h, bs=bs, at=at: h.tensor_tensor(out=at[:], in0=psum[:, bs, 0:128], in1=mask[:], op=ALU.mult),
                             reads=[banks[bs], mask.res], writes=[at.res])
                        for vb in range(nvb):
                            def fn(h, vb=vb, pc=pc, p4=p4, at=at, ob=ob):
                                o_ap = psum[:, ob[vb], p4 * 128:(p4 + 1) * 128]
                                h.matmul(o_ap, vtok[:, pc, vb * 128:(vb + 1) * 128], at[:], start=True, stop=False)
                                h.matmul(o_ap[:, 0:64], Sbf[:, 2 * pc, vb * 128:(vb + 1) * 128], qt[:, pc * 128:pc * 128 + 64],
                                         start=False, stop=False)
                                return h.matmul(o_ap[:, 64:128], Sbf[:, 2 * pc + 1, vb * 128:(vb + 1) * 128],
                                                qt[:, pc * 128 + 64:pc * 128 + 128], start=False, stop=True)
                            P.op("pe", fn, reads=[vtok.res, at.res, Sbf.res, qt.res], writes=[banks[ob[vb]]])
                    o_t = og.next(); s_t = sq.next(); g_t = gt.next(); r_t = rstd.next(); y_t = yb.next()
                    gsrc = hGB if kind == "gla" else hGC
                    ybase = W if kind == "gla" else 2 * W
                    tok = t0 + tt * 512
                    P.dma("sp", g_t[:], gsrc[hh * DV:(hh + 1) * DV, tok:tok + 512].rearrange("(b p) t -> p b t", p=128),
                          writes=[g_t.res])
                    for vb in range(nvb):
                        if kind == "gla":
                            P.op("act", lambda h, vb=vb, o_t=o_t, ob=ob: h.activation(out=o_t[:, vb, :], in_=ps_ap(ob[vb]), func=AF.Copy),
                                 reads=[banks[ob[vb]]], writes=[o_t.res])
                        else:
                            P.op("dve", lambda h, vb=vb, o_t=o_t, ob=ob, g_t=g_t: h.tensor_tensor(out=o_t[:, vb, :], in0=ps_ap(ob[vb]),
                                                                                                in1=g_t[:, vb, :], op=ALU.mult),
                                 reads=[banks[ob[vb]], g_t.res], writes=[o_t.res])
                    P.op("pool", lambda h, o_t=o_t, s_t=s_t: h.tensor_tensor(out=s_t[:], in0=o_t[:], in1=o_t[:], op=ALU.mult),
                         reads=[o_t.res], writes=[s_t.res])
                    br = next_bank()
                    onesm = ones_bf if nvb == 1 else ones_bf2
                    mm_group(br, ps_ap(br), [(onesm[:], s_t[:, vb, :]) for vb in range(nvb)], [onesm.res, s_t.res])
                    P.op("act", lambda h, br=br, r_t=r_t: h.activation(out=r_t[:], in_=ps_ap(br), func=AF.Ln, bias=ceps6[:, 0:1]),
                         reads=[banks[br], ceps6.res], writes=[r_t.res])
                    P.op("act", lambda h, r_t=r_t: h.activation(out=r_t[:], in_=r_t[:], func=AF.Exp, scale=-0.5),
                         reads=[r_t.res], writes=[r_t.res])
                    for vb in range(nvb):
                        wcol = gv[:, 4 + vb:5 + vb] if kind == "gla" else hnw_t[:, l:l + 1]
                        wres = gv.res if kind == "gla" else hnw_t.res
                        if kind == "gla":
                            P.op("dve", lambda h, vb=vb, o_t=o_t, r_t=r_t, wcol=wcol: h.scalar_tensor_tensor(
                                out=o_t[:, vb, :], in0=o_t[:, vb, :], scalar=wcol, in1=r_t[:], op0=ALU.mult, op1=ALU.mult),
                                reads=[o_t.res, r_t.res, wres], writes=[o_t.res])
                            P.op("pool", lambda h, vb=vb, o_t=o_t, y_t=y_t, g_t=g_t: h.tensor_tensor(out=y_t[:, vb, :], in0=o_t[:, vb, :],
                                                                                                   in1=g_t[:, vb, :], op=ALU.mult),
                                 reads=[o_t.res, g_t.res], writes=[y_t.res])
                        else:
                            P.op("dve", lambda h, vb=vb, o_t=o_t, r_t=r_t, wcol=wcol, y_t=y_t: h.scalar_tensor_tensor(
                                out=y_t[:, vb, :], in0=o_t[:, vb, :], scalar=wcol, in1=r_t[:], op0=ALU.mult, op1=ALU.mult),
                                reads=[o_t.res, r_t.res, wres], writes=[y_t.res])
                    P.dma("sp", yT[ybase + hh * DV:ybase + (hh + 1) * DV, tok:tok + 512].rearrange("(b p) t -> p b t", p=128),
                          y_t[:], reads=[y_t.res])
        P.barrier()

    ar_small = {}

    def s5_phase(l):
        ar.reset()
        prm = ar.alloc([128, 3, 32], F32)
        P.dma("sp", prm[:], s5p[l], writes=[prm.res])
        sv = ar.alloc([128, 2, 8], F32)
        P.dma("sp", sv[:], s5v[l], writes=[sv.res])
        Bre = ar.alloc([128, 32, 128], BF16)
        Bim = ar.alloc([128, 32, 128], BF16)
        P.dma("pool", Bre[:], s5b[l, 0], writes=[Bre.res])
        P.dma("pool", Bim[:], s5b[l, 1], writes=[Bim.res])
        Cre = ar.alloc([128, 32, 128], BF16)
        Cim = ar.alloc([128, 32, 128], BF16)
        sm = {n: ar.alloc([128, 32], F32) for n in
              ("lr", "dt", "r", "th", "thn", "a", "a2", "sn", "cs", "are", "aim", "den", "fre", "fim", "t1", "t2")}
        mark = ar.off
        c0 = ar.alloc([128, 32, 128], F32)
        c1 = ar.alloc([128, 32, 128], F32)
        c2 = ar.alloc([128, 32, 128], F32)
        P.dma("sp", c0[:], s5c[l, 0], writes=[c0.res])
        P.dma("sp", c1[:], s5c[l, 1], writes=[c1.res])

        def dv(fn, rd, wr):
            P.op("dve", fn, reads=[x.res for x in rd], writes=[x.res for x in wr])

        def ac(fn, rd, wr):
            P.op("act", fn, reads=[x.res for x in rd], writes=[x.res for x in wr])
        s = sm
        dv(lambda h: h.tensor_scalar_min(out=s["lr"][:], in0=prm[:, 0, :], scalar1=-1e-4), [prm], [s["lr"]])
        ac(lambda h: h.activation(out=s["dt"][:], in_=prm[:, 2, :], func=AF.Exp), [prm], [s["dt"]])
        dv(lambda h: h.tensor_tensor(out=s["t1"][:], in0=s["lr"][:], in1=s["dt"][:], op=ALU.mult), [s["lr"], s["dt"]], [s["t1"]])
        ac(lambda h: h.activation(out=s["r"][:], in_=s["t1"][:], func=AF.Exp), [s["t1"]], [s["r"]])
        dv(lambda h: h.tensor_tensor(out=s["th"][:], in0=prm[:, 1, :], in1=s["dt"][:], op=ALU.mult), [prm, s["dt"]], [s["th"]])

        I32 = mybir.dt.int32
        SIN_SCALE = 6.2831845

        def sincos(y, ki, fr, tq, f2, sn, cs):
            MAGIC = 12582912.0
            dv(lambda h: h.tensor_scalar(out=ki[:], in0=y[:], scalar1=MAGIC, scalar2=None, op0=ALU.add), [y], [ki])
            dv(lambda h: h.tensor_scalar(out=ki[:], in0=ki[:], scalar1=MAGIC, scalar2=None, op0=ALU.subtract), [ki], [ki])
            dv(lambda h: h.tensor_sub(out=fr[:], in0=y[:], in1=ki[:]), [y, ki], [fr])
            ac(lambda h: h.activation(out=sn[:], in_=fr[:], func=AF.Sin, scale=SIN_SCALE), [fr], [sn])
            ac(lambda h: h.activation(out=f2[:], in_=fr[:], func=AF.Sin, scale=0.5 * SIN_SCALE), [fr], [f2])
            dv(lambda h: h.tensor_tensor(out=tq[:], in0=f2[:], in1=f2[:], op=ALU.mult), [f2], [tq])
            dv(lambda h: h.tensor_scalar(out=cs[:], in0=tq[:], scalar1=-2.0, scalar2=1.0, op0=ALU.mult, op1=ALU.add), [tq], [cs])

        dv(lambda h: h.tensor_scalar(out=s["thn"][:], in0=s["th"][:], scalar1=1.0 / TWO_PI, scalar2=None, op0=ALU.mult),
           [s["th"]], [s["thn"]])
        sincos(s["thn"], s["a"], s["a2"], s["t1"], s["t2"], s["sn"], s["cs"])
        dv(lambda h: h.scalar_tensor_tensor(out=s["are"][:], in0=s["cs"][:], scalar=1.0, in1=s["r"][:], op0=ALU.mult, op1=ALU.mult),
           [s["cs"], s["r"]], [s["are"]])
        dv(lambda h: h.scalar_tensor_tensor(out=s["aim"][:], in0=s["sn"][:], scalar=1.0, in1=s["r"][:], op0=ALU.mult, op1=ALU.mult),
           [s["sn"], s["r"]], [s["aim"]])
        dv(lambda h: h.tensor_tensor(out=s["den"][:], in0=s["lr"][:], in1=s["lr"][:], op=ALU.mult), [s["lr"]], [s["den"]])
        dv(lambda h: h.tensor_tensor(out=s["t1"][:], in0=prm[:, 1, :], in1=prm[:, 1, :], op=ALU.mult), [prm], [s["t1"]])
        dv(lambda h: h.tensor_add(out=s["den"][:], in0=s["den"][:], in1=s["t1"][:]), [s["den"], s["t1"]], [s["den"]])
        dv(lambda h: h.reciprocal(out=s["den"][:], in_=s["den"][:]), [s["den"]], [s["den"]])
        dv(lambda h: h.tensor_scalar_add(out=s["t2"][:], in0=s["are"][:], scalar1=-1.0), [s["are"]], [s["t2"]])
        dv(lambda h: h.tensor_tensor(out=s["fre"][:], in0=s["t2"][:], in1=s["lr"][:], op=ALU.mult), [s["t2"], s["lr"]], [s["fre"]])
        dv(lambda h: h.tensor_tensor(out=s["t1"][:], in0=s["aim"][:], in1=prm[:, 1, :], op=ALU.mult), [s["aim"], prm], [s["t1"]])
        dv(lambda h: h.tensor_add(out=s["fre"][:], in0=s["fre"][:], in1=s["t1"][:]), [s["fre"], s["t1"]], [s["fre"]])
        dv(lambda h: h.tensor_tensor(out=s["fre"][:], in0=s["fre"][:], in1=s["den"][:], op=ALU.mult), [s["fre"], s["den"]], [s["fre"]])
        dv(lambda h: h.tensor_tensor(out=s["fim"][:], in0=s["aim"][:], in1=s["lr"][:], op=ALU.mult), [s["aim"], s["lr"]], [s["fim"]])
        dv(lambda h: h.tensor_tensor(out=s["t1"][:], in0=s["t2"][:], in1=prm[:, 1, :], op=ALU.mult), [s["t2"], prm], [s["t1"]])
        dv(lambda h: h.tensor_sub(out=s["fim"][:], in0=s["fim"][:], in1=s["t1"][:]), [s["fim"], s["t1"]], [s["fim"]])
        dv(lambda h: h.tensor_tensor(out=s["fim"][:], in0=s["fim"][:], in1=s["den"][:], op=ALU.mult), [s["fim"], s["den"]], [s["fim"]])
        bc = lambda t: t[:].unsqueeze(2).to_broadcast([128, 32, 128])
        dv(lambda h: h.tensor_tensor(out=c2[:], in0=c0[:], in1=bc(s["fre"]), op=ALU.mult), [c0, s["fre"]], [c2])
        P.op("pool", lambda h: h.tensor_tensor(out=Cre[:], in0=c1[:], in1=bc(s["fim"]), op=ALU.mult),
             reads=[c1.res, s["fim"].res], writes=[Cre.res])
        dv(lambda h: h.tensor_sub(out=Cre[:], in0=c2[:], in1=Cre[:]), [c2, Cre], [Cre])
        dv(lambda h: h.tensor_tensor(out=c2[:], in0=c0[:], in1=bc(s["fim"]), op=ALU.mult), [c0, s["fim"], Cre], [c2])
        P.op("pool", lambda h: h.tensor_tensor(out=c0[:], in0=c1[:], in1=bc(s["fre"]), op=ALU.mult),
             reads=[c1.res, s["fre"].res, c2.res], writes=[c0.res])
        dv(lambda h: h.scalar_tensor_tensor(out=Cim[:], in0=c2[:], scalar=-1.0, in1=c0[:], op0=ALU.mult, op1=ALU.subtract),
           [c2, c0], [Cim])
        P.barrier()
        ar.reset(mark)
        tidx = ar.alloc([128, TS], F32)
        P.op("pool", lambda h: h.iota(tidx[:], pattern=[[1, TS]], base=0, channel_multiplier=0,
                                      allow_small_or_imprecise_dtypes=True), writes=[tidx.res])
        tabA = [ar.alloc([128, TS], F32) for _ in range(4)]
        tabB = [ar.alloc([128, TS], F32) for _ in range(4)]
        wsets = []
        for _ in range(2):
            wsets.append((ar.alloc([128, TS], F32), ar.alloc([128, TS], F32), ar.alloc([128, TS], F32), ar.alloc([128, TS], F32),
                          ar.alloc([128, TS], F32), ar.alloc([128, TS], F32), ar.alloc([128, TS], BF16), ar.alloc([128, TS], BF16),
                          ar.alloc([128, TS], F32), ar.alloc([128, TS], F32)))
        kre, kim, k2re, k2im, t1, t2, sre, sim, t3, t4 = wsets[0]
        sit = [0]
        uall = ar.alloc([128, TS], BF16)
        yv = Rot([ar.alloc([128, 512], F32) for _ in range(2)])
        y2 = Rot([ar.alloc([128, 512], F32) for _ in range(2)])
        zo = Rot([ar.alloc([128, 512], BF16) for _ in range(2)])
        send = ar.alloc([128, 32, 4], F32)
        rbc = {}
        for fb in range(8):
            for j in range(4):
                sb = fb * 4 + j
                A, B = tabA[j], tabB[j]
                thn = s["thn"][:, sb:sb + 1]
                dv(lambda h, thn=thn: h.tensor_scalar(out=t1[:], in0=tidx[:], scalar1=thn, scalar2=None, op0=ALU.mult),
                   [tidx, s["thn"]], [t1])
                sincos(t1, k2re, t2, k2im, kre, B, A)
            for sp in range(NSP):
                t0 = sp * TS
                P.dma("sp", uall[:], hU[fb * 128:(fb + 1) * 128, t0:t0 + TS], writes=[uall.res])
                yb_ = [next_bank_in(0, 4) for _ in range(TS // 512)]
                def it_gen(j):
                    sb = fb * 4 + j
                    yield
                    A, B = tabA[j], tabB[j]
                    yield
                    kre, kim, k2re, k2im, t1, t2, sre, sim, t3, t4 = wsets[j % 2]
                    yield
                    for tt in range(TS // 512):
                        sl = slice(tt * 512, (tt + 1) * 512)
                        b1 = next_bank_in(4, 8)
                        mm_group(b1, ps_ap(b1), [(Bre[:, sb, :], uall[:, sl])], [Bre.res, uall.res])
                        P.op("act", lambda h, b1=b1, sl=sl: h.activation(out=kre[:, sl], in_=ps_ap(b1), func=AF.Copy),
                             reads=[banks[b1]], writes=[kre.res])
                        b2 = next_bank_in(4, 8)
                        mm_group(b2, ps_ap(b2), [(Bim[:, sb, :], uall[:, sl])], [Bim.res, uall.res])
                        P.op("act", lambda h, b2=b2, sl=sl: h.activation(out=kim[:, sl], in_=ps_ap(b2), func=AF.Copy),
                             reads=[banks[b2]], writes=[kim.res])
                    yield
                    dv(lambda h, A=A: h.tensor_tensor(out=t1[:], in0=A[:], in1=kre[:], op=ALU.mult), [A, kre], [t1])
                    yield
                    P.op("pool", lambda h, B=B: h.tensor_tensor(out=t2[:], in0=B[:], in1=kim[:], op=ALU.mult),
                         reads=[B.res, kim.res], writes=[t2.res])
                    yield
                    P.op("pool", lambda h, A=A: h.tensor_tensor(out=t3[:], in0=A[:], in1=kim[:], op=ALU.mult),
                         reads=[A.res, kim.res], writes=[t3.res])
                    yield
                    dv(lambda h, B=B: h.tensor_tensor(out=t4[:], in0=B[:], in1=kre[:], op=ALU.mult), [B, kre], [t4])
                    yield
                    dv(lambda h: h.tensor_add(out=k2re[:], in0=t1[:], in1=t2[:]), [t1, t2], [k2re])
                    yield
                    dv(lambda h: h.tensor_sub(out=k2im[:], in0=t3[:], in1=t4[:]), [t3, t4], [k2im])
                    yield
                    yield
                    rb = s["r"][:, sb:sb + 1].to_broadcast([128, TS])
                    yield
                    if sp == 0:
                        i_re, i_im = 0.0, 0.0
                        rd_i = []
                    else:
                        se = send[:, sb, :]
                        dv(lambda h, se=se, A=A: h.tensor_tensor(out=se[:, 2:3], in0=A[:, 1:2], in1=se[:, 0:1], op=ALU.mult), [A, send], [send])
                        dv(lambda h, se=se, B=B: h.tensor_tensor(out=se[:, 3:4], in0=B[:, 1:2], in1=se[:, 1:2], op=ALU.mult), [B, send], [send])
                        dv(lambda h, se=se: h.tensor_sub(out=se[:, 2:3], in0=se[:, 2:3], in1=se[:, 3:4]), [send], [send])
                        dv(lambda h, se=se, A=A: h.tensor_tensor(out=se[:, 3:4], in0=A[:, 1:2], in1=se[:, 1:2], op=ALU.mult), [A, send], [send])
                        dv(lambda h, se=se, B=B: h.scalar_tensor_tensor(out=se[:, 3:4], in0=B[:, 1:2], scalar=se[:, 0:1], in1=se[:, 3:4],
                                                                       op0=ALU.mult, op1=ALU.add), [B, send], [send])
                        i_re, i_im = se[:, 2:3], se[:, 3:4]
                        rd_i = [send]
                    yield
                    dv(lambda h, rb=rb, i_re=i_re: h.tensor_tensor_scan(out=kre[:], data0=rb, data1=k2re[:], initial=i_re,
                                                                        op0=ALU.mult, op1=ALU.add), [s["r"], k2re, kre] + rd_i, [kre])
                    yield
                    dv(lambda h, rb=rb, i_im=i_im: h.tensor_tensor_scan(out=kim[:], data0=rb, data1=k2im[:], initial=i_im,
                                                                        op0=ALU.mult, op1=ALU.add), [s["r"], k2im, kim] + rd_i, [kim])
                    yield
                    dv(lambda h, A=A: h.tensor_tensor(out=t1[:], in0=A[:], in1=kre[:], op=ALU.mult), [A, kre], [t1])
                    yield
                    P.op("pool", lambda h, B=B: h.tensor_tensor(out=t2[:], in0=B[:], in1=kim[:], op=ALU.mult),
                         reads=[B.res, kim.res], writes=[t2.res])
                    yield
                    P.op("pool", lambda h, A=A: h.tensor_tensor(out=t3[:], in0=A[:], in1=kim[:], op=ALU.mult),
                         reads=[A.res, kim.res], writes=[t3.res])
                    yield
                    dv(lambda h, B=B: h.tensor_tensor(out=t4[:], in0=B[:], in1=kre[:], op=ALU.mult), [B, kre], [t4])
                    yield
                    dv(lambda h: h.tensor_sub(out=sre[:], in0=t1[:], in1=t2[:]), [t1, t2], [sre])
                    yield
                    dv(lambda h: h.tensor_add(out=sim[:], in0=t3[:], in1=t4[:]), [t3, t4], [sim])
                    yield
                    if NSP > 1:
                        dv(lambda h, sb=sb: h.tensor_sub(out=send[:, sb, 0:1], in0=t1[:, TS - 1:TS], in1=t2[:, TS - 1:TS]), [t1, t2], [send])
                        dv(lambda h, sb=sb: h.tensor_add(out=send[:, sb, 1:2], in0=t3[:, TS - 1:TS], in1=t4[:, TS - 1:TS]), [t3, t4], [send])
                    yield
                    for tt in range(TS // 512):
                        sl = slice(tt * 512, (tt + 1) * 512)
                        mm_group(yb_[tt], ps_ap(yb_[tt]), [(Cre[:, sb, :], sre[:, sl]), (Cim[:, sb, :], sim[:, sl])],
                                 [Cre.res, Cim.res, sre.res, sim.res], start=(j == 0), stop=(j == 3))
                    yield
                for jp in (0, 2):
                    gens = [it_gen(jp), it_gen(jp + 1)]
                    live = list(gens)
                    while live:
                        for g_ in list(live):
                            try:
                                next(g_)
                            except StopIteration:
                                live.remove(g_)
                for tt in range(TS // 512):
                    sl = slice(tt * 512, (tt + 1) * 512)
                    y_ = yv.next(); w_ = y2.next(); z_ = zo.next()
                    b = yb_[tt]
                    P.op("dve", lambda h, b=b, y_=y_, sl=sl, fb=fb: h.scalar_tensor_tensor(
                        out=y_[:], in0=uall[:, sl], scalar=sv[:, 0, fb:fb + 1], in1=ps_ap(b), op0=ALU.mult, op1=ALU.add),
                        reads=[uall.res, sv.res, banks[b]], writes=[y_.res])
                    P.op("pool", lambda h, y_=y_, w_=w_: h.tensor_tensor(out=w_[:], in0=y_[:], in1=y_[:], op=ALU.mult),
                         reads=[y_.res], writes=[w_.res])
                    P.op("pool", lambda h, w_=w_: h.tensor_scalar(out=w_[:], in0=w_[:], scalar1=0.044715, scalar2=1.0,
                                                                  op0=ALU.mult, op1=ALU.add), reads=[w_.res], writes=[w_.res])
                    P.op("pool", lambda h, y_=y_, w_=w_: h.tensor_tensor(out=w_[:], in0=w_[:], in1=y_[:], op=ALU.mult),
                         reads=[y_.res, w_.res], writes=[w_.res])
                    P.op("act", lambda h, w_=w_: h.activation(out=w_[:], in_=w_[:], func=AF.Sigmoid, scale=1.5957691216057308),
                         reads=[w_.res], writes=[w_.res])
                    dv(lambda h, y_=y_, w_=w_, z_=z_: h.tensor_tensor(out=z_[:], in0=y_[:], in1=w_[:], op=ALU.mult), [y_, w_], [z_])
                    P.dma("sp", zT[fb * 128:(fb + 1) * 128, t0 + tt * 512:t0 + (tt + 1) * 512], z_[:], reads=[z_.res])
        P.barrier()

    def make_epi_win(l):
        def epi(bl, tag, c0, rows, tok0, aux):
            b = bl[0]
            if tag == "ua":
                evac_store(b, rows, 512, None, 1.0, BF16, hU[c0 - O_UA:c0 - O_UA + rows, tok0:tok0 + 512], aux)
            elif tag == "qb":
                evac_store(b, rows, 512, None, 128.0 ** -0.5, BF16, hQB[c0 - O_QB:c0 - O_QB + rows, tok0:tok0 + 512], aux)
            elif tag == "kb":
                evac_store(b, rows, 512, None, 1.0, BF16, hKB[c0 - O_KB:c0 - O_KB + rows, tok0:tok0 + 512], aux)
            elif tag == "gl":
                evac_store(b, rows, 512, None, 1.0, F32, hGL[0:16, tok0:tok0 + 512], aux, eng="act")
            elif tag == "gb":
                evac_store(b, rows, 512, AF.Silu, 1.0, BF16, hGB[c0 - O_GB:c0 - O_GB + rows, tok0:tok0 + 512], aux)
            elif tag == "qc":
                evac_store(b, rows, 512, AF.Silu, 1.0, BF16, hQC[c0 - O_QC:c0 - O_QC + rows, tok0:tok0 + 512], aux)
            elif tag == "fc":
                evac_store(b, rows, 512, None, 1.0, F32, hFC[c0 - O_FC:c0 - O_FC + rows, tok0:tok0 + 512], aux)
            elif tag == "gc":
                evac_store(b, rows, 512, AF.Sigmoid, 1.0, BF16, hGC[c0 - O_GC:c0 - O_GC + rows, tok0:tok0 + 512], aux)
            elif tag == "mg":
                evac_store(b, rows, 512, AF.Sigmoid, 1.0, BF16, hMG[c0 - O_MG:c0 - O_MG + rows, tok0:tok0 + 512], aux)
            elif tag == "vb":
                evac_store(b, 128, rows, None, 1.0, BF16, hVB[tok0:tok0 + 128, c0 - O_VB:c0 - O_VB + rows], aux)
            elif tag == "ic":
                evac_store(b, 128, rows, None, 1.0, BF16, hIC[tok0:tok0 + 128, c0 - O_IC:c0 - O_IC + rows], aux, eng="act")
        return epi

    def seg(off, width, tag):
        return [(off + i * 512, min(512, width - i * 512), tag) for i in range((width + 511) // 512)]

    def layer(l, last):
        TPh = min(2048, T)
        fm_cols = (seg(O_UA, 1024, "ua") + seg(O_QB, 512, "qb") + seg(O_KB, 512, "kb") + seg(O_GL, 16, "gl") +
                   seg(O_GB, 1024, "gb") + seg(O_QC, 1024, "qc") + seg(O_FC, 1024, "fc") + seg(O_GC, 1024, "gc") +
                   seg(O_MG, 3 * D, "mg"))
        gemm_phase(xT, D, w_in[l], fm_cols, "fm", make_epi_win(l), TPh)
        gemm_phase(xT, D, w_in[l], seg(O_VB, 1024, "vb") + seg(O_IC, 1024, "ic"), "tm", make_epi_win(l), TPh)
        s5_phase(l)
        gla_like(l, "gla")
        ar.reset()
        gla_like_h(l)
        def epi_glu(bl, tag, c0, rows, tok0, aux):
            b = bl[0]
            sf = aux["sf"].next(); g = aux["gb"].next(); so = aux["sb"].next()
            fbk, r0 = c0 // 128, c0 % 128
            P.op("act", lambda h: h.activation(out=sf[0:rows, :], in_=ps_ap(b, rows), func=AF.Sigmoid,
                                               bias=glu_b[:, fbk:fbk + 1]), reads=[banks[b], glu_b.res], writes=[sf.res])
            P.dma("sp", g[0:rows, 0, :], zT[c0:c0 + rows, tok0:tok0 + 512], writes=[g.res])
            P.op("dve", lambda h: h.tensor_tensor(out=so[0:rows, :], in0=sf[0:rows, :], in1=g[0:rows, 0, :], op=ALU.mult),
                 reads=[sf.res, g.res], writes=[so.res])
            P.dma("sp", yT[c0:c0 + rows, tok0:tok0 + 512], so[0:rows, :], reads=[so.res])
        glu_b = glu_holder["t"]
        P.dma("sp", glu_b[:], s5v[l, :, 1, :], writes=[glu_b.res])
        gemm_phase(zT, W, w_glu[l], seg(0, W, "glu"), "fm", epi_glu, TPh)
        def epi_up(bl, tag, c0, rows, tok0, aux):
            g = aux["gb"].next(); xf = aux["xf"].next(); so = aux["sb"].next()
            P.dma("sp", g[:], hMG[:, tok0:tok0 + 512].rearrange("(b n) t -> n b t", b=3)[c0:c0 + 128], writes=[g.res])
            for i in range(3):
                P.op("dve", lambda h, i=i: h.tensor_tensor(out=xf[:, i, :], in0=ps_ap(bl[i]), in1=g[:, i, :], op=ALU.mult),
                     reads=[banks[bl[i]], g.res], writes=[xf.res])
            P.op("pool", lambda h: h.tensor_add(out=xf[:, 0, :], in0=xf[:, 0, :], in1=xf[:, 1, :]), reads=[xf.res], writes=[xf.res])
            P.op("pool", lambda h: h.tensor_add(out=so[:], in0=xf[:, 0, :], in1=xf[:, 2, :]), reads=[xf.res], writes=[so.res])
            P.dma("sp", mT[c0:c0 + 128, tok0:tok0 + 512], so[:], reads=[so.res])
        gemm_phase(yT, 3 * W, w_up[l], seg(0, D, "up"), "fm", epi_up, min(1024, T), kgroups=[(0, 8), (8, 16), (16, 24)])
        def make_epi_res(xsrc):
            def epi(bl, tag, c0, width, tok0, aux):
                b = bl[0]
                xf = aux["xf"].next(); sf = aux["sf"].next()
                P.dma("sp", xf[:, 0, 0:width], xsrc[tok0:tok0 + 128, c0:c0 + width], writes=[xf.res])
                P.op("dve", lambda h: h.scalar_tensor_tensor(out=sf[:, 0:width], in0=xf[:, 0, 0:width], scalar=float(ALPHA),
                                                             in1=ps_ap(b, 128, width), op0=ALU.mult, op1=ALU.add),
                     reads=[xf.res, banks[b]], writes=[sf.res])
                P.dma("sp", zF[tok0:tok0 + 128, c0:c0 + width], sf[:, 0:width], reads=[sf.res])
            return epi
        xsrc = x_in if l == 0 else xF
        gemm_phase(mT, D, w_out[l], seg(0, D, "o"), "tm", make_epi_res(xsrc), TPh)
        ln_phase(zF, True, l, 0, xF, None)
        def epi_m1(bl, tag, c0, rows, tok0, aux):
            b = bl[0]
            sf = aux["sf"].next(); so = aux["sb"].next()
            P.op("act", lambda h: h.activation(out=sf[:], in_=ps_ap(b), func=AF.Relu), reads=[banks[b]], writes=[sf.res])
            P.op("pool", lambda h: h.tensor_tensor(out=so[:], in0=sf[:], in1=sf[:], op=ALU.mult), reads=[sf.res], writes=[so.res])
            P.dma("sp", hT[c0:c0 + 128, tok0:tok0 + 512], so[:], reads=[so.res])
        gemm_phase(xT, D, w_m1[l], seg(0, HID, "m1"), "fm", epi_m1, TPh)
        gemm_phase(hT, HID, w_m2[l], seg(0, D, "m2"), "tm", make_epi_res(xF), 1024, kbp=8)
        ln_phase(zF, True, l, 1, xF, out if last else None)

    glu_holder = {}

    def gla_like_h(l):
        gla_like(l, "hgrn")

    glu_holder["t"] = ar.alloc([128, 8], F32)
    oml = ar.alloc([128, 8], F32)
    ar_small["oml"] = oml
    ar.base = ar.off

    c_setup()
    ln_phase(x_in, False, 0, 0, None, None)
    for l in range(L):
        P.op("dve", lambda h, l=l: h.tensor_scalar(out=oml[:], in0=lb_all[:, :, l], scalar1=-1.0, scalar2=1.0,
                                                   op0=ALU.mult, op1=ALU.add), reads=[lb_all.res], writes=[oml.res])
        layer(l, l == L - 1)
    P.stopped = False
    P._barrier()
    P.emit()
    st.close()
    return nc


def prep_weights(inp, L):
    f = np.float32
    g = {}
    g["w_in"] = np.ascontiguousarray(inp["w_in"][:L], dtype=f)
    g["w_glu"] = np.ascontiguousarray(inp["s5_w_glu"][:L], dtype=f)
    g["w_up"] = np.ascontiguousarray(inp["w_up"][:L], dtype=f).reshape(L, 3 * W, D)
    g["w_out"] = np.ascontiguousarray(inp["w_out"][:L], dtype=f)
    g["w_m1"] = np.ascontiguousarray(inp["w_mlp_in"][:L], dtype=f)
    g["w_m2"] = np.ascontiguousarray(inp["w_mlp_out"][:L], dtype=f)
    g["lnp"] = np.ascontiguousarray(np.stack([inp["ln1_g"][:L], inp["ln1_b"][:L], inp["ln2_g"][:L], inp["ln2_b"][:L]], axis=1), dtype=f)
    lam_re = np.asarray(inp["s5_lam_re"][:L], f).reshape(L, 32, 128).transpose(0, 2, 1)
    lam_im = np.asarray(inp["s5_lam_im"][:L], f).reshape(L, 32, 128).transpose(0, 2, 1)
    ldt = np.repeat(np.asarray(inp["s5_log_dt"][:L], f)[:, :, None], 64, axis=2).reshape(L, 32, 128).transpose(0, 2, 1)
    g["s5p"] = np.ascontiguousarray(np.stack([lam_re, lam_im, ldt], axis=2), dtype=f)
    bpad = np.zeros((L, 2, 128, 32, 128), f)
    cpad = np.zeros((L, 2, 128, 32, 128), f)
    for ri, (bk, ck) in enumerate((("s5_b_re", "s5_c_re"), ("s5_b_im", "s5_c_im"))):
        Bm = np.asarray(inp[bk][:L], f)
        Cm = np.asarray(inp[ck][:L], f)
        for sb in range(32):
            for gi in range(2):
                gidx = 2 * sb + gi
                r0 = (gidx % 8) * 16
                bpad[:, ri, r0:r0 + 16, sb, gi * 64:(gi + 1) * 64] = Bm[:, gidx].transpose(0, 2, 1)
                cpad[:, ri, gi * 64:(gi + 1) * 64, sb, r0:r0 + 16] = Cm[:, gidx].transpose(0, 2, 1)
    g["s5b"] = bpad
    g["s5c"] = cpad
    dsk = np.asarray(inp["s5_d"][:L], f).reshape(L, 8, 128).transpose(0, 2, 1)
    bgl = np.asarray(inp["s5_b_glu"][:L], f).reshape(L, 8, 128).transpose(0, 2, 1)
    g["s5v"] = np.ascontiguousarray(np.stack([dsk, bgl], axis=2), dtype=f)
    g["glaw"] = np.ascontiguousarray(inp["gla_w_gate"][:L], dtype=f)
    bg = np.asarray(inp["gla_b_gate"][:L], f).reshape(L, 4, 128).transpose(0, 2, 1)
    nw = np.asarray(inp["gla_norm_w"][:L], f).reshape(L, 2, 128).transpose(0, 2, 1)
    g["glav"] = np.ascontiguousarray(np.concatenate([bg, nw], axis=2), dtype=f)
    g["hlb"] = np.ascontiguousarray(np.asarray(inp["hgrn_lb_logits"], f).reshape(DEPTH, 8, 128).transpose(2, 1, 0), dtype=f)
    g["hnw"] = np.ascontiguousarray(np.asarray(inp["hgrn_norm_w"][:L], f).T, dtype=f)
    return g


_CACHE = {}


def kernel(**inputs):
    x = np.asarray(inputs["x"], np.float32)
    Bn, T, _ = x.shape
    key = (T, DEPTH)
    if key not in _CACHE:
        _CACHE[key] = build(T, DEPTH)
    nc = _CACHE[key]
    wts = prep_weights(inputs, DEPTH)
    in_maps = []
    for b in range(Bn):
        m = dict(wts)
        m["x"] = np.ascontiguousarray(x[b])
        in_maps.append(m)
    res = run_bass_kernel_spmd(nc, in_maps, core_ids=list(range(Bn)))
    return np.stack([np.asarray(r["out"], np.float32) for r in res.results], axis=0)
```

```python
import contextlib
import math
import types
import numpy as np
import concourse.bass as bass
import concourse.mybir as mybir
from concourse.bass_utils import run_bass_kernel_spmd

F32 = mybir.dt.float32
BF16 = mybir.dt.bfloat16
AF = mybir.ActivationFunctionType
ALU = mybir.AluOpType

D = 2048
W = 1024
NIN = 14352
HID = 8192
DEPTH = 4
ALPHA = (2 * DEPTH) ** 0.25
TWO_PI = 2.0 * math.pi
O_UA, O_QB, O_KB, O_VB, O_GL, O_GB, O_QC, O_FC, O_IC, O_GC, O_MG = (
    0, 1024, 1536, 2048, 3072, 3088, 4112, 5136, 6160, 7184, 8208)


def _freeze(fn):
    if fn.__closure__ is None:
        return fn
    cells = []
    for c in fn.__closure__:
        try:
            cells.append(types.CellType(c.cell_contents))
        except ValueError:
            cells.append(c)
    g = types.FunctionType(fn.__code__, fn.__globals__, fn.__name__, fn.__defaults__, tuple(cells))
    g.__kwdefaults__ = fn.__kwdefaults__
    return g


class Res:
    __slots__ = ("w", "r")

    def __init__(self):
        self.w = None
        self.r = {}


class Prog:
    KQ = 8
    ENGS = ("pe", "act", "dve", "pool", "sp")

    def __init__(self, nc, st):
        self.nc = nc
        self.eng = {}
        hs = dict(pe=nc.tensor, act=nc.scalar, dve=nc.vector, pool=nc.gpsimd, sp=nc.sync)
        for name in self.ENGS:
            sem = st.enter_context(nc.semaphore("sem_" + name))
            self.eng[name] = dict(h=hs[name], sem=sem, n=0, prog=[], waited={})
        self.dq = {}
        for q in ("sp", "pool", "act"):
            sems = [st.enter_context(nc.semaphore(f"dq_{q}_{i}")) for i in range(self.KQ)]
            self.dq[q] = dict(sems=sems, n=0)
        self.dma_uid = 0
        self.stopped = False
        self.nphase = 0
        self.max_phase = 10 ** 9

    def _waits(self, eng, reads, writes):
        E = self.eng[eng]
        deps = []
        for r in reads:
            if r.w is not None:
                deps.append(r.w)
        for w in writes:
            if w.w is not None:
                deps.append(w.w)
            deps.extend(w.r.values())
        waits = []
        for (sem, val, key, src) in deps:
            if src == "pe" and eng == "pe":
                continue
            if E["waited"].get(key, 0) >= val:
                continue
            E["waited"][key] = val
            waits.append((sem, val))
        return waits

    def _record(self, tok, reads, writes, rkey):
        for r in reads:
            r.r[rkey] = tok
        for w in writes:
            w.w = tok
            w.r = {}

    def op(self, eng, fn, reads=(), writes=()):
        if self.stopped:
            return None
        E = self.eng[eng]
        fn = _freeze(fn)
        waits = self._waits(eng, reads, writes)
        E["n"] += 1
        sem = E["sem"]
        tok = (sem, E["n"], "e_" + eng, eng)

        def emit(h, fn=fn, waits=waits, sem=sem):
            for s, v in waits:
                h.wait_ge(s, v)
            fn(h).then_inc(sem, 1)

        E["prog"].append(emit)
        self._record(tok, reads, writes, eng)
        return tok

    def dma(self, q, out, in_, reads=(), writes=()):
        if self.stopped:
            return None
        E = self.eng[q]
        Dq = self.dq[q]
        waits = self._waits(q, reads, writes)
        n = Dq["n"]
        Dq["n"] += 1
        s = Dq["sems"][n % self.KQ]
        val = 16 * (n // self.KQ + 1)
        key = f"dq_{q}_{n % self.KQ}"
        if n >= self.KQ and E["waited"].get(key, 0) < val - 16:
            E["waited"][key] = val - 16
            waits.append((s, val - 16))
        tok = (s, val, key, "dma")

        def emit(h, waits=waits, s=s, out=out, in_=in_):
            for ss, v in waits:
                h.wait_ge(ss, v)
            h.dma_start(out=out, in_=in_).then_inc(s, 16)

        E["prog"].append(emit)
        self.dma_uid += 1
        self._record(tok, reads, writes, "dma%d" % self.dma_uid)
        return tok

    def barrier(self):
        if self.stopped:
            return
        self.nphase += 1
        if self.nphase >= self.max_phase:
            self._barrier()
            self.stopped = True
            return
        self._barrier()

    def _barrier(self):
        targets = []
        for name in self.ENGS:
            X = self.eng[name]
            if X["n"] > 0:
                targets.append((X["sem"], X["n"], "e_" + name))
        for q, Dq in self.dq.items():
            n = Dq["n"]
            for i in range(min(n, self.KQ)):
                cnt = (n - 1 - i) // self.KQ + 1
                targets.append((Dq["sems"][i], 16 * cnt, f"dq_{q}_{i}"))
        for name in self.ENGS:
            E = self.eng[name]
            waits = []
            for (sem, val, key) in targets:
                if key == "e_" + name and name == "pe":
                    continue
                if E["waited"].get(key, 0) >= val:
                    continue
                E["waited"][key] = val
                waits.append((sem, val))

            def emit(h, waits=waits):
                for s, v in waits:
                    h.wait_ge(s, v)

            E["prog"].append(emit)

    def emit(self):
        nc = self.nc
        with nc.Block() as block:
            @block.tensor
            def _(h):
                for f in self.eng["pe"]["prog"]:
                    f(h)

            @block.scalar
            def _(h):
                for f in self.eng["act"]["prog"]:
                    f(h)

            @block.vector
            def _(h):
                for f in self.eng["dve"]["prog"]:
                    f(h)

            @block.gpsimd
            def _(h):
                for f in self.eng["pool"]["prog"]:
                    f(h)

            @block.sync
            def _(h):
                for f in self.eng["sp"]["prog"]:
                    f(h)


class Tile:
    def __init__(self, t):
        self.t = t
        self.res = Res()

    def __getitem__(self, k):
        return self.t[k]


class Arena:
    def __init__(self, nc, limit):
        self.nc = nc
        self.off = 16384
        self.limit = limit
        self.cnt = 0
        self.base = 16384

    def reset(self, to=None):
        self.off = self.base if to is None else to

    def alloc(self, shape, dtype):
        nbytes = int(np.prod(shape[1:])) * (4 if dtype == F32 else 2)
        nbytes = (nbytes + 63) // 64 * 64
        assert self.off + nbytes <= self.limit, (self.off, nbytes, self.limit)
        self.cnt += 1
        t = self.nc.alloc_sbuf_tensor_at("sb%d" % self.cnt, list(shape), dtype, offset=self.off)
        self.off += nbytes
        return Tile(t)


class K:
    pass


def build(T, L, TSPAN=1024, max_phase=10 ** 9):
    nc = bass.Bass("TRN2", target_bir_lowering=False)
    st = contextlib.ExitStack()
    P = Prog(nc, st)
    P.max_phase = max_phase
    TS = min(TSPAN, T)
    NSP = T // TS

    def din(name, shape, dt=F32):
        return nc.dram_tensor(name, list(shape), dt, kind="ExternalInput").ap()

    def dscr(name, shape, dt):
        import os
        kind = "ExternalOutput" if os.environ.get("KDEBUG") else "Internal"
        return nc.dram_tensor(name, list(shape), dt, kind=kind).ap()

    x_in = din("x", [T, D])
    w_in = din("w_in", [L, D, NIN])
    w_glu = din("w_glu", [L, W, W])
    w_up = din("w_up", [L, 3 * W, D])
    w_out = din("w_out", [L, D, D])
    w_m1 = din("w_m1", [L, D, HID])
    w_m2 = din("w_m2", [L, HID, D])
    lnp = din("lnp", [L, 4, D])
    s5p = din("s5p", [L, 128, 3, 32])
    s5b = din("s5b", [L, 2, 128, 32, 128])
    s5c = din("s5c", [L, 2, 128, 32, 128])
    s5v = din("s5v", [L, 128, 2, 8])
    glaw = din("glaw", [L, 16, 512])
    glav = din("glav", [L, 128, 6])
    hlb = din("hlb", [128, 8, DEPTH])
    hnw = din("hnw", [128, L])
    out = nc.dram_tensor("out", [T, D], F32, kind="ExternalOutput").ap()

    xF = dscr("xF", [T, D], F32)
    zF = dscr("zF", [T, D], F32)
    xT = dscr("xT", [D, T], BF16)
    hU = dscr("hU", [W, T], BF16)
    hQB = dscr("hQB", [512, T], BF16)
    hKB = dscr("hKB", [512, T], BF16)
    hGL = dscr("hGL", [16, T], F32)
    hGB = dscr("hGB", [W, T], BF16)
    hVB = dscr("hVB", [T, W], BF16)
    hQC = dscr("hQC", [W, T], BF16)
    hFC = dscr("hFC", [W, T], F32)
    hIC = dscr("hIC", [T, W], BF16)
    hGC = dscr("hGC", [W, T], BF16)
    hMG = dscr("hMG", [3 * D, T], BF16)
    zT = dscr("zT", [W, T], BF16)
    yT = dscr("yT", [3 * W, T], BF16)
    mT = dscr("mT", [D, T], BF16)
    hT = dscr("hT", [HID, T], BF16)

    SB_LIMIT = 192 * 1024
    ar = Arena(nc, SB_LIMIT)
    psum = nc.alloc_psum_tensor("psum", [128, 8, 512], F32)
    banks = [Res() for _ in range(8)]
    bank_i = [0]

    def next_bank():
        b = bank_i[0] % 8
        bank_i[0] += 1
        return b

    sub_i = {}

    def next_bank_in(lo, hi):
        i = sub_i.get((lo, hi), 0)
        sub_i[(lo, hi)] = i + 1
        return lo + i % (hi - lo)

    ident = ar.alloc([128, 128], BF16)
    ones_f = ar.alloc([128, 128], F32)
    mask = ar.alloc([128, 128], F32)
    ones_bf = ar.alloc([128, 128], BF16)
    ones_bf2 = ar.alloc([128, 128], BF16)
    cneg_pi = ar.alloc([128, 1], F32)
    ceps5 = ar.alloc([128, 1], F32)
    ceps6 = ar.alloc([128, 1], F32)
    lb_all = ar.alloc([128, 8, DEPTH], F32)
    hnw_t = ar.alloc([128, L], F32)
    ar.base = ar.off

    def c_setup():
        P.op("pool", lambda h: h.memset(ones_f[:], 1.0), writes=[ones_f.res])
        P.op("pool", lambda h: h.memset(cneg_pi[:], -math.pi), writes=[cneg_pi.res])
        P.op("pool", lambda h: h.memset(ceps5[:], 1e-5), writes=[ceps5.res])
        P.op("pool", lambda h: h.memset(ceps6[:], 1e-6), writes=[ceps6.res])
        P.op("pool", lambda h: h.memset(ones_bf[:], 1.0 / 128.0), writes=[ones_bf.res])
        P.op("pool", lambda h: h.memset(ones_bf2[:], 1.0 / 256.0), writes=[ones_bf2.res])
        P.op("pool", lambda h: h.affine_select(out=ident[:], in_=ones_f[:], pattern=[[-1, 128]], base=0,
                                               channel_multiplier=1, compare_op=ALU.is_equal, fill=0.0),
             reads=[ones_f.res], writes=[ident.res])
        P.op("pool", lambda h: h.affine_select(out=mask[:], in_=ones_f[:], pattern=[[1, 128]], base=0,
                                               channel_multiplier=-1, compare_op=ALU.is_ge, fill=0.0),
             reads=[ones_f.res], writes=[mask.res])
        P.op("pool", lambda h: h.memset(mask[0:64, 64:128], 0.0), writes=[mask.res])
        P.dma("sp", lb_all[:], hlb, writes=[lb_all.res])
        P.dma("sp", hnw_t[:], hnw, writes=[hnw_t.res])
        P.op("act", lambda h: h.activation(out=lb_all[:], in_=lb_all[:], func=AF.Exp),
             reads=[lb_all.res], writes=[lb_all.res])
        ssum = ar.alloc([128, 8, 1], F32)
        P.op("dve", lambda h: h.tensor_add(out=ssum[:], in0=lb_all[:, :, 0:1], in1=lb_all[:, :, 1:2]),
             reads=[lb_all.res], writes=[ssum.res])
        for j in (2, 3):
            P.op("dve", lambda h, j=j: h.tensor_add(out=ssum[:], in0=ssum[:], in1=lb_all[:, :, j:j + 1]),
                 reads=[lb_all.res, ssum.res], writes=[ssum.res])
        P.op("dve", lambda h: h.reciprocal(out=ssum[:], in_=ssum[:]), reads=[ssum.res], writes=[ssum.res])
        P.op("dve", lambda h: h.tensor_tensor(out=lb_all[:], in0=lb_all[:], in1=ssum[:].to_broadcast([128, 8, DEPTH]),
                                              op=ALU.mult), reads=[lb_all.res, ssum.res], writes=[lb_all.res])
        P.op("dve", lambda h: h.tensor_add(out=lb_all[:, :, 2:3], in0=lb_all[:, :, 2:3], in1=lb_all[:, :, 1:2]),
             reads=[lb_all.res], writes=[lb_all.res])
        P.op("dve", lambda h: h.tensor_add(out=lb_all[:, :, 3:4], in0=lb_all[:, :, 3:4], in1=lb_all[:, :, 2:3]),
             reads=[lb_all.res], writes=[lb_all.res])
        P.op("dve", lambda h: h.memset(lb_all[:, :, 0:1], 0.0), writes=[lb_all.res])
        P.barrier()

    def ps_ap(b, rows=128, n=512):
        return psum[0:rows, b, 0:n]

    class Rot:
        def __init__(self, tiles):
            self.tiles = tiles
            self.i = 0

        def next(self):
            t = self.tiles[self.i % len(self.tiles)]
            self.i += 1
            return t

    def mm_group(bank, out_ap, pairs, extra_reads, start=True, stop=True):
        def fn(h):
            ins = None
            n = len(pairs)
            for i, (l, r) in enumerate(pairs):
                ins = h.matmul(out_ap, l, r, start=(start and i == 0), stop=(stop and i == n - 1))
            return ins
        return P.op("pe", fn, reads=extra_reads, writes=[banks[bank]])

    def gemm_phase(act_src, Kd, w_src, cols, mode, epi, TP, kgroups=None, kbp=None):
        ar.reset()
        KB = Kd // 128
        nparts = (1 if KB <= 24 else KB // 16) if kbp is None else KB // kbp
        KBP = KB // nparts
        actA = ar.alloc([128, KB * TP], BF16)
        wb = Rot([ar.alloc([128, KBP, 512], BF16) for _ in range(2)])
        aux = dict(
            sf=Rot([ar.alloc([128, 512], F32) for _ in range(4)]),
            sb=Rot([ar.alloc([128, 512], BF16) for _ in range(4)]),
            xf=Rot([ar.alloc([128, 3 if mode == "fm" else 1, 512], F32) for _ in range(2)]),
            gb=Rot([ar.alloc([128, 3 if mode == "fm" else 1, 512 if mode == "fm" else 8], BF16) for _ in range(2)]),
        )
        NQ = TP // 512
        actq = [Res() for _ in range(NQ)]
        if kgroups is None:
            kgroups = [(0, KBP)]
        actv = actA[:].rearrange("p (k t) -> p k t", t=TP)
        for tp in range(T // TP):
            for q in range(NQ):
                P.dma("sp", actv[:, :, q * 512:(q + 1) * 512],
                      act_src[:, tp * TP + q * 512:tp * TP + (q + 1) * 512].rearrange("(k p) t -> p k t", p=128),
                      writes=[actq[q]])
            loads = [(ci, pa) for ci in range(len(cols)) for pa in range(nparts)]
            wtiles = {}

            def load(idx):
                ci, pa = loads[idx]
                off, width, tag = cols[ci]
                wt = wb.next()
                P.dma("pool", wt[:, :, 0:width],
                      w_src[pa * KBP * 128:(pa + 1) * KBP * 128, off:off + width].rearrange("(k p) n -> p k n", p=128),
                      writes=[wt.res])
                wtiles[idx] = wt

            load(0)
            for idx in range(len(loads)):
                if idx + 1 < len(loads):
                    load(idx + 1)
                ci, pa = loads[idx]
                off, width, tag = cols[ci]
                wt = wtiles.pop(idx)
                if mode == "fm":
                    assert nparts == 1
                    for nb in range((width + 127) // 128):
                        rows = min(128, width - nb * 128)
                        for tt in range(TP // 512):
                            bl = []
                            for (k0, k1) in kgroups:
                                b = next_bank()
                                pairs = [(wt[:, kb, nb * 128:nb * 128 + rows], actv[:, kb, tt * 512:(tt + 1) * 512])
                                         for kb in range(k0, k1)]
                                mm_group(b, ps_ap(b, rows), pairs, [wt.res, actq[tt]])
                                bl.append(b)
                            epi(bl, tag, off + nb * 128, rows, tp * TP + tt * 512, aux)
                else:
                    nt = TP // 128
                    if nparts == 1:
                        for t1 in range(nt):
                            b = next_bank()
                            pairs = [(actv[:, kb, t1 * 128:(t1 + 1) * 128], wt[:, kb, 0:width]) for kb in range(KBP)]
                            mm_group(b, ps_ap(b, 128, width), pairs, [wt.res, actq[t1 // 4]])
                            epi([b], tag, off, width, tp * TP + t1 * 128, aux)
                    else:
                        assert nt <= 8
                        if pa == 0:
                            cur = [next_bank() for _ in range(nt)]
                            wtiles["cur"] = cur
                        cur = wtiles["cur"]
                        for t1 in range(nt):
                            b = cur[t1]
                            pairs = [(actv[:, pa * KBP + kb, t1 * 128:(t1 + 1) * 128], wt[:, kb, 0:width])
                                     for kb in range(KBP)]
                            mm_group(b, ps_ap(b, 128, width), pairs, [wt.res, actq[t1 // 4]],
                                     start=(pa == 0), stop=(pa == nparts - 1))
                            if pa == nparts - 1:
                                epi([b], tag, off, width, tp * TP + t1 * 128, aux)
        P.barrier()

    def evac_store(b, rows, n, func, scale, dt, dest, aux, bias=None, eng=None):
        stg = (aux["sb"] if dt == BF16 else aux["sf"]).next()
        src = ps_ap(b, rows, n)
        if func is None and (eng or "dve") == "dve":
            P.op("dve", lambda h: h.tensor_scalar(out=stg[0:rows, 0:n], in0=src, scalar1=float(scale), scalar2=None,
                                                  op0=ALU.mult), reads=[banks[b]], writes=[stg.res])
        else:
            f = AF.Copy if func is None else func
            if bias is None:
                P.op("act", lambda h: h.activation(out=stg[0:rows, 0:n], in_=src, func=f, scale=float(scale)),
                     reads=[banks[b]], writes=[stg.res])
            else:
                P.op("act", lambda h: h.activation(out=stg[0:rows, 0:n], in_=src, func=f, scale=float(scale),
                                                   bias=bias), reads=[banks[b]], writes=[stg.res])
        P.dma("sp", dest, stg[0:rows, 0:n], reads=[stg.res])

    def ln_phase(src, norm, l, which, dstF, dstOut):
        ar.reset()
        if norm:
            g_bc = ar.alloc([128, D], F32)
            b_bc = ar.alloc([128, D], F32)
            P.dma("sp", g_bc[:], lnp[l, 2 * which, :].partition_broadcast(128), writes=[g_bc.res])
            P.dma("sp", b_bc[:], lnp[l, 2 * which + 1, :].partition_broadcast(128), writes=[b_bc.res])
        zt = Rot([ar.alloc([128, D], F32) for _ in range(4)])
        jk = Rot([ar.alloc([128, D], F32) for _ in range(2)])
        xb = Rot([ar.alloc([128, D], BF16) for _ in range(3)])
        xtt = Rot([ar.alloc([128, 16, 128], BF16) for _ in range(3)])
        stat = Rot([ar.alloc([128, 8], F32) for _ in range(4)])
        for tt in range(T // 128):
            z = zt.next()
            P.dma("sp", z[:], src[tt * 128:(tt + 1) * 128, :], writes=[z.res])
            xbt = xb.next()
            if norm:
                s = stat.next()
                j = jk.next()
                P.op("act", lambda h: h.activation(out=j[:], in_=z[:], func=AF.Copy, accum_out=s[:, 0:1]),
                     reads=[z.res], writes=[j.res, s.res])
                P.op("act", lambda h: h.activation(out=j[:], in_=z[:], func=AF.Square, accum_out=s[:, 1:2]),
                     reads=[z.res], writes=[j.res, s.res])
                P.op("dve", lambda h: h.tensor_scalar(out=s[:, 2:3], in0=s[:, 0:1], scalar1=1.0 / D, scalar2=None,
                                                      op0=ALU.mult), reads=[s.res], writes=[s.res])
                P.op("dve", lambda h: h.tensor_tensor(out=s[:, 3:4], in0=s[:, 2:3], in1=s[:, 2:3], op=ALU.mult),
                     reads=[s.res], writes=[s.res])
                P.op("dve", lambda h: h.scalar_tensor_tensor(out=s[:, 4:5], in0=s[:, 1:2], scalar=1.0 / D,
                                                             in1=s[:, 3:4], op0=ALU.mult, op1=ALU.subtract),
                     reads=[s.res], writes=[s.res])
                P.op("act", lambda h: h.activation(out=s[:, 5:6], in_=s[:, 4:5], func=AF.Ln, bias=ceps5[:, 0:1]),
                     reads=[s.res, ceps5.res], writes=[s.res])
                P.op("act", lambda h: h.activation(out=s[:, 5:6], in_=s[:, 5:6], func=AF.Exp, scale=-0.5),
                     reads=[s.res], writes=[s.res])
                P.op("dve", lambda h: h.tensor_scalar(out=z[:], in0=z[:], scalar1=s[:, 2:3], scalar2=s[:, 5:6],
                                                      op0=ALU.subtract, op1=ALU.mult),
                     reads=[z.res, s.res], writes=[z.res])
                P.op("dve", lambda h: h.tensor_tensor(out=z[:], in0=z[:], in1=g_bc[:], op=ALU.mult),
                     reads=[z.res, g_bc.res], writes=[z.res])
                P.op("dve", lambda h: h.tensor_tensor(out=z[:], in0=z[:], in1=b_bc[:], op=ALU.add),
                     reads=[z.res, b_bc.res], writes=[z.res])
                P.dma("sp", dstF[tt * 128:(tt + 1) * 128, :], z[:], reads=[z.res])
                if dstOut is not None:
                    P.dma("sp", dstOut[tt * 128:(tt + 1) * 128, :], z[:], reads=[z.res])
            P.op("act", lambda h: h.activation(out=xbt[:], in_=z[:], func=AF.Copy), reads=[z.res], writes=[xbt.res])
            xt_ = xtt.next()
            for half in range(2):
                b = next_bank()
                pv = psum[:, b, :].bitcast(BF16).rearrange("p (k t) -> p k t", t=128)

                def fn(h, b=b, pv=pv, half=half, xbt=xbt):
                    ins = None
                    for k in range(8):
                        kk = half * 8 + k
                        ins = h.transpose(pv[:, k, :], xbt[:, kk * 128:(kk + 1) * 128], ident[:])
                    return ins
                P.op("pe", fn, reads=[xbt.res, ident.res], writes=[banks[b]])
                eng = "dve" if half == 0 else "act"
                if eng == "dve":
                    P.op("dve", lambda h, pv=pv, half=half, xt_=xt_: h.tensor_copy(out=xt_[:, half * 8:half * 8 + 8, :], in_=pv),
                         reads=[banks[b]], writes=[xt_.res])
                else:
                    P.op("act", lambda h, pv=pv, half=half, xt_=xt_: h.activation(out=xt_[:, half * 8:half * 8 + 8, :], in_=pv, func=AF.Copy),
                         reads=[banks[b]], writes=[xt_.res])
            P.dma("sp", xT[:, tt * 128:(tt + 1) * 128].rearrange("(k p) t -> p k t", p=128), xt_[:], reads=[xt_.res])
        P.barrier()

    def gla_like(l, kind):
        ar.reset()
        NC = TS // 64
        NPC = TS // 128
        nh = 4 if kind == "gla" else 8
        nvb = 2 if kind == "gla" else 1
        DV = 128 * nvb
        gs = (-1.0 / 16.0) if kind == "gla" else 1.0
        mask0 = ar.alloc([128, TS], F32)
        P.op("pool", lambda h: h.memset(mask0[:], 1.0), writes=[mask0.res])
        P.op("pool", lambda h: h.memset(mask0[:].rearrange("p (c i) -> p c i", i=64)[:, :, 0:1], 0.0), writes=[mask0.res])
        gsets = []
        for _ in range(2):
            gsets.append((ar.alloc([128, TS], F32), ar.alloc([128, TS], F32), ar.alloc([128, TS], F32), ar.alloc([128, TS], F32),
                          ar.alloc([128, TS], F32), ar.alloc([128, TS], BF16), ar.alloc([128, TS], BF16), ar.alloc([128, TS], BF16),
                          ar.alloc([128, TS], BF16), ar.alloc([128, TS], BF16), ar.alloc([128, NC], F32),
                          ar.alloc([128, NPC, DV], BF16), ar.alloc([128, NPC, 128], BF16)))
        git = [0]
        KV = ar.alloc([128, NC, DV], F32)
        Sall = ar.alloc([128, NC + 1, DV], F32)
        Sbf = ar.alloc([128, NC, DV], BF16)
        AT = Rot([ar.alloc([128, 128], BF16) for _ in range(3)])
        og = Rot([ar.alloc([128, nvb, 512], F32) for _ in range(2)])
        sq = Rot([ar.alloc([128, nvb, 512], BF16) for _ in range(2)])
        gt = Rot([ar.alloc([128, nvb, 512], BF16) for _ in range(2)])
        rstd = Rot([ar.alloc([128, 512], F32) for _ in range(2)])
        yb = Rot([ar.alloc([128, nvb, 512], BF16) for _ in range(2)])
        glt = ar.alloc([16, TS], F32)
        wg = ar.alloc([16, 512], F32)
        gv = ar.alloc([128, 6], F32)
        nbg = ar.alloc([128, 4], F32)
        if kind == "gla":
            P.dma("sp", wg[:], glaw[l], writes=[wg.res])
            P.dma("sp", gv[:], glav[l], writes=[gv.res])
            P.op("dve", lambda h: h.tensor_scalar(out=nbg[:], in0=gv[:, 0:4], scalar1=-1.0, scalar2=None, op0=ALU.mult),
                 reads=[gv.res], writes=[nbg.res])
        for hh in range(nh):
            P.op("dve", lambda h: h.memset(Sall[:, 0, :], 0.0), writes=[Sall.res])
            for sp in range(NSP):
                t0 = sp * TS
                fr, gg, G, tmp, tmp2, qb, kb_, qt, kt, kpT, egl, vtok, kptok = gsets[git[0] % 2]
                git[0] += 1
                if sp > 0:
                    P.op("dve", lambda h: h.tensor_copy(out=Sall[:, 0, :], in_=Sall[:, NC, :]),
                         reads=[Sall.res], writes=[Sall.res])
                if kind == "gla":
                    if hh == 0:
                        pass
                    P.dma("sp", glt[:], hGL[:, t0:t0 + TS], writes=[glt.res])
                    for tt in range(TS // 512):
                        b = next_bank()
                        mm_group(b, ps_ap(b), [(wg[:, hh * 128:(hh + 1) * 128], glt[:, tt * 512:(tt + 1) * 512])],
                                 [wg.res, glt.res])
                        P.op("act", lambda h, b=b, tt=tt: h.activation(out=fr[:, tt * 512:(tt + 1) * 512], in_=ps_ap(b),
                                                                       func=AF.Exp, scale=-1.0, bias=nbg[:, hh:hh + 1]),
                             reads=[banks[b], nbg.res], writes=[fr.res])
                    P.op("act", lambda h: h.activation(out=gg[:], in_=fr[:], func=AF.Ln, bias=1.0, scale=1.0),
                         reads=[fr.res], writes=[gg.res])
                    P.dma("sp", qb[:], hQB[hh * 128:(hh + 1) * 128, t0:t0 + TS], writes=[qb.res])
                    P.dma("sp", kb_[:], hKB[hh * 128:(hh + 1) * 128, t0:t0 + TS], writes=[kb_.res])
                    kk = kb_
                    P.dma("sp", vtok[:], hVB[t0:t0 + TS, hh * DV:(hh + 1) * DV].rearrange("(c p) v -> p c v", p=128),
                          writes=[vtok.res])
                else:
                    P.dma("sp", fr[:], hFC[hh * 128:(hh + 1) * 128, t0:t0 + TS], writes=[fr.res])
                    P.op("act", lambda h: h.activation(out=fr[:], in_=fr[:], func=AF.Sigmoid), reads=[fr.res], writes=[fr.res])
                    oml = ar_small["oml"]
                    P.op("dve", lambda h: h.tensor_scalar(out=fr[:], in0=fr[:], scalar1=oml[:, hh:hh + 1],
                                                          scalar2=lb_all[:, hh, l:l + 1], op0=ALU.mult, op1=ALU.add),
                         reads=[fr.res, oml.res, lb_all.res], writes=[fr.res])
                    P.op("act", lambda h: h.activation(out=gg[:], in_=fr[:], func=AF.Ln), reads=[fr.res], writes=[gg.res])
                    P.op("dve", lambda h: h.tensor_scalar(out=fr[:], in0=fr[:], scalar1=-1.0, scalar2=1.0,
                                                          op0=ALU.mult, op1=ALU.add), reads=[fr.res], writes=[fr.res])
                    kk = fr
                    P.dma("sp", qb[:], hQC[hh * 128:(hh + 1) * 128, t0:t0 + TS], writes=[qb.res])
                    P.dma("sp", vtok[:], hIC[t0:t0 + TS, hh * DV:(hh + 1) * DV].rearrange("(c p) v -> p c v", p=128),
                          writes=[vtok.res])
                P.op("dve", lambda h: h.tensor_tensor_scan(out=G[:], data0=mask0[:], data1=gg[:], initial=0.0,
                                                           op0=ALU.mult, op1=ALU.add),
                     reads=[mask0.res, gg.res], writes=[G.res])
                Gv = G[:].rearrange("p (c i) -> p c i", i=64)
                P.op("act", lambda h: h.activation(out=tmp[:], in_=G[:], func=AF.Exp, scale=gs), reads=[G.res], writes=[tmp.res])
                P.op("dve", lambda h: h.tensor_tensor(out=qt[:], in0=qb[:], in1=tmp[:], op=ALU.mult),
                     reads=[qb.res, tmp.res], writes=[qt.res])
                P.op("act", lambda h: h.activation(out=tmp2[:], in_=G[:], func=AF.Exp, scale=-gs), reads=[G.res], writes=[tmp2.res])
                P.op("dve", lambda h, kk=kk: h.tensor_tensor(out=kt[:], in0=kk[:], in1=tmp2[:], op=ALU.mult),
                     reads=[kk.res, tmp2.res], writes=[kt.res])
                P.op("dve", lambda h: h.tensor_tensor(out=tmp[:].rearrange("p (c i) -> p c i", i=64),
                                                      in0=Gv[:, :, 63:64].to_broadcast([128, NC, 64]), in1=Gv,
                                                      op=ALU.subtract), reads=[G.res, tmp.res, qt.res], writes=[tmp.res])
                P.op("act", lambda h: h.activation(out=tmp[:], in_=tmp[:], func=AF.Exp, scale=gs), reads=[tmp.res], writes=[tmp.res])
                P.op("dve", lambda h, kk=kk: h.tensor_tensor(out=kpT[:], in0=kk[:], in1=tmp[:], op=ALU.mult),
                     reads=[kk.res, tmp.res], writes=[kpT.res])
                P.op("act", lambda h: h.activation(out=egl[:], in_=Gv[:, :, 63], func=AF.Exp, scale=gs),
                     reads=[G.res], writes=[egl.res])
                for g8 in range((NPC + 7) // 8):
                    b = next_bank()
                    pv = psum[:, b, :].bitcast(BF16).rearrange("p (k t) -> p k t", t=128)
                    n8 = min(8, NPC - g8 * 8)

                    def fn(h, pv=pv, g8=g8, n8=n8):
                        ins = None
                        for k in range(n8):
                            pc = g8 * 8 + k
                            ins = h.transpose(pv[:, k, :], kpT[:, pc * 128:(pc + 1) * 128], ident[:])
                        return ins
                    P.op("pe", fn, reads=[kpT.res, ident.res], writes=[banks[b]])
                    P.op("act", lambda h, pv=pv, g8=g8, n8=n8: h.activation(out=kptok[:, g8 * 8:g8 * 8 + n8, :], in_=pv[:, 0:n8, :],
                                                                            func=AF.Copy), reads=[banks[b]], writes=[kptok.res])
                per_bank = 512 // DV
                KVv = KV[:].rearrange("p (c two) v -> p c two v", two=2)
                for p0 in range(0, NPC, per_bank):
                    bA, bB = next_bank(), next_bank()
                    pA = psum[:, bA, :].rearrange("p (c v) -> p c v", v=DV)
                    pB = psum[:, bB, :].rearrange("p (c v) -> p c v", v=DV)

                    def fn(h, p0=p0, pA=pA, pB=pB):
                        ins = None
                        for i in range(per_bank):
                            pc = p0 + i
                            h.matmul(pA[:, i, :], kptok[0:64, pc, :], vtok[0:64, pc, :], start=True, stop=True)
                            ins = h.matmul(pB[:, i, :], kptok[64:128, pc, :], vtok[64:128, pc, :], start=True, stop=True)
                        return ins
                    P.op("pe", fn, reads=[kptok.res, vtok.res], writes=[banks[bA], banks[bB]])
                    P.op("dve", lambda h, p0=p0, pA=pA: h.tensor_copy(out=KVv[:, p0:p0 + per_bank, 0, :], in_=pA),
                         reads=[banks[bA]], writes=[KV.res])
                    P.op("act", lambda h, p0=p0, pB=pB: h.activation(out=KVv[:, p0:p0 + per_bank, 1, :], in_=pB, func=AF.Copy),
                         reads=[banks[bB]], writes=[KV.res])
                for c in range(NC):
                    P.op("dve", lambda h, c=c: h.scalar_tensor_tensor(out=Sall[:, c + 1, :], in0=Sall[:, c, :], scalar=egl[:, c:c + 1],
                                                                     in1=KV[:, c, :], op0=ALU.mult, op1=ALU.add),
                         reads=[egl.res, KV.res, Sall.res], writes=[Sall.res])
                P.op("act", lambda h: h.activation(out=Sbf[:], in_=Sall[:, 0:NC, :], func=AF.Copy),
                     reads=[Sall.res], writes=[Sbf.res])
                for tt in range(TS // 512):
                    ob = [next_bank() for _ in range(nvb)]
                    for p4 in range(4):
                        pc = tt * 4 + p4
                        bs = next_bank()
                        mm_group(bs, psum[:, bs, 0:128], [(kt[:, pc * 128:(pc + 1) * 128], qt[:, pc * 128:(pc + 1) * 128])],
                                 [kt.res, qt.res])
                        at = AT.next()
                        P.op("dve", lambda h, bs=bs, at=at: h.tensor_tensor(out=at[:], in0=psum[:, bs, 0:128], in1=mask[:], op=ALU.mult),
                             reads=[banks[bs], mask.res], writes=[at.res])
                        for vb in range(nvb):
                            def fn(h, vb=vb, pc=pc, p4=p4, at=at, ob=ob):
                                o_ap = psum[:, ob[vb], p4 * 128:(p4 + 1) * 128]
                                h.matmul(o_ap, vtok[:, pc, vb * 128:(vb + 1) * 128], at[:], start=True, stop=False)
                                h.matmul(o_ap[:, 0:64], Sbf[:, 2 * pc, vb * 128:(vb + 1) * 128], qt[:, pc * 128:pc * 128 + 64],
                                         start=False, stop=False)
                                return h.matmul(o_ap[:, 64:128], Sbf[:, 2 * pc + 1, vb * 128:(vb + 1) * 128],
                                                qt[:, pc * 128 + 64:pc * 128 + 128], start=False, stop=True)
                            P.op("pe", fn, reads=[vtok.res, at.res, Sbf.res, qt.res], writes=[banks[ob[vb]]])
                    o_t = og.next(); s_t = sq.next(); g_t = gt.next(); r_t = rstd.next(); y_t = yb.next()
                    gsrc = hGB if kind == "gla" else hGC
                    ybase = W if kind == "gla" else 2 * W
                    tok = t0 + tt * 512
                    P.dma("sp", g_t[:], gsrc[hh * DV:(hh + 1) * DV, tok:tok + 512].rearrange("(b p) t -> p b t", p=128),
                          writes=[g_t.res])
                    for vb in range(nvb):
                        if kind == "gla":
                            P.op("act", lambda h, vb=vb, o_t=o_t, ob=ob: h.activation(out=o_t[:, vb, :], in_=ps_ap(ob[vb]), func=AF.Copy),
                                 reads=[banks[ob[vb]]], writes=[o_t.res])
                        else:
                            P.op("dve", lambda h, vb=vb, o_t=o_t, ob=ob, g_t=g_t: h.tensor_tensor(out=o_t[:, vb, :], in0=ps_ap(ob[vb]),
                                                                                                in1=g_t[:, vb, :], op=ALU.mult),
                                 reads=[banks[ob[vb]], g_t.res], writes=[o_t.res])
                    P.op("dve", lambda h, o_t=o_t, s_t=s_t: h.tensor_tensor(out=s_t[:], in0=o_t[:], in1=o_t[:], op=ALU.mult),
                         reads=[o_t.res], writes=[s_t.res])
                    br = next_bank()
                    onesm = ones_bf if nvb == 1 else ones_bf2
                    mm_group(br, ps_ap(br), [(onesm[:], s_t[:, vb, :]) for vb in range(nvb)], [onesm.res, s_t.res])
                    P.op("act", lambda h, br=br, r_t=r_t: h.activation(out=r_t[:], in_=ps_ap(br), func=AF.Ln, bias=ceps6[:, 0:1]),
                         reads=[banks[br], ceps6.res], writes=[r_t.res])
                    P.op("act", lambda h, r_t=r_t: h.activation(out=r_t[:], in_=r_t[:], func=AF.Exp, scale=-0.5),
                         reads=[r_t.res], writes=[r_t.res])
                    for vb in range(nvb):
                        wcol = gv[:, 4 + vb:5 + vb] if kind == "gla" else hnw_t[:, l:l + 1]
                        wres = gv.res if kind == "gla" else hnw_t.res
                        if kind == "gla":
                            P.op("dve", lambda h, vb=vb, o_t=o_t, r_t=r_t, wcol=wcol: h.scalar_tensor_tensor(
                                out=o_t[:, vb, :], in0=o_t[:, vb, :], scalar=wcol, in1=r_t[:], op0=ALU.mult, op1=ALU.mult),
                                reads=[o_t.res, r_t.res, wres], writes=[o_t.res])
                            P.op("dve", lambda h, vb=vb, o_t=o_t, y_t=y_t, g_t=g_t: h.tensor_tensor(out=y_t[:, vb, :], in0=o_t[:, vb, :],
                                                                                                   in1=g_t[:, vb, :], op=ALU.mult),
                                 reads=[o_t.res, g_t.res], writes=[y_t.res])
                        else:
                            P.op("dve", lambda h, vb=vb, o_t=o_t, r_t=r_t, wcol=wcol, y_t=y_t: h.scalar_tensor_tensor(
                                out=y_t[:, vb, :], in0=o_t[:, vb, :], scalar=wcol, in1=r_t[:], op0=ALU.mult, op1=ALU.mult),
                                reads=[o_t.res, r_t.res, wres], writes=[y_t.res])
                    P.dma("sp", yT[ybase + hh * DV:ybase + (hh + 1) * DV, tok:tok + 512].rearrange("(b p) t -> p b t", p=128),
                          y_t[:], reads=[y_t.res])
        P.barrier()

    ar_small = {}

    def s5_phase(l):
        ar.reset()
        prm = ar.alloc([128, 3, 32], F32)
        P.dma("sp", prm[:], s5p[l], writes=[prm.res])
        sv = ar.alloc([128, 2, 8], F32)
        P.dma("sp", sv[:], s5v[l], writes=[sv.res])
        Bre = ar.alloc([128, 32, 128], BF16)
        Bim = ar.alloc([128, 32, 128], BF16)
        P.dma("pool", Bre[:], s5b[l, 0], writes=[Bre.res])
        P.dma("pool", Bim[:], s5b[l, 1], writes=[Bim.res])
        Cre = ar.alloc([128, 32, 128], BF16)
        Cim = ar.alloc([128, 32, 128], BF16)
        sm = {n: ar.alloc([128, 32], F32) for n in
              ("lr", "dt", "r", "th", "thn", "a", "a2", "sn", "cs", "are", "aim", "den", "fre", "fim", "t1", "t2")}
        mark = ar.off
        c0 = ar.alloc([128, 32, 128], F32)
        c1 = ar.alloc([128, 32, 128], F32)
        c2 = ar.alloc([128, 32, 128], F32)
        P.dma("sp", c0[:], s5c[l, 0], writes=[c0.res])
        P.dma("sp", c1[:], s5c[l, 1], writes=[c1.res])

        def dv(fn, rd, wr):
            P.op("dve", fn, reads=[x.res for x in rd], writes=[x.res for x in wr])

        def ac(fn, rd, wr):
            P.op("act", fn, reads=[x.res for x in rd], writes=[x.res for x in wr])
        s = sm
        dv(lambda h: h.tensor_scalar_min(out=s["lr"][:], in0=prm[:, 0, :], scalar1=-1e-4), [prm], [s["lr"]])
        ac(lambda h: h.activation(out=s["dt"][:], in_=prm[:, 2, :], func=AF.Exp), [prm], [s["dt"]])
        dv(lambda h: h.tensor_tensor(out=s["t1"][:], in0=s["lr"][:], in1=s["dt"][:], op=ALU.mult), [s["lr"], s["dt"]], [s["t1"]])
        ac(lambda h: h.activation(out=s["r"][:], in_=s["t1"][:], func=AF.Exp), [s["t1"]], [s["r"]])
        dv(lambda h: h.tensor_tensor(out=s["th"][:], in0=prm[:, 1, :], in1=s["dt"][:], op=ALU.mult), [prm, s["dt"]], [s["th"]])

        I32 = mybir.dt.int32
        SIN_SCALE = 6.2831845

        def sincos(y, ki, fr, tq, f2, sn, cs):
            MAGIC = 12582912.0
            dv(lambda h: h.tensor_scalar(out=ki[:], in0=y[:], scalar1=MAGIC, scalar2=None, op0=ALU.add), [y], [ki])
            dv(lambda h: h.tensor_scalar(out=ki[:], in0=ki[:], scalar1=MAGIC, scalar2=None, op0=ALU.subtract), [ki], [ki])
            dv(lambda h: h.tensor_sub(out=fr[:], in0=y[:], in1=ki[:]), [y, ki], [fr])
            ac(lambda h: h.activation(out=sn[:], in_=fr[:], func=AF.Sin, scale=SIN_SCALE), [fr], [sn])
            ac(lambda h: h.activation(out=f2[:], in_=fr[:], func=AF.Sin, scale=0.5 * SIN_SCALE), [fr], [f2])
            dv(lambda h: h.tensor_tensor(out=tq[:], in0=f2[:], in1=f2[:], op=ALU.mult), [f2], [tq])
            dv(lambda h: h.tensor_scalar(out=cs[:], in0=tq[:], scalar1=-2.0, scalar2=1.0, op0=ALU.mult, op1=ALU.add), [tq], [cs])

        dv(lambda h: h.tensor_scalar(out=s["thn"][:], in0=s["th"][:], scalar1=1.0 / TWO_PI, scalar2=None, op0=ALU.mult),
           [s["th"]], [s["thn"]])
        sincos(s["thn"], s["a"], s["a2"], s["t1"], s["t2"], s["sn"], s["cs"])
        dv(lambda h: h.scalar_tensor_tensor(out=s["are"][:], in0=s["cs"][:], scalar=1.0, in1=s["r"][:], op0=ALU.mult, op1=ALU.mult),
           [s["cs"], s["r"]], [s["are"]])
        dv(lambda h: h.scalar_tensor_tensor(out=s["aim"][:], in0=s["sn"][:], scalar=1.0, in1=s["r"][:], op0=ALU.mult, op1=ALU.mult),
           [s["sn"], s["r"]], [s["aim"]])
        dv(lambda h: h.tensor_tensor(out=s["den"][:], in0=s["lr"][:], in1=s["lr"][:], op=ALU.mult), [s["lr"]], [s["den"]])
        dv(lambda h: h.tensor_tensor(out=s["t1"][:], in0=prm[:, 1, :], in1=prm[:, 1, :], op=ALU.mult), [prm], [s["t1"]])
        dv(lambda h: h.tensor_add(out=s["den"][:], in0=s["den"][:], in1=s["t1"][:]), [s["den"], s["t1"]], [s["den"]])
        dv(lambda h: h.reciprocal(out=s["den"][:], in_=s["den"][:]), [s["den"]], [s["den"]])
        dv(lambda h: h.tensor_scalar_add(out=s["t2"][:], in0=s["are"][:], scalar1=-1.0), [s["are"]], [s["t2"]])
        dv(lambda h: h.tensor_tensor(out=s["fre"][:], in0=s["t2"][:], in1=s["lr"][:], op=ALU.mult), [s["t2"], s["lr"]], [s["fre"]])
        dv(lambda h: h.tensor_tensor(out=s["t1"][:], in0=s["aim"][:], in1=prm[:, 1, :], op=ALU.mult), [s["aim"], prm], [s["t1"]])
        dv(lambda h: h.tensor_add(out=s["fre"][:], in0=s["fre"][:], in1=s["t1"][:]), [s["fre"], s["t1"]], [s["fre"]])
        dv(lambda h: h.tensor_tensor(out=s["fre"][:], in0=s["fre"][:], in1=s["den"][:], op=ALU.mult), [s["fre"], s["den"]], [s["fre"]])
        dv(lambda h: h.tensor_tensor(out=s["fim"][:], in0=s["aim"][:], in1=s["lr"][:], op=ALU.mult), [s["aim"], s["lr"]], [s["fim"]])
        dv(lambda h: h.tensor_tensor(out=s["t1"][:], in0=s["t2"][:], in1=prm[:, 1, :], op=ALU.mult), [s["t2"], prm], [s["t1"]])
        dv(lambda h: h.tensor_sub(out=s["fim"][:], in0=s["fim"][:], in1=s["t1"][:]), [s["fim"], s["t1"]], [s["fim"]])
        dv(lambda h: h.tensor_tensor(out=s["fim"][:], in0=s["fim"][:], in1=s["den"][:], op=ALU.mult), [s["fim"], s["den"]], [s["fim"]])
        bc = lambda t: t[:].unsqueeze(2).to_broadcast([128, 32, 128])
        dv(lambda h: h.tensor_tensor(out=c2[:], in0=c0[:], in1=bc(s["fre"]), op=ALU.mult), [c0, s["fre"]], [c2])
        P.op("dve", lambda h: h.tensor_tensor(out=Cre[:], in0=c1[:], in1=bc(s["fim"]), op=ALU.mult),
             reads=[c1.res, s["fim"].res], writes=[Cre.res])
        dv(lambda h: h.tensor_sub(out=Cre[:], in0=c2[:], in1=Cre[:]), [c2, Cre], [Cre])
        dv(lambda h: h.tensor_tensor(out=c2[:], in0=c0[:], in1=bc(s["fim"]), op=ALU.mult), [c0, s["fim"], Cre], [c2])
        P.op("dve", lambda h: h.tensor_tensor(out=c0[:], in0=c1[:], in1=bc(s["fre"]), op=ALU.mult),
             reads=[c1.res, s["fre"].res, c2.res], writes=[c0.res])
        dv(lambda h: h.scalar_tensor_tensor(out=Cim[:], in0=c2[:], scalar=-1.0, in1=c0[:], op0=ALU.mult, op1=ALU.subtract),
           [c2, c0], [Cim])
        P.barrier()
        ar.reset(mark)
        tidx = ar.alloc([128, TS], F32)
        P.op("pool", lambda h: h.iota(tidx[:], pattern=[[1, TS]], base=0, channel_multiplier=0,
                                      allow_small_or_imprecise_dtypes=True), writes=[tidx.res])
        tabA = [ar.alloc([128, TS], F32) for _ in range(4)]
        tabB = [ar.alloc([128, TS], F32) for _ in range(4)]
        wsets = []
        for _ in range(2):
            wsets.append((ar.alloc([128, TS], F32), ar.alloc([128, TS], F32), ar.alloc([128, TS], F32), ar.alloc([128, TS], F32),
                          ar.alloc([128, TS], F32), ar.alloc([128, TS], F32), ar.alloc([128, TS], BF16), ar.alloc([128, TS], BF16),
                          ar.alloc([128, TS], F32), ar.alloc([128, TS], F32)))
        kre, kim, k2re, k2im, t1, t2, sre, sim, t3, t4 = wsets[0]
        sit = [0]
        uall = ar.alloc([128, TS], BF16)
        yv = Rot([ar.alloc([128, 512], F32) for _ in range(2)])
        y2 = Rot([ar.alloc([128, 512], F32) for _ in range(2)])
        zo = Rot([ar.alloc([128, 512], BF16) for _ in range(2)])
        send = ar.alloc([128, 32, 4], F32)
        rbc = {}
        for fb in range(8):
            for j in range(4):
                sb = fb * 4 + j
                A, B = tabA[j], tabB[j]
                thn = s["thn"][:, sb:sb + 1]
                dv(lambda h, thn=thn: h.tensor_scalar(out=t1[:], in0=tidx[:], scalar1=thn, scalar2=None, op0=ALU.mult),
                   [tidx, s["thn"]], [t1])
                sincos(t1, k2re, t2, k2im, kre, B, A)
            for sp in range(NSP):
                t0 = sp * TS
                P.dma("sp", uall[:], hU[fb * 128:(fb + 1) * 128, t0:t0 + TS], writes=[uall.res])
                yb_ = [next_bank_in(0, 4) for _ in range(TS // 512)]
                def it_gen(j):
                    sb = fb * 4 + j
                    yield
                    A, B = tabA[j], tabB[j]
                    yield
                    kre, kim, k2re, k2im, t1, t2, sre, sim, t3, t4 = wsets[j % 2]
                    yield
                    for tt in range(TS // 512):
                        sl = slice(tt * 512, (tt + 1) * 512)
                        b1 = next_bank_in(4, 8)
                        mm_group(b1, ps_ap(b1), [(Bre[:, sb, :], uall[:, sl])], [Bre.res, uall.res])
                        P.op("act", lambda h, b1=b1, sl=sl: h.activation(out=kre[:, sl], in_=ps_ap(b1), func=AF.Copy),
                             reads=[banks[b1]], writes=[kre.res])
                        b2 = next_bank_in(4, 8)
                        mm_group(b2, ps_ap(b2), [(Bim[:, sb, :], uall[:, sl])], [Bim.res, uall.res])
                        P.op("act", lambda h, b2=b2, sl=sl: h.activation(out=kim[:, sl], in_=ps_ap(b2), func=AF.Copy),
                             reads=[banks[b2]], writes=[kim.res])
                    yield
                    dv(lambda h, A=A: h.tensor_tensor(out=t1[:], in0=A[:], in1=kre[:], op=ALU.mult), [A, kre], [t1])
                    yield
                    P.op("dve", lambda h, B=B: h.tensor_tensor(out=t2[:], in0=B[:], in1=kim[:], op=ALU.mult),
                         reads=[B.res, kim.res], writes=[t2.res])
                    yield
                    P.op("dve", lambda h, A=A: h.tensor_tensor(out=t3[:], in0=A[:], in1=kim[:], op=ALU.mult),
                         reads=[A.res, kim.res], writes=[t3.res])
                    yield
                    dv(lambda h, B=B: h.tensor_tensor(out=t4[:], in0=B[:], in1=kre[:], op=ALU.mult), [B, kre], [t4])
                    yield
                    dv(lambda h: h.tensor_add(out=k2re[:], in0=t1[:], in1=t2[:]), [t1, t2], [k2re])
                    yield
                    dv(lambda h: h.tensor_sub(out=k2im[:], in0=t3[:], in1=t4[:]), [t3, t4], [k2im])
                    yield
                    yield
                    rb = s["r"][:, sb:sb + 1].to_broadcast([128, TS])
                    yield
                    if sp == 0:
                        i_re, i_im = 0.0, 0.0
                        rd_i = []
                    else:
                        se = send[:, sb, :]
                        dv(lambda h, se=se, A=A: h.tensor_tensor(out=se[:, 2:3], in0=A[:, 1:2], in1=se[:, 0:1], op=ALU.mult), [A, send], [send])
                        dv(lambda h, se=se, B=B: h.tensor_tensor(out=se[:, 3:4], in0=B[:, 1:2], in1=se[:, 1:2], op=ALU.mult), [B, send], [send])
                        dv(lambda h, se=se: h.tensor_sub(out=se[:, 2:3], in0=se[:, 2:3], in1=se[:, 3:4]), [send], [send])
                        dv(lambda h, se=se, A=A: h.tensor_tensor(out=se[:, 3:4], in0=A[:, 1:2], in1=se[:, 1:2], op=ALU.mult), [A, send], [send])
                        dv(lambda h, se=se, B=B: h.scalar_tensor_tensor(out=se[:, 3:4], in0=B[:, 1:2], scalar=se[:, 0:1], in1=se[:, 3:4],
                                                                       op0=ALU.mult, op1=ALU.add), [B, send], [send])
                        i_re, i_im = se[:, 2:3], se[:, 3:4]
                        rd_i = [send]
                    yield
                    dv(lambda h, rb=rb, i_re=i_re: h.tensor_tensor_scan(out=kre[:], data0=rb, data1=k2re[:], initial=i_re,
                                                                        op0=ALU.mult, op1=ALU.add), [s["r"], k2re, kre] + rd_i, [kre])
                    yield
                    dv(lambda h, rb=rb, i_im=i_im: h.tensor_tensor_scan(out=kim[:], data0=rb, data1=k2im[:], initial=i_im,
                                                                        op0=ALU.mult, op1=ALU.add), [s["r"], k2im, kim] + rd_i, [kim])
                    yield
                    dv(lambda h, A=A: h.tensor_tensor(out=t1[:], in0=A[:], in1=kre[:], op=ALU.mult), [A, kre], [t1])
                    yield
                    P.op("dve", lambda h, B=B: h.tensor_tensor(out=t2[:], in0=B[:], in1=kim[:], op=ALU.mult),
                         reads=[B.res, kim.res], writes=[t2.res])
                    yield
                    P.op("dve", lambda h, A=A: h.tensor_tensor(out=t3[:], in0=A[:], in1=kim[:], op=ALU.mult),
                         reads=[A.res, kim.res], writes=[t3.res])
                    yield
                    dv(lambda h, B=B: h.tensor_tensor(out=t4[:], in0=B[:], in1=kre[:], op=ALU.mult), [B, kre], [t4])
                    yield
                    dv(lambda h: h.tensor_sub(out=sre[:], in0=t1[:], in1=t2[:]), [t1, t2], [sre])
                    yield
                    dv(lambda h: h.tensor_add(out=sim[:], in0=t3[:], in1=t4[:]), [t3, t4], [sim])
                    yield
                    if NSP > 1:
                        dv(lambda h, sb=sb: h.tensor_sub(out=send[:, sb, 0:1], in0=t1[:, TS - 1:TS], in1=t2[:, TS - 1:TS]), [t1, t2], [send])
                        dv(lambda h, sb=sb: h.tensor_add(out=send[:, sb, 1:2], in0=t3[:, TS - 1:TS], in1=t4[:, TS - 1:TS]), [t3, t4], [send])
                    yield
                    for tt in range(TS // 512):
                        sl = slice(tt * 512, (tt + 1) * 512)
                        mm_group(yb_[tt], ps_ap(yb_[tt]), [(Cre[:, sb, :], sre[:, sl]), (Cim[:, sb, :], sim[:, sl])],
                                 [Cre.res, Cim.res, sre.res, sim.res], start=(j == 0), stop=(j == 3))
                    yield
                for jp in (0, 2):
                    gens = [it_gen(jp), it_gen(jp + 1)]
                    live = list(gens)
                    while live:
                        for g_ in list(live):
                            try:
                                next(g_)
                            except StopIteration:
                                live.remove(g_)
                for tt in range(TS // 512):
                    sl = slice(tt * 512, (tt + 1) * 512)
                    y_ = yv.next(); w_ = y2.next(); z_ = zo.next()
                    b = yb_[tt]
                    P.op("dve", lambda h, b=b, y_=y_, sl=sl, fb=fb: h.scalar_tensor_tensor(
                        out=y_[:], in0=uall[:, sl], scalar=sv[:, 0, fb:fb + 1], in1=ps_ap(b), op0=ALU.mult, op1=ALU.add),
                        reads=[uall.res, sv.res, banks[b]], writes=[y_.res])
                    P.op("dve", lambda h, y_=y_, w_=w_: h.tensor_tensor(out=w_[:], in0=y_[:], in1=y_[:], op=ALU.mult),
                         reads=[y_.res], writes=[w_.res])
                    P.op("dve", lambda h, w_=w_: h.tensor_scalar(out=w_[:], in0=w_[:], scalar1=0.044715, scalar2=1.0,
                                                                  op0=ALU.mult, op1=ALU.add), reads=[w_.res], writes=[w_.res])
                    P.op("dve", lambda h, y_=y_, w_=w_: h.tensor_tensor(out=w_[:], in0=w_[:], in1=y_[:], op=ALU.mult),
                         reads=[y_.res, w_.res], writes=[w_.res])
                    P.op("act", lambda h, w_=w_: h.activation(out=w_[:], in_=w_[:], func=AF.Sigmoid, scale=1.5957691216057308),
                         reads=[w_.res], writes=[w_.res])
                    dv(lambda h, y_=y_, w_=w_, z_=z_: h.tensor_tensor(out=z_[:], in0=y_[:], in1=w_[:], op=ALU.mult), [y_, w_], [z_])
                    P.dma("sp", zT[fb * 128:(fb + 1) * 128, t0 + tt * 512:t0 + (tt + 1) * 512], z_[:], reads=[z_.res])
        P.barrier()

    def make_epi_win(l):
        def epi(bl, tag, c0, rows, tok0, aux):
            b = bl[0]
            if tag == "ua":
                evac_store(b, rows, 512, None, 1.0, BF16, hU[c0 - O_UA:c0 - O_UA + rows, tok0:tok0 + 512], aux)
            elif tag == "qb":
                evac_store(b, rows, 512, None, 128.0 ** -0.5, BF16, hQB[c0 - O_QB:c0 - O_QB + rows, tok0:tok0 + 512], aux)
            elif tag == "kb":
                evac_store(b, rows, 512, None, 1.0, BF16, hKB[c0 - O_KB:c0 - O_KB + rows, tok0:tok0 + 512], aux)
            elif tag == "gl":
                evac_store(b, rows, 512, None, 1.0, F32, hGL[0:16, tok0:tok0 + 512], aux, eng="act")
            elif tag == "gb":
                evac_store(b, rows, 512, AF.Silu, 1.0, BF16, hGB[c0 - O_GB:c0 - O_GB + rows, tok0:tok0 + 512], aux)
            elif tag == "qc":
                evac_store(b, rows, 512, AF.Silu, 1.0, BF16, hQC[c0 - O_QC:c0 - O_QC + rows, tok0:tok0 + 512], aux)
            elif tag == "fc":
                evac_store(b, rows, 512, None, 1.0, F32, hFC[c0 - O_FC:c0 - O_FC + rows, tok0:tok0 + 512], aux)
            elif tag == "gc":
                evac_store(b, rows, 512, AF.Sigmoid, 1.0, BF16, hGC[c0 - O_GC:c0 - O_GC + rows, tok0:tok0 + 512], aux)
            elif tag == "mg":
                evac_store(b, rows, 512, AF.Sigmoid, 1.0, BF16, hMG[c0 - O_MG:c0 - O_MG + rows, tok0:tok0 + 512], aux)
            elif tag == "vb":
                evac_store(b, 128, rows, None, 1.0, BF16, hVB[tok0:tok0 + 128, c0 - O_VB:c0 - O_VB + rows], aux)
            elif tag == "ic":
                evac_store(b, 128, rows, None, 1.0, BF16, hIC[tok0:tok0 + 128, c0 - O_IC:c0 - O_IC + rows], aux, eng="act")
        return epi

    def seg(off, width, tag):
        return [(off + i * 512, min(512, width - i * 512), tag) for i in range((width + 511) // 512)]

    def layer(l, last):
        TPh = min(2048, T)
        fm_cols = (seg(O_UA, 1024, "ua") + seg(O_QB, 512, "qb") + seg(O_KB, 512, "kb") + seg(O_GL, 16, "gl") +
                   seg(O_GB, 1024, "gb") + seg(O_QC, 1024, "qc") + seg(O_FC, 1024, "fc") + seg(O_GC, 1024, "gc") +
                   seg(O_MG, 3 * D, "mg"))
        gemm_phase(xT, D, w_in[l], fm_cols, "fm", make_epi_win(l), TPh)
        gemm_phase(xT, D, w_in[l], seg(O_VB, 1024, "vb") + seg(O_IC, 1024, "ic"), "tm", make_epi_win(l), TPh)
        s5_phase(l)
        gla_like(l, "gla")
        ar.reset()
        gla_like_h(l)
        def epi_glu(bl, tag, c0, rows, tok0, aux):
            b = bl[0]
            sf = aux["sf"].next(); g = aux["gb"].next(); so = aux["sb"].next()
            fbk, r0 = c0 // 128, c0 % 128
            P.op("act", lambda h: h.activation(out=sf[0:rows, :], in_=ps_ap(b, rows), func=AF.Sigmoid,
                                               bias=glu_b[:, fbk:fbk + 1]), reads=[banks[b], glu_b.res], writes=[sf.res])
            P.dma("sp", g[0:rows, 0, :], zT[c0:c0 + rows, tok0:tok0 + 512], writes=[g.res])
            P.op("dve", lambda h: h.tensor_tensor(out=so[0:rows, :], in0=sf[0:rows, :], in1=g[0:rows, 0, :], op=ALU.mult),
                 reads=[sf.res, g.res], writes=[so.res])
            P.dma("sp", yT[c0:c0 + rows, tok0:tok0 + 512], so[0:rows, :], reads=[so.res])
        glu_b = glu_holder["t"]
        P.dma("sp", glu_b[:], s5v[l, :, 1, :], writes=[glu_b.res])
        gemm_phase(zT, W, w_glu[l], seg(0, W, "glu"), "fm", epi_glu, TPh)
        def epi_up(bl, tag, c0, rows, tok0, aux):
            g = aux["gb"].next(); xf = aux["xf"].next(); so = aux["sb"].next()
            P.dma("sp", g[:], hMG[:, tok0:tok0 + 512].rearrange("(b n) t -> n b t", b=3)[c0:c0 + 128], writes=[g.res])
            for i in range(3):
                P.op("dve", lambda h, i=i: h.tensor_tensor(out=xf[:, i, :], in0=ps_ap(bl[i]), in1=g[:, i, :], op=ALU.mult),
                     reads=[banks[bl[i]], g.res], writes=[xf.res])
            P.op("dve", lambda h: h.tensor_add(out=xf[:, 0, :], in0=xf[:, 0, :], in1=xf[:, 1, :]), reads=[xf.res], writes=[xf.res])
            P.op("dve", lambda h: h.tensor_add(out=so[:], in0=xf[:, 0, :], in1=xf[:, 2, :]), reads=[xf.res], writes=[so.res])
            P.dma("sp", mT[c0:c0 + 128, tok0:tok0 + 512], so[:], reads=[so.res])
        gemm_phase(yT, 3 * W, w_up[l], seg(0, D, "up"), "fm", epi_up, min(1024, T), kgroups=[(0, 8), (8, 16), (16, 24)])
        def make_epi_res(xsrc):
            def epi(bl, tag, c0, width, tok0, aux):
                b = bl[0]
                xf = aux["xf"].next(); sf = aux["sf"].next()
                P.dma("sp", xf[:, 0, 0:width], xsrc[tok0:tok0 + 128, c0:c0 + width], writes=[xf.res])
                P.op("dve", lambda h: h.scalar_tensor_tensor(out=sf[:, 0:width], in0=xf[:, 0, 0:width], scalar=float(ALPHA),
                                                             in1=ps_ap(b, 128, width), op0=ALU.mult, op1=ALU.add),
                     reads=[xf.res, banks[b]], writes=[sf.res])
                P.dma("sp", zF[tok0:tok0 + 128, c0:c0 + width], sf[:, 0:width], reads=[sf.res])
            return epi
        xsrc = x_in if l == 0 else xF
        gemm_phase(mT, D, w_out[l], seg(0, D, "o"), "tm", make_epi_res(xsrc), TPh)
        ln_phase(zF, True, l, 0, xF, None)
        def epi_m1(bl, tag, c0, rows, tok0, aux):
            b = bl[0]
            sf = aux["sf"].next(); so = aux["sb"].next()
            P.op("act", lambda h: h.activation(out=sf[:], in_=ps_ap(b), func=AF.Relu), reads=[banks[b]], writes=[sf.res])
            P.op("act", lambda h: h.activation(out=so[:], in_=sf[:], func=AF.Square), reads=[sf.res], writes=[so.res])
            P.dma("sp", hT[c0:c0 + 128, tok0:tok0 + 512], so[:], reads=[so.res])
        gemm_phase(xT, D, w_m1[l], seg(0, HID, "m1"), "fm", epi_m1, TPh)
        gemm_phase(hT, HID, w_m2[l], seg(0, D, "m2"), "tm", make_epi_res(xF), 1024, kbp=8)
        ln_phase(zF, True, l, 1, xF, out if last else None)

    glu_holder = {}

    def gla_like_h(l):
        gla_like(l, "hgrn")

    glu_holder["t"] = ar.alloc([128, 8], F32)
    oml = ar.alloc([128, 8], F32)
    ar_small["oml"] = oml
    ar.base = ar.off

    c_setup()
    ln_phase(x_in, False, 0, 0, None, None)
    for l in range(L):
        P.op("dve", lambda h, l=l: h.tensor_scalar(out=oml[:], in0=lb_all[:, :, l], scalar1=-1.0, scalar2=1.0,
                                                   op0=ALU.mult, op1=ALU.add), reads=[lb_all.res], writes=[oml.res])
        layer(l, l == L - 1)
    P.stopped = False
    P._barrier()
    P.emit()
    st.close()
    return nc


def prep_weights(inp, L):
    f = np.float32
    g = {}
    g["w_in"] = np.ascontiguousarray(inp["w_in"][:L], dtype=f)
    g["w_glu"] = np.ascontiguousarray(inp["s5_w_glu"][:L], dtype=f)
    g["w_up"] = np.ascontiguousarray(inp["w_up"][:L], dtype=f).reshape(L, 3 * W, D)
    g["w_out"] = np.ascontiguousarray(inp["w_out"][:L], dtype=f)
    g["w_m1"] = np.ascontiguousarray(inp["w_mlp_in"][:L], dtype=f)
    g["w_m2"] = np.ascontiguousarray(inp["w_mlp_out"][:L], dtype=f)
    g["lnp"] = np.ascontiguousarray(np.stack([inp["ln1_g"][:L], inp["ln1_b"][:L], inp["ln2_g"][:L], inp["ln2_b"][:L]], axis=1), dtype=f)
    lam_re = np.asarray(inp["s5_lam_re"][:L], f).reshape(L, 32, 128).transpose(0, 2, 1)
    lam_im = np.asarray(inp["s5_lam_im"][:L], f).reshape(L, 32, 128).transpose(0, 2, 1)
    ldt = np.repeat(np.asarray(inp["s5_log_dt"][:L], f)[:, :, None], 64, axis=2).reshape(L, 32, 128).transpose(0, 2, 1)
    g["s5p"] = np.ascontiguousarray(np.stack([lam_re, lam_im, ldt], axis=2), dtype=f)
    bpad = np.zeros((L, 2, 128, 32, 128), f)
    cpad = np.zeros((L, 2, 128, 32, 128), f)
    for ri, (bk, ck) in enumerate((("s5_b_re", "s5_c_re"), ("s5_b_im", "s5_c_im"))):
        Bm = np.asarray(inp[bk][:L], f)
        Cm = np.asarray(inp[ck][:L], f)
        for sb in range(32):
            for gi in range(2):
                gidx = 2 * sb + gi
                r0 = (gidx % 8) * 16
                bpad[:, ri, r0:r0 + 16, sb, gi * 64:(gi + 1) * 64] = Bm[:, gidx].transpose(0, 2, 1)
                cpad[:, ri, gi * 64:(gi + 1) * 64, sb, r0:r0 + 16] = Cm[:, gidx].transpose(0, 2, 1)
    g["s5b"] = bpad
    g["s5c"] = cpad
    dsk = np.asarray(inp["s5_d"][:L], f).reshape(L, 8, 128).transpose(0, 2, 1)
    bgl = np.asarray(inp["s5_b_glu"][:L], f).reshape(L, 8, 128).transpose(0, 2, 1)
    g["s5v"] = np.ascontiguousarray(np.stack([dsk, bgl], axis=2), dtype=f)
    g["glaw"] = np.ascontiguousarray(inp["gla_w_gate"][:L], dtype=f)
    bg = np.asarray(inp["gla_b_gate"][:L], f).reshape(L, 4, 128).transpose(0, 2, 1)
    nw = np.asarray(inp["gla_norm_w"][:L], f).reshape(L, 2, 128).transpose(0, 2, 1)
    g["glav"] = np.ascontiguousarray(np.concatenate([bg, nw], axis=2), dtype=f)
    g["hlb"] = np.ascontiguousarray(np.asarray(inp["hgrn_lb_logits"], f).reshape(DEPTH, 8, 128).transpose(2, 1, 0), dtype=f)
    g["hnw"] = np.ascontiguousarray(np.asarray(inp["hgrn_norm_w"][:L], f).T, dtype=f)
    return g


_CACHE = {}


def kernel(**inputs):
    x = np.asarray(inputs["x"], np.float32)
    Bn, T, _ = x.shape
    key = (T, DEPTH)
    if key not in _CACHE:
        _CACHE[key] = build(T, DEPTH)
    nc = _CACHE[key]
    wts = prep_weights(inputs, DEPTH)
    in_maps = []
    for b in range(Bn):
        m = dict(wts)
        m["x"] = np.ascontiguousarray(x[b])
        in_maps.append(m)
    res = run_bass_kernel_spmd(nc, in_maps, core_ids=list(range(Bn)))
    return np.stack([np.asarray(r["out"], np.float32) for r in res.results], axis=0)
```

```python
import contextlib
import math
import types
import numpy as np
import concourse.bass as bass
import concourse.mybir as mybir
from concourse.bass_utils import run_bass_kernel_spmd

F32 = mybir.dt.float32
BF16 = mybir.dt.bfloat16
AF = mybir.ActivationFunctionType
ALU = mybir.AluOpType

D = 2048
W = 1024
NIN = 14352
HID = 8192
DEPTH = 4
ALPHA = (2 * DEPTH) ** 0.25
TWO_PI = 2.0 * math.pi
O_UA, O_QB, O_KB, O_VB, O_GL, O_GB, O_QC, O_FC, O_IC, O_GC, O_MG = (
    0, 1024, 1536, 2048, 3072, 3088, 4112, 5136, 6160, 7184, 8208)


def _freeze(fn):
    if fn.__closure__ is None:
        return fn
    cells = []
    for c in fn.__closure__:
        try:
            cells.append(types.CellType(c.cell_contents))
        except ValueError:
            cells.append(c)
    g = types.FunctionType(fn.__code__, fn.__globals__, fn.__name__, fn.__defaults__, tuple(cells))
    g.__kwdefaults__ = fn.__kwdefaults__
    return g


class Res:
    __slots__ = ("w", "r")

    def __init__(self):
        self.w = None
        self.r = {}


class Prog:
    KQ = 8
    ENGS = ("pe", "act", "dve", "pool", "sp")

    def __init__(self, nc, st):
        self.nc = nc
        self.eng = {}
        hs = dict(pe=nc.tensor, act=nc.scalar, dve=nc.vector, pool=nc.gpsimd, sp=nc.sync)
        for name in self.ENGS:
            sem = st.enter_context(nc.semaphore("sem_" + name))
            self.eng[name] = dict(h=hs[name], sem=sem, n=0, prog=[], waited={})
        self.dq = {}
        for q in ("sp", "pool", "act"):
            sems = [st.enter_context(nc.semaphore(f"dq_{q}_{i}")) for i in range(self.KQ)]
            self.dq[q] = dict(sems=sems, n=0)
        self.dma_uid = 0
        self.stopped = False
        self.nphase = 0
        self.max_phase = 10 ** 9

    def _waits(self, eng, reads, writes):
        E = self.eng[eng]
        deps = []
        for r in reads:
            if r.w is not None:
                deps.append(r.w)
        for w in writes:
            if w.w is not None:
                deps.append(w.w)
            deps.extend(w.r.values())
        waits = []
        for (sem, val, key, src) in deps:
            if src == "pe" and eng == "pe":
                continue
            if E["waited"].get(key, 0) >= val:
                continue
            E["waited"][key] = val
            waits.append((sem, val))
        return waits

    def _record(self, tok, reads, writes, rkey):
        for r in reads:
            r.r[rkey] = tok
        for w in writes:
            w.w = tok
            w.r = {}

    def op(self, eng, fn, reads=(), writes=()):
        if self.stopped:
            return None
        E = self.eng[eng]
        fn = _freeze(fn)
        waits = self._waits(eng, reads, writes)
        E["n"] += 1
        sem = E["sem"]
        tok = (sem, E["n"], "e_" + eng, eng)

        def emit(h, fn=fn, waits=waits, sem=sem):
            for s, v in waits:
                h.wait_ge(s, v)
            fn(h).then_inc(sem, 1)

        E["prog"].append(emit)
        self._record(tok, reads, writes, eng)
        return tok

    def dma(self, q, out, in_, reads=(), writes=()):
        if self.stopped:
            return None
        E = self.eng[q]
        Dq = self.dq[q]
        waits = self._waits(q, reads, writes)
        n = Dq["n"]
        Dq["n"] += 1
        s = Dq["sems"][n % self.KQ]
        val = 16 * (n // self.KQ + 1)
        key = f"dq_{q}_{n % self.KQ}"
        if n >= self.KQ and E["waited"].get(key, 0) < val - 16:
            E["waited"][key] = val - 16
            waits.append((s, val - 16))
        tok = (s, val, key, "dma")

        def emit(h, waits=waits, s=s, out=out, in_=in_):
            for ss, v in waits:
                h.wait_ge(ss, v)
            h.dma_start(out=out, in_=in_).then_inc(s, 16)

        E["prog"].append(emit)
        self.dma_uid += 1
        self._record(tok, reads, writes, "dma%d" % self.dma_uid)
        return tok

    def barrier(self):
        if self.stopped:
            return
        self.nphase += 1
        if self.nphase >= self.max_phase:
            self._barrier()
            self.stopped = True
            return
        self._barrier()

    def _barrier(self):
        targets = []
        for name in self.ENGS:
            X = self.eng[name]
            if X["n"] > 0:
                targets.append((X["sem"], X["n"], "e_" + name))
        for q, Dq in self.dq.items():
            n = Dq["n"]
            for i in range(min(n, self.KQ)):
                cnt = (n - 1 - i) // self.KQ + 1
                targets.append((Dq["sems"][i], 16 * cnt, f"dq_{q}_{i}"))
        for name in self.ENGS:
            E = self.eng[name]
            waits = []
            for (sem, val, key) in targets:
                if key == "e_" + name and name == "pe":
                    continue
                if E["waited"].get(key, 0) >= val:
                    continue
                E["waited"][key] = val
                waits.append((sem, val))

            def emit(h, waits=waits):
                for s, v in waits:
                    h.wait_ge(s, v)

            E["prog"].append(emit)

    def emit(self):
        nc = self.nc
        with nc.Block() as block:
            @block.tensor
            def _(h):
                for f in self.eng["pe"]["prog"]:
                    f(h)

            @block.scalar
            def _(h):
                for f in self.eng["act"]["prog"]:
                    f(h)

            @block.vector
            def _(h):
                for f in self.eng["dve"]["prog"]:
                    f(h)

            @block.gpsimd
            def _(h):
                for f in self.eng["pool"]["prog"]:
                    f(h)

            @block.sync
            def _(h):
                for f in self.eng["sp"]["prog"]:
                    f(h)


class Tile:
    def __init__(self, t):
        self.t = t
        self.res = Res()

    def __getitem__(self, k):
        return self.t[k]


class Arena:
    def __init__(self, nc, limit):
        self.nc = nc
        self.off = 16384
        self.limit = limit
        self.cnt = 0
        self.base = 16384

    def reset(self, to=None):
        self.off = self.base if to is None else to

    def alloc(self, shape, dtype):
        nbytes = int(np.prod(shape[1:])) * (4 if dtype == F32 else 2)
        nbytes = (nbytes + 63) // 64 * 64
        assert self.off + nbytes <= self.limit, (self.off, nbytes, self.limit)
        self.cnt += 1
        t = self.nc.alloc_sbuf_tensor_at("sb%d" % self.cnt, list(shape), dtype, offset=self.off)
        self.off += nbytes
        return Tile(t)


class K:
    pass


def build(T, L, TSPAN=1024, max_phase=10 ** 9):
    nc = bass.Bass("TRN2", target_bir_lowering=False)
    st = contextlib.ExitStack()
    P = Prog(nc, st)
    P.max_phase = max_phase
    TS = min(TSPAN, T)
    NSP = T // TS

    def din(name, shape, dt=F32):
        return nc.dram_tensor(name, list(shape), dt, kind="ExternalInput").ap()

    def dscr(name, shape, dt):
        import os
        kind = "ExternalOutput" if os.environ.get("KDEBUG") else "Internal"
        return nc.dram_tensor(name, list(shape), dt, kind=kind).ap()

    x_in = din("x", [T, D])
    w_in = din("w_in", [L, D, NIN])
    w_glu = din("w_glu", [L, W, W])
    w_up = din("w_up", [L, 3 * W, D])
    w_out = din("w_out", [L, D, D])
    w_m1 = din("w_m1", [L, D, HID])
    w_m2 = din("w_m2", [L, HID, D])
    lnp = din("lnp", [L, 4, D])
    s5p = din("s5p", [L, 128, 3, 32])
    s5b = din("s5b", [L, 2, 128, 32, 128])
    s5c = din("s5c", [L, 2, 128, 32, 128])
    s5v = din("s5v", [L, 128, 2, 8])
    glaw = din("glaw", [L, 16, 512])
    glav = din("glav", [L, 128, 6])
    hlb = din("hlb", [128, 8, DEPTH])
    hnw = din("hnw", [128, L])
    out = nc.dram_tensor("out", [T, D], F32, kind="ExternalOutput").ap()

    xF = dscr("xF", [T, D], F32)
    zF = dscr("zF", [T, D], F32)
    xT = dscr("xT", [D, T], BF16)
    hU = dscr("hU", [W, T], BF16)
    hQB = dscr("hQB", [512, T], BF16)
    hKB = dscr("hKB", [512, T], BF16)
    hGL = dscr("hGL", [16, T], F32)
    hGB = dscr("hGB", [W, T], BF16)
    hVB = dscr("hVB", [T, W], BF16)
    hQC = dscr("hQC", [W, T], BF16)
    hFC = dscr("hFC", [W, T], F32)
    hIC = dscr("hIC", [T, W], BF16)
    hGC = dscr("hGC", [W, T], BF16)
    hMG = dscr("hMG", [3 * D, T], BF16)
    zT = dscr("zT", [W, T], BF16)
    yT = dscr("yT", [3 * W, T], BF16)
    mT = dscr("mT", [D, T], BF16)
    hT = dscr("hT", [HID, T], BF16)

    SB_LIMIT = 192 * 1024
    ar = Arena(nc, SB_LIMIT)
    psum = nc.alloc_psum_tensor("psum", [128, 8, 512], F32)
    banks = [Res() for _ in range(8)]
    bank_i = [0]

    def next_bank():
        b = bank_i[0] % 8
        bank_i[0] += 1
        return b

    sub_i = {}

    def next_bank_in(lo, hi):
        i = sub_i.get((lo, hi), 0)
        sub_i[(lo, hi)] = i + 1
        return lo + i % (hi - lo)

    ident = ar.alloc([128, 128], BF16)
    ones_f = ar.alloc([128, 128], F32)
    mask = ar.alloc([128, 128], F32)
    ones_bf = ar.alloc([128, 128], BF16)
    ones_bf2 = ar.alloc([128, 128], BF16)
    cneg_pi = ar.alloc([128, 1], F32)
    ceps5 = ar.alloc([128, 1], F32)
    ceps6 = ar.alloc([128, 1], F32)
    lb_all = ar.alloc([128, 8, DEPTH], F32)
    hnw_t = ar.alloc([128, L], F32)
    ar.base = ar.off

    def c_setup():
        P.op("pool", lambda h: h.memset(ones_f[:], 1.0), writes=[ones_f.res])
        P.op("pool", lambda h: h.memset(cneg_pi[:], -math.pi), writes=[cneg_pi.res])
        P.op("pool", lambda h: h.memset(ceps5[:], 1e-5), writes=[ceps5.res])
        P.op("pool", lambda h: h.memset(ceps6[:], 1e-6), writes=[ceps6.res])
        P.op("pool", lambda h: h.memset(ones_bf[:], 1.0 / 128.0), writes=[ones_bf.res])
        P.op("pool", lambda h: h.memset(ones_bf2[:], 1.0 / 256.0), writes=[ones_bf2.res])
        P.op("pool", lambda h: h.affine_select(out=ident[:], in_=ones_f[:], pattern=[[-1, 128]], base=0,
                                               channel_multiplier=1, compare_op=ALU.is_equal, fill=0.0),
             reads=[ones_f.res], writes=[ident.res])
        P.op("pool", lambda h: h.affine_select(out=mask[:], in_=ones_f[:], pattern=[[1, 128]], base=0,
                                               channel_multiplier=-1, compare_op=ALU.is_ge, fill=0.0),
             reads=[ones_f.res], writes=[mask.res])
        P.op("pool", lambda h: h.memset(mask[0:64, 64:128], 0.0), writes=[mask.res])
        P.dma("sp", lb_all[:], hlb, writes=[lb_all.res])
        P.dma("sp", hnw_t[:], hnw, writes=[hnw_t.res])
        P.op("act", lambda h: h.activation(out=lb_all[:], in_=lb_all[:], func=AF.Exp),
             reads=[lb_all.res], writes=[lb_all.res])
        ssum = ar.alloc([128, 8, 1], F32)
        P.op("dve", lambda h: h.tensor_add(out=ssum[:], in0=lb_all[:, :, 0:1], in1=lb_all[:, :, 1:2]),
             reads=[lb_all.res], writes=[ssum.res])
        for j in (2, 3):
            P.op("dve", lambda h, j=j: h.tensor_add(out=ssum[:], in0=ssum[:], in1=lb_all[:, :, j:j + 1]),
                 reads=[lb_all.res, ssum.res], writes=[ssum.res])
        P.op("dve", lambda h: h.reciprocal(out=ssum[:], in_=ssum[:]), reads=[ssum.res], writes=[ssum.res])
        P.op("dve", lambda h: h.tensor_tensor(out=lb_all[:], in0=lb_all[:], in1=ssum[:].to_broadcast([128, 8, DEPTH]),
                                              op=ALU.mult), reads=[lb_all.res, ssum.res], writes=[lb_all.res])
        P.op("dve", lambda h: h.tensor_add(out=lb_all[:, :, 2:3], in0=lb_all[:, :, 2:3], in1=lb_all[:, :, 1:2]),
             reads=[lb_all.res], writes=[lb_all.res])
        P.op("dve", lambda h: h.tensor_add(out=lb_all[:, :, 3:4], in0=lb_all[:, :, 3:4], in1=lb_all[:, :, 2:3]),
             reads=[lb_all.res], writes=[lb_all.res])
        P.op("dve", lambda h: h.memset(lb_all[:, :, 0:1], 0.0), writes=[lb_all.res])
        P.barrier()

    def ps_ap(b, rows=128, n=512):
        return psum[0:rows, b, 0:n]

    class Rot:
        def __init__(self, tiles):
            self.tiles = tiles
            self.i = 0

        def next(self):
            t = self.tiles[self.i % len(self.tiles)]
            self.i += 1
            return t

    def mm_group(bank, out_ap, pairs, extra_reads, start=True, stop=True):
        def fn(h):
            ins = None
            n = len(pairs)
            for i, (l, r) in enumerate(pairs):
                ins = h.matmul(out_ap, l, r, start=(start and i == 0), stop=(stop and i == n - 1))
            return ins
        return P.op("pe", fn, reads=extra_reads, writes=[banks[bank]])

    def gemm_phase(act_src, Kd, w_src, cols, mode, epi, TP, kgroups=None, kbp=None):
        ar.reset()
        KB = Kd // 128
        nparts = (1 if KB <= 24 else KB // 16) if kbp is None else KB // kbp
        KBP = KB // nparts
        actA = ar.alloc([128, KB * TP], BF16)
        wb = Rot([ar.alloc([128, KBP, 512], BF16) for _ in range(2)])
        aux = dict(
            sf=Rot([ar.alloc([128, 512], F32) for _ in range(4)]),
            sb=Rot([ar.alloc([128, 512], BF16) for _ in range(4)]),
            xf=Rot([ar.alloc([128, 3 if mode == "fm" else 1, 512], F32) for _ in range(2)]),
            gb=Rot([ar.alloc([128, 3 if mode == "fm" else 1, 512 if mode == "fm" else 8], BF16) for _ in range(2)]),
        )
        NQ = TP // 512
        actq = [Res() for _ in range(NQ)]
        if kgroups is None:
            kgroups = [(0, KBP)]
        actv = actA[:].rearrange("p (k t) -> p k t", t=TP)
        for tp in range(T // TP):
            for q in range(NQ):
                P.dma("sp", actv[:, :, q * 512:(q + 1) * 512],
                      act_src[:, tp * TP + q * 512:tp * TP + (q + 1) * 512].rearrange("(k p) t -> p k t", p=128),
                      writes=[actq[q]])
            loads = [(ci, pa) for ci in range(len(cols)) for pa in range(nparts)]
            wtiles = {}

            def load(idx):
                ci, pa = loads[idx]
                off, width, tag = cols[ci]
                wt = wb.next()
                P.dma("pool", wt[:, :, 0:width],
                      w_src[pa * KBP * 128:(pa + 1) * KBP * 128, off:off + width].rearrange("(k p) n -> p k n", p=128),
                      writes=[wt.res])
                wtiles[idx] = wt

            load(0)
            for idx in range(len(loads)):
                if idx + 1 < len(loads):
                    load(idx + 1)
                ci, pa = loads[idx]
                off, width, tag = cols[ci]
                wt = wtiles.pop(idx)
                if mode == "fm":
                    assert nparts == 1
                    for nb in range((width + 127) // 128):
                        rows = min(128, width - nb * 128)
                        for tt in range(TP // 512):
                            bl = []
                            for (k0, k1) in kgroups:
                                b = next_bank()
                                pairs = [(wt[:, kb, nb * 128:nb * 128 + rows], actv[:, kb, tt * 512:(tt + 1) * 512])
                                         for kb in range(k0, k1)]
                                mm_group(b, ps_ap(b, rows), pairs, [wt.res, actq[tt]])
                                bl.append(b)
                            epi(bl, tag, off + nb * 128, rows, tp * TP + tt * 512, aux)
                else:
                    nt = TP // 128
                    if nparts == 1:
                        for t1 in range(nt):
                            b = next_bank()
                            pairs = [(actv[:, kb, t1 * 128:(t1 + 1) * 128], wt[:, kb, 0:width]) for kb in range(KBP)]
                            mm_group(b, ps_ap(b, 128, width), pairs, [wt.res, actq[t1 // 4]])
                            epi([b], tag, off, width, tp * TP + t1 * 128, aux)
                    else:
                        assert nt <= 8
                        if pa == 0:
                            cur = [next_bank() for _ in range(nt)]
                            wtiles["cur"] = cur
                        cur = wtiles["cur"]
                        for t1 in range(nt):
                            b = cur[t1]
                            pairs = [(actv[:, pa * KBP + kb, t1 * 128:(t1 + 1) * 128], wt[:, kb, 0:width])
                                     for kb in range(KBP)]
                            mm_group(b, ps_ap(b, 128, width), pairs, [wt.res, actq[t1 // 4]],
                                     start=(pa == 0), stop=(pa == nparts - 1))
                            if pa == nparts - 1:
                                epi([b], tag, off, width, tp * TP + t1 * 128, aux)
        P.barrier()

    def evac_store(b, rows, n, func, scale, dt, dest, aux, bias=None, eng=None):
        stg = (aux["sb"] if dt == BF16 else aux["sf"]).next()
        src = ps_ap(b, rows, n)
        if func is None and (eng or "dve") == "dve":
            P.op("dve", lambda h: h.tensor_scalar(out=stg[0:rows, 0:n], in0=src, scalar1=float(scale), scalar2=None,
                                                  op0=ALU.mult), reads=[banks[b]], writes=[stg.res])
        else:
            f = AF.Copy if func is None else func
            if bias is None:
                P.op("act", lambda h: h.activation(out=stg[0:rows, 0:n], in_=src, func=f, scale=float(scale)),
                     reads=[banks[b]], writes=[stg.res])
            else:
                P.op("act", lambda h: h.activation(out=stg[0:rows, 0:n], in_=src, func=f, scale=float(scale),
                                                   bias=bias), reads=[banks[b]], writes=[stg.res])
        P.dma("sp", dest, stg[0:rows, 0:n], reads=[stg.res])

    def ln_phase(src, norm, l, which, dstF, dstOut):
        ar.reset()
        if norm:
            g_bc = ar.alloc([128, D], F32)
            b_bc = ar.alloc([128, D], F32)
            P.dma("sp", g_bc[:], lnp[l, 2 * which, :].partition_broadcast(128), writes=[g_bc.res])
            P.dma("sp", b_bc[:], lnp[l, 2 * which + 1, :].partition_broadcast(128), writes=[b_bc.res])
        zt = Rot([ar.alloc([128, D], F32) for _ in range(4)])
        jk = Rot([ar.alloc([128, D], F32) for _ in range(2)])
        xb = Rot([ar.alloc([128, D], BF16) for _ in range(3)])
        xtt = Rot([ar.alloc([128, 16, 128], BF16) for _ in range(3)])
        stat = Rot([ar.alloc([128, 8], F32) for _ in range(4)])
        for tt in range(T // 128):
            z = zt.next()
            P.dma("sp", z[:], src[tt * 128:(tt + 1) * 128, :], writes=[z.res])
            xbt = xb.next()
            if norm:
                s = stat.next()
                j = jk.next()
                P.op("act", lambda h: h.activation(out=j[:], in_=z[:], func=AF.Copy, accum_out=s[:, 0:1]),
                     reads=[z.res], writes=[j.res, s.res])
                P.op("act", lambda h: h.activation(out=j[:], in_=z[:], func=AF.Square, accum_out=s[:, 1:2]),
                     reads=[z.res], writes=[j.res, s.res])
                P.op("dve", lambda h: h.tensor_scalar(out=s[:, 2:3], in0=s[:, 0:1], scalar1=1.0 / D, scalar2=None,
                                                      op0=ALU.mult), reads=[s.res], writes=[s.res])
                P.op("dve", lambda h: h.tensor_tensor(out=s[:, 3:4], in0=s[:, 2:3], in1=s[:, 2:3], op=ALU.mult),
                     reads=[s.res], writes=[s.res])
                P.op("dve", lambda h: h.scalar_tensor_tensor(out=s[:, 4:5], in0=s[:, 1:2], scalar=1.0 / D,
                                                             in1=s[:, 3:4], op0=ALU.mult, op1=ALU.subtract),
                     reads=[s.res], writes=[s.res])
                P.op("act", lambda h: h.activation(out=s[:, 5:6], in_=s[:, 4:5], func=AF.Ln, bias=ceps5[:, 0:1]),
                     reads=[s.res, ceps5.res], writes=[s.res])
                P.op("act", lambda h: h.activation(out=s[:, 5:6], in_=s[:, 5:6], func=AF.Exp, scale=-0.5),
                     reads=[s.res], writes=[s.res])
                P.op("dve", lambda h: h.tensor_scalar(out=z[:], in0=z[:], scalar1=s[:, 2:3], scalar2=s[:, 5:6],
                                                      op0=ALU.subtract, op1=ALU.mult),
                     reads=[z.res, s.res], writes=[z.res])
                P.op("dve", lambda h: h.tensor_tensor(out=z[:], in0=z[:], in1=g_bc[:], op=ALU.mult),
                     reads=[z.res, g_bc.res], writes=[z.res])
                P.op("dve", lambda h: h.tensor_tensor(out=z[:], in0=z[:], in1=b_bc[:], op=ALU.add),
                     reads=[z.res, b_bc.res], writes=[z.res])
                P.dma("sp", dstF[tt * 128:(tt + 1) * 128, :], z[:], reads=[z.res])
                if dstOut is not None:
                    P.dma("sp", dstOut[tt * 128:(tt + 1) * 128, :], z[:], reads=[z.res])
            P.op("act", lambda h: h.activation(out=xbt[:], in_=z[:], func=AF.Copy), reads=[z.res], writes=[xbt.res])
            xt_ = xtt.next()
            for half in range(2):
                b = next_bank()
                pv = psum[:, b, :].bitcast(BF16).rearrange("p (k t) -> p k t", t=128)

                def fn(h, b=b, pv=pv, half=half, xbt=xbt):
                    ins = None
                    for k in range(8):
                        kk = half * 8 + k
                        ins = h.transpose(pv[:, k, :], xbt[:, kk * 128:(kk + 1) * 128], ident[:])
                    return ins
                P.op("pe", fn, reads=[xbt.res, ident.res], writes=[banks[b]])
                eng = "dve" if half == 0 else "act"
                if eng == "dve":
                    P.op("dve", lambda h, pv=pv, half=half, xt_=xt_: h.tensor_copy(out=xt_[:, half * 8:half * 8 + 8, :], in_=pv),
                         reads=[banks[b]], writes=[xt_.res])
                else:
                    P.op("act", lambda h, pv=pv, half=half, xt_=xt_: h.activation(out=xt_[:, half * 8:half * 8 + 8, :], in_=pv, func=AF.Copy),
                         reads=[banks[b]], writes=[xt_.res])
            P.dma("sp", xT[:, tt * 128:(tt + 1) * 128].rearrange("(k p) t -> p k t", p=128), xt_[:], reads=[xt_.res])
        P.barrier()

    def gla_like(l, kind):
        ar.reset()
        NC = TS // 64
        NPC = TS // 128
        nh = 4 if kind == "gla" else 8
        nvb = 2 if kind == "gla" else 1
        DV = 128 * nvb
        gs = (-1.0 / 16.0) if kind == "gla" else 1.0
        mask0 = ar.alloc([128, TS], F32)
        P.op("pool", lambda h: h.memset(mask0[:], 1.0), writes=[mask0.res])
        P.op("pool", lambda h: h.memset(mask0[:].rearrange("p (c i) -> p c i", i=64)[:, :, 0:1], 0.0), writes=[mask0.res])
        gsets = []
        for _ in range(2):
            gsets.append((ar.alloc([128, TS], F32), ar.alloc([128, TS], F32), ar.alloc([128, TS], F32), ar.alloc([128, TS], F32),
                          ar.alloc([128, TS], F32), ar.alloc([128, TS], BF16), ar.alloc([128, TS], BF16), ar.alloc([128, TS], BF16),
                          ar.alloc([128, TS], BF16), ar.alloc([128, TS], BF16), ar.alloc([128, NC], F32),
                          ar.alloc([128, NPC, DV], BF16), ar.alloc([128, NPC, 128], BF16)))
        git = [0]
        KV = ar.alloc([128, NC, DV], F32)
        Sall = ar.alloc([128, NC + 1, DV], F32)
        Sbf = ar.alloc([128, NC, DV], BF16)
        AT = Rot([ar.alloc([128, 128], BF16) for _ in range(3)])
        og = Rot([ar.alloc([128, nvb, 512], F32) for _ in range(2)])
        sq = Rot([ar.alloc([128, nvb, 512], BF16) for _ in range(2)])
        gt = Rot([ar.alloc([128, nvb, 512], BF16) for _ in range(2)])
        rstd = Rot([ar.alloc([128, 512], F32) for _ in range(2)])
        yb = Rot([ar.alloc([128, nvb, 512], BF16) for _ in range(2)])
        glt = ar.alloc([16, TS], F32)
        wg = ar.alloc([16, 512], F32)
        gv = ar.alloc([128, 6], F32)
        nbg = ar.alloc([128, 4], F32)
        if kind == "gla":
            P.dma("sp", wg[:], glaw[l], writes=[wg.res])
            P.dma("sp", gv[:], glav[l], writes=[gv.res])
            P.op("dve", lambda h: h.tensor_scalar(out=nbg[:], in0=gv[:, 0:4], scalar1=-1.0, scalar2=None, op0=ALU.mult),
                 reads=[gv.res], writes=[nbg.res])
        for hh in range(nh):
            P.op("dve", lambda h: h.memset(Sall[:, 0, :], 0.0), writes=[Sall.res])
            for sp in range(NSP):
                t0 = sp * TS
                fr, gg, G, tmp, tmp2, qb, kb_, qt, kt, kpT, egl, vtok, kptok = gsets[git[0] % 2]
                git[0] += 1
                if sp > 0:
                    P.op("dve", lambda h: h.tensor_copy(out=Sall[:, 0, :], in_=Sall[:, NC, :]),
                         reads=[Sall.res], writes=[Sall.res])
                if kind == "gla":
                    if hh == 0:
                        pass
                    P.dma("sp", glt[:], hGL[:, t0:t0 + TS], writes=[glt.res])
                    for tt in range(TS // 512):
                        b = next_bank()
                        mm_group(b, ps_ap(b), [(wg[:, hh * 128:(hh + 1) * 128], glt[:, tt * 512:(tt + 1) * 512])],
                                 [wg.res, glt.res])
                        P.op("act", lambda h, b=b, tt=tt: h.activation(out=fr[:, tt * 512:(tt + 1) * 512], in_=ps_ap(b),
                                                                       func=AF.Exp, scale=-1.0, bias=nbg[:, hh:hh + 1]),
                             reads=[banks[b], nbg.res], writes=[fr.res])
                    P.op("act", lambda h: h.activation(out=gg[:], in_=fr[:], func=AF.Ln, bias=1.0, scale=1.0),
                         reads=[fr.res], writes=[gg.res])
                    P.dma("sp", qb[:], hQB[hh * 128:(hh + 1) * 128, t0:t0 + TS], writes=[qb.res])
                    P.dma("sp", kb_[:], hKB[hh * 128:(hh + 1) * 128, t0:t0 + TS], writes=[kb_.res])
                    kk = kb_
                    P.dma("sp", vtok[:], hVB[t0:t0 + TS, hh * DV:(hh + 1) * DV].rearrange("(c p) v -> p c v", p=128),
                          writes=[vtok.res])
                else:
                    P.dma("sp", fr[:], hFC[hh * 128:(hh + 1) * 128, t0:t0 + TS], writes=[fr.res])
                    P.op("act", lambda h: h.activation(out=fr[:], in_=fr[:], func=AF.Sigmoid), reads=[fr.res], writes=[fr.res])
                    oml = ar_small["oml"]
                    P.op("dve", lambda h: h.tensor_scalar(out=fr[:], in0=fr[:], scalar1=oml[:, hh:hh + 1],
                                                          scalar2=lb_all[:, hh, l:l + 1], op0=ALU.mult, op1=ALU.add),
                         reads=[fr.res, oml.res, lb_all.res], writes=[fr.res])
                    P.op("act", lambda h: h.activation(out=gg[:], in_=fr[:], func=AF.Ln), reads=[fr.res], writes=[gg.res])
                    P.op("dve", lambda h: h.tensor_scalar(out=fr[:], in0=fr[:], scalar1=-1.0, scalar2=1.0,
                                                          op0=ALU.mult, op1=ALU.add), reads=[fr.res], writes=[fr.res])
                    kk = fr
                    P.dma("sp", qb[:], hQC[hh * 128:(hh + 1) * 128, t0:t0 + TS], writes=[qb.res])
                    P.dma("sp", vtok[:], hIC[t0:t0 + TS, hh * DV:(hh + 1) * DV].rearrange("(c p) v -> p c v", p=128),
                          writes=[vtok.res])
                P.op("dve", lambda h: h.tensor_tensor_scan(out=G[:], data0=mask0[:], data1=gg[:], initial=0.0,
                                                           op0=ALU.mult, op1=ALU.add),
                     reads=[mask0.res, gg.res], writes=[G.res])
                Gv = G[:].rearrange("p (c i) -> p c i", i=64)
                P.op("act", lambda h: h.activation(out=tmp[:], in_=G[:], func=AF.Exp, scale=gs), reads=[G.res], writes=[tmp.res])
                P.op("dve", lambda h: h.tensor_tensor(out=qt[:], in0=qb[:], in1=tmp[:], op=ALU.mult),
                     reads=[qb.res, tmp.res], writes=[qt.res])
                P.op("act", lambda h: h.activation(out=tmp2[:], in_=G[:], func=AF.Exp, scale=-gs), reads=[G.res], writes=[tmp2.res])
                P.op("dve", lambda h, kk=kk: h.tensor_tensor(out=kt[:], in0=kk[:], in1=tmp2[:], op=ALU.mult),
                     reads=[kk.res, tmp2.res], writes=[kt.res])
                P.op("dve", lambda h: h.tensor_tensor(out=tmp[:].rearrange("p (c i) -> p c i", i=64),
                                                      in0=Gv[:, :, 63:64].to_broadcast([128, NC, 64]), in1=Gv,
                                                      op=ALU.subtract), reads=[G.res, tmp.res, qt.res], writes=[tmp.res])
                P.op("act", lambda h: h.activation(out=tmp[:], in_=tmp[:], func=AF.Exp, scale=gs), reads=[tmp.res], writes=[tmp.res])
                P.op("dve", lambda h, kk=kk: h.tensor_tensor(out=kpT[:], in0=kk[:], in1=tmp[:], op=ALU.mult),
                     reads=[kk.res, tmp.res], writes=[kpT.res])
                P.op("act", lambda h: h.activation(out=egl[:], in_=Gv[:, :, 63], func=AF.Exp, scale=gs),
                     reads=[G.res], writes=[egl.res])
                for g8 in range((NPC + 7) // 8):
                    b = next_bank()
                    pv = psum[:, b, :].bitcast(BF16).rearrange("p (k t) -> p k t", t=128)
                    n8 = min(8, NPC - g8 * 8)

                    def fn(h, pv=pv, g8=g8, n8=n8):
                        ins = None
                        for k in range(n8):
                            pc = g8 * 8 + k
                            ins = h.transpose(pv[:, k, :], kpT[:, pc * 128:(pc + 1) * 128], ident[:])
                        return ins
                    P.op("pe", fn, reads=[kpT.res, ident.res], writes=[banks[b]])
                    P.op("act", lambda h, pv=pv, g8=g8, n8=n8: h.activation(out=kptok[:, g8 * 8:g8 * 8 + n8, :], in_=pv[:, 0:n8, :],
                                                                            func=AF.Copy), reads=[banks[b]], writes=[kptok.res])
                per_bank = 512 // DV
                KVv = KV[:].rearrange("p (c two) v -> p c two v", two=2)
                for p0 in range(0, NPC, per_bank):
                    bA, bB = next_bank(), next_bank()
                    pA = psum[:, bA, :].rearrange("p (c v) -> p c v", v=DV)
                    pB = psum[:, bB, :].rearrange("p (c v) -> p c v", v=DV)

                    def fn(h, p0=p0, pA=pA, pB=pB):
                        ins = None
                        for i in range(per_bank):
                            pc = p0 + i
                            h.matmul(pA[:, i, :], kptok[0:64, pc, :], vtok[0:64, pc, :], start=True, stop=True)
                            ins = h.matmul(pB[:, i, :], kptok[64:128, pc, :], vtok[64:128, pc, :], start=True, stop=True)
                        return ins
                    P.op("pe", fn, reads=[kptok.res, vtok.res], writes=[banks[bA], banks[bB]])
                    P.op("dve", lambda h, p0=p0, pA=pA: h.tensor_copy(out=KVv[:, p0:p0 + per_bank, 0, :], in_=pA),
                         reads=[banks[bA]], writes=[KV.res])
                    P.op("act", lambda h, p0=p0, pB=pB: h.activation(out=KVv[:, p0:p0 + per_bank, 1, :], in_=pB, func=AF.Copy),
                         reads=[banks[bB]], writes=[KV.res])
                for c in range(NC):
                    P.op("dve", lambda h, c=c: h.scalar_tensor_tensor(out=Sall[:, c + 1, :], in0=Sall[:, c, :], scalar=egl[:, c:c + 1],
                                                                     in1=KV[:, c, :], op0=ALU.mult, op1=ALU.add),
                         reads=[egl.res, KV.res, Sall.res], writes=[Sall.res])
                P.op("act", lambda h: h.activation(out=Sbf[:], in_=Sall[:, 0:NC, :], func=AF.Copy),
                     reads=[Sall.res], writes=[Sbf.res])
                for tt in range(TS // 512):
                    ob = [next_bank() for _ in range(nvb)]
                    for p4 in range(4):
                        pc = tt * 4 + p4
                        bs = next_bank()
                        mm_group(bs, psum[:, bs, 0:128], [(kt[:, pc * 128:(pc + 1) * 128], qt[:, pc * 128:(pc + 1) * 128])],
                                 [kt.res, qt.res])
                        at = AT.next()
                        P.op("dve", lambda h, bs=bs, at=at: h.tensor_tensor(out=at[:], in0=psum[:, bs, 0:128], in1=mask[:], op=ALU.mult),
                             reads=[banks[bs], mask.res], writes=[at.res])
                        for vb in range(nvb):
                            def fn(h, vb=vb, pc=pc, p4=p4, at=at, ob=ob):
                                o_ap = psum[:, ob[vb], p4 * 128:(p4 + 1) * 128]
                                h.matmul(o_ap, vtok[:, pc, vb * 128:(vb + 1) * 128], at[:], start=True, stop=False)
                                h.matmul(o_ap[:, 0:64], Sbf[:, 2 * pc, vb * 128:(vb + 1) * 128], qt[:, pc * 128:pc * 128 + 64],
                                         start=False, stop=False)
                                return h.matmul(o_ap[:, 64:128], Sbf[:, 2 * pc + 1, vb * 128:(vb + 1) * 128],
                                                qt[:, pc * 128 + 64:pc * 128 + 128], start=False, stop=True)
                            P.op("pe", fn, reads=[vtok.res, at.res, Sbf.res, qt.res], writes=[banks[ob[vb]]])
                    o_t = og.next(); s_t = sq.next(); g_t = gt.next(); r_t = rstd.next(); y_t = yb.next()
                    gsrc = hGB if kind == "gla" else hGC
                    ybase = W if kind == "gla" else 2 * W
                    tok = t0 + tt * 512
                    P.dma("sp", g_t[:], gsrc[hh * DV:(hh + 1) * DV, tok:tok + 512].rearrange("(b p) t -> p b t", p=128),
                          writes=[g_t.res])
                    for vb in range(nvb):
                        if kind == "gla":
                            P.op("act", lambda h, vb=vb, o_t=o_t, ob=ob: h.activation(out=o_t[:, vb, :], in_=ps_ap(ob[vb]), func=AF.Copy),
                                 reads=[banks[ob[vb]]], writes=[o_t.res])
                        else:
                            P.op("dve", lambda h, vb=vb, o_t=o_t, ob=ob, g_t=g_t: h.tensor_tensor(out=o_t[:, vb, :], in0=ps_ap(ob[vb]),
                                                                                                in1=g_t[:, vb, :], op=ALU.mult),
                                 reads=[banks[ob[vb]], g_t.res], writes=[o_t.res])
                    P.op("dve", lambda h, o_t=o_t, s_t=s_t: h.tensor_tensor(out=s_t[:], in0=o_t[:], in1=o_t[:], op=ALU.mult),
                         reads=[o_t.res], writes=[s_t.res])
                    br = next_bank()
                    onesm = ones_bf if nvb == 1 else ones_bf2
                    mm_group(br, ps_ap(br), [(onesm[:], s_t[:, vb, :]) for vb in range(nvb)], [onesm.res, s_t.res])
                    P.op("act", lambda h, br=br, r_t=r_t: h.activation(out=r_t[:], in_=ps_ap(br), func=AF.Ln, bias=ceps6[:, 0:1]),
                         reads=[banks[br], ceps6.res], writes=[r_t.res])
                    P.op("act", lambda h, r_t=r_t: h.activation(out=r_t[:], in_=r_t[:], func=AF.Exp, scale=-0.5),
                         reads=[r_t.res], writes=[r_t.res])
                    for vb in range(nvb):
                        wcol = gv[:, 4 + vb:5 + vb] if kind == "gla" else hnw_t[:, l:l + 1]
                        wres = gv.res if kind == "gla" else hnw_t.res
                        if kind == "gla":
                            P.op("dve", lambda h, vb=vb, o_t=o_t, r_t=r_t, wcol=wcol: h.scalar_tensor_tensor(
                                out=o_t[:, vb, :], in0=o_t[:, vb, :], scalar=wcol, in1=r_t[:], op0=ALU.mult, op1=ALU.mult),
                                reads=[o_t.res, r_t.res, wres], writes=[o_t.res])
                            P.op("dve", lambda h, vb=vb, o_t=o_t, y_t=y_t, g_t=g_t: h.tensor_tensor(out=y_t[:, vb, :], in0=o_t[:, vb, :],
                                                                                                   in1=g_t[:, vb, :], op=ALU.mult),
                                 reads=[o_t.res, g_t.res], writes=[y_t.res])
                        else:
                            P.op("dve", lambda h, vb=vb, o_t=o_t, r_t=r_t, wcol=wcol, y_t=y_t: h.scalar_tensor_tensor(
                                out=y_t[:, vb, :], in0=o_t[:, vb, :], scalar=wcol, in1=r_t[:], op0=ALU.mult, op1=ALU.mult),
                                reads=[o_t.res, r_t.res, wres], writes=[y_t.res])
                    P.dma("sp", yT[ybase + hh * DV:ybase + (hh + 1) * DV, tok:tok + 512].rearrange("(b p) t -> p b t", p=128),
                          y_t[:], reads=[y_t.res])
        P.barrier()

    ar_small = {}

    def s5_phase(l):
        ar.reset()
        prm = ar.alloc([128, 3, 32], F32)
        P.dma("sp", prm[:], s5p[l], writes=[prm.res])
        sv = ar.alloc([128, 2, 8], F32)
        P.dma("sp", sv[:], s5v[l], writes=[sv.res])
        Bre = ar.alloc([128, 32, 128], BF16)
        Bim = ar.alloc([128, 32, 128], BF16)
        P.dma("pool", Bre[:], s5b[l, 0], writes=[Bre.res])
        P.dma("pool", Bim[:], s5b[l, 1], writes=[Bim.res])
        Cre = ar.alloc([128, 32, 128], BF16)
        Cim = ar.alloc([128, 32, 128], BF16)
        sm = {n: ar.alloc([128, 32], F32) for n in
              ("lr", "dt", "r", "th", "thn", "a", "a2", "sn", "cs", "are", "aim", "den", "fre", "fim", "t1", "t2")}
        mark = ar.off
        c0 = ar.alloc([128, 32, 128], F32)
        c1 = ar.alloc([128, 32, 128], F32)
        c2 = ar.alloc([128, 32, 128], F32)
        P.dma("sp", c0[:], s5c[l, 0], writes=[c0.res])
        P.dma("sp", c1[:], s5c[l, 1], writes=[c1.res])

        def dv(fn, rd, wr):
            P.op("dve", fn, reads=[x.res for x in rd], writes=[x.res for x in wr])

        def ac(fn, rd, wr):
            P.op("act", fn, reads=[x.res for x in rd], writes=[x.res for x in wr])
        s = sm
        dv(lambda h: h.tensor_scalar_min(out=s["lr"][:], in0=prm[:, 0, :], scalar1=-1e-4), [prm], [s["lr"]])
        ac(lambda h: h.activation(out=s["dt"][:], in_=prm[:, 2, :], func=AF.Exp), [prm], [s["dt"]])
        dv(lambda h: h.tensor_tensor(out=s["t1"][:], in0=s["lr"][:], in1=s["dt"][:], op=ALU.mult), [s["lr"], s["dt"]], [s["t1"]])
        ac(lambda h: h.activation(out=s["r"][:], in_=s["t1"][:], func=AF.Exp), [s["t1"]], [s["r"]])
        dv(lambda h: h.tensor_tensor(out=s["th"][:], in0=prm[:, 1, :], in1=s["dt"][:], op=ALU.mult), [prm, s["dt"]], [s["th"]])

        I32 = mybir.dt.int32
        SIN_SCALE = 6.2831845

        def sincos(y, ki, fr, tq, f2, sn, cs):
            MAGIC = 12582912.0
            dv(lambda h: h.tensor_scalar(out=ki[:], in0=y[:], scalar1=MAGIC, scalar2=None, op0=ALU.add), [y], [ki])
            dv(lambda h: h.tensor_scalar(out=ki[:], in0=ki[:], scalar1=MAGIC, scalar2=None, op0=ALU.subtract), [ki], [ki])
            dv(lambda h: h.tensor_sub(out=fr[:], in0=y[:], in1=ki[:]), [y, ki], [fr])
            ac(lambda h: h.activation(out=sn[:], in_=fr[:], func=AF.Sin, scale=SIN_SCALE), [fr], [sn])
            ac(lambda h: h.activation(out=f2[:], in_=fr[:], func=AF.Sin, scale=0.5 * SIN_SCALE), [fr], [f2])
            dv(lambda h: h.tensor_tensor(out=tq[:], in0=f2[:], in1=f2[:], op=ALU.mult), [f2], [tq])
            dv(lambda h: h.tensor_scalar(out=cs[:], in0=tq[:], scalar1=-2.0, scalar2=1.0, op0=ALU.mult, op1=ALU.add), [tq], [cs])

        dv(lambda h: h.tensor_scalar(out=s["thn"][:], in0=s["th"][:], scalar1=1.0 / TWO_PI, scalar2=None, op0=ALU.mult),
           [s["th"]], [s["thn"]])
        sincos(s["thn"], s["a"], s["a2"], s["t1"], s["t2"], s["sn"], s["cs"])
        dv(lambda h: h.scalar_tensor_tensor(out=s["are"][:], in0=s["cs"][:], scalar=1.0, in1=s["r"][:], op0=ALU.mult, op1=ALU.mult),
           [s["cs"], s["r"]], [s["are"]])
        dv(lambda h: h.scalar_tensor_tensor(out=s["aim"][:], in0=s["sn"][:], scalar=1.0, in1=s["r"][:], op0=ALU.mult, op1=ALU.mult),
           [s["sn"], s["r"]], [s["aim"]])
        dv(lambda h: h.tensor_tensor(out=s["den"][:], in0=s["lr"][:], in1=s["lr"][:], op=ALU.mult), [s["lr"]], [s["den"]])
        dv(lambda h: h.tensor_tensor(out=s["t1"][:], in0=prm[:, 1, :], in1=prm[:, 1, :], op=ALU.mult), [prm], [s["t1"]])
        dv(lambda h: h.tensor_add(out=s["den"][:], in0=s["den"][:], in1=s["t1"][:]), [s["den"], s["t1"]], [s["den"]])
        dv(lambda h: h.reciprocal(out=s["den"][:], in_=s["den"][:]), [s["den"]], [s["den"]])
        dv(lambda h: h.tensor_scalar_add(out=s["t2"][:], in0=s["are"][:], scalar1=-1.0), [s["are"]], [s["t2"]])
        dv(lambda h: h.tensor_tensor(out=s["fre"][:], in0=s["t2"][:], in1=s["lr"][:], op=ALU.mult), [s["t2"], s["lr"]], [s["fre"]])
        dv(lambda h: h.tensor_tensor(out=s["t1"][:], in0=s["aim"][:], in1=prm[:, 1, :], op=ALU.mult), [s["aim"], prm], [s["t1"]])
        dv(lambda h: h.tensor_add(out=s["fre"][:], in0=s["fre"][:], in1=s["t1"][:]), [s["fre"], s["t1"]], [s["fre"]])
        dv(lambda h: h.tensor_tensor(out=s["fre"][:], in0=s["fre"][:], in1=s["den"][:], op=ALU.mult), [s["fre"], s["den"]], [s["fre"]])
        dv(lambda h: h.tensor_tensor(out=s["fim"][:], in0=s["aim"][:], in1=s["lr"][:], op=ALU.mult), [s["aim"], s["lr"]], [s["fim"]])
        dv(lambda h: h.tensor_tensor(out=s["t1"][:], in0=s["t2"][:], in1=prm[:, 1, :], op=ALU.mult), [s["t2"], prm], [s["t1"]])
        dv(lambda h: h.tensor_sub(out=s["fim"][:], in0=s["fim"][:], in1=s["t1"][:]), [s["fim"], s["t1"]], [s["fim"]])
        dv(lambda h: h.tensor_tensor(out=s["fim"][:], in0=s["fim"][:], in1=s["den"][:], op=ALU.mult), [s["fim"], s["den"]], [s["fim"]])
        bc = lambda t: t[:].unsqueeze(2).to_broadcast([128, 32, 128])
        dv(lambda h: h.tensor_tensor(out=c2[:], in0=c0[:], in1=bc(s["fre"]), op=ALU.mult), [c0, s["fre"]], [c2])
        P.op("dve", lambda h: h.tensor_tensor(out=Cre[:], in0=c1[:], in1=bc(s["fim"]), op=ALU.mult),
             reads=[c1.res, s["fim"].res], writes=[Cre.res])
        dv(lambda h: h.tensor_sub(out=Cre[:], in0=c2[:], in1=Cre[:]), [c2, Cre], [Cre])
        dv(lambda h: h.tensor_tensor(out=c2[:], in0=c0[:], in1=bc(s["fim"]), op=ALU.mult), [c0, s["fim"], Cre], [c2])
        P.op("dve", lambda h: h.tensor_tensor(out=c0[:], in0=c1[:], in1=bc(s["fre"]), op=ALU.mult),
             reads=[c1.res, s["fre"].res, c2.res], writes=[c0.res])
        dv(lambda h: h.scalar_tensor_tensor(out=Cim[:], in0=c2[:], scalar=-1.0, in1=c0[:], op0=ALU.mult, op1=ALU.subtract),
           [c2, c0], [Cim])
        P.barrier()
        ar.reset(mark)
        tidx = ar.alloc([128, TS], F32)
        P.op("pool", lambda h: h.iota(tidx[:], pattern=[[1, TS]], base=0, channel_multiplier=0,
                                      allow_small_or_imprecise_dtypes=True), writes=[tidx.res])
        tabA = [ar.alloc([128, TS], BF16) for _ in range(4)]
        tabB = [ar.alloc([128, TS], BF16) for _ in range(4)]
        scr = [ar.alloc([128, TS], F32) for _ in range(5)]
        wsets = []
        for _ in range(2):
            wsets.append(tuple(ar.alloc([128, TS], BF16) for _ in range(10)))
        kre, kim, k2re, k2im, t1, t2, sre, sim, t3, t4 = wsets[0]
        sit = [0]
        uall = ar.alloc([128, TS], BF16)
        yv = Rot([ar.alloc([128, 512], F32) for _ in range(2)])
        y2 = Rot([ar.alloc([128, 512], F32) for _ in range(2)])
        zo = Rot([ar.alloc([128, 512], BF16) for _ in range(2)])
        send = ar.alloc([128, 32, 4], F32)
        rbc = {}
        for fb in range(8):
            for j in range(4):
                sb = fb * 4 + j
                A, B = tabA[j], tabB[j]
                thn = s["thn"][:, sb:sb + 1]
                dv(lambda h, thn=thn: h.tensor_scalar(out=scr[0][:], in0=tidx[:], scalar1=thn, scalar2=None, op0=ALU.mult),
                   [tidx, s["thn"]], [scr[0]])
                sincos(scr[0], scr[1], scr[2], scr[3], scr[4], B, A)
            for sp in range(NSP):
                t0 = sp * TS
                P.dma("sp", uall[:], hU[fb * 128:(fb + 1) * 128, t0:t0 + TS], writes=[uall.res])
                yb_ = [next_bank_in(0, 4) for _ in range(TS // 512)]
                def it_gen(j):
                    sb = fb * 4 + j
                    yield
                    A, B = tabA[j], tabB[j]
                    yield
                    kre, kim, k2re, k2im, t1, t2, sre, sim, t3, t4 = wsets[j % 2]
                    yield
                    for tt in range(TS // 512):
                        sl = slice(tt * 512, (tt + 1) * 512)
                        b1 = next_bank_in(4, 8)
                        mm_group(b1, ps_ap(b1), [(Bre[:, sb, :], uall[:, sl])], [Bre.res, uall.res])
                        P.op("act", lambda h, b1=b1, sl=sl: h.activation(out=kre[:, sl], in_=ps_ap(b1), func=AF.Copy),
                             reads=[banks[b1]], writes=[kre.res])
                        b2 = next_bank_in(4, 8)
                        mm_group(b2, ps_ap(b2), [(Bim[:, sb, :], uall[:, sl])], [Bim.res, uall.res])
                        P.op("act", lambda h, b2=b2, sl=sl: h.activation(out=kim[:, sl], in_=ps_ap(b2), func=AF.Copy),
                             reads=[banks[b2]], writes=[kim.res])
                    yield
                    dv(lambda h, A=A: h.tensor_tensor(out=t1[:], in0=A[:], in1=kre[:], op=ALU.mult), [A, kre], [t1])
                    yield
                    P.op("dve", lambda h, B=B: h.tensor_tensor(out=t2[:], in0=B[:], in1=kim[:], op=ALU.mult),
                         reads=[B.res, kim.res], writes=[t2.res])
                    yield
                    P.op("dve", lambda h, A=A: h.tensor_tensor(out=t3[:], in0=A[:], in1=kim[:], op=ALU.mult),
                         reads=[A.res, kim.res], writes=[t3.res])
                    yield
                    dv(lambda h, B=B: h.tensor_tensor(out=t4[:], in0=B[:], in1=kre[:], op=ALU.mult), [B, kre], [t4])
                    yield
                    dv(lambda h: h.tensor_add(out=k2re[:], in0=t1[:], in1=t2[:]), [t1, t2], [k2re])
                    yield
                    dv(lambda h: h.tensor_sub(out=k2im[:], in0=t3[:], in1=t4[:]), [t3, t4], [k2im])
                    yield
                    yield
                    rb = s["r"][:, sb:sb + 1].to_broadcast([128, TS])
                    yield
                    if sp == 0:
                        i_re, i_im = 0.0, 0.0
                        rd_i = []
                    else:
                        se = send[:, sb, :]
                        dv(lambda h, se=se, A=A: h.tensor_tensor(out=se[:, 2:3], in0=A[:, 1:2], in1=se[:, 0:1], op=ALU.mult), [A, send], [send])
                        dv(lambda h, se=se, B=B: h.tensor_tensor(out=se[:, 3:4], in0=B[:, 1:2], in1=se[:, 1:2], op=ALU.mult), [B, send], [send])
                        dv(lambda h, se=se: h.tensor_sub(out=se[:, 2:3], in0=se[:, 2:3], in1=se[:, 3:4]), [send], [send])
                        dv(lambda h, se=se, A=A: h.tensor_tensor(out=se[:, 3:4], in0=A[:, 1:2], in1=se[:, 1:2], op=ALU.mult), [A, send], [send])
                        dv(lambda h, se=se, B=B: h.scalar_tensor_tensor(out=se[:, 3:4], in0=B[:, 1:2], scalar=se[:, 0:1], in1=se[:, 3:4],
                                                                       op0=ALU.mult, op1=ALU.add), [B, send], [send])
                        i_re, i_im = se[:, 2:3], se[:, 3:4]
                        rd_i = [send]
                    yield
                    dv(lambda h, rb=rb, i_re=i_re: h.tensor_tensor_scan(out=kre[:], data0=rb, data1=k2re[:], initial=i_re,
                                                                        op0=ALU.mult, op1=ALU.add), [s["r"], k2re, kre] + rd_i, [kre])
                    yield
                    dv(lambda h, rb=rb, i_im=i_im: h.tensor_tensor_scan(out=kim[:], data0=rb, data1=k2im[:], initial=i_im,
                                                                        op0=ALU.mult, op1=ALU.add), [s["r"], k2im, kim] + rd_i, [kim])
                    yield
                    dv(lambda h, A=A: h.tensor_tensor(out=t1[:], in0=A[:], in1=kre[:], op=ALU.mult), [A, kre], [t1])
                    yield
                    P.op("dve", lambda h, B=B: h.tensor_tensor(out=t2[:], in0=B[:], in1=kim[:], op=ALU.mult),
                         reads=[B.res, kim.res], writes=[t2.res])
                    yield
                    P.op("dve", lambda h, A=A: h.tensor_tensor(out=t3[:], in0=A[:], in1=kim[:], op=ALU.mult),
                         reads=[A.res, kim.res], writes=[t3.res])
                    yield
                    dv(lambda h, B=B: h.tensor_tensor(out=t4[:], in0=B[:], in1=kre[:], op=ALU.mult), [B, kre], [t4])
                    yield
                    dv(lambda h: h.tensor_sub(out=sre[:], in0=t1[:], in1=t2[:]), [t1, t2], [sre])
                    yield
                    dv(lambda h: h.tensor_add(out=sim[:], in0=t3[:], in1=t4[:]), [t3, t4], [sim])
                    yield
                    if NSP > 1:
                        dv(lambda h, sb=sb: h.tensor_sub(out=send[:, sb, 0:1], in0=t1[:, TS - 1:TS], in1=t2[:, TS - 1:TS]), [t1, t2], [send])
                        dv(lambda h, sb=sb: h.tensor_add(out=send[:, sb, 1:2], in0=t3[:, TS - 1:TS], in1=t4[:, TS - 1:TS]), [t3, t4], [send])
                    yield
                    for tt in range(TS // 512):
                        sl = slice(tt * 512, (tt + 1) * 512)
                        mm_group(yb_[tt], ps_ap(yb_[tt]), [(Cre[:, sb, :], sre[:, sl]), (Cim[:, sb, :], sim[:, sl])],
                                 [Cre.res, Cim.res, sre.res, sim.res], start=(j == 0), stop=(j == 3))
                    yield
                for jp in (0, 2):
                    gens = [it_gen(jp), it_gen(jp + 1)]
                    live = list(gens)
                    while live:
                        for g_ in list(live):
                            try:
                                next(g_)
                            except StopIteration:
                                live.remove(g_)
                for tt in range(TS // 512):
                    sl = slice(tt * 512, (tt + 1) * 512)
                    y_ = yv.next(); w_ = y2.next(); z_ = zo.next()
                    b = yb_[tt]
                    P.op("dve", lambda h, b=b, y_=y_, sl=sl, fb=fb: h.scalar_tensor_tensor(
                        out=y_[:], in0=uall[:, sl], scalar=sv[:, 0, fb:fb + 1], in1=ps_ap(b), op0=ALU.mult, op1=ALU.add),
                        reads=[uall.res, sv.res, banks[b]], writes=[y_.res])
                    P.op("dve", lambda h, y_=y_, w_=w_: h.tensor_tensor(out=w_[:], in0=y_[:], in1=y_[:], op=ALU.mult),
                         reads=[y_.res], writes=[w_.res])
                    P.op("dve", lambda h, w_=w_: h.tensor_scalar(out=w_[:], in0=w_[:], scalar1=0.044715, scalar2=1.0,
                                                                  op0=ALU.mult, op1=ALU.add), reads=[w_.res], writes=[w_.res])
                    P.op("dve", lambda h, y_=y_, w_=w_: h.tensor_tensor(out=w_[:], in0=w_[:], in1=y_[:], op=ALU.mult),
                         reads=[y_.res, w_.res], writes=[w_.res])
                    P.op("act", lambda h, w_=w_: h.activation(out=w_[:], in_=w_[:], func=AF.Sigmoid, scale=1.5957691216057308),
                         reads=[w_.res], writes=[w_.res])
                    dv(lambda h, y_=y_, w_=w_, z_=z_: h.tensor_tensor(out=z_[:], in0=y_[:], in1=w_[:], op=ALU.mult), [y_, w_], [z_])
                    P.dma("sp", zT[fb * 128:(fb + 1) * 128, t0 + tt * 512:t0 + (tt + 1) * 512], z_[:], reads=[z_.res])
        P.barrier()

    def make_epi_win(l):
        def epi(bl, tag, c0, rows, tok0, aux):
            b = bl[0]
            if tag == "ua":
                evac_store(b, rows, 512, None, 1.0, BF16, hU[c0 - O_UA:c0 - O_UA + rows, tok0:tok0 + 512], aux)
            elif tag == "qb":
                evac_store(b, rows, 512, None, 128.0 ** -0.5, BF16, hQB[c0 - O_QB:c0 - O_QB + rows, tok0:tok0 + 512], aux)
            elif tag == "kb":
                evac_store(b, rows, 512, None, 1.0, BF16, hKB[c0 - O_KB:c0 - O_KB + rows, tok0:tok0 + 512], aux)
            elif tag == "gl":
                evac_store(b, rows, 512, None, 1.0, F32, hGL[0:16, tok0:tok0 + 512], aux, eng="act")
            elif tag == "gb":
                evac_store(b, rows, 512, AF.Silu, 1.0, BF16, hGB[c0 - O_GB:c0 - O_GB + rows, tok0:tok0 + 512], aux)
            elif tag == "qc":
                evac_store(b, rows, 512, AF.Silu, 1.0, BF16, hQC[c0 - O_QC:c0 - O_QC + rows, tok0:tok0 + 512], aux)
            elif tag == "fc":
                evac_store(b, rows, 512, None, 1.0, F32, hFC[c0 - O_FC:c0 - O_FC + rows, tok0:tok0 + 512], aux)
            elif tag == "gc":
                evac_store(b, rows, 512, AF.Sigmoid, 1.0, BF16, hGC[c0 - O_GC:c0 - O_GC + rows, tok0:tok0 + 512], aux)
            elif tag == "mg":
                evac_store(b, rows, 512, AF.Sigmoid, 1.0, BF16, hMG[c0 - O_MG:c0 - O_MG + rows, tok0:tok0 + 512], aux)
            elif tag == "vb":
                evac_store(b, 128, rows, None, 1.0, BF16, hVB[tok0:tok0 + 128, c0 - O_VB:c0 - O_VB + rows], aux)
            elif tag == "ic":
                evac_store(b, 128, rows, None, 1.0, BF16, hIC[tok0:tok0 + 128, c0 - O_IC:c0 - O_IC + rows], aux, eng="act")
        return epi

    def seg(off, width, tag):
        return [(off + i * 512, min(512, width - i * 512), tag) for i in range((width + 511) // 512)]

    def layer(l, last):
        TPh = min(2048, T)
        fm_cols = (seg(O_UA, 1024, "ua") + seg(O_QB, 512, "qb") + seg(O_KB, 512, "kb") + seg(O_GL, 16, "gl") +
                   seg(O_GB, 1024, "gb") + seg(O_QC, 1024, "qc") + seg(O_FC, 1024, "fc") + seg(O_GC, 1024, "gc") +
                   seg(O_MG, 3 * D, "mg"))
        gemm_phase(xT, D, w_in[l], fm_cols, "fm", make_epi_win(l), TPh)
        gemm_phase(xT, D, w_in[l], seg(O_VB, 1024, "vb") + seg(O_IC, 1024, "ic"), "tm", make_epi_win(l), TPh)
        s5_phase(l)
        gla_like(l, "gla")
        ar.reset()
        gla_like_h(l)
        def epi_glu(bl, tag, c0, rows, tok0, aux):
            b = bl[0]
            sf = aux["sf"].next(); g = aux["gb"].next(); so = aux["sb"].next()
            fbk, r0 = c0 // 128, c0 % 128
            P.op("act", lambda h: h.activation(out=sf[0:rows, :], in_=ps_ap(b, rows), func=AF.Sigmoid,
                                               bias=glu_b[:, fbk:fbk + 1]), reads=[banks[b], glu_b.res], writes=[sf.res])
            P.dma("sp", g[0:rows, 0, :], zT[c0:c0 + rows, tok0:tok0 + 512], writes=[g.res])
            P.op("dve", lambda h: h.tensor_tensor(out=so[0:rows, :], in0=sf[0:rows, :], in1=g[0:rows, 0, :], op=ALU.mult),
                 reads=[sf.res, g.res], writes=[so.res])
            P.dma("sp", yT[c0:c0 + rows, tok0:tok0 + 512], so[0:rows, :], reads=[so.res])
        glu_b = glu_holder["t"]
        P.dma("sp", glu_b[:], s5v[l, :, 1, :], writes=[glu_b.res])
        gemm_phase(zT, W, w_glu[l], seg(0, W, "glu"), "fm", epi_glu, TPh)
        def epi_up(bl, tag, c0, rows, tok0, aux):
            g = aux["gb"].next(); xf = aux["xf"].next(); so = aux["sb"].next()
            P.dma("sp", g[:], hMG[:, tok0:tok0 + 512].rearrange("(b n) t -> n b t", b=3)[c0:c0 + 128], writes=[g.res])
            for i in range(3):
                P.op("dve", lambda h, i=i: h.tensor_tensor(out=xf[:, i, :], in0=ps_ap(bl[i]), in1=g[:, i, :], op=ALU.mult),
                     reads=[banks[bl[i]], g.res], writes=[xf.res])
            P.op("dve", lambda h: h.tensor_add(out=xf[:, 0, :], in0=xf[:, 0, :], in1=xf[:, 1, :]), reads=[xf.res], writes=[xf.res])
            P.op("dve", lambda h: h.tensor_add(out=so[:], in0=xf[:, 0, :], in1=xf[:, 2, :]), reads=[xf.res], writes=[so.res])
            P.dma("sp", mT[c0:c0 + 128, tok0:tok0 + 512], so[:], reads=[so.res])
        gemm_phase(yT, 3 * W, w_up[l], seg(0, D, "up"), "fm", epi_up, min(1024, T), kgroups=[(0, 8), (8, 16), (16, 24)])
        def make_epi_res(xsrc):
            def epi(bl, tag, c0, width, tok0, aux):
                b = bl[0]
                xf = aux["xf"].next(); sf = aux["sf"].next()
                P.dma("sp", xf[:, 0, 0:width], xsrc[tok0:tok0 + 128, c0:c0 + width], writes=[xf.res])
                P.op("dve", lambda h: h.scalar_tensor_tensor(out=sf[:, 0:width], in0=xf[:, 0, 0:width], scalar=float(ALPHA),
                                                             in1=ps_ap(b, 128, width), op0=ALU.mult, op1=ALU.add),
                     reads=[xf.res, banks[b]], writes=[sf.res])
                P.dma("sp", zF[tok0:tok0 + 128, c0:c0 + width], sf[:, 0:width], reads=[sf.res])
            return epi
        xsrc = x_in if l == 0 else xF
        gemm_phase(mT, D, w_out[l], seg(0, D, "o"), "tm", make_epi_res(xsrc), TPh)
        ln_phase(zF, True, l, 0, xF, None)
        def epi_m1(bl, tag, c0, rows, tok0, aux):
            b = bl[0]
            sf = aux["sf"].next(); so = aux["sb"].next()
            P.op("act", lambda h: h.activation(out=sf[:], in_=ps_ap(b), func=AF.Relu), reads=[banks[b]], writes=[sf.res])
            P.op("act", lambda h: h.activation(out=so[:], in_=sf[:], func=AF.Square), reads=[sf.res], writes=[so.res])
            P.dma("sp", hT[c0:c0 + 128, tok0:tok0 + 512], so[:], reads=[so.res])
        gemm_phase(xT, D, w_m1[l], seg(0, HID, "m1"), "fm", epi_m1, TPh)
        gemm_phase(hT, HID, w_m2[l], seg(0, D, "m2"), "tm", make_epi_res(xF), 1024, kbp=8)
        ln_phase(zF, True, l, 1, xF, out if last else None)

    glu_holder = {}

    def gla_like_h(l):
        gla_like(l, "hgrn")

    glu_holder["t"] = ar.alloc([128, 8], F32)
    oml = ar.alloc([128, 8], F32)
    ar_small["oml"] = oml
    ar.base = ar.off

    c_setup()
    ln_phase(x_in, False, 0, 0, None, None)
    for l in range(L):
        P.op("dve", lambda h, l=l: h.tensor_scalar(out=oml[:], in0=lb_all[:, :, l], scalar1=-1.0, scalar2=1.0,
                                                   op0=ALU.mult, op1=ALU.add), reads=[lb_all.res], writes=[oml.res])
        layer(l, l == L - 1)
    P.stopped = False
    P._barrier()
    P.emit()
    st.close()
    return nc


def prep_weights(inp, L):
    f = np.float32
    g = {}
    g["w_in"] = np.ascontiguousarray(inp["w_in"][:L], dtype=f)
    g["w_glu"] = np.ascontiguousarray(inp["s5_w_glu"][:L], dtype=f)
    g["w_up"] = np.ascontiguousarray(inp["w_up"][:L], dtype=f).reshape(L, 3 * W, D)
    g["w_out"] = np.ascontiguousarray(inp["w_out"][:L], dtype=f)
    g["w_m1"] = np.ascontiguousarray(inp["w_mlp_in"][:L], dtype=f)
    g["w_m2"] = np.ascontiguousarray(inp["w_mlp_out"][:L], dtype=f)
    g["lnp"] = np.ascontiguousarray(np.stack([inp["ln1_g"][:L], inp["ln1_b"][:L], inp["ln2_g"][:L], inp["ln2_b"][:L]], axis=1), dtype=f)
    lam_re = np.asarray(inp["s5_lam_re"][:L], f).reshape(L, 32, 128).transpose(0, 2, 1)
    lam_im = np.asarray(inp["s5_lam_im"][:L], f).reshape(L, 32, 128).transpose(0, 2, 1)
    ldt = np.repeat(np.asarray(inp["s5_log_dt"][:L], f)[:, :, None], 64, axis=2).reshape(L, 32, 128).transpose(0, 2, 1)
    g["s5p"] = np.ascontiguousarray(np.stack([lam_re, lam_im, ldt], axis=2), dtype=f)
    bpad = np.zeros((L, 2, 128, 32, 128), f)
    cpad = np.zeros((L, 2, 128, 32, 128), f)
    for ri, (bk, ck) in enumerate((("s5_b_re", "s5_c_re"), ("s5_b_im", "s5_c_im"))):
        Bm = np.asarray(inp[bk][:L], f)
        Cm = np.asarray(inp[ck][:L], f)
        for sb in range(32):
            for gi in range(2):
                gidx = 2 * sb + gi
                r0 = (gidx % 8) * 16
                bpad[:, ri, r0:r0 + 16, sb, gi * 64:(gi + 1) * 64] = Bm[:, gidx].transpose(0, 2, 1)
                cpad[:, ri, gi * 64:(gi + 1) * 64, sb, r0:r0 + 16] = Cm[:, gidx].transpose(0, 2, 1)
    g["s5b"] = bpad
    g["s5c"] = cpad
    dsk = np.asarray(inp["s5_d"][:L], f).reshape(L, 8, 128).transpose(0, 2, 1)
    bgl = np.asarray(inp["s5_b_glu"][:L], f).reshape(L, 8, 128).transpose(0, 2, 1)
    g["s5v"] = np.ascontiguousarray(np.stack([dsk, bgl], axis=2), dtype=f)
    g["glaw"] = np.ascontiguousarray(inp["gla_w_gate"][:L], dtype=f)
    bg = np.asarray(inp["gla_b_gate"][:L], f).reshape(L, 4, 128).transpose(0, 2, 1)
    nw = np.asarray(inp["gla_norm_w"][:L], f).reshape(L, 2, 128).transpose(0, 2, 1)
    g["glav"] = np.ascontiguousarray(np.concatenate([bg, nw], axis=2), dtype=f)
    g["hlb"] = np.ascontiguousarray(np.asarray(inp["hgrn_lb_logits"], f).reshape(DEPTH, 8, 128).transpose(2, 1, 0), dtype=f)
    g["hnw"] = np.ascontiguousarray(np.asarray(inp["hgrn_norm_w"][:L], f).T, dtype=f)
    return g


_CACHE = {}


def kernel(**inputs):
    x = np.asarray(inputs["x"], np.float32)
    Bn, T, _ = x.shape
    key = (T, DEPTH)
    if key not in _CACHE:
        _CACHE[key] = build(T, DEPTH)
    nc = _CACHE[key]
    wts = prep_weights(inputs, DEPTH)
    in_maps = []
    for b in range(Bn):
        m = dict(wts)
        m["x"] = np.ascontiguousarray(x[b])
        in_maps.append(m)
    res = run_bass_kernel_spmd(nc, in_maps, core_ids=list(range(Bn)))
    return np.stack([np.asarray(r["out"], np.float32) for r in res.results], axis=0)
```

```python
import contextlib
import math
import types
import numpy as np
import concourse.bass as bass
import concourse.mybir as mybir
from concourse.bass_utils import run_bass_kernel_spmd

F32 = mybir.dt.float32
BF16 = mybir.dt.bfloat16
AF = mybir.ActivationFunctionType
ALU = mybir.AluOpType

D = 2048
W = 1024
NIN = 14352
HID = 8192
DEPTH = 4
ALPHA = (2 * DEPTH) ** 0.25
TWO_PI = 2.0 * math.pi
O_UA, O_QB, O_KB, O_VB, O_GL, O_GB, O_QC, O_FC, O_IC, O_GC, O_MG = (
    0, 1024, 1536, 2048, 3072, 3088, 4112, 5136, 6160, 7184, 8208)


def _freeze(fn):
    if fn.__closure__ is None:
        return fn
    cells = []
    for c in fn.__closure__:
        try:
            cells.append(types.CellType(c.cell_contents))
        except ValueError:
            cells.append(c)
    g = types.FunctionType(fn.__code__, fn.__globals__, fn.__name__, fn.__defaults__, tuple(cells))
    g.__kwdefaults__ = fn.__kwdefaults__
    return g


class Res:
    __slots__ = ("w", "r")

    def __init__(self):
        self.w = None
        self.r = {}


class Prog:
    KQ = 8
    ENGS = ("pe", "act", "dve", "pool", "sp")

    def __init__(self, nc, st):
        self.nc = nc
        self.eng = {}
        hs = dict(pe=nc.tensor, act=nc.scalar, dve=nc.vector, pool=nc.gpsimd, sp=nc.sync)
        for name in self.ENGS:
            sem = st.enter_context(nc.semaphore("sem_" + name))
            self.eng[name] = dict(h=hs[name], sem=sem, n=0, prog=[], waited={})
        self.dq = {}
        for q in ("sp", "pool", "act"):
            sems = [st.enter_context(nc.semaphore(f"dq_{q}_{i}")) for i in range(self.KQ)]
            self.dq[q] = dict(sems=sems, n=0)
        self.dma_uid = 0
        self.stopped = False
        self.nphase = 0
        self.max_phase = 10 ** 9

    def _waits(self, eng, reads, writes):
        E = self.eng[eng]
        deps = []
        for r in reads:
            if r.w is not None:
                deps.append(r.w)
        for w in writes:
            if w.w is not None:
                deps.append(w.w)
            deps.extend(w.r.values())
        waits = []
        for (sem, val, key, src) in deps:
            if src == "pe" and eng == "pe":
                continue
            if E["waited"].get(key, 0) >= val:
                continue
            E["waited"][key] = val
            waits.append((sem, val))
        return waits

    def _record(self, tok, reads, writes, rkey):
        for r in reads:
            r.r[rkey] = tok
        for w in writes:
            w.w = tok
            w.r = {}

    def op(self, eng, fn, reads=(), writes=()):
        if self.stopped:
            return None
        E = self.eng[eng]
        fn = _freeze(fn)
        waits = self._waits(eng, reads, writes)
        E["n"] += 1
        sem = E["sem"]
        tok = (sem, E["n"], "e_" + eng, eng)

        def emit(h, fn=fn, waits=waits, sem=sem):
            for s, v in waits:
                h.wait_ge(s, v)
            fn(h).then_inc(sem, 1)

        E["prog"].append(emit)
        self._record(tok, reads, writes, eng)
        return tok

    def dma(self, q, out, in_, reads=(), writes=()):
        if self.stopped:
            return None
        E = self.eng[q]
        Dq = self.dq[q]
        waits = self._waits(q, reads, writes)
        n = Dq["n"]
        Dq["n"] += 1
        s = Dq["sems"][n % self.KQ]
        val = 16 * (n // self.KQ + 1)
        key = f"dq_{q}_{n % self.KQ}"
        if n >= self.KQ and E["waited"].get(key, 0) < val - 16:
            E["waited"][key] = val - 16
            waits.append((s, val - 16))
        tok = (s, val, key, "dma")

        def emit(h, waits=waits, s=s, out=out, in_=in_):
            for ss, v in waits:
                h.wait_ge(ss, v)
            h.dma_start(out=out, in_=in_).then_inc(s, 16)

        E["prog"].append(emit)
        self.dma_uid += 1
        self._record(tok, reads, writes, "dma%d" % self.dma_uid)
        return tok

    def barrier(self):
        if self.stopped:
            return
        self.nphase += 1
        if self.nphase >= self.max_phase:
            self._barrier()
            self.stopped = True
            return
        self._barrier()

    def _barrier(self):
        targets = []
        for name in self.ENGS:
            X = self.eng[name]
            if X["n"] > 0:
                targets.append((X["sem"], X["n"], "e_" + name))
        for q, Dq in self.dq.items():
            n = Dq["n"]
            for i in range(min(n, self.KQ)):
                cnt = (n - 1 - i) // self.KQ + 1
                targets.append((Dq["sems"][i], 16 * cnt, f"dq_{q}_{i}"))
        for name in self.ENGS:
            E = self.eng[name]
            waits = []
            for (sem, val, key) in targets:
                if key == "e_" + name and name == "pe":
                    continue
                if E["waited"].get(key, 0) >= val:
                    continue
                E["waited"][key] = val
                waits.append((sem, val))

            def emit(h, waits=waits):
                for s, v in waits:
                    h.wait_ge(s, v)

            E["prog"].append(emit)

    def emit(self):
        nc = self.nc
        with nc.Block() as block:
            @block.tensor
            def _(h):
                for f in self.eng["pe"]["prog"]:
                    f(h)

            @block.scalar
            def _(h):
                for f in self.eng["act"]["prog"]:
                    f(h)

            @block.vector
            def _(h):
                for f in self.eng["dve"]["prog"]:
                    f(h)

            @block.gpsimd
            def _(h):
                for f in self.eng["pool"]["prog"]:
                    f(h)

            @block.sync
            def _(h):
                for f in self.eng["sp"]["prog"]:
                    f(h)


class Tile:
    def __init__(self, t):
        self.t = t
        self.res = Res()

    def __getitem__(self, k):
        return self.t[k]


class Arena:
    def __init__(self, nc, limit):
        self.nc = nc
        self.off = 16384
        self.limit = limit
        self.cnt = 0
        self.base = 16384

    def reset(self, to=None):
        self.off = self.base if to is None else to

    def alloc(self, shape, dtype):
        nbytes = int(np.prod(shape[1:])) * (4 if dtype == F32 else 2)
        nbytes = (nbytes + 63) // 64 * 64
        assert self.off + nbytes <= self.limit, (self.off, nbytes, self.limit)
        self.cnt += 1
        t = self.nc.alloc_sbuf_tensor_at("sb%d" % self.cnt, list(shape), dtype, offset=self.off)
        self.off += nbytes
        return Tile(t)


class K:
    pass


def build(T, L, TSPAN=1024, max_phase=10 ** 9):
    nc = bass.Bass("TRN2", target_bir_lowering=False)
    st = contextlib.ExitStack()
    P = Prog(nc, st)
    P.max_phase = max_phase
    TS = min(TSPAN, T)
    NSP = T // TS

    def din(name, shape, dt=F32):
        return nc.dram_tensor(name, list(shape), dt, kind="ExternalInput").ap()

    def dscr(name, shape, dt):
        import os
        kind = "ExternalOutput" if os.environ.get("KDEBUG") else "Internal"
        return nc.dram_tensor(name, list(shape), dt, kind=kind).ap()

    x_in = din("x", [T, D])
    w_in = din("w_in", [L, D, NIN])
    w_glu = din("w_glu", [L, W, W])
    w_up = din("w_up", [L, 3 * W, D])
    w_out = din("w_out", [L, D, D])
    w_m1 = din("w_m1", [L, D, HID])
    w_m2 = din("w_m2", [L, HID, D])
    lnp = din("lnp", [L, 4, D])
    s5p = din("s5p", [L, 128, 3, 32])
    s5b = din("s5b", [L, 2, 128, 32, 128])
    s5c = din("s5c", [L, 2, 128, 32, 128])
    s5v = din("s5v", [L, 128, 2, 8])
    glaw = din("glaw", [L, 16, 512])
    glav = din("glav", [L, 128, 6])
    hlb = din("hlb", [128, 8, DEPTH])
    hnw = din("hnw", [128, L])
    out = nc.dram_tensor("out", [T, D], F32, kind="ExternalOutput").ap()

    xF = dscr("xF", [T, D], F32)
    zF = dscr("zF", [T, D], F32)
    xT = dscr("xT", [D, T], BF16)
    hU = dscr("hU", [W, T], BF16)
    hQB = dscr("hQB", [512, T], BF16)
    hKB = dscr("hKB", [512, T], BF16)
    hGL = dscr("hGL", [16, T], F32)
    hGB = dscr("hGB", [W, T], BF16)
    hVB = dscr("hVB", [T, W], BF16)
    hQC = dscr("hQC", [W, T], BF16)
    hFC = dscr("hFC", [W, T], F32)
    hIC = dscr("hIC", [T, W], BF16)
    hGC = dscr("hGC", [W, T], BF16)
    hMG = dscr("hMG", [3 * D, T], BF16)
    zT = dscr("zT", [W, T], BF16)
    yT = dscr("yT", [3 * W, T], BF16)
    mT = dscr("mT", [D, T], BF16)
    hT = dscr("hT", [HID, T], BF16)

    SB_LIMIT = 192 * 1024
    ar = Arena(nc, SB_LIMIT)
    psum = nc.alloc_psum_tensor("psum", [128, 8, 512], F32)
    banks = [Res() for _ in range(8)]
    bank_i = [0]

    def next_bank():
        b = bank_i[0] % 8
        bank_i[0] += 1
        return b

    sub_i = {}

    def next_bank_in(lo, hi):
        i = sub_i.get((lo, hi), 0)
        sub_i[(lo, hi)] = i + 1
        return lo + i % (hi - lo)

    ident = ar.alloc([128, 128], BF16)
    ones_f = ar.alloc([128, 128], F32)
    mask = ar.alloc([128, 128], F32)
    ones_bf = ar.alloc([128, 128], BF16)
    ones_bf2 = ar.alloc([128, 128], BF16)
    cneg_pi = ar.alloc([128, 1], F32)
    ceps5 = ar.alloc([128, 1], F32)
    ceps6 = ar.alloc([128, 1], F32)
    lb_all = ar.alloc([128, 8, DEPTH], F32)
    hnw_t = ar.alloc([128, L], F32)
    ar.base = ar.off

    def c_setup():
        P.op("pool", lambda h: h.memset(ones_f[:], 1.0), writes=[ones_f.res])
        P.op("pool", lambda h: h.memset(cneg_pi[:], -math.pi), writes=[cneg_pi.res])
        P.op("pool", lambda h: h.memset(ceps5[:], 1e-5), writes=[ceps5.res])
        P.op("pool", lambda h: h.memset(ceps6[:], 1e-6), writes=[ceps6.res])
        P.op("pool", lambda h: h.memset(ones_bf[:], 1.0 / 128.0), writes=[ones_bf.res])
        P.op("pool", lambda h: h.memset(ones_bf2[:], 1.0 / 256.0), writes=[ones_bf2.res])
        P.op("pool", lambda h: h.affine_select(out=ident[:], in_=ones_f[:], pattern=[[-1, 128]], base=0,
                                               channel_multiplier=1, compare_op=ALU.is_equal, fill=0.0),
             reads=[ones_f.res], writes=[ident.res])
        P.op("pool", lambda h: h.affine_select(out=mask[:], in_=ones_f[:], pattern=[[1, 128]], base=0,
                                               channel_multiplier=-1, compare_op=ALU.is_ge, fill=0.0),
             reads=[ones_f.res], writes=[mask.res])
        P.op("pool", lambda h: h.memset(mask[0:64, 64:128], 0.0), writes=[mask.res])
        P.dma("sp", lb_all[:], hlb, writes=[lb_all.res])
        P.dma("sp", hnw_t[:], hnw, writes=[hnw_t.res])
        P.op("act", lambda h: h.activation(out=lb_all[:], in_=lb_all[:], func=AF.Exp),
             reads=[lb_all.res], writes=[lb_all.res])
        ssum = ar.alloc([128, 8, 1], F32)
        P.op("dve", lambda h: h.tensor_add(out=ssum[:], in0=lb_all[:, :, 0:1], in1=lb_all[:, :, 1:2]),
             reads=[lb_all.res], writes=[ssum.res])
        for j in (2, 3):
            P.op("dve", lambda h, j=j: h.tensor_add(out=ssum[:], in0=ssum[:], in1=lb_all[:, :, j:j + 1]),
                 reads=[lb_all.res, ssum.res], writes=[ssum.res])
        P.op("dve", lambda h: h.reciprocal(out=ssum[:], in_=ssum[:]), reads=[ssum.res], writes=[ssum.res])
        P.op("dve", lambda h: h.tensor_tensor(out=lb_all[:], in0=lb_all[:], in1=ssum[:].to_broadcast([128, 8, DEPTH]),
                                              op=ALU.mult), reads=[lb_all.res, ssum.res], writes=[lb_all.res])
        P.op("dve", lambda h: h.tensor_add(out=lb_all[:, :, 2:3], in0=lb_all[:, :, 2:3], in1=lb_all[:, :, 1:2]),
             reads=[lb_all.res], writes=[lb_all.res])
        P.op("dve", lambda h: h.tensor_add(out=lb_all[:, :, 3:4], in0=lb_all[:, :, 3:4], in1=lb_all[:, :, 2:3]),
             reads=[lb_all.res], writes=[lb_all.res])
        P.op("dve", lambda h: h.memset(lb_all[:, :, 0:1], 0.0), writes=[lb_all.res])
        P.barrier()

    def ps_ap(b, rows=128, n=512):
        return psum[0:rows, b, 0:n]

    class Rot:
        def __init__(self, tiles):
            self.tiles = tiles
            self.i = 0

        def next(self):
            t = self.tiles[self.i % len(self.tiles)]
            self.i += 1
            return t

    def mm_group(bank, out_ap, pairs, extra_reads, start=True, stop=True):
        def fn(h):
            ins = None
            n = len(pairs)
            for i, (l, r) in enumerate(pairs):
                ins = h.matmul(out_ap, l, r, start=(start and i == 0), stop=(stop and i == n - 1))
            return ins
        return P.op("pe", fn, reads=extra_reads, writes=[banks[bank]])

    def gemm_phase(act_src, Kd, w_src, cols, mode, epi, TP, kgroups=None, kbp=None):
        ar.reset()
        KB = Kd // 128
        nparts = (1 if KB <= 24 else KB // 16) if kbp is None else KB // kbp
        KBP = KB // nparts
        actA = ar.alloc([128, KB * TP], BF16)
        wb = Rot([ar.alloc([128, KBP, 512], BF16) for _ in range(2)])
        aux = dict(
            sf=Rot([ar.alloc([128, 512], F32) for _ in range(4)]),
            sb=Rot([ar.alloc([128, 512], BF16) for _ in range(4)]),
            xf=Rot([ar.alloc([128, 3 if mode == "fm" else 1, 512], F32) for _ in range(2)]),
            gb=Rot([ar.alloc([128, 3 if mode == "fm" else 1, 512 if mode == "fm" else 8], BF16) for _ in range(2)]),
        )
        NQ = TP // 512
        actq = [Res() for _ in range(NQ)]
        if kgroups is None:
            kgroups = [(0, KBP)]
        actv = actA[:].rearrange("p (k t) -> p k t", t=TP)
        for tp in range(T // TP):
            for q in range(NQ):
                P.dma("sp", actv[:, :, q * 512:(q + 1) * 512],
                      act_src[:, tp * TP + q * 512:tp * TP + (q + 1) * 512].rearrange("(k p) t -> p k t", p=128),
                      writes=[actq[q]])
            loads = [(ci, pa) for ci in range(len(cols)) for pa in range(nparts)]
            wtiles = {}

            def load(idx):
                ci, pa = loads[idx]
                off, width, tag = cols[ci]
                wt = wb.next()
                P.dma("pool", wt[:, :, 0:width],
                      w_src[pa * KBP * 128:(pa + 1) * KBP * 128, off:off + width].rearrange("(k p) n -> p k n", p=128),
                      writes=[wt.res])
                wtiles[idx] = wt

            load(0)
            for idx in range(len(loads)):
                if idx + 1 < len(loads):
                    load(idx + 1)
                ci, pa = loads[idx]
                off, width, tag = cols[ci]
                wt = wtiles.pop(idx)
                if mode == "fm":
                    assert nparts == 1
                    for nb in range((width + 127) // 128):
                        rows = min(128, width - nb * 128)
                        for tt in range(TP // 512):
                            bl = []
                            for (k0, k1) in kgroups:
                                b = next_bank()
                                pairs = [(wt[:, kb, nb * 128:nb * 128 + rows], actv[:, kb, tt * 512:(tt + 1) * 512])
                                         for kb in range(k0, k1)]
                                mm_group(b, ps_ap(b, rows), pairs, [wt.res, actq[tt]])
                                bl.append(b)
                            epi(bl, tag, off + nb * 128, rows, tp * TP + tt * 512, aux)
                else:
                    nt = TP // 128
                    if nparts == 1:
                        for t1 in range(nt):
                            b = next_bank()
                            pairs = [(actv[:, kb, t1 * 128:(t1 + 1) * 128], wt[:, kb, 0:width]) for kb in range(KBP)]
                            mm_group(b, ps_ap(b, 128, width), pairs, [wt.res, actq[t1 // 4]])
                            epi([b], tag, off, width, tp * TP + t1 * 128, aux)
                    else:
                        assert nt <= 8
                        if pa == 0:
                            cur = [next_bank() for _ in range(nt)]
                            wtiles["cur"] = cur
                        cur = wtiles["cur"]
                        for t1 in range(nt):
                            b = cur[t1]
                            pairs = [(actv[:, pa * KBP + kb, t1 * 128:(t1 + 1) * 128], wt[:, kb, 0:width])
                                     for kb in range(KBP)]
                            mm_group(b, ps_ap(b, 128, width), pairs, [wt.res, actq[t1 // 4]],
                                     start=(pa == 0), stop=(pa == nparts - 1))
                            if pa == nparts - 1:
                                epi([b], tag, off, width, tp * TP + t1 * 128, aux)
        P.barrier()

    def evac_store(b, rows, n, func, scale, dt, dest, aux, bias=None, eng=None):
        stg = (aux["sb"] if dt == BF16 else aux["sf"]).next()
        src = ps_ap(b, rows, n)
        if func is None and (eng or "dve") == "dve":
            P.op("dve", lambda h: h.tensor_scalar(out=stg[0:rows, 0:n], in0=src, scalar1=float(scale), scalar2=None,
                                                  op0=ALU.mult), reads=[banks[b]], writes=[stg.res])
        else:
            f = AF.Copy if func is None else func
            if bias is None:
                P.op("act", lambda h: h.activation(out=stg[0:rows, 0:n], in_=src, func=f, scale=float(scale)),
                     reads=[banks[b]], writes=[stg.res])
            else:
                P.op("act", lambda h: h.activation(out=stg[0:rows, 0:n], in_=src, func=f, scale=float(scale),
                                                   bias=bias), reads=[banks[b]], writes=[stg.res])
        P.dma("sp", dest, stg[0:rows, 0:n], reads=[stg.res])

    def ln_phase(src, norm, l, which, dstF, dstOut):
        ar.reset()
        if norm:
            g_bc = ar.alloc([128, D], F32)
            b_bc = ar.alloc([128, D], F32)
            P.dma("sp", g_bc[:], lnp[l, 2 * which, :].partition_broadcast(128), writes=[g_bc.res])
            P.dma("sp", b_bc[:], lnp[l, 2 * which + 1, :].partition_broadcast(128), writes=[b_bc.res])
        zt = Rot([ar.alloc([128, D], F32) for _ in range(4)])
        jk = Rot([ar.alloc([128, D], F32) for _ in range(2)])
        xb = Rot([ar.alloc([128, D], BF16) for _ in range(3)])
        xtt = Rot([ar.alloc([128, 16, 128], BF16) for _ in range(3)])
        stat = Rot([ar.alloc([128, 8], F32) for _ in range(4)])
        for tt in range(T // 128):
            z = zt.next()
            P.dma("sp", z[:], src[tt * 128:(tt + 1) * 128, :], writes=[z.res])
            xbt = xb.next()
            if norm:
                s = stat.next()
                j = jk.next()
                P.op("act", lambda h: h.activation(out=j[:], in_=z[:], func=AF.Copy, accum_out=s[:, 0:1]),
                     reads=[z.res], writes=[j.res, s.res])
                P.op("act", lambda h: h.activation(out=j[:], in_=z[:], func=AF.Square, accum_out=s[:, 1:2]),
                     reads=[z.res], writes=[j.res, s.res])
                P.op("dve", lambda h: h.tensor_scalar(out=s[:, 2:3], in0=s[:, 0:1], scalar1=1.0 / D, scalar2=None,
                                                      op0=ALU.mult), reads=[s.res], writes=[s.res])
                P.op("dve", lambda h: h.tensor_tensor(out=s[:, 3:4], in0=s[:, 2:3], in1=s[:, 2:3], op=ALU.mult),
                     reads=[s.res], writes=[s.res])
                P.op("dve", lambda h: h.scalar_tensor_tensor(out=s[:, 4:5], in0=s[:, 1:2], scalar=1.0 / D,
                                                             in1=s[:, 3:4], op0=ALU.mult, op1=ALU.subtract),
                     reads=[s.res], writes=[s.res])
                P.op("act", lambda h: h.activation(out=s[:, 5:6], in_=s[:, 4:5], func=AF.Ln, bias=ceps5[:, 0:1]),
                     reads=[s.res, ceps5.res], writes=[s.res])
                P.op("act", lambda h: h.activation(out=s[:, 5:6], in_=s[:, 5:6], func=AF.Exp, scale=-0.5),
                     reads=[s.res], writes=[s.res])
                P.op("dve", lambda h: h.tensor_scalar(out=z[:], in0=z[:], scalar1=s[:, 2:3], scalar2=s[:, 5:6],
                                                      op0=ALU.subtract, op1=ALU.mult),
                     reads=[z.res, s.res], writes=[z.res])
                P.op("dve", lambda h: h.tensor_tensor(out=z[:], in0=z[:], in1=g_bc[:], op=ALU.mult),
                     reads=[z.res, g_bc.res], writes=[z.res])
                P.op("dve", lambda h: h.tensor_tensor(out=z[:], in0=z[:], in1=b_bc[:], op=ALU.add),
                     reads=[z.res, b_bc.res], writes=[z.res])
                P.dma("sp", dstF[tt * 128:(tt + 1) * 128, :], z[:], reads=[z.res])
                if dstOut is not None:
                    P.dma("sp", dstOut[tt * 128:(tt + 1) * 128, :], z[:], reads=[z.res])
            P.op("act", lambda h: h.activation(out=xbt[:], in_=z[:], func=AF.Copy), reads=[z.res], writes=[xbt.res])
            xt_ = xtt.next()
            for half in range(2):
                b = next_bank()
                pv = psum[:, b, :].bitcast(BF16).rearrange("p (k t) -> p k t", t=128)

                def fn(h, b=b, pv=pv, half=half, xbt=xbt):
                    ins = None
                    for k in range(8):
                        kk = half * 8 + k
                        ins = h.transpose(pv[:, k, :], xbt[:, kk * 128:(kk + 1) * 128], ident[:])
                    return ins
                P.op("pe", fn, reads=[xbt.res, ident.res], writes=[banks[b]])
                eng = "dve" if half == 0 else "act"
                if eng == "dve":
                    P.op("dve", lambda h, pv=pv, half=half, xt_=xt_: h.tensor_copy(out=xt_[:, half * 8:half * 8 + 8, :], in_=pv),
                         reads=[banks[b]], writes=[xt_.res])
                else:
                    P.op("act", lambda h, pv=pv, half=half, xt_=xt_: h.activation(out=xt_[:, half * 8:half * 8 + 8, :], in_=pv, func=AF.Copy),
                         reads=[banks[b]], writes=[xt_.res])
            P.dma("sp", xT[:, tt * 128:(tt + 1) * 128].rearrange("(k p) t -> p k t", p=128), xt_[:], reads=[xt_.res])
        P.barrier()

    def gla_like(l, kind):
        ar.reset()
        NC = TS // 64
        NPC = TS // 128
        nh = 4 if kind == "gla" else 8
        nvb = 2 if kind == "gla" else 1
        DV = 128 * nvb
        gs = (-1.0 / 16.0) if kind == "gla" else 1.0
        mask0 = ar.alloc([128, TS], F32)
        P.op("pool", lambda h: h.memset(mask0[:], 1.0), writes=[mask0.res])
        P.op("pool", lambda h: h.memset(mask0[:].rearrange("p (c i) -> p c i", i=64)[:, :, 0:1], 0.0), writes=[mask0.res])
        gsets = []
        for _ in range(2):
            gsets.append((ar.alloc([128, TS], F32), ar.alloc([128, TS], F32), ar.alloc([128, TS], F32), ar.alloc([128, TS], F32),
                          ar.alloc([128, TS], F32), ar.alloc([128, TS], BF16), ar.alloc([128, TS], BF16), ar.alloc([128, TS], BF16),
                          ar.alloc([128, TS], BF16), ar.alloc([128, TS], BF16), ar.alloc([128, NC], F32),
                          ar.alloc([128, NPC, DV], BF16), ar.alloc([128, NPC, 128], BF16)))
        git = [0]
        KV = ar.alloc([128, NC, DV], F32)
        Sall = ar.alloc([128, NC + 1, DV], F32)
        Sbf = ar.alloc([128, NC, DV], BF16)
        AT = Rot([ar.alloc([128, 128], BF16) for _ in range(3)])
        og = Rot([ar.alloc([128, nvb, 512], F32) for _ in range(2)])
        sq = Rot([ar.alloc([128, nvb, 512], BF16) for _ in range(2)])
        gt = Rot([ar.alloc([128, nvb, 512], BF16) for _ in range(2)])
        rstd = Rot([ar.alloc([128, 512], F32) for _ in range(2)])
        yb = Rot([ar.alloc([128, nvb, 512], BF16) for _ in range(2)])
        glt = ar.alloc([16, TS], F32)
        wg = ar.alloc([16, 512], F32)
        gv = ar.alloc([128, 6], F32)
        nbg = ar.alloc([128, 4], F32)
        if kind == "gla":
            P.dma("sp", wg[:], glaw[l], writes=[wg.res])
            P.dma("sp", gv[:], glav[l], writes=[gv.res])
            P.op("dve", lambda h: h.tensor_scalar(out=nbg[:], in0=gv[:, 0:4], scalar1=-1.0, scalar2=None, op0=ALU.mult),
                 reads=[gv.res], writes=[nbg.res])
        def front(hh, sp, itn):
            fr, gg, G, tmp, tmp2, qb, kb_, qt, kt, kpT, egl, vtok, kptok = gsets[itn % 2]
            t0 = sp * TS
            yield
            if kind == "gla":
                if hh == 0:
                    pass
                P.dma("sp", glt[:], hGL[:, t0:t0 + TS], writes=[glt.res])
                for tt in range(TS // 512):
                    b = next_bank_in(7, 8)
                    mm_group(b, ps_ap(b), [(wg[:, hh * 128:(hh + 1) * 128], glt[:, tt * 512:(tt + 1) * 512])],
                             [wg.res, glt.res])
                    P.op("act", lambda h, b=b, tt=tt: h.activation(out=fr[:, tt * 512:(tt + 1) * 512], in_=ps_ap(b),
                                                                   func=AF.Exp, scale=-1.0, bias=nbg[:, hh:hh + 1]),
                         reads=[banks[b], nbg.res], writes=[fr.res])
                P.op("act", lambda h: h.activation(out=gg[:], in_=fr[:], func=AF.Ln, bias=1.0, scale=1.0),
                     reads=[fr.res], writes=[gg.res])
                P.dma("sp", qb[:], hQB[hh * 128:(hh + 1) * 128, t0:t0 + TS], writes=[qb.res])
                P.dma("sp", kb_[:], hKB[hh * 128:(hh + 1) * 128, t0:t0 + TS], writes=[kb_.res])
                kk = kb_
                P.dma("sp", vtok[:], hVB[t0:t0 + TS, hh * DV:(hh + 1) * DV].rearrange("(c p) v -> p c v", p=128),
                      writes=[vtok.res])
            else:
                P.dma("sp", fr[:], hFC[hh * 128:(hh + 1) * 128, t0:t0 + TS], writes=[fr.res])
                P.op("act", lambda h: h.activation(out=fr[:], in_=fr[:], func=AF.Sigmoid), reads=[fr.res], writes=[fr.res])
                oml = ar_small["oml"]
                P.op("dve", lambda h: h.tensor_scalar(out=fr[:], in0=fr[:], scalar1=oml[:, hh:hh + 1],
                                                      scalar2=lb_all[:, hh, l:l + 1], op0=ALU.mult, op1=ALU.add),
                     reads=[fr.res, oml.res, lb_all.res], writes=[fr.res])
                P.op("act", lambda h: h.activation(out=gg[:], in_=fr[:], func=AF.Ln), reads=[fr.res], writes=[gg.res])
                P.op("dve", lambda h: h.tensor_scalar(out=fr[:], in0=fr[:], scalar1=-1.0, scalar2=1.0,
                                                      op0=ALU.mult, op1=ALU.add), reads=[fr.res], writes=[fr.res])
                kk = fr
                P.dma("sp", qb[:], hQC[hh * 128:(hh + 1) * 128, t0:t0 + TS], writes=[qb.res])
                P.dma("sp", vtok[:], hIC[t0:t0 + TS, hh * DV:(hh + 1) * DV].rearrange("(c p) v -> p c v", p=128),
                      writes=[vtok.res])
            yield
            P.op("dve", lambda h: h.tensor_tensor_scan(out=G[:], data0=mask0[:], data1=gg[:], initial=0.0,
                                                       op0=ALU.mult, op1=ALU.add),
                 reads=[mask0.res, gg.res], writes=[G.res])
            yield
            Gv = G[:].rearrange("p (c i) -> p c i", i=64)
            yield
            P.op("act", lambda h: h.activation(out=tmp[:], in_=G[:], func=AF.Exp, scale=gs), reads=[G.res], writes=[tmp.res])
            yield
            P.op("dve", lambda h: h.tensor_tensor(out=qt[:], in0=qb[:], in1=tmp[:], op=ALU.mult),
                 reads=[qb.res, tmp.res], writes=[qt.res])
            yield
            P.op("act", lambda h: h.activation(out=tmp2[:], in_=G[:], func=AF.Exp, scale=-gs), reads=[G.res], writes=[tmp2.res])
            yield
            P.op("dve", lambda h, kk=kk: h.tensor_tensor(out=kt[:], in0=kk[:], in1=tmp2[:], op=ALU.mult),
                 reads=[kk.res, tmp2.res], writes=[kt.res])
            yield
            P.op("dve", lambda h: h.tensor_tensor(out=tmp[:].rearrange("p (c i) -> p c i", i=64),
                                                  in0=Gv[:, :, 63:64].to_broadcast([128, NC, 64]), in1=Gv,
                                                  op=ALU.subtract), reads=[G.res, tmp.res, qt.res], writes=[tmp.res])
            yield
            P.op("act", lambda h: h.activation(out=tmp[:], in_=tmp[:], func=AF.Exp, scale=gs), reads=[tmp.res], writes=[tmp.res])
            yield
            P.op("dve", lambda h, kk=kk: h.tensor_tensor(out=kpT[:], in0=kk[:], in1=tmp[:], op=ALU.mult),
                 reads=[kk.res, tmp.res], writes=[kpT.res])
            yield
            P.op("act", lambda h: h.activation(out=egl[:], in_=Gv[:, :, 63], func=AF.Exp, scale=gs),
                 reads=[G.res], writes=[egl.res])
            yield
            for g8 in range((NPC + 7) // 8):
                b = next_bank_in(7, 8)
                pv = psum[:, b, :].bitcast(BF16).rearrange("p (k t) -> p k t", t=128)
                n8 = min(8, NPC - g8 * 8)

                def fn(h, pv=pv, g8=g8, n8=n8):
                    ins = None
                    for k in range(n8):
                        pc = g8 * 8 + k
                        ins = h.transpose(pv[:, k, :], kpT[:, pc * 128:(pc + 1) * 128], ident[:])
                    return ins
                P.op("pe", fn, reads=[kpT.res, ident.res], writes=[banks[b]])
                P.op("act", lambda h, pv=pv, g8=g8, n8=n8: h.activation(out=kptok[:, g8 * 8:g8 * 8 + n8, :], in_=pv[:, 0:n8, :],
                                                                        func=AF.Copy), reads=[banks[b]], writes=[kptok.res])
            yield
        def back(hh, sp, itn):
            fr, gg, G, tmp, tmp2, qb, kb_, qt, kt, kpT, egl, vtok, kptok = gsets[itn % 2]
            t0 = sp * TS
            kk = kb_ if kind == "gla" else fr
            if sp == 0:
                P.op("dve", lambda h: h.memset(Sall[:, 0, :], 0.0), writes=[Sall.res])
            if sp > 0:
                P.op("dve", lambda h: h.tensor_copy(out=Sall[:, 0, :], in_=Sall[:, NC, :]),
                     reads=[Sall.res], writes=[Sall.res])

            yield
            per_bank = 512 // DV
            yield
            KVv = KV[:].rearrange("p (c two) v -> p c two v", two=2)
            yield
            for p0 in range(0, NPC, per_bank):
                bA, bB = next_bank_in(4, 7), next_bank_in(4, 7)
                pA = psum[:, bA, :].rearrange("p (c v) -> p c v", v=DV)
                pB = psum[:, bB, :].rearrange("p (c v) -> p c v", v=DV)

                def fn(h, p0=p0, pA=pA, pB=pB):
                    ins = None
                    for i in range(per_bank):
                        pc = p0 + i
                        h.matmul(pA[:, i, :], kptok[0:64, pc, :], vtok[0:64, pc, :], start=True, stop=True)
                        ins = h.matmul(pB[:, i, :], kptok[64:128, pc, :], vtok[64:128, pc, :], start=True, stop=True)
                    return ins
                P.op("pe", fn, reads=[kptok.res, vtok.res], writes=[banks[bA], banks[bB]])
                P.op("dve", lambda h, p0=p0, pA=pA: h.tensor_copy(out=KVv[:, p0:p0 + per_bank, 0, :], in_=pA),
                     reads=[banks[bA]], writes=[KV.res])
                P.op("act", lambda h, p0=p0, pB=pB: h.activation(out=KVv[:, p0:p0 + per_bank, 1, :], in_=pB, func=AF.Copy),
                     reads=[banks[bB]], writes=[KV.res])
            yield
            for c in range(NC):
                P.op("dve", lambda h, c=c: h.scalar_tensor_tensor(out=Sall[:, c + 1, :], in0=Sall[:, c, :], scalar=egl[:, c:c + 1],
                                                                 in1=KV[:, c, :], op0=ALU.mult, op1=ALU.add),
                     reads=[egl.res, KV.res, Sall.res], writes=[Sall.res])
            yield
            P.op("act", lambda h: h.activation(out=Sbf[:], in_=Sall[:, 0:NC, :], func=AF.Copy),
                 reads=[Sall.res], writes=[Sbf.res])
            yield
            for tt in range(TS // 512):
                ob = [next_bank_in(0, 4) for _ in range(nvb)]
                for p4 in range(4):
                    pc = tt * 4 + p4
                    bs = next_bank_in(4, 7)
                    mm_group(bs, psum[:, bs, 0:128], [(kt[:, pc * 128:(pc + 1) * 128], qt[:, pc * 128:(pc + 1) * 128])],
                             [kt.res, qt.res])
                    at = AT.next()
                    P.op("dve", lambda h, bs=bs, at=at: h.tensor_tensor(out=at[:], in0=psum[:, bs, 0:128], in1=mask[:], op=ALU.mult),
                         reads=[banks[bs], mask.res], writes=[at.res])
                    for vb in range(nvb):
                        def fn(h, vb=vb, pc=pc, p4=p4, at=at, ob=ob):
                            o_ap = psum[:, ob[vb], p4 * 128:(p4 + 1) * 128]
                            h.matmul(o_ap, vtok[:, pc, vb * 128:(vb + 1) * 128], at[:], start=True, stop=False)
                            h.matmul(o_ap[:, 0:64], Sbf[:, 2 * pc, vb * 128:(vb + 1) * 128], qt[:, pc * 128:pc * 128 + 64],
                                     start=False, stop=False)
                            return h.matmul(o_ap[:, 64:128], Sbf[:, 2 * pc + 1, vb * 128:(vb + 1) * 128],
                                            qt[:, pc * 128 + 64:pc * 128 + 128], start=False, stop=True)
                        P.op("pe", fn, reads=[vtok.res, at.res, Sbf.res, qt.res], writes=[banks[ob[vb]]])
                o_t = og.next(); s_t = sq.next(); g_t = gt.next(); r_t = rstd.next(); y_t = yb.next()
                gsrc = hGB if kind == "gla" else hGC
                ybase = W if kind == "gla" else 2 * W
                tok = t0 + tt * 512
                P.dma("sp", g_t[:], gsrc[hh * DV:(hh + 1) * DV, tok:tok + 512].rearrange("(b p) t -> p b t", p=128),
                      writes=[g_t.res])
                for vb in range(nvb):
                    if kind == "gla":
                        P.op("act", lambda h, vb=vb, o_t=o_t, ob=ob: h.activation(out=o_t[:, vb, :], in_=ps_ap(ob[vb]), func=AF.Copy),
                             reads=[banks[ob[vb]]], writes=[o_t.res])
                    else:
                        P.op("dve", lambda h, vb=vb, o_t=o_t, ob=ob, g_t=g_t: h.tensor_tensor(out=o_t[:, vb, :], in0=ps_ap(ob[vb]),
                                                                                            in1=g_t[:, vb, :], op=ALU.mult),
                             reads=[banks[ob[vb]], g_t.res], writes=[o_t.res])
                P.op("dve", lambda h, o_t=o_t, s_t=s_t: h.tensor_tensor(out=s_t[:], in0=o_t[:], in1=o_t[:], op=ALU.mult),
                     reads=[o_t.res], writes=[s_t.res])
                br = next_bank_in(4, 7)
                onesm = ones_bf if nvb == 1 else ones_bf2
                mm_group(br, ps_ap(br), [(onesm[:], s_t[:, vb, :]) for vb in range(nvb)], [onesm.res, s_t.res])
                P.op("act", lambda h, br=br, r_t=r_t: h.activation(out=r_t[:], in_=ps_ap(br), func=AF.Ln, bias=ceps6[:, 0:1]),
                     reads=[banks[br], ceps6.res], writes=[r_t.res])
                P.op("act", lambda h, r_t=r_t: h.activation(out=r_t[:], in_=r_t[:], func=AF.Exp, scale=-0.5),
                     reads=[r_t.res], writes=[r_t.res])
                for vb in range(nvb):
                    wcol = gv[:, 4 + vb:5 + vb] if kind == "gla" else hnw_t[:, l:l + 1]
                    wres = gv.res if kind == "gla" else hnw_t.res
                    if kind == "gla":
                        P.op("dve", lambda h, vb=vb, o_t=o_t, r_t=r_t, wcol=wcol: h.scalar_tensor_tensor(
                            out=o_t[:, vb, :], in0=o_t[:, vb, :], scalar=wcol, in1=r_t[:], op0=ALU.mult, op1=ALU.mult),
                            reads=[o_t.res, r_t.res, wres], writes=[o_t.res])
                        P.op("dve", lambda h, vb=vb, o_t=o_t, y_t=y_t, g_t=g_t: h.tensor_tensor(out=y_t[:, vb, :], in0=o_t[:, vb, :],
                                                                                               in1=g_t[:, vb, :], op=ALU.mult),
                             reads=[o_t.res, g_t.res], writes=[y_t.res])
                    else:
                        P.op("dve", lambda h, vb=vb, o_t=o_t, r_t=r_t, wcol=wcol, y_t=y_t: h.scalar_tensor_tensor(
                            out=y_t[:, vb, :], in0=o_t[:, vb, :], scalar=wcol, in1=r_t[:], op0=ALU.mult, op1=ALU.mult),
                            reads=[o_t.res, r_t.res, wres], writes=[y_t.res])
                P.dma("sp", yT[ybase + hh * DV:ybase + (hh + 1) * DV, tok:tok + 512].rearrange("(b p) t -> p b t", p=128),
                      y_t[:], reads=[y_t.res])
            yield
        its = [(hh, sp) for hh in range(nh) for sp in range(NSP)]

        def drive(gens):
            live = list(gens)
            while live:
                for g_ in list(live):
                    try:
                        next(g_)
                    except StopIteration:
                        live.remove(g_)
        drive([front(its[0][0], its[0][1], 0)])
        for n_, (hh_, sp_) in enumerate(its):
            gens = [back(hh_, sp_, n_)]
            if n_ + 1 < len(its):
                gens.append(front(its[n_ + 1][0], its[n_ + 1][1], n_ + 1))
            drive(gens)
        P.barrier()

    ar_small = {}

    def s5_phase(l):
        ar.reset()
        prm = ar.alloc([128, 3, 32], F32)
        P.dma("sp", prm[:], s5p[l], writes=[prm.res])
        sv = ar.alloc([128, 2, 8], F32)
        P.dma("sp", sv[:], s5v[l], writes=[sv.res])
        Bre = ar.alloc([128, 32, 128], BF16)
        Bim = ar.alloc([128, 32, 128], BF16)
        P.dma("pool", Bre[:], s5b[l, 0], writes=[Bre.res])
        P.dma("pool", Bim[:], s5b[l, 1], writes=[Bim.res])
        Cre = ar.alloc([128, 32, 128], BF16)
        Cim = ar.alloc([128, 32, 128], BF16)
        sm = {n: ar.alloc([128, 32], F32) for n in
              ("lr", "dt", "r", "th", "thn", "a", "a2", "sn", "cs", "are", "aim", "den", "fre", "fim", "t1", "t2")}
        mark = ar.off
        c0 = ar.alloc([128, 32, 128], F32)
        c1 = ar.alloc([128, 32, 128], F32)
        c2 = ar.alloc([128, 32, 128], F32)
        P.dma("sp", c0[:], s5c[l, 0], writes=[c0.res])
        P.dma("sp", c1[:], s5c[l, 1], writes=[c1.res])

        def dv(fn, rd, wr):
            P.op("dve", fn, reads=[x.res for x in rd], writes=[x.res for x in wr])

        def ac(fn, rd, wr):
            P.op("act", fn, reads=[x.res for x in rd], writes=[x.res for x in wr])
        s = sm
        dv(lambda h: h.tensor_scalar_min(out=s["lr"][:], in0=prm[:, 0, :], scalar1=-1e-4), [prm], [s["lr"]])
        ac(lambda h: h.activation(out=s["dt"][:], in_=prm[:, 2, :], func=AF.Exp), [prm], [s["dt"]])
        dv(lambda h: h.tensor_tensor(out=s["t1"][:], in0=s["lr"][:], in1=s["dt"][:], op=ALU.mult), [s["lr"], s["dt"]], [s["t1"]])
        ac(lambda h: h.activation(out=s["r"][:], in_=s["t1"][:], func=AF.Exp), [s["t1"]], [s["r"]])
        dv(lambda h: h.tensor_tensor(out=s["th"][:], in0=prm[:, 1, :], in1=s["dt"][:], op=ALU.mult), [prm, s["dt"]], [s["th"]])

        I32 = mybir.dt.int32
        SIN_SCALE = 6.2831845

        def sincos(y, ki, fr, tq, f2, sn, cs):
            MAGIC = 12582912.0
            dv(lambda h: h.tensor_scalar(out=ki[:], in0=y[:], scalar1=MAGIC, scalar2=None, op0=ALU.add), [y], [ki])
            dv(lambda h: h.tensor_scalar(out=ki[:], in0=ki[:], scalar1=MAGIC, scalar2=None, op0=ALU.subtract), [ki], [ki])
            dv(lambda h: h.tensor_sub(out=fr[:], in0=y[:], in1=ki[:]), [y, ki], [fr])
            ac(lambda h: h.activation(out=sn[:], in_=fr[:], func=AF.Sin, scale=SIN_SCALE), [fr], [sn])
            ac(lambda h: h.activation(out=f2[:], in_=fr[:], func=AF.Sin, scale=0.5 * SIN_SCALE), [fr], [f2])
            dv(lambda h: h.tensor_tensor(out=tq[:], in0=f2[:], in1=f2[:], op=ALU.mult), [f2], [tq])
            dv(lambda h: h.tensor_scalar(out=cs[:], in0=tq[:], scalar1=-2.0, scalar2=1.0, op0=ALU.mult, op1=ALU.add), [tq], [cs])

        dv(lambda h: h.tensor_scalar(out=s["thn"][:], in0=s["th"][:], scalar1=1.0 / TWO_PI, scalar2=None, op0=ALU.mult),
           [s["th"]], [s["thn"]])
        sincos(s["thn"], s["a"], s["a2"], s["t1"], s["t2"], s["sn"], s["cs"])
        dv(lambda h: h.scalar_tensor_tensor(out=s["are"][:], in0=s["cs"][:], scalar=1.0, in1=s["r"][:], op0=ALU.mult, op1=ALU.mult),
           [s["cs"], s["r"]], [s["are"]])
        dv(lambda h: h.scalar_tensor_tensor(out=s["aim"][:], in0=s["sn"][:], scalar=1.0, in1=s["r"][:], op0=ALU.mult, op1=ALU.mult),
           [s["sn"], s["r"]], [s["aim"]])
        dv(lambda h: h.tensor_tensor(out=s["den"][:], in0=s["lr"][:], in1=s["lr"][:], op=ALU.mult), [s["lr"]], [s["den"]])
        dv(lambda h: h.tensor_tensor(out=s["t1"][:], in0=prm[:, 1, :], in1=prm[:, 1, :], op=ALU.mult), [prm], [s["t1"]])
        dv(lambda h: h.tensor_add(out=s["den"][:], in0=s["den"][:], in1=s["t1"][:]), [s["den"], s["t1"]], [s["den"]])
        dv(lambda h: h.reciprocal(out=s["den"][:], in_=s["den"][:]), [s["den"]], [s["den"]])
        dv(lambda h: h.tensor_scalar_add(out=s["t2"][:], in0=s["are"][:], scalar1=-1.0), [s["are"]], [s["t2"]])
        dv(lambda h: h.tensor_tensor(out=s["fre"][:], in0=s["t2"][:], in1=s["lr"][:], op=ALU.mult), [s["t2"], s["lr"]], [s["fre"]])
        dv(lambda h: h.tensor_tensor(out=s["t1"][:], in0=s["aim"][:], in1=prm[:, 1, :], op=ALU.mult), [s["aim"], prm], [s["t1"]])
        dv(lambda h: h.tensor_add(out=s["fre"][:], in0=s["fre"][:], in1=s["t1"][:]), [s["fre"], s["t1"]], [s["fre"]])
        dv(lambda h: h.tensor_tensor(out=s["fre"][:], in0=s["fre"][:], in1=s["den"][:], op=ALU.mult), [s["fre"], s["den"]], [s["fre"]])
        dv(lambda h: h.tensor_tensor(out=s["fim"][:], in0=s["aim"][:], in1=s["lr"][:], op=ALU.mult), [s["aim"], s["lr"]], [s["fim"]])
        dv(lambda h: h.tensor_tensor(out=s["t1"][:], in0=s["t2"][:], in1=prm[:, 1, :], op=ALU.mult), [s["t2"], prm], [s["t1"]])
        dv(lambda h: h.tensor_sub(out=s["fim"][:], in0=s["fim"][:], in1=s["t1"][:]), [s["fim"], s["t1"]], [s["fim"]])
        dv(lambda h: h.tensor_tensor(out=s["fim"][:], in0=s["fim"][:], in1=s["den"][:], op=ALU.mult), [s["fim"], s["den"]], [s["fim"]])
        bc = lambda t: t[:].unsqueeze(2).to_broadcast([128, 32, 128])
        dv(lambda h: h.tensor_tensor(out=c2[:], in0=c0[:], in1=bc(s["fre"]), op=ALU.mult), [c0, s["fre"]], [c2])
        P.op("dve", lambda h: h.tensor_tensor(out=Cre[:], in0=c1[:], in1=bc(s["fim"]), op=ALU.mult),
             reads=[c1.res, s["fim"].res], writes=[Cre.res])
        dv(lambda h: h.tensor_sub(out=Cre[:], in0=c2[:], in1=Cre[:]), [c2, Cre], [Cre])
        dv(lambda h: h.tensor_tensor(out=c2[:], in0=c0[:], in1=bc(s["fim"]), op=ALU.mult), [c0, s["fim"], Cre], [c2])
        P.op("dve", lambda h: h.tensor_tensor(out=c0[:], in0=c1[:], in1=bc(s["fre"]), op=ALU.mult),
             reads=[c1.res, s["fre"].res, c2.res], writes=[c0.res])
        dv(lambda h: h.scalar_tensor_tensor(out=Cim[:], in0=c2[:], scalar=-1.0, in1=c0[:], op0=ALU.mult, op1=ALU.subtract),
           [c2, c0], [Cim])
        P.barrier()
        ar.reset(mark)
        tidx = ar.alloc([128, TS], F32)
        P.op("pool", lambda h: h.iota(tidx[:], pattern=[[1, TS]], base=0, channel_multiplier=0,
                                      allow_small_or_imprecise_dtypes=True), writes=[tidx.res])
        tabA = [ar.alloc([128, TS], BF16) for _ in range(4)]
        tabB = [ar.alloc([128, TS], BF16) for _ in range(4)]
        scr = [ar.alloc([128, TS], F32) for _ in range(5)]
        wsets = []
        for _ in range(2):
            wsets.append(tuple(ar.alloc([128, TS], BF16) for _ in range(10)))
        kre, kim, k2re, k2im, t1, t2, sre, sim, t3, t4 = wsets[0]
        sit = [0]
        uall = ar.alloc([128, TS], BF16)
        yv = Rot([ar.alloc([128, 512], F32) for _ in range(2)])
        y2 = Rot([ar.alloc([128, 512], F32) for _ in range(2)])
        zo = Rot([ar.alloc([128, 512], BF16) for _ in range(2)])
        send = ar.alloc([128, 32, 4], F32)
        rbc = {}
        for fb in range(8):
            for j in range(4):
                sb = fb * 4 + j
                A, B = tabA[j], tabB[j]
                thn = s["thn"][:, sb:sb + 1]
                dv(lambda h, thn=thn: h.tensor_scalar(out=scr[0][:], in0=tidx[:], scalar1=thn, scalar2=None, op0=ALU.mult),
                   [tidx, s["thn"]], [scr[0]])
                sincos(scr[0], scr[1], scr[2], scr[3], scr[4], B, A)
            for sp in range(NSP):
                t0 = sp * TS
                P.dma("sp", uall[:], hU[fb * 128:(fb + 1) * 128, t0:t0 + TS], writes=[uall.res])
                yb_ = [next_bank_in(0, 4) for _ in range(TS // 512)]
                def it_gen(j):
                    sb = fb * 4 + j
                    yield
                    A, B = tabA[j], tabB[j]
                    yield
                    kre, kim, k2re, k2im, t1, t2, sre, sim, t3, t4 = wsets[j % 2]
                    yield
                    for tt in range(TS // 512):
                        sl = slice(tt * 512, (tt + 1) * 512)
                        b1 = next_bank_in(4, 8)
                        mm_group(b1, ps_ap(b1), [(Bre[:, sb, :], uall[:, sl])], [Bre.res, uall.res])
                        P.op("act", lambda h, b1=b1, sl=sl: h.activation(out=kre[:, sl], in_=ps_ap(b1), func=AF.Copy),
                             reads=[banks[b1]], writes=[kre.res])
                        b2 = next_bank_in(4, 8)
                        mm_group(b2, ps_ap(b2), [(Bim[:, sb, :], uall[:, sl])], [Bim.res, uall.res])
                        P.op("act", lambda h, b2=b2, sl=sl: h.activation(out=kim[:, sl], in_=ps_ap(b2), func=AF.Copy),
                             reads=[banks[b2]], writes=[kim.res])
                    yield
                    dv(lambda h, A=A: h.tensor_tensor(out=t1[:], in0=A[:], in1=kre[:], op=ALU.mult), [A, kre], [t1])
                    yield
                    P.op("dve", lambda h, B=B: h.tensor_tensor(out=t2[:], in0=B[:], in1=kim[:], op=ALU.mult),
                         reads=[B.res, kim.res], writes=[t2.res])
                    yield
                    P.op("dve", lambda h, A=A: h.tensor_tensor(out=t3[:], in0=A[:], in1=kim[:], op=ALU.mult),
                         reads=[A.res, kim.res], writes=[t3.res])
                    yield
                    dv(lambda h, B=B: h.tensor_tensor(out=t4[:], in0=B[:], in1=kre[:], op=ALU.mult), [B, kre], [t4])
                    yield
                    dv(lambda h: h.tensor_add(out=k2re[:], in0=t1[:], in1=t2[:]), [t1, t2], [k2re])
                    yield
                    dv(lambda h: h.tensor_sub(out=k2im[:], in0=t3[:], in1=t4[:]), [t3, t4], [k2im])
                    yield
                    yield
                    rb = s["r"][:, sb:sb + 1].to_broadcast([128, TS])
                    yield
                    if sp == 0:
                        i_re, i_im = 0.0, 0.0
                        rd_i = []
                    else:
                        se = send[:, sb, :]
                        dv(lambda h, se=se, A=A: h.tensor_tensor(out=se[:, 2:3], in0=A[:, 1:2], in1=se[:, 0:1], op=ALU.mult), [A, send], [send])
                        dv(lambda h, se=se, B=B: h.tensor_tensor(out=se[:, 3:4], in0=B[:, 1:2], in1=se[:, 1:2], op=ALU.mult), [B, send], [send])
                        dv(lambda h, se=se: h.tensor_sub(out=se[:, 2:3], in0=se[:, 2:3], in1=se[:, 3:4]), [send], [send])
                        dv(lambda h, se=se, A=A: h.tensor_tensor(out=se[:, 3:4], in0=A[:, 1:2], in1=se[:, 1:2], op=ALU.mult), [A, send], [send])
                        dv(lambda h, se=se, B=B: h.scalar_tensor_tensor(out=se[:, 3:4], in0=B[:, 1:2], scalar=se[:, 0:1], in1=se[:, 3:4],
                                                                       op0=ALU.mult, op1=ALU.add), [B, send], [send])
                        i_re, i_im = se[:, 2:3], se[:, 3:4]
                        rd_i = [send]
                    yield
                    dv(lambda h, rb=rb, i_re=i_re: h.tensor_tensor_scan(out=kre[:], data0=rb, data1=k2re[:], initial=i_re,
                                                                        op0=ALU.mult, op1=ALU.add), [s["r"], k2re, kre] + rd_i, [kre])
                    yield
                    dv(lambda h, rb=rb, i_im=i_im: h.tensor_tensor_scan(out=kim[:], data0=rb, data1=k2im[:], initial=i_im,
                                                                        op0=ALU.mult, op1=ALU.add), [s["r"], k2im, kim] + rd_i, [kim])
                    yield
                    dv(lambda h, A=A: h.tensor_tensor(out=t1[:], in0=A[:], in1=kre[:], op=ALU.mult), [A, kre], [t1])
                    yield
                    P.op("dve", lambda h, B=B: h.tensor_tensor(out=t2[:], in0=B[:], in1=kim[:], op=ALU.mult),
                         reads=[B.res, kim.res], writes=[t2.res])
                    yield
                    P.op("dve", lambda h, A=A: h.tensor_tensor(out=t3[:], in0=A[:], in1=kim[:], op=ALU.mult),
                         reads=[A.res, kim.res], writes=[t3.res])
                    yield
                    dv(lambda h, B=B: h.tensor_tensor(out=t4[:], in0=B[:], in1=kre[:], op=ALU.mult), [B, kre], [t4])
                    yield
                    dv(lambda h: h.tensor_sub(out=sre[:], in0=t1[:], in1=t2[:]), [t1, t2], [sre])
                    yield
                    dv(lambda h: h.tensor_add(out=sim[:], in0=t3[:], in1=t4[:]), [t3, t4], [sim])
                    yield
                    if NSP > 1:
                        dv(lambda h, sb=sb: h.tensor_sub(out=send[:, sb, 0:1], in0=t1[:, TS - 1:TS], in1=t2[:, TS - 1:TS]), [t1, t2], [send])
                        dv(lambda h, sb=sb: h.tensor_add(out=send[:, sb, 1:2], in0=t3[:, TS - 1:TS], in1=t4[:, TS - 1:TS]), [t3, t4], [send])
                    yield
                    for tt in range(TS // 512):
                        sl = slice(tt * 512, (tt + 1) * 512)
                        mm_group(yb_[tt], ps_ap(yb_[tt]), [(Cre[:, sb, :], sre[:, sl]), (Cim[:, sb, :], sim[:, sl])],
                                 [Cre.res, Cim.res, sre.res, sim.res], start=(j == 0), stop=(j == 3))
                    yield
                for jp in (0, 2):
                    gens = [it_gen(jp), it_gen(jp + 1)]
                    live = list(gens)
                    while live:
                        for g_ in list(live):
                            try:
                                next(g_)
                            except StopIteration:
                                live.remove(g_)
                for tt in range(TS // 512):
                    sl = slice(tt * 512, (tt + 1) * 512)
                    y_ = yv.next(); w_ = y2.next(); z_ = zo.next()
                    b = yb_[tt]
                    P.op("dve", lambda h, b=b, y_=y_, sl=sl, fb=fb: h.scalar_tensor_tensor(
                        out=y_[:], in0=uall[:, sl], scalar=sv[:, 0, fb:fb + 1], in1=ps_ap(b), op0=ALU.mult, op1=ALU.add),
                        reads=[uall.res, sv.res, banks[b]], writes=[y_.res])
                    P.op("dve", lambda h, y_=y_, w_=w_: h.tensor_tensor(out=w_[:], in0=y_[:], in1=y_[:], op=ALU.mult),
                         reads=[y_.res], writes=[w_.res])
                    P.op("dve", lambda h, w_=w_: h.tensor_scalar(out=w_[:], in0=w_[:], scalar1=0.044715, scalar2=1.0,
                                                                  op0=ALU.mult, op1=ALU.add), reads=[w_.res], writes=[w_.res])
                    P.op("dve", lambda h, y_=y_, w_=w_: h.tensor_tensor(out=w_[:], in0=w_[:], in1=y_[:], op=ALU.mult),
                         reads=[y_.res, w_.res], writes=[w_.res])
                    P.op("act", lambda h, w_=w_: h.activation(out=w_[:], in_=w_[:], func=AF.Sigmoid, scale=1.5957691216057308),
                         reads=[w_.res], writes=[w_.res])
                    dv(lambda h, y_=y_, w_=w_, z_=z_: h.tensor_tensor(out=z_[:], in0=y_[:], in1=w_[:], op=ALU.mult), [y_, w_], [z_])
                    P.dma("sp", zT[fb * 128:(fb + 1) * 128, t0 + tt * 512:t0 + (tt + 1) * 512], z_[:], reads=[z_.res])
        P.barrier()

    def make_epi_win(l):
        def epi(bl, tag, c0, rows, tok0, aux):
            b = bl[0]
            if tag == "ua":
                evac_store(b, rows, 512, None, 1.0, BF16, hU[c0 - O_UA:c0 - O_UA + rows, tok0:tok0 + 512], aux)
            elif tag == "qb":
                evac_store(b, rows, 512, None, 128.0 ** -0.5, BF16, hQB[c0 - O_QB:c0 - O_QB + rows, tok0:tok0 + 512], aux)
            elif tag == "kb":
                evac_store(b, rows, 512, None, 1.0, BF16, hKB[c0 - O_KB:c0 - O_KB + rows, tok0:tok0 + 512], aux)
            elif tag == "gl":
                evac_store(b, rows, 512, None, 1.0, F32, hGL[0:16, tok0:tok0 + 512], aux, eng="act")
            elif tag == "gb":
                evac_store(b, rows, 512, AF.Silu, 1.0, BF16, hGB[c0 - O_GB:c0 - O_GB + rows, tok0:tok0 + 512], aux)
            elif tag == "qc":
                evac_store(b, rows, 512, AF.Silu, 1.0, BF16, hQC[c0 - O_QC:c0 - O_QC + rows, tok0:tok0 + 512], aux)
            elif tag == "fc":
                evac_store(b, rows, 512, None, 1.0, F32, hFC[c0 - O_FC:c0 - O_FC + rows, tok0:tok0 + 512], aux)
            elif tag == "gc":
                evac_store(b, rows, 512, AF.Sigmoid, 1.0, BF16, hGC[c0 - O_GC:c0 - O_GC + rows, tok0:tok0 + 512], aux)
            elif tag == "mg":
                evac_store(b, rows, 512, AF.Sigmoid, 1.0, BF16, hMG[c0 - O_MG:c0 - O_MG + rows, tok0:tok0 + 512], aux)
            elif tag == "vb":
                evac_store(b, 128, rows, None, 1.0, BF16, hVB[tok0:tok0 + 128, c0 - O_VB:c0 - O_VB + rows], aux)
            elif tag == "ic":
                evac_store(b, 128, rows, None, 1.0, BF16, hIC[tok0:tok0 + 128, c0 - O_IC:c0 - O_IC + rows], aux, eng="act")
        return epi

    def seg(off, width, tag):
        return [(off + i * 512, min(512, width - i * 512), tag) for i in range((width + 511) // 512)]

    def layer(l, last):
        TPh = min(2048, T)
        fm_cols = (seg(O_UA, 1024, "ua") + seg(O_QB, 512, "qb") + seg(O_KB, 512, "kb") + seg(O_GL, 16, "gl") +
                   seg(O_GB, 1024, "gb") + seg(O_QC, 1024, "qc") + seg(O_FC, 1024, "fc") + seg(O_GC, 1024, "gc") +
                   seg(O_MG, 3 * D, "mg"))
        gemm_phase(xT, D, w_in[l], fm_cols, "fm", make_epi_win(l), TPh)
        gemm_phase(xT, D, w_in[l], seg(O_VB, 1024, "vb") + seg(O_IC, 1024, "ic"), "tm", make_epi_win(l), TPh)
        s5_phase(l)
        gla_like(l, "gla")
        ar.reset()
        gla_like_h(l)
        def epi_glu(bl, tag, c0, rows, tok0, aux):
            b = bl[0]
            sf = aux["sf"].next(); g = aux["gb"].next(); so = aux["sb"].next()
            fbk, r0 = c0 // 128, c0 % 128
            P.op("act", lambda h: h.activation(out=sf[0:rows, :], in_=ps_ap(b, rows), func=AF.Sigmoid,
                                               bias=glu_b[:, fbk:fbk + 1]), reads=[banks[b], glu_b.res], writes=[sf.res])
            P.dma("sp", g[0:rows, 0, :], zT[c0:c0 + rows, tok0:tok0 + 512], writes=[g.res])
            P.op("dve", lambda h: h.tensor_tensor(out=so[0:rows, :], in0=sf[0:rows, :], in1=g[0:rows, 0, :], op=ALU.mult),
                 reads=[sf.res, g.res], writes=[so.res])
            P.dma("sp", yT[c0:c0 + rows, tok0:tok0 + 512], so[0:rows, :], reads=[so.res])
        glu_b = glu_holder["t"]
        P.dma("sp", glu_b[:], s5v[l, :, 1, :], writes=[glu_b.res])
        gemm_phase(zT, W, w_glu[l], seg(0, W, "glu"), "fm", epi_glu, TPh)
        def epi_up(bl, tag, c0, rows, tok0, aux):
            g = aux["gb"].next(); xf = aux["xf"].next(); so = aux["sb"].next()
            P.dma("sp", g[:], hMG[:, tok0:tok0 + 512].rearrange("(b n) t -> n b t", b=3)[c0:c0 + 128], writes=[g.res])
            for i in range(3):
                P.op("dve", lambda h, i=i: h.tensor_tensor(out=xf[:, i, :], in0=ps_ap(bl[i]), in1=g[:, i, :], op=ALU.mult),
                     reads=[banks[bl[i]], g.res], writes=[xf.res])
            P.op("dve", lambda h: h.tensor_add(out=xf[:, 0, :], in0=xf[:, 0, :], in1=xf[:, 1, :]), reads=[xf.res], writes=[xf.res])
            P.op("dve", lambda h: h.tensor_add(out=so[:], in0=xf[:, 0, :], in1=xf[:, 2, :]), reads=[xf.res], writes=[so.res])
            P.dma("sp", mT[c0:c0 + 128, tok0:tok0 + 512], so[:], reads=[so.res])
        gemm_phase(yT, 3 * W, w_up[l], seg(0, D, "up"), "fm", epi_up, min(1024, T), kgroups=[(0, 8), (8, 16), (16, 24)])
        def make_epi_res(xsrc):
            def epi(bl, tag, c0, width, tok0, aux):
                b = bl[0]
                xf = aux["xf"].next(); sf = aux["sf"].next()
                P.dma("sp", xf[:, 0, 0:width], xsrc[tok0:tok0 + 128, c0:c0 + width], writes=[xf.res])
                P.op("dve", lambda h: h.scalar_tensor_tensor(out=sf[:, 0:width], in0=xf[:, 0, 0:width], scalar=float(ALPHA),
                                                             in1=ps_ap(b, 128, width), op0=ALU.mult, op1=ALU.add),
                     reads=[xf.res, banks[b]], writes=[sf.res])
                P.dma("sp", zF[tok0:tok0 + 128, c0:c0 + width], sf[:, 0:width], reads=[sf.res])
            return epi
        xsrc = x_in if l == 0 else xF
        gemm_phase(mT, D, w_out[l], seg(0, D, "o"), "tm", make_epi_res(xsrc), TPh)
        ln_phase(zF, True, l, 0, xF, None)
        def epi_m1(bl, tag, c0, rows, tok0, aux):
            b = bl[0]
            sf = aux["sf"].next(); so = aux["sb"].next()
            P.op("act", lambda h: h.activation(out=sf[:], in_=ps_ap(b), func=AF.Relu), reads=[banks[b]], writes=[sf.res])
            P.op("act", lambda h: h.activation(out=so[:], in_=sf[:], func=AF.Square), reads=[sf.res], writes=[so.res])
            P.dma("sp", hT[c0:c0 + 128, tok0:tok0 + 512], so[:], reads=[so.res])
        gemm_phase(xT, D, w_m1[l], seg(0, HID, "m1"), "fm", epi_m1, TPh)
        gemm_phase(hT, HID, w_m2[l], seg(0, D, "m2"), "tm", make_epi_res(xF), 1024, kbp=8)
        ln_phase(zF, True, l, 1, xF, out if last else None)

    glu_holder = {}

    def gla_like_h(l):
        gla_like(l, "hgrn")

    glu_holder["t"] = ar.alloc([128, 8], F32)
    oml = ar.alloc([128, 8], F32)
    ar_small["oml"] = oml
    ar.base = ar.off

    c_setup()
    ln_phase(x_in, False, 0, 0, None, None)
    for l in range(L):
        P.op("dve", lambda h, l=l: h.tensor_scalar(out=oml[:], in0=lb_all[:, :, l], scalar1=-1.0, scalar2=1.0,
                                                   op0=ALU.mult, op1=ALU.add), reads=[lb_all.res], writes=[oml.res])
        layer(l, l == L - 1)
    P.stopped = False
    P._barrier()
    P.emit()
    st.close()
    return nc


def prep_weights(inp, L):
    f = np.float32
    g = {}
    g["w_in"] = np.ascontiguousarray(inp["w_in"][:L], dtype=f)
    g["w_glu"] = np.ascontiguousarray(inp["s5_w_glu"][:L], dtype=f)
    g["w_up"] = np.ascontiguousarray(inp["w_up"][:L], dtype=f).reshape(L, 3 * W, D)
    g["w_out"] = np.ascontiguousarray(inp["w_out"][:L], dtype=f)
    g["w_m1"] = np.ascontiguousarray(inp["w_mlp_in"][:L], dtype=f)
    g["w_m2"] = np.ascontiguousarray(inp["w_mlp_out"][:L], dtype=f)
    g["lnp"] = np.ascontiguousarray(np.stack([inp["ln1_g"][:L], inp["ln1_b"][:L], inp["ln2_g"][:L], inp["ln2_b"][:L]], axis=1), dtype=f)
    lam_re = np.asarray(inp["s5_lam_re"][:L], f).reshape(L, 32, 128).transpose(0, 2, 1)
    lam_im = np.asarray(inp["s5_lam_im"][:L], f).reshape(L, 32, 128).transpose(0, 2, 1)
    ldt = np.repeat(np.asarray(inp["s5_log_dt"][:L], f)[:, :, None], 64, axis=2).reshape(L, 32, 128).transpose(0, 2, 1)
    g["s5p"] = np.ascontiguousarray(np.stack([lam_re, lam_im, ldt], axis=2), dtype=f)
    bpad = np.zeros((L, 2, 128, 32, 128), f)
    cpad = np.zeros((L, 2, 128, 32, 128), f)
    for ri, (bk, ck) in enumerate((("s5_b_re", "s5_c_re"), ("s5_b_im", "s5_c_im"))):
        Bm = np.asarray(inp[bk][:L], f)
        Cm = np.asarray(inp[ck][:L], f)
        for sb in range(32):
            for gi in range(2):
                gidx = 2 * sb + gi
                r0 = (gidx % 8) * 16
                bpad[:, ri, r0:r0 + 16, sb, gi * 64:(gi + 1) * 64] = Bm[:, gidx].transpose(0, 2, 1)
                cpad[:, ri, gi * 64:(gi + 1) * 64, sb, r0:r0 + 16] = Cm[:, gidx].transpose(0, 2, 1)
    g["s5b"] = bpad
    g["s5c"] = cpad
    dsk = np.asarray(inp["s5_d"][:L], f).reshape(L, 8, 128).transpose(0, 2, 1)
    bgl = np.asarray(inp["s5_b_glu"][:L], f).reshape(L, 8, 128).transpose(0, 2, 1)
    g["s5v"] = np.ascontiguousarray(np.stack([dsk, bgl], axis=2), dtype=f)
    g["glaw"] = np.ascontiguousarray(inp["gla_w_gate"][:L], dtype=f)
    bg = np.asarray(inp["gla_b_gate"][:L], f).reshape(L, 4, 128).transpose(0, 2, 1)
    nw = np.asarray(inp["gla_norm_w"][:L], f).reshape(L, 2, 128).transpose(0, 2, 1)
    g["glav"] = np.ascontiguousarray(np.concatenate([bg, nw], axis=2), dtype=f)
    g["hlb"] = np.ascontiguousarray(np.asarray(inp["hgrn_lb_logits"], f).reshape(DEPTH, 8, 128).transpose(2, 1, 0), dtype=f)
    g["hnw"] = np.ascontiguousarray(np.asarray(inp["hgrn_norm_w"][:L], f).T, dtype=f)
    return g


_CACHE = {}


def kernel(**inputs):
    x = np.asarray(inputs["x"], np.float32)
    Bn, T, _ = x.shape
    key = (T, DEPTH)
    if key not in _CACHE:
        _CACHE[key] = build(T, DEPTH)
    nc = _CACHE[key]
    wts = prep_weights(inputs, DEPTH)
    in_maps = []
    for b in range(Bn):
        m = dict(wts)
        m["x"] = np.ascontiguousarray(x[b])
        in_maps.append(m)
    res = run_bass_kernel_spmd(nc, in_maps, core_ids=list(range(Bn)))
    return np.stack([np.asarray(r["out"], np.float32) for r in res.results], axis=0)
```

```python
import contextlib
import math
import types
import numpy as np
import concourse.bass as bass
import concourse.mybir as mybir
from concourse.bass_utils import run_bass_kernel_spmd

F32 = mybir.dt.float32
BF16 = mybir.dt.bfloat16
AF = mybir.ActivationFunctionType
ALU = mybir.AluOpType

D = 2048
W = 1024
NIN = 14352
HID = 8192
DEPTH = 4
ALPHA = (2 * DEPTH) ** 0.25
TWO_PI = 2.0 * math.pi
O_UA, O_QB, O_KB, O_VB, O_GL, O_GB, O_QC, O_FC, O_IC, O_GC, O_MG = (
    0, 1024, 1536, 2048, 3072, 3088, 4112, 5136, 6160, 7184, 8208)


def _freeze(fn):
    if fn.__closure__ is None:
        return fn
    cells = []
    for c in fn.__closure__:
        try:
            cells.append(types.CellType(c.cell_contents))
        except ValueError:
            cells.append(c)
    g = types.FunctionType(fn.__code__, fn.__globals__, fn.__name__, fn.__defaults__, tuple(cells))
    g.__kwdefaults__ = fn.__kwdefaults__
    return g


class Res:
    __slots__ = ("w", "r")

    def __init__(self):
        self.w = None
        self.r = {}


class Prog:
    KQ = 8
    ENGS = ("pe", "act", "dve", "pool", "sp")

    def __init__(self, nc, st):
        self.nc = nc
        self.eng = {}
        hs = dict(pe=nc.tensor, act=nc.scalar, dve=nc.vector, pool=nc.gpsimd, sp=nc.sync)
        for name in self.ENGS:
            sem = st.enter_context(nc.semaphore("sem_" + name))
            self.eng[name] = dict(h=hs[name], sem=sem, n=0, prog=[], waited={})
        self.dq = {}
        for q in ("sp", "pool", "act"):
            sems = [st.enter_context(nc.semaphore(f"dq_{q}_{i}")) for i in range(self.KQ)]
            self.dq[q] = dict(sems=sems, n=0)
        self.dma_uid = 0
        self.stopped = False
        self.nphase = 0
        self.max_phase = 10 ** 9

    def _waits(self, eng, reads, writes):
        E = self.eng[eng]
        deps = []
        for r in reads:
            if r.w is not None:
                deps.append(r.w)
        for w in writes:
            if w.w is not None:
                deps.append(w.w)
            deps.extend(w.r.values())
        waits = []
        for (sem, val, key, src) in deps:
            if src == "pe" and eng == "pe":
                continue
            if E["waited"].get(key, 0) >= val:
                continue
            E["waited"][key] = val
            waits.append((sem, val))
        return waits

    def _record(self, tok, reads, writes, rkey):
        for r in reads:
            r.r[rkey] = tok
        for w in writes:
            w.w = tok
            w.r = {}

    def op(self, eng, fn, reads=(), writes=()):
        if self.stopped:
            return None
        E = self.eng[eng]
        fn = _freeze(fn)
        waits = self._waits(eng, reads, writes)
        E["n"] += 1
        sem = E["sem"]
        tok = (sem, E["n"], "e_" + eng, eng)

        def emit(h, fn=fn, waits=waits, sem=sem):
            for s, v in waits:
                h.wait_ge(s, v)
            fn(h).then_inc(sem, 1)

        E["prog"].append(emit)
        self._record(tok, reads, writes, eng)
        return tok

    def dma(self, q, out, in_, reads=(), writes=()):
        if self.stopped:
            return None
        E = self.eng[q]
        Dq = self.dq[q]
        waits = self._waits(q, reads, writes)
        n = Dq["n"]
        Dq["n"] += 1
        s = Dq["sems"][n % self.KQ]
        val = 16 * (n // self.KQ + 1)
        key = f"dq_{q}_{n % self.KQ}"
        if n >= self.KQ and E["waited"].get(key, 0) < val - 16:
            E["waited"][key] = val - 16
            waits.append((s, val - 16))
        tok = (s, val, key, "dma")

        def emit(h, waits=waits, s=s, out=out, in_=in_):
            for ss, v in waits:
                h.wait_ge(ss, v)
            h.dma_start(out=out, in_=in_).then_inc(s, 16)

        E["prog"].append(emit)
        self.dma_uid += 1
        self._record(tok, reads, writes, "dma%d" % self.dma_uid)
        return tok

    def barrier(self):
        if self.stopped:
            return
        self.nphase += 1
        if self.nphase >= self.max_phase:
            self._barrier()
            self.stopped = True
            return
        self._barrier()

    def _barrier(self):
        targets = []
        for name in self.ENGS:
            X = self.eng[name]
            if X["n"] > 0:
                targets.append((X["sem"], X["n"], "e_" + name))
        for q, Dq in self.dq.items():
            n = Dq["n"]
            for i in range(min(n, self.KQ)):
                cnt = (n - 1 - i) // self.KQ + 1
                targets.append((Dq["sems"][i], 16 * cnt, f"dq_{q}_{i}"))
        for name in self.ENGS:
            E = self.eng[name]
            waits = []
            for (sem, val, key) in targets:
                if key == "e_" + name and name == "pe":
                    continue
                if E["waited"].get(key, 0) >= val:
                    continue
                E["waited"][key] = val
                waits.append((sem, val))

            def emit(h, waits=waits):
                for s, v in waits:
                    h.wait_ge(s, v)

            E["prog"].append(emit)

    def emit(self):
        nc = self.nc
        with nc.Block() as block:
            @block.tensor
            def _(h):
                for f in self.eng["pe"]["prog"]:
                    f(h)

            @block.scalar
            def _(h):
                for f in self.eng["act"]["prog"]:
                    f(h)

            @block.vector
            def _(h):
                for f in self.eng["dve"]["prog"]:
                    f(h)

            @block.gpsimd
            def _(h):
                for f in self.eng["pool"]["prog"]:
                    f(h)

            @block.sync
            def _(h):
                for f in self.eng["sp"]["prog"]:
                    f(h)


class Tile:
    def __init__(self, t):
        self.t = t
        self.res = Res()

    def __getitem__(self, k):
        return self.t[k]


class Arena:
    def __init__(self, nc, limit):
        self.nc = nc
        self.off = 16384
        self.limit = limit
        self.cnt = 0
        self.base = 16384

    def reset(self, to=None):
        self.off = self.base if to is None else to

    def alloc(self, shape, dtype):
        nbytes = int(np.prod(shape[1:])) * (4 if dtype == F32 else 2)
        nbytes = (nbytes + 63) // 64 * 64
        assert self.off + nbytes <= self.limit, (self.off, nbytes, self.limit)
        self.cnt += 1
        t = self.nc.alloc_sbuf_tensor_at("sb%d" % self.cnt, list(shape), dtype, offset=self.off)
        self.off += nbytes
        return Tile(t)


class K:
    pass


def build(T, L, TSPAN=1024, max_phase=10 ** 9):
    nc = bass.Bass("TRN2", target_bir_lowering=False)
    st = contextlib.ExitStack()
    P = Prog(nc, st)
    P.max_phase = max_phase
    TS = min(TSPAN, T)
    NSP = T // TS

    def din(name, shape, dt=F32):
        return nc.dram_tensor(name, list(shape), dt, kind="ExternalInput").ap()

    def dscr(name, shape, dt):
        import os
        kind = "ExternalOutput" if os.environ.get("KDEBUG") else "Internal"
        return nc.dram_tensor(name, list(shape), dt, kind=kind).ap()

    x_in = din("x", [T, D])
    w_in = din("w_in", [L, D, NIN])
    w_glu = din("w_glu", [L, W, W])
    w_up = din("w_up", [L, 3 * W, D])
    w_out = din("w_out", [L, D, D])
    w_m1 = din("w_m1", [L, D, HID])
    w_m2 = din("w_m2", [L, HID, D])
    lnp = din("lnp", [L, 4, D])
    s5p = din("s5p", [L, 128, 3, 32])
    s5b = din("s5b", [L, 2, 128, 32, 128])
    s5c = din("s5c", [L, 2, 128, 32, 128])
    s5v = din("s5v", [L, 128, 2, 8])
    glaw = din("glaw", [L, 16, 512])
    glav = din("glav", [L, 128, 6])
    hlb = din("hlb", [128, 8, DEPTH])
    hnw = din("hnw", [128, L])
    out = nc.dram_tensor("out", [T, D], F32, kind="ExternalOutput").ap()

    xF = dscr("xF", [T, D], F32)
    zF = dscr("zF", [T, D], F32)
    xT = dscr("xT", [D, T], BF16)
    hU = dscr("hU", [W, T], BF16)
    hQB = dscr("hQB", [512, T], BF16)
    hKB = dscr("hKB", [512, T], BF16)
    hGL = dscr("hGL", [16, T], F32)
    hGB = dscr("hGB", [W, T], BF16)
    hVB = dscr("hVB", [T, W], BF16)
    hQC = dscr("hQC", [W, T], BF16)
    hFC = dscr("hFC", [W, T], F32)
    hIC = dscr("hIC", [T, W], BF16)
    hGC = dscr("hGC", [W, T], BF16)
    hMG = dscr("hMG", [3 * D, T], BF16)
    zT = dscr("zT", [W, T], BF16)
    yT = dscr("yT", [3 * W, T], BF16)
    mT = dscr("mT", [D, T], BF16)
    hT = dscr("hT", [HID, T], BF16)

    SB_LIMIT = 192 * 1024
    ar = Arena(nc, SB_LIMIT)
    psum = nc.alloc_psum_tensor("psum", [128, 8, 512], F32)
    banks = [Res() for _ in range(8)]
    bank_i = [0]

    def next_bank():
        b = bank_i[0] % 8
        bank_i[0] += 1
        return b

    sub_i = {}

    def next_bank_in(lo, hi):
        i = sub_i.get((lo, hi), 0)
        sub_i[(lo, hi)] = i + 1
        return lo + i % (hi - lo)

    ident = ar.alloc([128, 128], BF16)
    ones_f = ar.alloc([128, 128], F32)
    mask = ar.alloc([128, 128], F32)
    ones_bf = ar.alloc([128, 128], BF16)
    ones_bf2 = ar.alloc([128, 128], BF16)
    cneg_pi = ar.alloc([128, 1], F32)
    ceps5 = ar.alloc([128, 1], F32)
    ceps6 = ar.alloc([128, 1], F32)
    lb_all = ar.alloc([128, 8, DEPTH], F32)
    hnw_t = ar.alloc([128, L], F32)
    ar.base = ar.off

    def c_setup():
        P.op("pool", lambda h: h.memset(ones_f[:], 1.0), writes=[ones_f.res])
        P.op("pool", lambda h: h.memset(cneg_pi[:], -math.pi), writes=[cneg_pi.res])
        P.op("pool", lambda h: h.memset(ceps5[:], 1e-5), writes=[ceps5.res])
        P.op("pool", lambda h: h.memset(ceps6[:], 1e-6), writes=[ceps6.res])
        P.op("pool", lambda h: h.memset(ones_bf[:], 1.0 / 128.0), writes=[ones_bf.res])
        P.op("pool", lambda h: h.memset(ones_bf2[:], 1.0 / 256.0), writes=[ones_bf2.res])
        P.op("pool", lambda h: h.affine_select(out=ident[:], in_=ones_f[:], pattern=[[-1, 128]], base=0,
                                               channel_multiplier=1, compare_op=ALU.is_equal, fill=0.0),
             reads=[ones_f.res], writes=[ident.res])
        P.op("pool", lambda h: h.affine_select(out=mask[:], in_=ones_f[:], pattern=[[1, 128]], base=0,
                                               channel_multiplier=-1, compare_op=ALU.is_ge, fill=0.0),
             reads=[ones_f.res], writes=[mask.res])
        P.op("pool", lambda h: h.memset(mask[0:64, 64:128], 0.0), writes=[mask.res])
        P.dma("sp", lb_all[:], hlb, writes=[lb_all.res])
        P.dma("sp", hnw_t[:], hnw, writes=[hnw_t.res])
        P.op("act", lambda h: h.activation(out=lb_all[:], in_=lb_all[:], func=AF.Exp),
             reads=[lb_all.res], writes=[lb_all.res])
        ssum = ar.alloc([128, 8, 1], F32)
        P.op("dve", lambda h: h.tensor_add(out=ssum[:], in0=lb_all[:, :, 0:1], in1=lb_all[:, :, 1:2]),
             reads=[lb_all.res], writes=[ssum.res])
        for j in (2, 3):
            P.op("dve", lambda h, j=j: h.tensor_add(out=ssum[:], in0=ssum[:], in1=lb_all[:, :, j:j + 1]),
                 reads=[lb_all.res, ssum.res], writes=[ssum.res])
        P.op("dve", lambda h: h.reciprocal(out=ssum[:], in_=ssum[:]), reads=[ssum.res], writes=[ssum.res])
        P.op("dve", lambda h: h.tensor_tensor(out=lb_all[:], in0=lb_all[:], in1=ssum[:].to_broadcast([128, 8, DEPTH]),
                                              op=ALU.mult), reads=[lb_all.res, ssum.res], writes=[lb_all.res])
        P.op("dve", lambda h: h.tensor_add(out=lb_all[:, :, 2:3], in0=lb_all[:, :, 2:3], in1=lb_all[:, :, 1:2]),
             reads=[lb_all.res], writes=[lb_all.res])
        P.op("dve", lambda h: h.tensor_add(out=lb_all[:, :, 3:4], in0=lb_all[:, :, 3:4], in1=lb_all[:, :, 2:3]),
             reads=[lb_all.res], writes=[lb_all.res])
        P.op("dve", lambda h: h.memset(lb_all[:, :, 0:1], 0.0), writes=[lb_all.res])
        P.barrier()

    def ps_ap(b, rows=128, n=512):
        return psum[0:rows, b, 0:n]

    class Rot:
        def __init__(self, tiles):
            self.tiles = tiles
            self.i = 0

        def next(self):
            t = self.tiles[self.i % len(self.tiles)]
            self.i += 1
            return t

    def mm_group(bank, out_ap, pairs, extra_reads, start=True, stop=True):
        def fn(h):
            ins = None
            n = len(pairs)
            for i, (l, r) in enumerate(pairs):
                ins = h.matmul(out_ap, l, r, start=(start and i == 0), stop=(stop and i == n - 1))
            return ins
        return P.op("pe", fn, reads=extra_reads, writes=[banks[bank]])

    def gemm_phase(act_src, Kd, w_src, cols, mode, epi, TP, kgroups=None, kbp=None):
        ar.reset()
        KB = Kd // 128
        nparts = (1 if KB <= 24 else KB // 16) if kbp is None else KB // kbp
        KBP = KB // nparts
        actA = ar.alloc([128, KB * TP], BF16)
        wb = Rot([ar.alloc([128, KBP, 512], BF16) for _ in range(2)])
        aux = dict(
            sf=Rot([ar.alloc([128, 512], F32) for _ in range(4)]),
            sb=Rot([ar.alloc([128, 512], BF16) for _ in range(4)]),
            xf=Rot([ar.alloc([128, 3 if mode == "fm" else 1, 512], F32) for _ in range(2)]),
            gb=Rot([ar.alloc([128, 3 if mode == "fm" else 1, 512 if mode == "fm" else 8], BF16) for _ in range(2)]),
        )
        NQ = TP // 512
        actq = [Res() for _ in range(NQ)]
        if kgroups is None:
            kgroups = [(0, KBP)]
        actv = actA[:].rearrange("p (k t) -> p k t", t=TP)
        for tp in range(T // TP):
            for q in range(NQ):
                P.dma("sp", actv[:, :, q * 512:(q + 1) * 512],
                      act_src[:, tp * TP + q * 512:tp * TP + (q + 1) * 512].rearrange("(k p) t -> p k t", p=128),
                      writes=[actq[q]])
            loads = [(ci, pa) for ci in range(len(cols)) for pa in range(nparts)]
            wtiles = {}

            def load(idx):
                ci, pa = loads[idx]
                off, width, tag = cols[ci]
                wt = wb.next()
                P.dma("pool", wt[:, :, 0:width],
                      w_src[pa * KBP * 128:(pa + 1) * KBP * 128, off:off + width].rearrange("(k p) n -> p k n", p=128),
                      writes=[wt.res])
                wtiles[idx] = wt

            load(0)
            for idx in range(len(loads)):
                if idx + 1 < len(loads):
                    load(idx + 1)
                ci, pa = loads[idx]
                off, width, tag = cols[ci]
                wt = wtiles.pop(idx)
                if mode == "fm":
                    assert nparts == 1
                    for nb in range((width + 127) // 128):
                        rows = min(128, width - nb * 128)
                        for tt in range(TP // 512):
                            bl = []
                            for (k0, k1) in kgroups:
                                b = next_bank()
                                pairs = [(wt[:, kb, nb * 128:nb * 128 + rows], actv[:, kb, tt * 512:(tt + 1) * 512])
                                         for kb in range(k0, k1)]
                                mm_group(b, ps_ap(b, rows), pairs, [wt.res, actq[tt]])
                                bl.append(b)
                            epi(bl, tag, off + nb * 128, rows, tp * TP + tt * 512, aux)
                else:
                    nt = TP // 128
                    if nparts == 1:
                        for t1 in range(nt):
                            b = next_bank()
                            pairs = [(actv[:, kb, t1 * 128:(t1 + 1) * 128], wt[:, kb, 0:width]) for kb in range(KBP)]
                            mm_group(b, ps_ap(b, 128, width), pairs, [wt.res, actq[t1 // 4]])
                            epi([b], tag, off, width, tp * TP + t1 * 128, aux)
                    else:
                        assert nt <= 8
                        if pa == 0:
                            cur = [next_bank() for _ in range(nt)]
                            wtiles["cur"] = cur
                        cur = wtiles["cur"]
                        for t1 in range(nt):
                            b = cur[t1]
                            pairs = [(actv[:, pa * KBP + kb, t1 * 128:(t1 + 1) * 128], wt[:, kb, 0:width])
                                     for kb in range(KBP)]
                            mm_group(b, ps_ap(b, 128, width), pairs, [wt.res, actq[t1 // 4]],
                                     start=(pa == 0), stop=(pa == nparts - 1))
                            if pa == nparts - 1:
                                epi([b], tag, off, width, tp * TP + t1 * 128, aux)
        P.barrier()

    def evac_store(b, rows, n, func, scale, dt, dest, aux, bias=None, eng=None):
        stg = (aux["sb"] if dt == BF16 else aux["sf"]).next()
        src = ps_ap(b, rows, n)
        if func is None and (eng or "dve") == "dve":
            P.op("dve", lambda h: h.tensor_scalar(out=stg[0:rows, 0:n], in0=src, scalar1=float(scale), scalar2=None,
                                                  op0=ALU.mult), reads=[banks[b]], writes=[stg.res])
        else:
            f = AF.Copy if func is None else func
            if bias is None:
                P.op("act", lambda h: h.activation(out=stg[0:rows, 0:n], in_=src, func=f, scale=float(scale)),
                     reads=[banks[b]], writes=[stg.res])
            else:
                P.op("act", lambda h: h.activation(out=stg[0:rows, 0:n], in_=src, func=f, scale=float(scale),
                                                   bias=bias), reads=[banks[b]], writes=[stg.res])
        P.dma("sp", dest, stg[0:rows, 0:n], reads=[stg.res])

    def ln_phase(src, norm, l, which, dstF, dstOut):
        ar.reset()
        if norm:
            g_bc = ar.alloc([128, D], F32)
            b_bc = ar.alloc([128, D], F32)
            P.dma("sp", g_bc[:], lnp[l, 2 * which, :].partition_broadcast(128), writes=[g_bc.res])
            P.dma("sp", b_bc[:], lnp[l, 2 * which + 1, :].partition_broadcast(128), writes=[b_bc.res])
        zt = Rot([ar.alloc([128, D], F32) for _ in range(4)])
        jk = Rot([ar.alloc([128, D], F32) for _ in range(2)])
        xb = Rot([ar.alloc([128, D], BF16) for _ in range(3)])
        xtt = Rot([ar.alloc([128, 16, 128], BF16) for _ in range(3)])
        stat = Rot([ar.alloc([128, 8], F32) for _ in range(4)])
        def tile_gen(tt):
            yield
            z = zt.next()
            yield
            P.dma("sp", z[:], src[tt * 128:(tt + 1) * 128, :], writes=[z.res])
            yield
            xbt = xb.next()
            yield
            if norm:
                s = stat.next()
                j = jk.next()
                P.op("act", lambda h: h.activation(out=j[:], in_=z[:], func=AF.Copy, accum_out=s[:, 0:1]),
                     reads=[z.res], writes=[j.res, s.res])
                P.op("act", lambda h: h.activation(out=j[:], in_=z[:], func=AF.Square, accum_out=s[:, 1:2]),
                     reads=[z.res], writes=[j.res, s.res])
                P.op("dve", lambda h: h.tensor_scalar(out=s[:, 2:3], in0=s[:, 0:1], scalar1=1.0 / D, scalar2=None,
                                                      op0=ALU.mult), reads=[s.res], writes=[s.res])
                P.op("dve", lambda h: h.tensor_tensor(out=s[:, 3:4], in0=s[:, 2:3], in1=s[:, 2:3], op=ALU.mult),
                     reads=[s.res], writes=[s.res])
                P.op("dve", lambda h: h.scalar_tensor_tensor(out=s[:, 4:5], in0=s[:, 1:2], scalar=1.0 / D,
                                                             in1=s[:, 3:4], op0=ALU.mult, op1=ALU.subtract),
                     reads=[s.res], writes=[s.res])
                P.op("act", lambda h: h.activation(out=s[:, 5:6], in_=s[:, 4:5], func=AF.Ln, bias=ceps5[:, 0:1]),
                     reads=[s.res, ceps5.res], writes=[s.res])
                P.op("act", lambda h: h.activation(out=s[:, 5:6], in_=s[:, 5:6], func=AF.Exp, scale=-0.5),
                     reads=[s.res], writes=[s.res])
                P.op("dve", lambda h: h.tensor_scalar(out=z[:], in0=z[:], scalar1=s[:, 2:3], scalar2=s[:, 5:6],
                                                      op0=ALU.subtract, op1=ALU.mult),
                     reads=[z.res, s.res], writes=[z.res])
                P.op("dve", lambda h: h.tensor_tensor(out=z[:], in0=z[:], in1=g_bc[:], op=ALU.mult),
                     reads=[z.res, g_bc.res], writes=[z.res])
                P.op("dve", lambda h: h.tensor_tensor(out=z[:], in0=z[:], in1=b_bc[:], op=ALU.add),
                     reads=[z.res, b_bc.res], writes=[z.res])
                P.dma("sp", dstF[tt * 128:(tt + 1) * 128, :], z[:], reads=[z.res])
                if dstOut is not None:
                    P.dma("sp", dstOut[tt * 128:(tt + 1) * 128, :], z[:], reads=[z.res])
            yield
            P.op("act", lambda h: h.activation(out=xbt[:], in_=z[:], func=AF.Copy), reads=[z.res], writes=[xbt.res])
            yield
            xt_ = xtt.next()
            yield
            for half in range(2):
                b = next_bank()
                pv = psum[:, b, :].bitcast(BF16).rearrange("p (k t) -> p k t", t=128)

                def fn(h, b=b, pv=pv, half=half, xbt=xbt):
                    ins = None
                    for k in range(8):
                        kk = half * 8 + k
                        ins = h.transpose(pv[:, k, :], xbt[:, kk * 128:(kk + 1) * 128], ident[:])
                    return ins
                P.op("pe", fn, reads=[xbt.res, ident.res], writes=[banks[b]])
                eng = "dve" if half == 0 else "act"
                if eng == "dve":
                    P.op("dve", lambda h, pv=pv, half=half, xt_=xt_: h.tensor_copy(out=xt_[:, half * 8:half * 8 + 8, :], in_=pv),
                         reads=[banks[b]], writes=[xt_.res])
                else:
                    P.op("act", lambda h, pv=pv, half=half, xt_=xt_: h.activation(out=xt_[:, half * 8:half * 8 + 8, :], in_=pv, func=AF.Copy),
                         reads=[banks[b]], writes=[xt_.res])
            yield
            P.dma("sp", xT[:, tt * 128:(tt + 1) * 128].rearrange("(k p) t -> p k t", p=128), xt_[:], reads=[xt_.res])
            yield
        NT_ = T // 128
        for g0 in range(0, NT_, 3):
            live = [tile_gen(t_) for t_ in range(g0, min(g0 + 3, NT_))]
            while live:
                for g_ in list(live):
                    try:
                        next(g_)
                    except StopIteration:
                        live.remove(g_)
        P.barrier()

    def gla_like(l, kind):
        ar.reset()
        NC = TS // 64
        NPC = TS // 128
        nh = 4 if kind == "gla" else 8
        nvb = 2 if kind == "gla" else 1
        DV = 128 * nvb
        gs = (-1.0 / 16.0) if kind == "gla" else 1.0
        mask0 = ar.alloc([128, TS], F32)
        P.op("pool", lambda h: h.memset(mask0[:], 1.0), writes=[mask0.res])
        P.op("pool", lambda h: h.memset(mask0[:].rearrange("p (c i) -> p c i", i=64)[:, :, 0:1], 0.0), writes=[mask0.res])
        gsets = []
        for _ in range(2):
            gsets.append((ar.alloc([128, TS], F32), ar.alloc([128, TS], F32), ar.alloc([128, TS], F32), ar.alloc([128, TS], F32),
                          ar.alloc([128, TS], F32), ar.alloc([128, TS], BF16), ar.alloc([128, TS], BF16), ar.alloc([128, TS], BF16),
                          ar.alloc([128, TS], BF16), ar.alloc([128, TS], BF16), ar.alloc([128, NC], F32),
                          ar.alloc([128, NPC, DV], BF16), ar.alloc([128, NPC, 128], BF16)))
        git = [0]
        KV = ar.alloc([128, NC, DV], F32)
        Sall = ar.alloc([128, NC + 1, DV], F32)
        Sbf = ar.alloc([128, NC, DV], BF16)
        AT = Rot([ar.alloc([128, 128], BF16) for _ in range(3)])
        og = Rot([ar.alloc([128, nvb, 512], F32) for _ in range(2)])
        sq = Rot([ar.alloc([128, nvb, 512], BF16) for _ in range(2)])
        gt = Rot([ar.alloc([128, nvb, 512], BF16) for _ in range(2)])
        rstd = Rot([ar.alloc([128, 512], F32) for _ in range(2)])
        yb = Rot([ar.alloc([128, nvb, 512], BF16) for _ in range(2)])
        glt = ar.alloc([16, TS], F32)
        wg = ar.alloc([16, 512], F32)
        gv = ar.alloc([128, 6], F32)
        nbg = ar.alloc([128, 4], F32)
        if kind == "gla":
            P.dma("sp", wg[:], glaw[l], writes=[wg.res])
            P.dma("sp", gv[:], glav[l], writes=[gv.res])
            P.op("dve", lambda h: h.tensor_scalar(out=nbg[:], in0=gv[:, 0:4], scalar1=-1.0, scalar2=None, op0=ALU.mult),
                 reads=[gv.res], writes=[nbg.res])
        def front(hh, sp, itn):
            fr, gg, G, tmp, tmp2, qb, kb_, qt, kt, kpT, egl, vtok, kptok = gsets[itn % 2]
            t0 = sp * TS
            yield
            if kind == "gla":
                if hh == 0:
                    pass
                P.dma("sp", glt[:], hGL[:, t0:t0 + TS], writes=[glt.res])
                for tt in range(TS // 512):
                    b = next_bank_in(7, 8)
                    mm_group(b, ps_ap(b), [(wg[:, hh * 128:(hh + 1) * 128], glt[:, tt * 512:(tt + 1) * 512])],
                             [wg.res, glt.res])
                    P.op("act", lambda h, b=b, tt=tt: h.activation(out=fr[:, tt * 512:(tt + 1) * 512], in_=ps_ap(b),
                                                                   func=AF.Exp, scale=-1.0, bias=nbg[:, hh:hh + 1]),
                         reads=[banks[b], nbg.res], writes=[fr.res])
                P.op("act", lambda h: h.activation(out=gg[:], in_=fr[:], func=AF.Ln, bias=1.0, scale=1.0),
                     reads=[fr.res], writes=[gg.res])
                P.dma("sp", qb[:], hQB[hh * 128:(hh + 1) * 128, t0:t0 + TS], writes=[qb.res])
                P.dma("sp", kb_[:], hKB[hh * 128:(hh + 1) * 128, t0:t0 + TS], writes=[kb_.res])
                kk = kb_
                P.dma("sp", vtok[:], hVB[t0:t0 + TS, hh * DV:(hh + 1) * DV].rearrange("(c p) v -> p c v", p=128),
                      writes=[vtok.res])
            else:
                P.dma("sp", fr[:], hFC[hh * 128:(hh + 1) * 128, t0:t0 + TS], writes=[fr.res])
                P.op("act", lambda h: h.activation(out=fr[:], in_=fr[:], func=AF.Sigmoid), reads=[fr.res], writes=[fr.res])
                oml = ar_small["oml"]
                P.op("dve", lambda h: h.tensor_scalar(out=fr[:], in0=fr[:], scalar1=oml[:, hh:hh + 1],
                                                      scalar2=lb_all[:, hh, l:l + 1], op0=ALU.mult, op1=ALU.add),
                     reads=[fr.res, oml.res, lb_all.res], writes=[fr.res])
                P.op("act", lambda h: h.activation(out=gg[:], in_=fr[:], func=AF.Ln), reads=[fr.res], writes=[gg.res])
                P.op("dve", lambda h: h.tensor_scalar(out=fr[:], in0=fr[:], scalar1=-1.0, scalar2=1.0,
                                                      op0=ALU.mult, op1=ALU.add), reads=[fr.res], writes=[fr.res])
                kk = fr
                P.dma("sp", qb[:], hQC[hh * 128:(hh + 1) * 128, t0:t0 + TS], writes=[qb.res])
                P.dma("sp", vtok[:], hIC[t0:t0 + TS, hh * DV:(hh + 1) * DV].rearrange("(c p) v -> p c v", p=128),
                      writes=[vtok.res])
            yield
            P.op("dve", lambda h: h.tensor_tensor_scan(out=G[:], data0=mask0[:], data1=gg[:], initial=0.0,
                                                       op0=ALU.mult, op1=ALU.add),
                 reads=[mask0.res, gg.res], writes=[G.res])
            yield
            Gv = G[:].rearrange("p (c i) -> p c i", i=64)
            yield
            P.op("act", lambda h: h.activation(out=tmp[:], in_=G[:], func=AF.Exp, scale=gs), reads=[G.res], writes=[tmp.res])
            yield
            P.op("dve", lambda h: h.tensor_tensor(out=qt[:], in0=qb[:], in1=tmp[:], op=ALU.mult),
                 reads=[qb.res, tmp.res], writes=[qt.res])
            yield
            P.op("act", lambda h: h.activation(out=tmp2[:], in_=G[:], func=AF.Exp, scale=-gs), reads=[G.res], writes=[tmp2.res])
            yield
            P.op("dve", lambda h, kk=kk: h.tensor_tensor(out=kt[:], in0=kk[:], in1=tmp2[:], op=ALU.mult),
                 reads=[kk.res, tmp2.res], writes=[kt.res])
            yield
            P.op("dve", lambda h: h.tensor_tensor(out=tmp[:].rearrange("p (c i) -> p c i", i=64),
                                                  in0=Gv[:, :, 63:64].to_broadcast([128, NC, 64]), in1=Gv,
                                                  op=ALU.subtract), reads=[G.res, tmp.res, qt.res], writes=[tmp.res])
            yield
            P.op("act", lambda h: h.activation(out=tmp[:], in_=tmp[:], func=AF.Exp, scale=gs), reads=[tmp.res], writes=[tmp.res])
            yield
            P.op("dve", lambda h, kk=kk: h.tensor_tensor(out=kpT[:], in0=kk[:], in1=tmp[:], op=ALU.mult),
                 reads=[kk.res, tmp.res], writes=[kpT.res])
            yield
            P.op("act", lambda h: h.activation(out=egl[:], in_=Gv[:, :, 63], func=AF.Exp, scale=gs),
                 reads=[G.res], writes=[egl.res])
            yield
            for g8 in range((NPC + 7) // 8):
                b = next_bank_in(7, 8)
                pv = psum[:, b, :].bitcast(BF16).rearrange("p (k t) -> p k t", t=128)
                n8 = min(8, NPC - g8 * 8)

                def fn(h, pv=pv, g8=g8, n8=n8):
                    ins = None
                    for k in range(n8):
                        pc = g8 * 8 + k
                        ins = h.transpose(pv[:, k, :], kpT[:, pc * 128:(pc + 1) * 128], ident[:])
                    return ins
                P.op("pe", fn, reads=[kpT.res, ident.res], writes=[banks[b]])
                P.op("act", lambda h, pv=pv, g8=g8, n8=n8: h.activation(out=kptok[:, g8 * 8:g8 * 8 + n8, :], in_=pv[:, 0:n8, :],
                                                                        func=AF.Copy), reads=[banks[b]], writes=[kptok.res])
            yield
        def back(hh, sp, itn):
            fr, gg, G, tmp, tmp2, qb, kb_, qt, kt, kpT, egl, vtok, kptok = gsets[itn % 2]
            t0 = sp * TS
            kk = kb_ if kind == "gla" else fr
            if sp == 0:
                P.op("dve", lambda h: h.memset(Sall[:, 0, :], 0.0), writes=[Sall.res])
            if sp > 0:
                P.op("dve", lambda h: h.tensor_copy(out=Sall[:, 0, :], in_=Sall[:, NC, :]),
                     reads=[Sall.res], writes=[Sall.res])

            yield
            per_bank = 512 // DV
            yield
            KVv = KV[:].rearrange("p (c two) v -> p c two v", two=2)
            yield
            for p0 in range(0, NPC, per_bank):
                bA, bB = next_bank_in(4, 7), next_bank_in(4, 7)
                pA = psum[:, bA, :].rearrange("p (c v) -> p c v", v=DV)
                pB = psum[:, bB, :].rearrange("p (c v) -> p c v", v=DV)

                def fn(h, p0=p0, pA=pA, pB=pB):
                    ins = None
                    for i in range(per_bank):
                        pc = p0 + i
                        h.matmul(pA[:, i, :], kptok[0:64, pc, :], vtok[0:64, pc, :], start=True, stop=True)
                        ins = h.matmul(pB[:, i, :], kptok[64:128, pc, :], vtok[64:128, pc, :], start=True, stop=True)
                    return ins
                P.op("pe", fn, reads=[kptok.res, vtok.res], writes=[banks[bA], banks[bB]])
                P.op("dve", lambda h, p0=p0, pA=pA: h.tensor_copy(out=KVv[:, p0:p0 + per_bank, 0, :], in_=pA),
                     reads=[banks[bA]], writes=[KV.res])
                P.op("act", lambda h, p0=p0, pB=pB: h.activation(out=KVv[:, p0:p0 + per_bank, 1, :], in_=pB, func=AF.Copy),
                     reads=[banks[bB]], writes=[KV.res])
            yield
            for c in range(NC):
                P.op("dve", lambda h, c=c: h.scalar_tensor_tensor(out=Sall[:, c + 1, :], in0=Sall[:, c, :], scalar=egl[:, c:c + 1],
                                                                 in1=KV[:, c, :], op0=ALU.mult, op1=ALU.add),
                     reads=[egl.res, KV.res, Sall.res], writes=[Sall.res])
            yield
            P.op("act", lambda h: h.activation(out=Sbf[:], in_=Sall[:, 0:NC, :], func=AF.Copy),
                 reads=[Sall.res], writes=[Sbf.res])
            yield
            for tt in range(TS // 512):
                ob = [next_bank_in(0, 4) for _ in range(nvb)]
                for p4 in range(4):
                    pc = tt * 4 + p4
                    bs = next_bank_in(4, 7)
                    mm_group(bs, psum[:, bs, 0:128], [(kt[:, pc * 128:(pc + 1) * 128], qt[:, pc * 128:(pc + 1) * 128])],
                             [kt.res, qt.res])
                    at = AT.next()
                    P.op("dve", lambda h, bs=bs, at=at: h.tensor_tensor(out=at[:], in0=psum[:, bs, 0:128], in1=mask[:], op=ALU.mult),
                         reads=[banks[bs], mask.res], writes=[at.res])
                    for vb in range(nvb):
                        def fn(h, vb=vb, pc=pc, p4=p4, at=at, ob=ob):
                            o_ap = psum[:, ob[vb], p4 * 128:(p4 + 1) * 128]
                            h.matmul(o_ap, vtok[:, pc, vb * 128:(vb + 1) * 128], at[:], start=True, stop=False)
                            h.matmul(o_ap[:, 0:64], Sbf[:, 2 * pc, vb * 128:(vb + 1) * 128], qt[:, pc * 128:pc * 128 + 64],
                                     start=False, stop=False)
                            return h.matmul(o_ap[:, 64:128], Sbf[:, 2 * pc + 1, vb * 128:(vb + 1) * 128],
                                            qt[:, pc * 128 + 64:pc * 128 + 128], start=False, stop=True)
                        P.op("pe", fn, reads=[vtok.res, at.res, Sbf.res, qt.res], writes=[banks[ob[vb]]])
                o_t = og.next(); s_t = sq.next(); g_t = gt.next(); r_t = rstd.next(); y_t = yb.next()
                gsrc = hGB if kind == "gla" else hGC
                ybase = W if kind == "gla" else 2 * W
                tok = t0 + tt * 512
                P.dma("sp", g_t[:], gsrc[hh * DV:(hh + 1) * DV, tok:tok + 512].rearrange("(b p) t -> p b t", p=128),
                      writes=[g_t.res])
                for vb in range(nvb):
                    if kind == "gla":
                        P.op("act", lambda h, vb=vb, o_t=o_t, ob=ob: h.activation(out=o_t[:, vb, :], in_=ps_ap(ob[vb]), func=AF.Copy),
                             reads=[banks[ob[vb]]], writes=[o_t.res])
                    else:
                        P.op("dve", lambda h, vb=vb, o_t=o_t, ob=ob, g_t=g_t: h.tensor_tensor(out=o_t[:, vb, :], in0=ps_ap(ob[vb]),
                                                                                            in1=g_t[:, vb, :], op=ALU.mult),
                             reads=[banks[ob[vb]], g_t.res], writes=[o_t.res])
                P.op("dve", lambda h, o_t=o_t, s_t=s_t: h.tensor_tensor(out=s_t[:], in0=o_t[:], in1=o_t[:], op=ALU.mult),
                     reads=[o_t.res], writes=[s_t.res])
                br = next_bank_in(4, 7)
                onesm = ones_bf if nvb == 1 else ones_bf2
                mm_group(br, ps_ap(br), [(onesm[:], s_t[:, vb, :]) for vb in range(nvb)], [onesm.res, s_t.res])
                P.op("act", lambda h, br=br, r_t=r_t: h.activation(out=r_t[:], in_=ps_ap(br), func=AF.Ln, bias=ceps6[:, 0:1]),
                     reads=[banks[br], ceps6.res], writes=[r_t.res])
                P.op("act", lambda h, r_t=r_t: h.activation(out=r_t[:], in_=r_t[:], func=AF.Exp, scale=-0.5),
                     reads=[r_t.res], writes=[r_t.res])
                for vb in range(nvb):
                    wcol = gv[:, 4 + vb:5 + vb] if kind == "gla" else hnw_t[:, l:l + 1]
                    wres = gv.res if kind == "gla" else hnw_t.res
                    if kind == "gla":
                        P.op("dve", lambda h, vb=vb, o_t=o_t, r_t=r_t, wcol=wcol: h.scalar_tensor_tensor(
                            out=o_t[:, vb, :], in0=o_t[:, vb, :], scalar=wcol, in1=r_t[:], op0=ALU.mult, op1=ALU.mult),
                            reads=[o_t.res, r_t.res, wres], writes=[o_t.res])
                        P.op("dve", lambda h, vb=vb, o_t=o_t, y_t=y_t, g_t=g_t: h.tensor_tensor(out=y_t[:, vb, :], in0=o_t[:, vb, :],
                                                                                               in1=g_t[:, vb, :], op=ALU.mult),
                             reads=[o_t.res, g_t.res], writes=[y_t.res])
                    else:
                        P.op("dve", lambda h, vb=vb, o_t=o_t, r_t=r_t, wcol=wcol, y_t=y_t: h.scalar_tensor_tensor(
                            out=y_t[:, vb, :], in0=o_t[:, vb, :], scalar=wcol, in1=r_t[:], op0=ALU.mult, op1=ALU.mult),
                            reads=[o_t.res, r_t.res, wres], writes=[y_t.res])
                P.dma("sp", yT[ybase + hh * DV:ybase + (hh + 1) * DV, tok:tok + 512].rearrange("(b p) t -> p b t", p=128),
                      y_t[:], reads=[y_t.res])
            yield
        its = [(hh, sp) for hh in range(nh) for sp in range(NSP)]

        def drive(gens):
            live = list(gens)
            while live:
                for g_ in list(live):
                    try:
                        next(g_)
                    except StopIteration:
                        live.remove(g_)
        drive([front(its[0][0], its[0][1], 0)])
        for n_, (hh_, sp_) in enumerate(its):
            gens = [back(hh_, sp_, n_)]
            if n_ + 1 < len(its):
                gens.append(front(its[n_ + 1][0], its[n_ + 1][1], n_ + 1))
            drive(gens)
        P.barrier()

    ar_small = {}

    def s5_phase(l):
        ar.reset()
        prm = ar.alloc([128, 3, 32], F32)
        P.dma("sp", prm[:], s5p[l], writes=[prm.res])
        sv = ar.alloc([128, 2, 8], F32)
        P.dma("sp", sv[:], s5v[l], writes=[sv.res])
        Bre = ar.alloc([128, 32, 128], BF16)
        Bim = ar.alloc([128, 32, 128], BF16)
        P.dma("pool", Bre[:], s5b[l, 0], writes=[Bre.res])
        P.dma("pool", Bim[:], s5b[l, 1], writes=[Bim.res])
        Cre = ar.alloc([128, 32, 128], BF16)
        Cim = ar.alloc([128, 32, 128], BF16)
        sm = {n: ar.alloc([128, 32], F32) for n in
              ("lr", "dt", "r", "th", "thn", "a", "a2", "sn", "cs", "are", "aim", "den", "fre", "fim", "t1", "t2")}
        mark = ar.off
        c0 = ar.alloc([128, 32, 128], F32)
        c1 = ar.alloc([128, 32, 128], F32)
        c2 = ar.alloc([128, 32, 128], F32)
        P.dma("sp", c0[:], s5c[l, 0], writes=[c0.res])
        P.dma("sp", c1[:], s5c[l, 1], writes=[c1.res])

        def dv(fn, rd, wr):
            P.op("dve", fn, reads=[x.res for x in rd], writes=[x.res for x in wr])

        def ac(fn, rd, wr):
            P.op("act", fn, reads=[x.res for x in rd], writes=[x.res for x in wr])
        s = sm
        dv(lambda h: h.tensor_scalar_min(out=s["lr"][:], in0=prm[:, 0, :], scalar1=-1e-4), [prm], [s["lr"]])
        ac(lambda h: h.activation(out=s["dt"][:], in_=prm[:, 2, :], func=AF.Exp), [prm], [s["dt"]])
        dv(lambda h: h.tensor_tensor(out=s["t1"][:], in0=s["lr"][:], in1=s["dt"][:], op=ALU.mult), [s["lr"], s["dt"]], [s["t1"]])
        ac(lambda h: h.activation(out=s["r"][:], in_=s["t1"][:], func=AF.Exp), [s["t1"]], [s["r"]])
        dv(lambda h: h.tensor_tensor(out=s["th"][:], in0=prm[:, 1, :], in1=s["dt"][:], op=ALU.mult), [prm, s["dt"]], [s["th"]])

        I32 = mybir.dt.int32
        SIN_SCALE = 6.2831845

        def sincos(y, ki, fr, tq, f2, sn, cs):
            MAGIC = 12582912.0
            dv(lambda h: h.tensor_scalar(out=ki[:], in0=y[:], scalar1=MAGIC, scalar2=None, op0=ALU.add), [y], [ki])
            dv(lambda h: h.tensor_scalar(out=ki[:], in0=ki[:], scalar1=MAGIC, scalar2=None, op0=ALU.subtract), [ki], [ki])
            dv(lambda h: h.tensor_sub(out=fr[:], in0=y[:], in1=ki[:]), [y, ki], [fr])
            ac(lambda h: h.activation(out=sn[:], in_=fr[:], func=AF.Sin, scale=SIN_SCALE), [fr], [sn])
            ac(lambda h: h.activation(out=f2[:], in_=fr[:], func=AF.Sin, scale=0.5 * SIN_SCALE), [fr], [f2])
            dv(lambda h: h.tensor_tensor(out=tq[:], in0=f2[:], in1=f2[:], op=ALU.mult), [f2], [tq])
            dv(lambda h: h.tensor_scalar(out=cs[:], in0=tq[:], scalar1=-2.0, scalar2=1.0, op0=ALU.mult, op1=ALU.add), [tq], [cs])

        dv(lambda h: h.tensor_scalar(out=s["thn"][:], in0=s["th"][:], scalar1=1.0 / TWO_PI, scalar2=None, op0=ALU.mult),
           [s["th"]], [s["thn"]])
        sincos(s["thn"], s["a"], s["a2"], s["t1"], s["t2"], s["sn"], s["cs"])
        dv(lambda h: h.scalar_tensor_tensor(out=s["are"][:], in0=s["cs"][:], scalar=1.0, in1=s["r"][:], op0=ALU.mult, op1=ALU.mult),
           [s["cs"], s["r"]], [s["are"]])
        dv(lambda h: h.scalar_tensor_tensor(out=s["aim"][:], in0=s["sn"][:], scalar=1.0, in1=s["r"][:], op0=ALU.mult, op1=ALU.mult),
           [s["sn"], s["r"]], [s["aim"]])
        dv(lambda h: h.tensor_tensor(out=s["den"][:], in0=s["lr"][:], in1=s["lr"][:], op=ALU.mult), [s["lr"]], [s["den"]])
        dv(lambda h: h.tensor_tensor(out=s["t1"][:], in0=prm[:, 1, :], in1=prm[:, 1, :], op=ALU.mult), [prm], [s["t1"]])
        dv(lambda h: h.tensor_add(out=s["den"][:], in0=s["den"][:], in1=s["t1"][:]), [s["den"], s["t1"]], [s["den"]])
        dv(lambda h: h.reciprocal(out=s["den"][:], in_=s["den"][:]), [s["den"]], [s["den"]])
        dv(lambda h: h.tensor_scalar_add(out=s["t2"][:], in0=s["are"][:], scalar1=-1.0), [s["are"]], [s["t2"]])
        dv(lambda h: h.tensor_tensor(out=s["fre"][:], in0=s["t2"][:], in1=s["lr"][:], op=ALU.mult), [s["t2"], s["lr"]], [s["fre"]])
        dv(lambda h: h.tensor_tensor(out=s["t1"][:], in0=s["aim"][:], in1=prm[:, 1, :], op=ALU.mult), [s["aim"], prm], [s["t1"]])
        dv(lambda h: h.tensor_add(out=s["fre"][:], in0=s["fre"][:], in1=s["t1"][:]), [s["fre"], s["t1"]], [s["fre"]])
        dv(lambda h: h.tensor_tensor(out=s["fre"][:], in0=s["fre"][:], in1=s["den"][:], op=ALU.mult), [s["fre"], s["den"]], [s["fre"]])
        dv(lambda h: h.tensor_tensor(out=s["fim"][:], in0=s["aim"][:], in1=s["lr"][:], op=ALU.mult), [s["aim"], s["lr"]], [s["fim"]])
        dv(lambda h: h.tensor_tensor(out=s["t1"][:], in0=s["t2"][:], in1=prm[:, 1, :], op=ALU.mult), [s["t2"], prm], [s["t1"]])
        dv(lambda h: h.tensor_sub(out=s["fim"][:], in0=s["fim"][:], in1=s["t1"][:]), [s["fim"], s["t1"]], [s["fim"]])
        dv(lambda h: h.tensor_tensor(out=s["fim"][:], in0=s["fim"][:], in1=s["den"][:], op=ALU.mult), [s["fim"], s["den"]], [s["fim"]])
        bc = lambda t: t[:].unsqueeze(2).to_broadcast([128, 32, 128])
        dv(lambda h: h.tensor_tensor(out=c2[:], in0=c0[:], in1=bc(s["fre"]), op=ALU.mult), [c0, s["fre"]], [c2])
        P.op("dve", lambda h: h.tensor_tensor(out=Cre[:], in0=c1[:], in1=bc(s["fim"]), op=ALU.mult),
             reads=[c1.res, s["fim"].res], writes=[Cre.res])
        dv(lambda h: h.tensor_sub(out=Cre[:], in0=c2[:], in1=Cre[:]), [c2, Cre], [Cre])
        dv(lambda h: h.tensor_tensor(out=c2[:], in0=c0[:], in1=bc(s["fim"]), op=ALU.mult), [c0, s["fim"], Cre], [c2])
        P.op("dve", lambda h: h.tensor_tensor(out=c0[:], in0=c1[:], in1=bc(s["fre"]), op=ALU.mult),
             reads=[c1.res, s["fre"].res, c2.res], writes=[c0.res])
        dv(lambda h: h.scalar_tensor_tensor(out=Cim[:], in0=c2[:], scalar=-1.0, in1=c0[:], op0=ALU.mult, op1=ALU.subtract),
           [c2, c0], [Cim])
        P.barrier()
        ar.reset(mark)
        tidx = ar.alloc([128, TS], F32)
        P.op("pool", lambda h: h.iota(tidx[:], pattern=[[1, TS]], base=0, channel_multiplier=0,
                                      allow_small_or_imprecise_dtypes=True), writes=[tidx.res])
        tabA = [ar.alloc([128, TS], BF16) for _ in range(4)]
        tabB = [ar.alloc([128, TS], BF16) for _ in range(4)]
        scr = [ar.alloc([128, TS], F32) for _ in range(5)]
        wsets = []
        for _ in range(2):
            wsets.append(tuple(ar.alloc([128, TS], BF16) for _ in range(10)))
        kre, kim, k2re, k2im, t1, t2, sre, sim, t3, t4 = wsets[0]
        sit = [0]
        uall = ar.alloc([128, TS], BF16)
        yv = Rot([ar.alloc([128, 512], F32) for _ in range(2)])
        y2 = Rot([ar.alloc([128, 512], F32) for _ in range(2)])
        zo = Rot([ar.alloc([128, 512], BF16) for _ in range(2)])
        send = ar.alloc([128, 32, 4], F32)
        rbc = {}
        for fb in range(8):
            for j in range(4):
                sb = fb * 4 + j
                A, B = tabA[j], tabB[j]
                thn = s["thn"][:, sb:sb + 1]
                dv(lambda h, thn=thn: h.tensor_scalar(out=scr[0][:], in0=tidx[:], scalar1=thn, scalar2=None, op0=ALU.mult),
                   [tidx, s["thn"]], [scr[0]])
                sincos(scr[0], scr[1], scr[2], scr[3], scr[4], B, A)
            for sp in range(NSP):
                t0 = sp * TS
                P.dma("sp", uall[:], hU[fb * 128:(fb + 1) * 128, t0:t0 + TS], writes=[uall.res])
                yb_ = [next_bank_in(0, 4) for _ in range(TS // 512)]
                def it_gen(j):
                    sb = fb * 4 + j
                    yield
                    A, B = tabA[j], tabB[j]
                    yield
                    kre, kim, k2re, k2im, t1, t2, sre, sim, t3, t4 = wsets[j % 2]
                    yield
                    for tt in range(TS // 512):
                        sl = slice(tt * 512, (tt + 1) * 512)
                        b1 = next_bank_in(4, 8)
                        mm_group(b1, ps_ap(b1), [(Bre[:, sb, :], uall[:, sl])], [Bre.res, uall.res])
                        P.op("act", lambda h, b1=b1, sl=sl: h.activation(out=kre[:, sl], in_=ps_ap(b1), func=AF.Copy),
                             reads=[banks[b1]], writes=[kre.res])
                        b2 = next_bank_in(4, 8)
                        mm_group(b2, ps_ap(b2), [(Bim[:, sb, :], uall[:, sl])], [Bim.res, uall.res])
                        P.op("act", lambda h, b2=b2, sl=sl: h.activation(out=kim[:, sl], in_=ps_ap(b2), func=AF.Copy),
                             reads=[banks[b2]], writes=[kim.res])
                    yield
                    dv(lambda h, A=A: h.tensor_tensor(out=t1[:], in0=A[:], in1=kre[:], op=ALU.mult), [A, kre], [t1])
                    yield
                    P.op("dve", lambda h, B=B: h.tensor_tensor(out=t2[:], in0=B[:], in1=kim[:], op=ALU.mult),
                         reads=[B.res, kim.res], writes=[t2.res])
                    yield
                    P.op("dve", lambda h, A=A: h.tensor_tensor(out=t3[:], in0=A[:], in1=kim[:], op=ALU.mult),
                         reads=[A.res, kim.res], writes=[t3.res])
                    yield
                    dv(lambda h, B=B: h.tensor_tensor(out=t4[:], in0=B[:], in1=kre[:], op=ALU.mult), [B, kre], [t4])
                    yield
                    dv(lambda h: h.tensor_add(out=k2re[:], in0=t1[:], in1=t2[:]), [t1, t2], [k2re])
                    yield
                    dv(lambda h: h.tensor_sub(out=k2im[:], in0=t3[:], in1=t4[:]), [t3, t4], [k2im])
                    yield
                    yield
                    rb = s["r"][:, sb:sb + 1].to_broadcast([128, TS])
                    yield
                    if sp == 0:
                        i_re, i_im = 0.0, 0.0
                        rd_i = []
                    else:
                        se = send[:, sb, :]
                        dv(lambda h, se=se, A=A: h.tensor_tensor(out=se[:, 2:3], in0=A[:, 1:2], in1=se[:, 0:1], op=ALU.mult), [A, send], [send])
                        dv(lambda h, se=se, B=B: h.tensor_tensor(out=se[:, 3:4], in0=B[:, 1:2], in1=se[:, 1:2], op=ALU.mult), [B, send], [send])
                        dv(lambda h, se=se: h.tensor_sub(out=se[:, 2:3], in0=se[:, 2:3], in1=se[:, 3:4]), [send], [send])
                        dv(lambda h, se=se, A=A: h.tensor_tensor(out=se[:, 3:4], in0=A[:, 1:2], in1=se[:, 1:2], op=ALU.mult), [A, send], [send])
                        dv(lambda h, se=se, B=B: h.scalar_tensor_tensor(out=se[:, 3:4], in0=B[:, 1:2], scalar=se[:, 0:1], in1=se[:, 3:4],
                                                                       op0=ALU.mult, op1=ALU.add), [B, send], [send])
                        i_re, i_im = se[:, 2:3], se[:, 3:4]
                        rd_i = [send]
                    yield
                    dv(lambda h, rb=rb, i_re=i_re: h.tensor_tensor_scan(out=kre[:], data0=rb, data1=k2re[:], initial=i_re,
                                                                        op0=ALU.mult, op1=ALU.add), [s["r"], k2re, kre] + rd_i, [kre])
                    yield
                    dv(lambda h, rb=rb, i_im=i_im: h.tensor_tensor_scan(out=kim[:], data0=rb, data1=k2im[:], initial=i_im,
                                                                        op0=ALU.mult, op1=ALU.add), [s["r"], k2im, kim] + rd_i, [kim])
                    yield
                    dv(lambda h, A=A: h.tensor_tensor(out=t1[:], in0=A[:], in1=kre[:], op=ALU.mult), [A, kre], [t1])
                    yield
                    P.op("dve", lambda h, B=B: h.tensor_tensor(out=t2[:], in0=B[:], in1=kim[:], op=ALU.mult),
                         reads=[B.res, kim.res], writes=[t2.res])
                    yield
                    P.op("dve", lambda h, A=A: h.tensor_tensor(out=t3[:], in0=A[:], in1=kim[:], op=ALU.mult),
                         reads=[A.res, kim.res], writes=[t3.res])
                    yield
                    dv(lambda h, B=B: h.tensor_tensor(out=t4[:], in0=B[:], in1=kre[:], op=ALU.mult), [B, kre], [t4])
                    yield
                    dv(lambda h: h.tensor_sub(out=sre[:], in0=t1[:], in1=t2[:]), [t1, t2], [sre])
                    yield
                    dv(lambda h: h.tensor_add(out=sim[:], in0=t3[:], in1=t4[:]), [t3, t4], [sim])
                    yield
                    if NSP > 1:
                        dv(lambda h, sb=sb: h.tensor_sub(out=send[:, sb, 0:1], in0=t1[:, TS - 1:TS], in1=t2[:, TS - 1:TS]), [t1, t2], [send])
                        dv(lambda h, sb=sb: h.tensor_add(out=send[:, sb, 1:2], in0=t3[:, TS - 1:TS], in1=t4[:, TS - 1:TS]), [t3, t4], [send])
                    yield
                    for tt in range(TS // 512):
                        sl = slice(tt * 512, (tt + 1) * 512)
                        mm_group(yb_[tt], ps_ap(yb_[tt]), [(Cre[:, sb, :], sre[:, sl]), (Cim[:, sb, :], sim[:, sl])],
                                 [Cre.res, Cim.res, sre.res, sim.res], start=(j == 0), stop=(j == 3))
                    yield
                for jp in (0, 2):
                    gens = [it_gen(jp), it_gen(jp + 1)]
                    live = list(gens)
                    while live:
                        for g_ in list(live):
                            try:
                                next(g_)
                            except StopIteration:
                                live.remove(g_)
                for tt in range(TS // 512):
                    sl = slice(tt * 512, (tt + 1) * 512)
                    y_ = yv.next(); w_ = y2.next(); z_ = zo.next()
                    b = yb_[tt]
                    P.op("dve", lambda h, b=b, y_=y_, sl=sl, fb=fb: h.scalar_tensor_tensor(
                        out=y_[:], in0=uall[:, sl], scalar=sv[:, 0, fb:fb + 1], in1=ps_ap(b), op0=ALU.mult, op1=ALU.add),
                        reads=[uall.res, sv.res, banks[b]], writes=[y_.res])
                    P.op("dve", lambda h, y_=y_, w_=w_: h.tensor_tensor(out=w_[:], in0=y_[:], in1=y_[:], op=ALU.mult),
                         reads=[y_.res], writes=[w_.res])
                    P.op("dve", lambda h, w_=w_: h.tensor_scalar(out=w_[:], in0=w_[:], scalar1=0.044715, scalar2=1.0,
                                                                  op0=ALU.mult, op1=ALU.add), reads=[w_.res], writes=[w_.res])
                    P.op("dve", lambda h, y_=y_, w_=w_: h.tensor_tensor(out=w_[:], in0=w_[:], in1=y_[:], op=ALU.mult),
                         reads=[y_.res, w_.res], writes=[w_.res])
                    P.op("act", lambda h, w_=w_: h.activation(out=w_[:], in_=w_[:], func=AF.Sigmoid, scale=1.5957691216057308),
                         reads=[w_.res], writes=[w_.res])
                    dv(lambda h, y_=y_, w_=w_, z_=z_: h.tensor_tensor(out=z_[:], in0=y_[:], in1=w_[:], op=ALU.mult), [y_, w_], [z_])
                    P.dma("sp", zT[fb * 128:(fb + 1) * 128, t0 + tt * 512:t0 + (tt + 1) * 512], z_[:], reads=[z_.res])
        P.barrier()

    def make_epi_win(l):
        def epi(bl, tag, c0, rows, tok0, aux):
            b = bl[0]
            if tag == "ua":
                evac_store(b, rows, 512, None, 1.0, BF16, hU[c0 - O_UA:c0 - O_UA + rows, tok0:tok0 + 512], aux)
            elif tag == "qb":
                evac_store(b, rows, 512, None, 128.0 ** -0.5, BF16, hQB[c0 - O_QB:c0 - O_QB + rows, tok0:tok0 + 512], aux)
            elif tag == "kb":
                evac_store(b, rows, 512, None, 1.0, BF16, hKB[c0 - O_KB:c0 - O_KB + rows, tok0:tok0 + 512], aux)
            elif tag == "gl":
                evac_store(b, rows, 512, None, 1.0, F32, hGL[0:16, tok0:tok0 + 512], aux, eng="act")
            elif tag == "gb":
                evac_store(b, rows, 512, AF.Silu, 1.0, BF16, hGB[c0 - O_GB:c0 - O_GB + rows, tok0:tok0 + 512], aux)
            elif tag == "qc":
                evac_store(b, rows, 512, AF.Silu, 1.0, BF16, hQC[c0 - O_QC:c0 - O_QC + rows, tok0:tok0 + 512], aux)
            elif tag == "fc":
                evac_store(b, rows, 512, None, 1.0, F32, hFC[c0 - O_FC:c0 - O_FC + rows, tok0:tok0 + 512], aux)
            elif tag == "gc":
                evac_store(b, rows, 512, AF.Sigmoid, 1.0, BF16, hGC[c0 - O_GC:c0 - O_GC + rows, tok0:tok0 + 512], aux)
            elif tag == "mg":
                evac_store(b, rows, 512, AF.Sigmoid, 1.0, BF16, hMG[c0 - O_MG:c0 - O_MG + rows, tok0:tok0 + 512], aux)
            elif tag == "vb":
                evac_store(b, 128, rows, None, 1.0, BF16, hVB[tok0:tok0 + 128, c0 - O_VB:c0 - O_VB + rows], aux)
            elif tag == "ic":
                evac_store(b, 128, rows, None, 1.0, BF16, hIC[tok0:tok0 + 128, c0 - O_IC:c0 - O_IC + rows], aux, eng="act")
        return epi

    def seg(off, width, tag):
        return [(off + i * 512, min(512, width - i * 512), tag) for i in range((width + 511) // 512)]

    def layer(l, last):
        TPh = min(2048, T)
        fm_cols = (seg(O_UA, 1024, "ua") + seg(O_QB, 512, "qb") + seg(O_KB, 512, "kb") + seg(O_GL, 16, "gl") +
                   seg(O_GB, 1024, "gb") + seg(O_QC, 1024, "qc") + seg(O_FC, 1024, "fc") + seg(O_GC, 1024, "gc") +
                   seg(O_MG, 3 * D, "mg"))
        gemm_phase(xT, D, w_in[l], fm_cols, "fm", make_epi_win(l), TPh)
        gemm_phase(xT, D, w_in[l], seg(O_VB, 1024, "vb") + seg(O_IC, 1024, "ic"), "tm", make_epi_win(l), TPh)
        s5_phase(l)
        gla_like(l, "gla")
        ar.reset()
        gla_like_h(l)
        def epi_glu(bl, tag, c0, rows, tok0, aux):
            b = bl[0]
            sf = aux["sf"].next(); g = aux["gb"].next(); so = aux["sb"].next()
            fbk, r0 = c0 // 128, c0 % 128
            P.op("act", lambda h: h.activation(out=sf[0:rows, :], in_=ps_ap(b, rows), func=AF.Sigmoid,
                                               bias=glu_b[:, fbk:fbk + 1]), reads=[banks[b], glu_b.res], writes=[sf.res])
            P.dma("sp", g[0:rows, 0, :], zT[c0:c0 + rows, tok0:tok0 + 512], writes=[g.res])
            P.op("dve", lambda h: h.tensor_tensor(out=so[0:rows, :], in0=sf[0:rows, :], in1=g[0:rows, 0, :], op=ALU.mult),
                 reads=[sf.res, g.res], writes=[so.res])
            P.dma("sp", yT[c0:c0 + rows, tok0:tok0 + 512], so[0:rows, :], reads=[so.res])
        glu_b = glu_holder["t"]
        P.dma("sp", glu_b[:], s5v[l, :, 1, :], writes=[glu_b.res])
        gemm_phase(zT, W, w_glu[l], seg(0, W, "glu"), "fm", epi_glu, TPh)
        def epi_up(bl, tag, c0, rows, tok0, aux):
            g = aux["gb"].next(); xf = aux["xf"].next(); so = aux["sb"].next()
            P.dma("sp", g[:], hMG[:, tok0:tok0 + 512].rearrange("(b n) t -> n b t", b=3)[c0:c0 + 128], writes=[g.res])
            for i in range(3):
                P.op("dve", lambda h, i=i: h.tensor_tensor(out=xf[:, i, :], in0=ps_ap(bl[i]), in1=g[:, i, :], op=ALU.mult),
                     reads=[banks[bl[i]], g.res], writes=[xf.res])
            P.op("dve", lambda h: h.tensor_add(out=xf[:, 0, :], in0=xf[:, 0, :], in1=xf[:, 1, :]), reads=[xf.res], writes=[xf.res])
            P.op("dve", lambda h: h.tensor_add(out=so[:], in0=xf[:, 0, :], in1=xf[:, 2, :]), reads=[xf.res], writes=[so.res])
            P.dma("sp", mT[c0:c0 + 128, tok0:tok0 + 512], so[:], reads=[so.res])
        gemm_phase(yT, 3 * W, w_up[l], seg(0, D, "up"), "fm", epi_up, min(1024, T), kgroups=[(0, 8), (8, 16), (16, 24)])
        def make_epi_res(xsrc):
            def epi(bl, tag, c0, width, tok0, aux):
                b = bl[0]
                xf = aux["xf"].next(); sf = aux["sf"].next()
                P.dma("sp", xf[:, 0, 0:width], xsrc[tok0:tok0 + 128, c0:c0 + width], writes=[xf.res])
                P.op("dve", lambda h: h.scalar_tensor_tensor(out=sf[:, 0:width], in0=xf[:, 0, 0:width], scalar=float(ALPHA),
                                                             in1=ps_ap(b, 128, width), op0=ALU.mult, op1=ALU.add),
                     reads=[xf.res, banks[b]], writes=[sf.res])
                P.dma("sp", zF[tok0:tok0 + 128, c0:c0 + width], sf[:, 0:width], reads=[sf.res])
            return epi
        xsrc = x_in if l == 0 else xF
        gemm_phase(mT, D, w_out[l], seg(0, D, "o"), "tm", make_epi_res(xsrc), TPh)
        ln_phase(zF, True, l, 0, xF, None)
        def epi_m1(bl, tag, c0, rows, tok0, aux):
            b = bl[0]
            sf = aux["sf"].next(); so = aux["sb"].next()
            P.op("act", lambda h: h.activation(out=sf[:], in_=ps_ap(b), func=AF.Relu), reads=[banks[b]], writes=[sf.res])
            P.op("act", lambda h: h.activation(out=so[:], in_=sf[:], func=AF.Square), reads=[sf.res], writes=[so.res])
            P.dma("sp", hT[c0:c0 + 128, tok0:tok0 + 512], so[:], reads=[so.res])
        gemm_phase(xT, D, w_m1[l], seg(0, HID, "m1"), "fm", epi_m1, TPh)
        gemm_phase(hT, HID, w_m2[l], seg(0, D, "m2"), "tm", make_epi_res(xF), 1024, kbp=8)
        ln_phase(zF, True, l, 1, xF, out if last else None)

    glu_holder = {}

    def gla_like_h(l):
        gla_like(l, "hgrn")

    glu_holder["t"] = ar.alloc([128, 8], F32)
    oml = ar.alloc([128, 8], F32)
    ar_small["oml"] = oml
    ar.base = ar.off

    c_setup()
    ln_phase(x_in, False, 0, 0, None, None)
    for l in range(L):
        P.op("dve", lambda h, l=l: h.tensor_scalar(out=oml[:], in0=lb_all[:, :, l], scalar1=-1.0, scalar2=1.0,
                                                   op0=ALU.mult, op1=ALU.add), reads=[lb_all.res], writes=[oml.res])
        layer(l, l == L - 1)
    P.stopped = False
    P._barrier()
    P.emit()
    st.close()
    return nc


def prep_weights(inp, L):
    f = np.float32
    g = {}
    g["w_in"] = np.ascontiguousarray(inp["w_in"][:L], dtype=f)
    g["w_glu"] = np.ascontiguousarray(inp["s5_w_glu"][:L], dtype=f)
    g["w_up"] = np.ascontiguousarray(inp["w_up"][:L], dtype=f).reshape(L, 3 * W, D)
    g["w_out"] = np.ascontiguousarray(inp["w_out"][:L], dtype=f)
    g["w_m1"] = np.ascontiguousarray(inp["w_mlp_in"][:L], dtype=f)
    g["w_m2"] = np.ascontiguousarray(inp["w_mlp_out"][:L], dtype=f)
    g["lnp"] = np.ascontiguousarray(np.stack([inp["ln1_g"][:L], inp["ln1_b"][:L], inp["ln2_g"][:L], inp["ln2_b"][:L]], axis=1), dtype=f)
    lam_re = np.asarray(inp["s5_lam_re"][:L], f).reshape(L, 32, 128).transpose(0, 2, 1)
    lam_im = np.asarray(inp["s5_lam_im"][:L], f).reshape(L, 32, 128).transpose(0, 2, 1)
    ldt = np.repeat(np.asarray(inp["s5_log_dt"][:L], f)[:, :, None], 64, axis=2).reshape(L, 32, 128).transpose(0, 2, 1)
    g["s5p"] = np.ascontiguousarray(np.stack([lam_re, lam_im, ldt], axis=2), dtype=f)
    bpad = np.zeros((L, 2, 128, 32, 128), f)
    cpad = np.zeros((L, 2, 128, 32, 128), f)
    for ri, (bk, ck) in enumerate((("s5_b_re", "s5_c_re"), ("s5_b_im", "s5_c_im"))):
        Bm = np.asarray(inp[bk][:L], f)
        Cm = np.asarray(inp[ck][:L], f)
        for sb in range(32):
            for gi in range(2):
                gidx = 2 * sb + gi
                r0 = (gidx % 8) * 16
                bpad[:, ri, r0:r0 + 16, sb, gi * 64:(gi + 1) * 64] = Bm[:, gidx].transpose(0, 2, 1)
                cpad[:, ri, gi * 64:(gi + 1) * 64, sb, r0:r0 + 16] = Cm[:, gidx].transpose(0, 2, 1)
    g["s5b"] = bpad
    g["s5c"] = cpad
    dsk = np.asarray(inp["s5_d"][:L], f).reshape(L, 8, 128).transpose(0, 2, 1)
    bgl = np.asarray(inp["s5_b_glu"][:L], f).reshape(L, 8, 128).transpose(0, 2, 1)
    g["s5v"] = np.ascontiguousarray(np.stack([dsk, bgl], axis=2), dtype=f)
    g["glaw"] = np.ascontiguousarray(inp["gla_w_gate"][:L], dtype=f)
    bg = np.asarray(inp["gla_b_gate"][:L], f).reshape(L, 4, 128).transpose(0, 2, 1)
    nw = np.asarray(inp["gla_norm_w"][:L], f).reshape(L, 2, 128).transpose(0, 2, 1)
    g["glav"] = np.ascontiguousarray(np.concatenate([bg, nw], axis=2), dtype=f)
    g["hlb"] = np.ascontiguousarray(np.asarray(inp["hgrn_lb_logits"], f).reshape(DEPTH, 8, 128).transpose(2, 1, 0), dtype=f)
    g["hnw"] = np.ascontiguousarray(np.asarray(inp["hgrn_norm_w"][:L], f).T, dtype=f)
    return g


_CACHE = {}


def kernel(**inputs):
    x = np.asarray(inputs["x"], np.float32)
    Bn, T, _ = x.shape
    key = (T, DEPTH)
    if key not in _CACHE:
        _CACHE[key] = build(T, DEPTH)
    nc = _CACHE[key]
    wts = prep_weights(inputs, DEPTH)
    in_maps = []
    for b in range(Bn):
        m = dict(wts)
        m["x"] = np.ascontiguousarray(x[b])
        in_maps.append(m)
    res = run_bass_kernel_spmd(nc, in_maps, core_ids=list(range(Bn)))
    return np.stack([np.asarray(r["out"], np.float32) for r in res.results], axis=0)
```

```python
import contextlib
import math
import types
import numpy as np
import concourse.bass as bass
import concourse.mybir as mybir
from concourse.bass_utils import run_bass_kernel_spmd

F32 = mybir.dt.float32
BF16 = mybir.dt.bfloat16
AF = mybir.ActivationFunctionType
ALU = mybir.AluOpType

D = 2048
W = 1024
NIN = 14352
HID = 8192
DEPTH = 4
ALPHA = (2 * DEPTH) ** 0.25
TWO_PI = 2.0 * math.pi
O_UA, O_QB, O_KB, O_VB, O_GL, O_GB, O_QC, O_FC, O_IC, O_GC, O_MG = (
    0, 1024, 1536, 2048, 3072, 3088, 4112, 5136, 6160, 7184, 8208)


def _freeze(fn):
    if fn.__closure__ is None:
        return fn
    cells = []
    for c in fn.__closure__:
        try:
            cells.append(types.CellType(c.cell_contents))
        except ValueError:
            cells.append(c)
    g = types.FunctionType(fn.__code__, fn.__globals__, fn.__name__, fn.__defaults__, tuple(cells))
    g.__kwdefaults__ = fn.__kwdefaults__
    return g


class Res:
    __slots__ = ("w", "r")

    def __init__(self):
        self.w = None
        self.r = {}


class Prog:
    KQ = 8
    ENGS = ("pe", "act", "dve", "pool", "sp")

    def __init__(self, nc, st):
        self.nc = nc
        self.eng = {}
        hs = dict(pe=nc.tensor, act=nc.scalar, dve=nc.vector, pool=nc.gpsimd, sp=nc.sync)
        for name in self.ENGS:
            sem = st.enter_context(nc.semaphore("sem_" + name))
            self.eng[name] = dict(h=hs[name], sem=sem, n=0, prog=[], waited={})
        self.dq = {}
        for q in ("sp", "pool", "act"):
            sems = [st.enter_context(nc.semaphore(f"dq_{q}_{i}")) for i in range(self.KQ)]
            self.dq[q] = dict(sems=sems, n=0)
        self.dma_uid = 0
        self.stopped = False
        self.nphase = 0
        self.max_phase = 10 ** 9

    def _waits(self, eng, reads, writes):
        E = self.eng[eng]
        deps = []
        for r in reads:
            if r.w is not None:
                deps.append(r.w)
        for w in writes:
            if w.w is not None:
                deps.append(w.w)
            deps.extend(w.r.values())
        waits = []
        for (sem, val, key, src) in deps:
            if src == "pe" and eng == "pe":
                continue
            if E["waited"].get(key, 0) >= val:
                continue
            E["waited"][key] = val
            waits.append((sem, val))
        return waits

    def _record(self, tok, reads, writes, rkey):
        for r in reads:
            r.r[rkey] = tok
        for w in writes:
            w.w = tok
            w.r = {}

    def op(self, eng, fn, reads=(), writes=()):
        if self.stopped:
            return None
        E = self.eng[eng]
        fn = _freeze(fn)
        waits = self._waits(eng, reads, writes)
        E["n"] += 1
        sem = E["sem"]
        tok = (sem, E["n"], "e_" + eng, eng)

        def emit(h, fn=fn, waits=waits, sem=sem):
            for s, v in waits:
                h.wait_ge(s, v)
            fn(h).then_inc(sem, 1)

        E["prog"].append(emit)
        self._record(tok, reads, writes, eng)
        return tok

    def dma(self, q, out, in_, reads=(), writes=()):
        if self.stopped:
            return None
        E = self.eng[q]
        Dq = self.dq[q]
        waits = self._waits(q, reads, writes)
        n = Dq["n"]
        Dq["n"] += 1
        s = Dq["sems"][n % self.KQ]
        val = 16 * (n // self.KQ + 1)
        key = f"dq_{q}_{n % self.KQ}"
        if n >= self.KQ and E["waited"].get(key, 0) < val - 16:
            E["waited"][key] = val - 16
            waits.append((s, val - 16))
        tok = (s, val, key, "dma")

        def emit(h, waits=waits, s=s, out=out, in_=in_):
            for ss, v in waits:
                h.wait_ge(ss, v)
            h.dma_start(out=out, in_=in_).then_inc(s, 16)

        E["prog"].append(emit)
        self.dma_uid += 1
        self._record(tok, reads, writes, "dma%d" % self.dma_uid)
        return tok

    def barrier(self):
        if self.stopped:
            return
        self.nphase += 1
        if self.nphase >= self.max_phase:
            self._barrier()
            self.stopped = True
            return
        self._barrier()

    def _barrier(self):
        targets = []
        for name in self.ENGS:
            X = self.eng[name]
            if X["n"] > 0:
                targets.append((X["sem"], X["n"], "e_" + name))
        for q, Dq in self.dq.items():
            n = Dq["n"]
            for i in range(min(n, self.KQ)):
                cnt = (n - 1 - i) // self.KQ + 1
                targets.append((Dq["sems"][i], 16 * cnt, f"dq_{q}_{i}"))
        for name in self.ENGS:
            E = self.eng[name]
            waits = []
            for (sem, val, key) in targets:
                if key == "e_" + name and name == "pe":
                    continue
                if E["waited"].get(key, 0) >= val:
                    continue
                E["waited"][key] = val
                waits.append((sem, val))

            def emit(h, waits=waits):
                for s, v in waits:
                    h.wait_ge(s, v)

            E["prog"].append(emit)

    def emit(self):
        nc = self.nc
        with nc.Block() as block:
            @block.tensor
            def _(h):
                for f in self.eng["pe"]["prog"]:
                    f(h)

            @block.scalar
            def _(h):
                for f in self.eng["act"]["prog"]:
                    f(h)

            @block.vector
            def _(h):
                for f in self.eng["dve"]["prog"]:
                    f(h)

            @block.gpsimd
            def _(h):
                for f in self.eng["pool"]["prog"]:
                    f(h)

            @block.sync
            def _(h):
                for f in self.eng["sp"]["prog"]:
                    f(h)


class Tile:
    def __init__(self, t):
        self.t = t
        self.res = Res()

    def __getitem__(self, k):
        return self.t[k]


class Arena:
    def __init__(self, nc, limit):
        self.nc = nc
        self.off = 16384
        self.limit = limit
        self.cnt = 0
        self.base = 16384

    def reset(self, to=None):
        self.off = self.base if to is None else to

    def alloc(self, shape, dtype):
        nbytes = int(np.prod(shape[1:])) * (4 if dtype == F32 else 2)
        nbytes = (nbytes + 63) // 64 * 64
        assert self.off + nbytes <= self.limit, (self.off, nbytes, self.limit)
        self.cnt += 1
        t = self.nc.alloc_sbuf_tensor_at("sb%d" % self.cnt, list(shape), dtype, offset=self.off)
        self.off += nbytes
        return Tile(t)


class K:
    pass


def build(T, L, TSPAN=1024, max_phase=10 ** 9):
    nc = bass.Bass("TRN2", target_bir_lowering=False)
    st = contextlib.ExitStack()
    P = Prog(nc, st)
    P.max_phase = max_phase
    TS = min(TSPAN, T)
    NSP = T // TS

    def din(name, shape, dt=F32):
        return nc.dram_tensor(name, list(shape), dt, kind="ExternalInput").ap()

    def dscr(name, shape, dt):
        import os
        kind = "ExternalOutput" if os.environ.get("KDEBUG") else "Internal"
        return nc.dram_tensor(name, list(shape), dt, kind=kind).ap()

    x_in = din("x", [T, D])
    w_in = din("w_in", [L, D, NIN])
    w_glu = din("w_glu", [L, W, W])
    w_up = din("w_up", [L, 3 * W, D])
    w_out = din("w_out", [L, D, D])
    w_m1 = din("w_m1", [L, D, HID])
    w_m2 = din("w_m2", [L, HID, D])
    lnp = din("lnp", [L, 4, D])
    s5p = din("s5p", [L, 128, 3, 32])
    s5b = din("s5b", [L, 2, 128, 32, 128])
    s5c = din("s5c", [L, 2, 128, 32, 128])
    s5v = din("s5v", [L, 128, 2, 8])
    glaw = din("glaw", [L, 16, 512])
    glav = din("glav", [L, 128, 6])
    hlb = din("hlb", [128, 8, DEPTH])
    hnw = din("hnw", [128, L])
    out = nc.dram_tensor("out", [T, D], F32, kind="ExternalOutput").ap()

    xF = dscr("xF", [T, D], F32)
    zF = dscr("zF", [T, D], F32)
    xT = dscr("xT", [D, T], BF16)
    hU = dscr("hU", [W, T], BF16)
    hQB = dscr("hQB", [512, T], BF16)
    hKB = dscr("hKB", [512, T], BF16)
    hGL = dscr("hGL", [16, T], F32)
    hGB = dscr("hGB", [W, T], BF16)
    hVB = dscr("hVB", [T, W], BF16)
    hQC = dscr("hQC", [W, T], BF16)
    hFC = dscr("hFC", [W, T], F32)
    hIC = dscr("hIC", [T, W], BF16)
    hGC = dscr("hGC", [W, T], BF16)
    hMG = dscr("hMG", [3 * D, T], BF16)
    zT = dscr("zT", [W, T], BF16)
    yT = dscr("yT", [3 * W, T], BF16)
    mT = dscr("mT", [D, T], BF16)
    hT = dscr("hT", [HID, T], BF16)

    SB_LIMIT = 192 * 1024
    ar = Arena(nc, SB_LIMIT)
    psum = nc.alloc_psum_tensor("psum", [128, 8, 512], F32)
    banks = [Res() for _ in range(8)]
    bank_i = [0]

    def next_bank():
        b = bank_i[0] % 8
        bank_i[0] += 1
        return b

    sub_i = {}

    def next_bank_in(lo, hi):
        i = sub_i.get((lo, hi), 0)
        sub_i[(lo, hi)] = i + 1
        return lo + i % (hi - lo)

    ident = ar.alloc([128, 128], BF16)
    ones_f = ar.alloc([128, 128], F32)
    mask = ar.alloc([128, 128], F32)
    ones_bf = ar.alloc([128, 128], BF16)
    ones_bf2 = ar.alloc([128, 128], BF16)
    cneg_pi = ar.alloc([128, 1], F32)
    ceps5 = ar.alloc([128, 1], F32)
    ceps6 = ar.alloc([128, 1], F32)
    lb_all = ar.alloc([128, 8, DEPTH], F32)
    hnw_t = ar.alloc([128, L], F32)
    ar.base = ar.off

    def c_setup():
        P.op("pool", lambda h: h.memset(ones_f[:], 1.0), writes=[ones_f.res])
        P.op("pool", lambda h: h.memset(cneg_pi[:], -math.pi), writes=[cneg_pi.res])
        P.op("pool", lambda h: h.memset(ceps5[:], 1e-5), writes=[ceps5.res])
        P.op("pool", lambda h: h.memset(ceps6[:], 1e-6), writes=[ceps6.res])
        P.op("pool", lambda h: h.memset(ones_bf[:], 1.0 / 128.0), writes=[ones_bf.res])
        P.op("pool", lambda h: h.memset(ones_bf2[:], 1.0 / 256.0), writes=[ones_bf2.res])
        P.op("pool", lambda h: h.affine_select(out=ident[:], in_=ones_f[:], pattern=[[-1, 128]], base=0,
                                               channel_multiplier=1, compare_op=ALU.is_equal, fill=0.0),
             reads=[ones_f.res], writes=[ident.res])
        P.op("pool", lambda h: h.affine_select(out=mask[:], in_=ones_f[:], pattern=[[1, 128]], base=0,
                                               channel_multiplier=-1, compare_op=ALU.is_ge, fill=0.0),
             reads=[ones_f.res], writes=[mask.res])
        P.op("pool", lambda h: h.memset(mask[0:64, 64:128], 0.0), writes=[mask.res])
        P.dma("sp", lb_all[:], hlb, writes=[lb_all.res])
        P.dma("sp", hnw_t[:], hnw, writes=[hnw_t.res])
        P.op("act", lambda h: h.activation(out=lb_all[:], in_=lb_all[:], func=AF.Exp),
             reads=[lb_all.res], writes=[lb_all.res])
        ssum = ar.alloc([128, 8, 1], F32)
        P.op("dve", lambda h: h.tensor_add(out=ssum[:], in0=lb_all[:, :, 0:1], in1=lb_all[:, :, 1:2]),
             reads=[lb_all.res], writes=[ssum.res])
        for j in (2, 3):
            P.op("dve", lambda h, j=j: h.tensor_add(out=ssum[:], in0=ssum[:], in1=lb_all[:, :, j:j + 1]),
                 reads=[lb_all.res, ssum.res], writes=[ssum.res])
        P.op("dve", lambda h: h.reciprocal(out=ssum[:], in_=ssum[:]), reads=[ssum.res], writes=[ssum.res])
        P.op("dve", lambda h: h.tensor_tensor(out=lb_all[:], in0=lb_all[:], in1=ssum[:].to_broadcast([128, 8, DEPTH]),
                                              op=ALU.mult), reads=[lb_all.res, ssum.res], writes=[lb_all.res])
        P.op("dve", lambda h: h.tensor_add(out=lb_all[:, :, 2:3], in0=lb_all[:, :, 2:3], in1=lb_all[:, :, 1:2]),
             reads=[lb_all.res], writes=[lb_all.res])
        P.op("dve", lambda h: h.tensor_add(out=lb_all[:, :, 3:4], in0=lb_all[:, :, 3:4], in1=lb_all[:, :, 2:3]),
             reads=[lb_all.res], writes=[lb_all.res])
        P.op("dve", lambda h: h.memset(lb_all[:, :, 0:1], 0.0), writes=[lb_all.res])
        P.barrier()

    def ps_ap(b, rows=128, n=512):
        return psum[0:rows, b, 0:n]

    class Rot:
        def __init__(self, tiles):
            self.tiles = tiles
            self.i = 0

        def next(self):
            t = self.tiles[self.i % len(self.tiles)]
            self.i += 1
            return t

    def mm_group(bank, out_ap, pairs, extra_reads, start=True, stop=True):
        def fn(h):
            ins = None
            n = len(pairs)
            for i, (l, r) in enumerate(pairs):
                ins = h.matmul(out_ap, l, r, start=(start and i == 0), stop=(stop and i == n - 1))
            return ins
        return P.op("pe", fn, reads=extra_reads, writes=[banks[bank]])

    def gemm_phase(act_src, Kd, w_src, cols, mode, epi, TP, kgroups=None, kbp=None):
        ar.reset()
        KB = Kd // 128
        nparts = (1 if KB <= 24 else KB // 16) if kbp is None else KB // kbp
        KBP = KB // nparts
        actA = ar.alloc([128, KB * TP], BF16)
        wb = Rot([ar.alloc([128, KBP, 512], BF16) for _ in range(2)])
        aux = dict(
            sf=Rot([ar.alloc([128, 512], F32) for _ in range(4)]),
            sb=Rot([ar.alloc([128, 512], BF16) for _ in range(4)]),
            xf=Rot([ar.alloc([128, 3 if mode == "fm" else 1, 512], F32) for _ in range(2)]),
            gb=Rot([ar.alloc([128, 3 if mode == "fm" else 1, 512 if mode == "fm" else 8], BF16) for _ in range(2)]),
        )
        NQ = TP // 512
        actq = [Res() for _ in range(NQ)]
        if kgroups is None:
            kgroups = [(0, KBP)]
        actv = actA[:].rearrange("p (k t) -> p k t", t=TP)
        for tp in range(T // TP):
            for q in range(NQ):
                P.dma("sp", actv[:, :, q * 512:(q + 1) * 512],
                      act_src[:, tp * TP + q * 512:tp * TP + (q + 1) * 512].rearrange("(k p) t -> p k t", p=128),
                      writes=[actq[q]])
            loads = [(ci, pa) for ci in range(len(cols)) for pa in range(nparts)]
            wtiles = {}

            def load(idx):
                ci, pa = loads[idx]
                off, width, tag = cols[ci]
                wt = wb.next()
                P.dma("pool", wt[:, :, 0:width],
                      w_src[pa * KBP * 128:(pa + 1) * KBP * 128, off:off + width].rearrange("(k p) n -> p k n", p=128),
                      writes=[wt.res])
                wtiles[idx] = wt

            load(0)
            for idx in range(len(loads)):
                if idx + 1 < len(loads):
                    load(idx + 1)
                ci, pa = loads[idx]
                off, width, tag = cols[ci]
                wt = wtiles.pop(idx)
                if mode == "fm":
                    assert nparts == 1
                    for nb in range((width + 127) // 128):
                        rows = min(128, width - nb * 128)
                        for tt in range(TP // 512):
                            bl = []
                            for (k0, k1) in kgroups:
                                b = next_bank()
                                pairs = [(wt[:, kb, nb * 128:nb * 128 + rows], actv[:, kb, tt * 512:(tt + 1) * 512])
                                         for kb in range(k0, k1)]
                                mm_group(b, ps_ap(b, rows), pairs, [wt.res, actq[tt]])
                                bl.append(b)
                            epi(bl, tag, off + nb * 128, rows, tp * TP + tt * 512, aux)
                else:
                    nt = TP // 128
                    if nparts == 1:
                        for t1 in range(nt):
                            b = next_bank()
                            pairs = [(actv[:, kb, t1 * 128:(t1 + 1) * 128], wt[:, kb, 0:width]) for kb in range(KBP)]
                            mm_group(b, ps_ap(b, 128, width), pairs, [wt.res, actq[t1 // 4]])
                            epi([b], tag, off, width, tp * TP + t1 * 128, aux)
                    else:
                        assert nt <= 8
                        if pa == 0:
                            cur = [next_bank() for _ in range(nt)]
                            wtiles["cur"] = cur
                        cur = wtiles["cur"]
                        for t1 in range(nt):
                            b = cur[t1]
                            pairs = [(actv[:, pa * KBP + kb, t1 * 128:(t1 + 1) * 128], wt[:, kb, 0:width])
                                     for kb in range(KBP)]
                            mm_group(b, ps_ap(b, 128, width), pairs, [wt.res, actq[t1 // 4]],
                                     start=(pa == 0), stop=(pa == nparts - 1))
                            if pa == nparts - 1:
                                epi([b], tag, off, width, tp * TP + t1 * 128, aux)
        P.barrier()

    def evac_store(b, rows, n, func, scale, dt, dest, aux, bias=None, eng=None):
        stg = (aux["sb"] if dt == BF16 else aux["sf"]).next()
        src = ps_ap(b, rows, n)
        if func is None and (eng or "dve") == "dve":
            P.op("dve", lambda h: h.tensor_scalar(out=stg[0:rows, 0:n], in0=src, scalar1=float(scale), scalar2=None,
                                                  op0=ALU.mult), reads=[banks[b]], writes=[stg.res])
        else:
            f = AF.Copy if func is None else func
            if bias is None:
                P.op("act", lambda h: h.activation(out=stg[0:rows, 0:n], in_=src, func=f, scale=float(scale)),
                     reads=[banks[b]], writes=[stg.res])
            else:
                P.op("act", lambda h: h.activation(out=stg[0:rows, 0:n], in_=src, func=f, scale=float(scale),
                                                   bias=bias), reads=[banks[b]], writes=[stg.res])
        P.dma("sp", dest, stg[0:rows, 0:n], reads=[stg.res])

    def ln_phase(src, norm, l, which, dstF, dstOut):
        ar.reset()
        if norm:
            g_bc = ar.alloc([128, D], F32)
            b_bc = ar.alloc([128, D], F32)
            P.dma("sp", g_bc[:], lnp[l, 2 * which, :].partition_broadcast(128), writes=[g_bc.res])
            P.dma("sp", b_bc[:], lnp[l, 2 * which + 1, :].partition_broadcast(128), writes=[b_bc.res])
        zt = Rot([ar.alloc([128, D], F32) for _ in range(4)])
        jk = Rot([ar.alloc([128, D], F32) for _ in range(2)])
        xb = Rot([ar.alloc([128, D], BF16) for _ in range(3)])
        xtt = Rot([ar.alloc([128, 16, 128], BF16) for _ in range(3)])
        stat = Rot([ar.alloc([128, 8], F32) for _ in range(4)])
        for tt in range(T // 128):
            z = zt.next()
            P.dma("sp", z[:], src[tt * 128:(tt + 1) * 128, :], writes=[z.res])
            xbt = xb.next()
            if norm:
                s = stat.next()
                j = jk.next()
                P.op("act", lambda h: h.activation(out=j[:], in_=z[:], func=AF.Copy, accum_out=s[:, 0:1]),
                     reads=[z.res], writes=[j.res, s.res])
                P.op("act", lambda h: h.activation(out=j[:], in_=z[:], func=AF.Square, accum_out=s[:, 1:2]),
                     reads=[z.res], writes=[j.res, s.res])
                P.op("dve", lambda h: h.tensor_scalar(out=s[:, 2:3], in0=s[:, 0:1], scalar1=1.0 / D, scalar2=None,
                                                      op0=ALU.mult), reads=[s.res], writes=[s.res])
                P.op("dve", lambda h: h.tensor_tensor(out=s[:, 3:4], in0=s[:, 2:3], in1=s[:, 2:3], op=ALU.mult),
                     reads=[s.res], writes=[s.res])
                P.op("dve", lambda h: h.scalar_tensor_tensor(out=s[:, 4:5], in0=s[:, 1:2], scalar=1.0 / D,
                                                             in1=s[:, 3:4], op0=ALU.mult, op1=ALU.subtract),
                     reads=[s.res], writes=[s.res])
                P.op("act", lambda h: h.activation(out=s[:, 5:6], in_=s[:, 4:5], func=AF.Ln, bias=ceps5[:, 0:1]),
                     reads=[s.res, ceps5.res], writes=[s.res])
                P.op("act", lambda h: h.activation(out=s[:, 5:6], in_=s[:, 5:6], func=AF.Exp, scale=-0.5),
                     reads=[s.res], writes=[s.res])
                P.op("dve", lambda h: h.tensor_scalar(out=z[:], in0=z[:], scalar1=s[:, 2:3], scalar2=s[:, 5:6],
                                                      op0=ALU.subtract, op1=ALU.mult),
                     reads=[z.res, s.res], writes=[z.res])
                P.op("dve", lambda h: h.tensor_tensor(out=z[:], in0=z[:], in1=g_bc[:], op=ALU.mult),
                     reads=[z.res, g_bc.res], writes=[z.res])
                P.op("dve", lambda h: h.tensor_tensor(out=z[:], in0=z[:], in1=b_bc[:], op=ALU.add),
                     reads=[z.res, b_bc.res], writes=[z.res])
                P.dma("sp", dstF[tt * 128:(tt + 1) * 128, :], z[:], reads=[z.res])
                if dstOut is not None:
                    P.dma("sp", dstOut[tt * 128:(tt + 1) * 128, :], z[:], reads=[z.res])
            P.op("act", lambda h: h.activation(out=xbt[:], in_=z[:], func=AF.Copy), reads=[z.res], writes=[xbt.res])
            xt_ = xtt.next()
            for half in range(2):
                b = next_bank()
                pv = psum[:, b, :].bitcast(BF16).rearrange("p (k t) -> p k t", t=128)

                def fn(h, b=b, pv=pv, half=half, xbt=xbt):
                    ins = None
                    for k in range(8):
                        kk = half * 8 + k
                        ins = h.transpose(pv[:, k, :], xbt[:, kk * 128:(kk + 1) * 128], ident[:])
                    return ins
                P.op("pe", fn, reads=[xbt.res, ident.res], writes=[banks[b]])
                eng = "dve" if half == 0 else "act"
                if eng == "dve":
                    P.op("dve", lambda h, pv=pv, half=half, xt_=xt_: h.tensor_copy(out=xt_[:, half * 8:half * 8 + 8, :], in_=pv),
                         reads=[banks[b]], writes=[xt_.res])
                else:
                    P.op("act", lambda h, pv=pv, half=half, xt_=xt_: h.activation(out=xt_[:, half * 8:half * 8 + 8, :], in_=pv, func=AF.Copy),
                         reads=[banks[b]], writes=[xt_.res])
            P.dma("sp", xT[:, tt * 128:(tt + 1) * 128].rearrange("(k p) t -> p k t", p=128), xt_[:], reads=[xt_.res])
        P.barrier()

    def gla_like(l, kind):
        ar.reset()
        NC = TS // 64
        NPC = TS // 128
        nh = 4 if kind == "gla" else 8
        nvb = 2 if kind == "gla" else 1
        DV = 128 * nvb
        gs = (-1.0 / 16.0) if kind == "gla" else 1.0
        mask0 = ar.alloc([128, TS], F32)
        P.op("pool", lambda h: h.memset(mask0[:], 1.0), writes=[mask0.res])
        P.op("pool", lambda h: h.memset(mask0[:].rearrange("p (c i) -> p c i", i=64)[:, :, 0:1], 0.0), writes=[mask0.res])
        gsets = []
        for _ in range(2):
            gsets.append((ar.alloc([128, TS], F32), ar.alloc([128, TS], F32), ar.alloc([128, TS], F32), ar.alloc([128, TS], F32),
                          ar.alloc([128, TS], F32), ar.alloc([128, TS], BF16), ar.alloc([128, TS], BF16), ar.alloc([128, TS], BF16),
                          ar.alloc([128, TS], BF16), ar.alloc([128, TS], BF16), ar.alloc([128, NC], F32),
                          ar.alloc([128, NPC, DV], BF16), ar.alloc([128, NPC, 128], BF16)))
        git = [0]
        KV = ar.alloc([128, NC, DV], F32)
        Sall = ar.alloc([128, NC + 1, DV], F32)
        Sbf = ar.alloc([128, NC, DV], BF16)
        AT = Rot([ar.alloc([128, 128], BF16) for _ in range(3)])
        og = Rot([ar.alloc([128, nvb, 512], F32) for _ in range(2)])
        sq = Rot([ar.alloc([128, nvb, 512], BF16) for _ in range(2)])
        gt = Rot([ar.alloc([128, nvb, 512], BF16) for _ in range(2)])
        rstd = Rot([ar.alloc([128, 512], F32) for _ in range(2)])
        yb = Rot([ar.alloc([128, nvb, 512], BF16) for _ in range(2)])
        glt = ar.alloc([16, TS], F32)
        wg = ar.alloc([16, 512], F32)
        gv = ar.alloc([128, 6], F32)
        nbg = ar.alloc([128, 4], F32)
        if kind == "gla":
            P.dma("sp", wg[:], glaw[l], writes=[wg.res])
            P.dma("sp", gv[:], glav[l], writes=[gv.res])
            P.op("dve", lambda h: h.tensor_scalar(out=nbg[:], in0=gv[:, 0:4], scalar1=-1.0, scalar2=None, op0=ALU.mult),
                 reads=[gv.res], writes=[nbg.res])
        def front(hh, sp, itn):
            fr, gg, G, tmp, tmp2, qb, kb_, qt, kt, kpT, egl, vtok, kptok = gsets[itn % 2]
            t0 = sp * TS
            yield
            if kind == "gla":
                if hh == 0:
                    pass
                P.dma("sp", glt[:], hGL[:, t0:t0 + TS], writes=[glt.res])
                for tt in range(TS // 512):
                    b = next_bank_in(7, 8)
                    mm_group(b, ps_ap(b), [(wg[:, hh * 128:(hh + 1) * 128], glt[:, tt * 512:(tt + 1) * 512])],
                             [wg.res, glt.res])
                    P.op("act", lambda h, b=b, tt=tt: h.activation(out=fr[:, tt * 512:(tt + 1) * 512], in_=ps_ap(b),
                                                                   func=AF.Exp, scale=-1.0, bias=nbg[:, hh:hh + 1]),
                         reads=[banks[b], nbg.res], writes=[fr.res])
                P.op("act", lambda h: h.activation(out=gg[:], in_=fr[:], func=AF.Ln, bias=1.0, scale=1.0),
                     reads=[fr.res], writes=[gg.res])
                P.dma("sp", qb[:], hQB[hh * 128:(hh + 1) * 128, t0:t0 + TS], writes=[qb.res])
                P.dma("sp", kb_[:], hKB[hh * 128:(hh + 1) * 128, t0:t0 + TS], writes=[kb_.res])
                kk = kb_
                P.dma("sp", vtok[:], hVB[t0:t0 + TS, hh * DV:(hh + 1) * DV].rearrange("(c p) v -> p c v", p=128),
                      writes=[vtok.res])
            else:
                P.dma("sp", fr[:], hFC[hh * 128:(hh + 1) * 128, t0:t0 + TS], writes=[fr.res])
                P.op("act", lambda h: h.activation(out=fr[:], in_=fr[:], func=AF.Sigmoid), reads=[fr.res], writes=[fr.res])
                oml = ar_small["oml"]
                P.op("dve", lambda h: h.tensor_scalar(out=fr[:], in0=fr[:], scalar1=oml[:, hh:hh + 1],
                                                      scalar2=lb_all[:, hh, l:l + 1], op0=ALU.mult, op1=ALU.add),
                     reads=[fr.res, oml.res, lb_all.res], writes=[fr.res])
                P.op("act", lambda h: h.activation(out=gg[:], in_=fr[:], func=AF.Ln), reads=[fr.res], writes=[gg.res])
                P.op("dve", lambda h: h.tensor_scalar(out=fr[:], in0=fr[:], scalar1=-1.0, scalar2=1.0,
                                                      op0=ALU.mult, op1=ALU.add), reads=[fr.res], writes=[fr.res])
                kk = fr
                P.dma("sp", qb[:], hQC[hh * 128:(hh + 1) * 128, t0:t0 + TS], writes=[qb.res])
                P.dma("sp", vtok[:], hIC[t0:t0 + TS, hh * DV:(hh + 1) * DV].rearrange("(c p) v -> p c v", p=128),
                      writes=[vtok.res])
            yield
            P.op("dve", lambda h: h.tensor_tensor_scan(out=G[:], data0=mask0[:], data1=gg[:], initial=0.0,
                                                       op0=ALU.mult, op1=ALU.add),
                 reads=[mask0.res, gg.res], writes=[G.res])
            yield
            Gv = G[:].rearrange("p (c i) -> p c i", i=64)
            yield
            P.op("act", lambda h: h.activation(out=tmp[:], in_=G[:], func=AF.Exp, scale=gs), reads=[G.res], writes=[tmp.res])
            yield
            P.op("dve", lambda h: h.tensor_tensor(out=qt[:], in0=qb[:], in1=tmp[:], op=ALU.mult),
                 reads=[qb.res, tmp.res], writes=[qt.res])
            yield
            P.op("act", lambda h: h.activation(out=tmp2[:], in_=G[:], func=AF.Exp, scale=-gs), reads=[G.res], writes=[tmp2.res])
            yield
            P.op("dve", lambda h, kk=kk: h.tensor_tensor(out=kt[:], in0=kk[:], in1=tmp2[:], op=ALU.mult),
                 reads=[kk.res, tmp2.res], writes=[kt.res])
            yield
            P.op("dve", lambda h: h.tensor_tensor(out=tmp[:].rearrange("p (c i) -> p c i", i=64),
                                                  in0=Gv[:, :, 63:64].to_broadcast([128, NC, 64]), in1=Gv,
                                                  op=ALU.subtract), reads=[G.res, tmp.res, qt.res], writes=[tmp.res])
            yield
            P.op("act", lambda h: h.activation(out=tmp[:], in_=tmp[:], func=AF.Exp, scale=gs), reads=[tmp.res], writes=[tmp.res])
            yield
            P.op("dve", lambda h, kk=kk: h.tensor_tensor(out=kpT[:], in0=kk[:], in1=tmp[:], op=ALU.mult),
                 reads=[kk.res, tmp.res], writes=[kpT.res])
            yield
            P.op("act", lambda h: h.activation(out=egl[:], in_=Gv[:, :, 63], func=AF.Exp, scale=gs),
                 reads=[G.res], writes=[egl.res])
            yield
            for g8 in range((NPC + 7) // 8):
                b = next_bank_in(7, 8)
                pv = psum[:, b, :].bitcast(BF16).rearrange("p (k t) -> p k t", t=128)
                n8 = min(8, NPC - g8 * 8)

                def fn(h, pv=pv, g8=g8, n8=n8):
                    ins = None
                    for k in range(n8):
                        pc = g8 * 8 + k
                        ins = h.transpose(pv[:, k, :], kpT[:, pc * 128:(pc + 1) * 128], ident[:])
                    return ins
                P.op("pe", fn, reads=[kpT.res, ident.res], writes=[banks[b]])
                P.op("act", lambda h, pv=pv, g8=g8, n8=n8: h.activation(out=kptok[:, g8 * 8:g8 * 8 + n8, :], in_=pv[:, 0:n8, :],
                                                                        func=AF.Copy), reads=[banks[b]], writes=[kptok.res])
            yield
        def back(hh, sp, itn):
            fr, gg, G, tmp, tmp2, qb, kb_, qt, kt, kpT, egl, vtok, kptok = gsets[itn % 2]
            t0 = sp * TS
            kk = kb_ if kind == "gla" else fr
            if sp == 0:
                P.op("dve", lambda h: h.memset(Sall[:, 0, :], 0.0), writes=[Sall.res])
            if sp > 0:
                P.op("dve", lambda h: h.tensor_copy(out=Sall[:, 0, :], in_=Sall[:, NC, :]),
                     reads=[Sall.res], writes=[Sall.res])

            yield
            per_bank = 512 // DV
            yield
            KVv = KV[:].rearrange("p (c two) v -> p c two v", two=2)
            yield
            for p0 in range(0, NPC, per_bank):
                bA, bB = next_bank_in(4, 7), next_bank_in(4, 7)
                pA = psum[:, bA, :].rearrange("p (c v) -> p c v", v=DV)
                pB = psum[:, bB, :].rearrange("p (c v) -> p c v", v=DV)

                def fn(h, p0=p0, pA=pA, pB=pB):
                    ins = None
                    for i in range(per_bank):
                        pc = p0 + i
                        h.matmul(pA[:, i, :], kptok[0:64, pc, :], vtok[0:64, pc, :], start=True, stop=True)
                        ins = h.matmul(pB[:, i, :], kptok[64:128, pc, :], vtok[64:128, pc, :], start=True, stop=True)
                    return ins
                P.op("pe", fn, reads=[kptok.res, vtok.res], writes=[banks[bA], banks[bB]])
                P.op("dve", lambda h, p0=p0, pA=pA: h.tensor_copy(out=KVv[:, p0:p0 + per_bank, 0, :], in_=pA),
                     reads=[banks[bA]], writes=[KV.res])
                P.op("act", lambda h, p0=p0, pB=pB: h.activation(out=KVv[:, p0:p0 + per_bank, 1, :], in_=pB, func=AF.Copy),
                     reads=[banks[bB]], writes=[KV.res])
            yield
            for c in range(NC):
                P.op("dve", lambda h, c=c: h.scalar_tensor_tensor(out=Sall[:, c + 1, :], in0=Sall[:, c, :], scalar=egl[:, c:c + 1],
                                                                 in1=KV[:, c, :], op0=ALU.mult, op1=ALU.add),
                     reads=[egl.res, KV.res, Sall.res], writes=[Sall.res])
            yield
            P.op("act", lambda h: h.activation(out=Sbf[:], in_=Sall[:, 0:NC, :], func=AF.Copy),
                 reads=[Sall.res], writes=[Sbf.res])
            yield
            for tt in range(TS // 512):
                ob = [next_bank_in(0, 4) for _ in range(nvb)]
                for p4 in range(4):
                    pc = tt * 4 + p4
                    bs = next_bank_in(4, 7)
                    mm_group(bs, psum[:, bs, 0:128], [(kt[:, pc * 128:(pc + 1) * 128], qt[:, pc * 128:(pc + 1) * 128])],
                             [kt.res, qt.res])
                    at = AT.next()
                    P.op("dve", lambda h, bs=bs, at=at: h.tensor_tensor(out=at[:], in0=psum[:, bs, 0:128], in1=mask[:], op=ALU.mult),
                         reads=[banks[bs], mask.res], writes=[at.res])
                    for vb in range(nvb):
                        def fn(h, vb=vb, pc=pc, p4=p4, at=at, ob=ob):
                            o_ap = psum[:, ob[vb], p4 * 128:(p4 + 1) * 128]
                            h.matmul(o_ap, vtok[:, pc, vb * 128:(vb + 1) * 128], at[:], start=True, stop=False)
                            h.matmul(o_ap[:, 0:64], Sbf[:, 2 * pc, vb * 128:(vb + 1) * 128], qt[:, pc * 128:pc * 128 + 64],
                                     start=False, stop=False)
                            return h.matmul(o_ap[:, 64:128], Sbf[:, 2 * pc + 1, vb * 128:(vb + 1) * 128],
                                            qt[:, pc * 128 + 64:pc * 128 + 128], start=False, stop=True)
                        P.op("pe", fn, reads=[vtok.res, at.res, Sbf.res, qt.res], writes=[banks[ob[vb]]])
                o_t = og.next(); s_t = sq.next(); g_t = gt.next(); r_t = rstd.next(); y_t = yb.next()
                gsrc = hGB if kind == "gla" else hGC
                ybase = W if kind == "gla" else 2 * W
                tok = t0 + tt * 512
                P.dma("act", g_t[:], gsrc[hh * DV:(hh + 1) * DV, tok:tok + 512].rearrange("(b p) t -> p b t", p=128),
                      writes=[g_t.res])
                for vb in range(nvb):
                    if kind == "gla":
                        P.op("act", lambda h, vb=vb, o_t=o_t, ob=ob: h.activation(out=o_t[:, vb, :], in_=ps_ap(ob[vb]), func=AF.Copy),
                             reads=[banks[ob[vb]]], writes=[o_t.res])
                    else:
                        P.op("dve", lambda h, vb=vb, o_t=o_t, ob=ob, g_t=g_t: h.tensor_tensor(out=o_t[:, vb, :], in0=ps_ap(ob[vb]),
                                                                                            in1=g_t[:, vb, :], op=ALU.mult),
                             reads=[banks[ob[vb]], g_t.res], writes=[o_t.res])
                P.op("dve", lambda h, o_t=o_t, s_t=s_t: h.tensor_tensor(out=s_t[:], in0=o_t[:], in1=o_t[:], op=ALU.mult),
                     reads=[o_t.res], writes=[s_t.res])
                br = next_bank_in(4, 7)
                onesm = ones_bf if nvb == 1 else ones_bf2
                mm_group(br, ps_ap(br), [(onesm[:], s_t[:, vb, :]) for vb in range(nvb)], [onesm.res, s_t.res])
                P.op("act", lambda h, br=br, r_t=r_t: h.activation(out=r_t[:], in_=ps_ap(br), func=AF.Ln, bias=ceps6[:, 0:1]),
                     reads=[banks[br], ceps6.res], writes=[r_t.res])
                P.op("act", lambda h, r_t=r_t: h.activation(out=r_t[:], in_=r_t[:], func=AF.Exp, scale=-0.5),
                     reads=[r_t.res], writes=[r_t.res])
                for vb in range(nvb):
                    wcol = gv[:, 4 + vb:5 + vb] if kind == "gla" else hnw_t[:, l:l + 1]
                    wres = gv.res if kind == "gla" else hnw_t.res
                    if kind == "gla":
                        P.op("dve", lambda h, vb=vb, o_t=o_t, r_t=r_t, wcol=wcol: h.scalar_tensor_tensor(
                            out=o_t[:, vb, :], in0=o_t[:, vb, :], scalar=wcol, in1=r_t[:], op0=ALU.mult, op1=ALU.mult),
                            reads=[o_t.res, r_t.res, wres], writes=[o_t.res])
                        P.op("dve", lambda h, vb=vb, o_t=o_t, y_t=y_t, g_t=g_t: h.tensor_tensor(out=y_t[:, vb, :], in0=o_t[:, vb, :],
                                                                                               in1=g_t[:, vb, :], op=ALU.mult),
                             reads=[o_t.res, g_t.res], writes=[y_t.res])
                    else:
                        P.op("dve", lambda h, vb=vb, o_t=o_t, r_t=r_t, wcol=wcol, y_t=y_t: h.scalar_tensor_tensor(
                            out=y_t[:, vb, :], in0=o_t[:, vb, :], scalar=wcol, in1=r_t[:], op0=ALU.mult, op1=ALU.mult),
                            reads=[o_t.res, r_t.res, wres], writes=[y_t.res])
                P.dma("sp", yT[ybase + hh * DV:ybase + (hh + 1) * DV, tok:tok + 512].rearrange("(b p) t -> p b t", p=128),
                      y_t[:], reads=[y_t.res])
            yield
        its = [(hh, sp) for hh in range(nh) for sp in range(NSP)]

        def drive(gens):
            live = list(gens)
            while live:
                for g_ in list(live):
                    try:
                        next(g_)
                    except StopIteration:
                        live.remove(g_)
        drive([front(its[0][0], its[0][1], 0)])
        for n_, (hh_, sp_) in enumerate(its):
            gens = [back(hh_, sp_, n_)]
            if n_ + 1 < len(its):
                gens.append(front(its[n_ + 1][0], its[n_ + 1][1], n_ + 1))
            drive(gens)
        P.barrier()

    ar_small = {}

    def s5_phase(l):
        ar.reset()
        prm = ar.alloc([128, 3, 32], F32)
        P.dma("sp", prm[:], s5p[l], writes=[prm.res])
        sv = ar.alloc([128, 2, 8], F32)
        P.dma("sp", sv[:], s5v[l], writes=[sv.res])
        Bre = ar.alloc([128, 32, 128], BF16)
        Bim = ar.alloc([128, 32, 128], BF16)
        P.dma("pool", Bre[:], s5b[l, 0], writes=[Bre.res])
        P.dma("pool", Bim[:], s5b[l, 1], writes=[Bim.res])
        Cre = ar.alloc([128, 32, 128], BF16)
        Cim = ar.alloc([128, 32, 128], BF16)
        sm = {n: ar.alloc([128, 32], F32) for n in
              ("lr", "dt", "r", "th", "thn", "a", "a2", "sn", "cs", "are", "aim", "den", "fre", "fim", "t1", "t2")}
        mark = ar.off
        c0 = ar.alloc([128, 32, 128], F32)
        c1 = ar.alloc([128, 32, 128], F32)
        c2 = ar.alloc([128, 32, 128], F32)
        P.dma("sp", c0[:], s5c[l, 0], writes=[c0.res])
        P.dma("sp", c1[:], s5c[l, 1], writes=[c1.res])

        def dv(fn, rd, wr):
            P.op("dve", fn, reads=[x.res for x in rd], writes=[x.res for x in wr])

        def ac(fn, rd, wr):
            P.op("act", fn, reads=[x.res for x in rd], writes=[x.res for x in wr])
        s = sm
        dv(lambda h: h.tensor_scalar_min(out=s["lr"][:], in0=prm[:, 0, :], scalar1=-1e-4), [prm], [s["lr"]])
        ac(lambda h: h.activation(out=s["dt"][:], in_=prm[:, 2, :], func=AF.Exp), [prm], [s["dt"]])
        dv(lambda h: h.tensor_tensor(out=s["t1"][:], in0=s["lr"][:], in1=s["dt"][:], op=ALU.mult), [s["lr"], s["dt"]], [s["t1"]])
        ac(lambda h: h.activation(out=s["r"][:], in_=s["t1"][:], func=AF.Exp), [s["t1"]], [s["r"]])
        dv(lambda h: h.tensor_tensor(out=s["th"][:], in0=prm[:, 1, :], in1=s["dt"][:], op=ALU.mult), [prm, s["dt"]], [s["th"]])

        I32 = mybir.dt.int32
        SIN_SCALE = 6.2831845

        def sincos(y, ki, fr, tq, f2, sn, cs):
            MAGIC = 12582912.0
            dv(lambda h: h.tensor_scalar(out=ki[:], in0=y[:], scalar1=MAGIC, scalar2=None, op0=ALU.add), [y], [ki])
            dv(lambda h: h.tensor_scalar(out=ki[:], in0=ki[:], scalar1=MAGIC, scalar2=None, op0=ALU.subtract), [ki], [ki])
            dv(lambda h: h.tensor_sub(out=fr[:], in0=y[:], in1=ki[:]), [y, ki], [fr])
            ac(lambda h: h.activation(out=sn[:], in_=fr[:], func=AF.Sin, scale=SIN_SCALE), [fr], [sn])
            ac(lambda h: h.activation(out=f2[:], in_=fr[:], func=AF.Sin, scale=0.5 * SIN_SCALE), [fr], [f2])
            dv(lambda h: h.tensor_tensor(out=tq[:], in0=f2[:], in1=f2[:], op=ALU.mult), [f2], [tq])
            dv(lambda h: h.tensor_scalar(out=cs[:], in0=tq[:], scalar1=-2.0, scalar2=1.0, op0=ALU.mult, op1=ALU.add), [tq], [cs])

        dv(lambda h: h.tensor_scalar(out=s["thn"][:], in0=s["th"][:], scalar1=1.0 / TWO_PI, scalar2=None, op0=ALU.mult),
           [s["th"]], [s["thn"]])
        sincos(s["thn"], s["a"], s["a2"], s["t1"], s["t2"], s["sn"], s["cs"])
        dv(lambda h: h.scalar_tensor_tensor(out=s["are"][:], in0=s["cs"][:], scalar=1.0, in1=s["r"][:], op0=ALU.mult, op1=ALU.mult),
           [s["cs"], s["r"]], [s["are"]])
        dv(lambda h: h.scalar_tensor_tensor(out=s["aim"][:], in0=s["sn"][:], scalar=1.0, in1=s["r"][:], op0=ALU.mult, op1=ALU.mult),
           [s["sn"], s["r"]], [s["aim"]])
        dv(lambda h: h.tensor_tensor(out=s["den"][:], in0=s["lr"][:], in1=s["lr"][:], op=ALU.mult), [s["lr"]], [s["den"]])
        dv(lambda h: h.tensor_tensor(out=s["t1"][:], in0=prm[:, 1, :], in1=prm[:, 1, :], op=ALU.mult), [prm], [s["t1"]])
        dv(lambda h: h.tensor_add(out=s["den"][:], in0=s["den"][:], in1=s["t1"][:]), [s["den"], s["t1"]], [s["den"]])
        dv(lambda h: h.reciprocal(out=s["den"][:], in_=s["den"][:]), [s["den"]], [s["den"]])
        dv(lambda h: h.tensor_scalar_add(out=s["t2"][:], in0=s["are"][:], scalar1=-1.0), [s["are"]], [s["t2"]])
        dv(lambda h: h.tensor_tensor(out=s["fre"][:], in0=s["t2"][:], in1=s["lr"][:], op=ALU.mult), [s["t2"], s["lr"]], [s["fre"]])
        dv(lambda h: h.tensor_tensor(out=s["t1"][:], in0=s["aim"][:], in1=prm[:, 1, :], op=ALU.mult), [s["aim"], prm], [s["t1"]])
        dv(lambda h: h.tensor_add(out=s["fre"][:], in0=s["fre"][:], in1=s["t1"][:]), [s["fre"], s["t1"]], [s["fre"]])
        dv(lambda h: h.tensor_tensor(out=s["fre"][:], in0=s["fre"][:], in1=s["den"][:], op=ALU.mult), [s["fre"], s["den"]], [s["fre"]])
        dv(lambda h: h.tensor_tensor(out=s["fim"][:], in0=s["aim"][:], in1=s["lr"][:], op=ALU.mult), [s["aim"], s["lr"]], [s["fim"]])
        dv(lambda h: h.tensor_tensor(out=s["t1"][:], in0=s["t2"][:], in1=prm[:, 1, :], op=ALU.mult), [s["t2"], prm], [s["t1"]])
        dv(lambda h: h.tensor_sub(out=s["fim"][:], in0=s["fim"][:], in1=s["t1"][:]), [s["fim"], s["t1"]], [s["fim"]])
        dv(lambda h: h.tensor_tensor(out=s["fim"][:], in0=s["fim"][:], in1=s["den"][:], op=ALU.mult), [s["fim"], s["den"]], [s["fim"]])
        bc = lambda t: t[:].unsqueeze(2).to_broadcast([128, 32, 128])
        dv(lambda h: h.tensor_tensor(out=c2[:], in0=c0[:], in1=bc(s["fre"]), op=ALU.mult), [c0, s["fre"]], [c2])
        P.op("dve", lambda h: h.tensor_tensor(out=Cre[:], in0=c1[:], in1=bc(s["fim"]), op=ALU.mult),
             reads=[c1.res, s["fim"].res], writes=[Cre.res])
        dv(lambda h: h.tensor_sub(out=Cre[:], in0=c2[:], in1=Cre[:]), [c2, Cre], [Cre])
        dv(lambda h: h.tensor_tensor(out=c2[:], in0=c0[:], in1=bc(s["fim"]), op=ALU.mult), [c0, s["fim"], Cre], [c2])
        P.op("dve", lambda h: h.tensor_tensor(out=c0[:], in0=c1[:], in1=bc(s["fre"]), op=ALU.mult),
             reads=[c1.res, s["fre"].res, c2.res], writes=[c0.res])
        dv(lambda h: h.scalar_tensor_tensor(out=Cim[:], in0=c2[:], scalar=-1.0, in1=c0[:], op0=ALU.mult, op1=ALU.subtract),
           [c2, c0], [Cim])
        P.barrier()
        ar.reset(mark)
        tidx = ar.alloc([128, TS], F32)
        P.op("pool", lambda h: h.iota(tidx[:], pattern=[[1, TS]], base=0, channel_multiplier=0,
                                      allow_small_or_imprecise_dtypes=True), writes=[tidx.res])
        tabA = [ar.alloc([128, TS], BF16) for _ in range(4)]
        tabB = [ar.alloc([128, TS], BF16) for _ in range(4)]
        scr = [ar.alloc([128, TS], F32) for _ in range(5)]
        wsets = []
        for _ in range(2):
            wsets.append(tuple(ar.alloc([128, TS], BF16) for _ in range(10)))
        kre, kim, k2re, k2im, t1, t2, sre, sim, t3, t4 = wsets[0]
        sit = [0]
        uall = ar.alloc([128, TS], BF16)
        yv = Rot([ar.alloc([128, 512], F32) for _ in range(2)])
        y2 = Rot([ar.alloc([128, 512], F32) for _ in range(2)])
        zo = Rot([ar.alloc([128, 512], BF16) for _ in range(2)])
        send = ar.alloc([128, 32, 4], F32)
        rbc = {}
        for fb in range(8):
            for j in range(4):
                sb = fb * 4 + j
                A, B = tabA[j], tabB[j]
                thn = s["thn"][:, sb:sb + 1]
                dv(lambda h, thn=thn: h.tensor_scalar(out=scr[0][:], in0=tidx[:], scalar1=thn, scalar2=None, op0=ALU.mult),
                   [tidx, s["thn"]], [scr[0]])
                sincos(scr[0], scr[1], scr[2], scr[3], scr[4], B, A)
            for sp in range(NSP):
                t0 = sp * TS
                P.dma("sp", uall[:], hU[fb * 128:(fb + 1) * 128, t0:t0 + TS], writes=[uall.res])
                yb_ = [next_bank_in(0, 4) for _ in range(TS // 512)]
                def it_gen(j):
                    sb = fb * 4 + j
                    yield
                    A, B = tabA[j], tabB[j]
                    yield
                    kre, kim, k2re, k2im, t1, t2, sre, sim, t3, t4 = wsets[j % 2]
                    yield
                    for tt in range(TS // 512):
                        sl = slice(tt * 512, (tt + 1) * 512)
                        b1 = next_bank_in(4, 8)
                        mm_group(b1, ps_ap(b1), [(Bre[:, sb, :], uall[:, sl])], [Bre.res, uall.res])
                        P.op("act", lambda h, b1=b1, sl=sl: h.activation(out=kre[:, sl], in_=ps_ap(b1), func=AF.Copy),
                             reads=[banks[b1]], writes=[kre.res])
                        b2 = next_bank_in(4, 8)
                        mm_group(b2, ps_ap(b2), [(Bim[:, sb, :], uall[:, sl])], [Bim.res, uall.res])
                        P.op("act", lambda h, b2=b2, sl=sl: h.activation(out=kim[:, sl], in_=ps_ap(b2), func=AF.Copy),
                             reads=[banks[b2]], writes=[kim.res])
                    yield
                    dv(lambda h, A=A: h.tensor_tensor(out=t1[:], in0=A[:], in1=kre[:], op=ALU.mult), [A, kre], [t1])
                    yield
                    P.op("dve", lambda h, B=B: h.tensor_tensor(out=t2[:], in0=B[:], in1=kim[:], op=ALU.mult),
                         reads=[B.res, kim.res], writes=[t2.res])
                    yield
                    P.op("dve", lambda h, A=A: h.tensor_tensor(out=t3[:], in0=A[:], in1=kim[:], op=ALU.mult),
                         reads=[A.res, kim.res], writes=[t3.res])
                    yield
                    dv(lambda h, B=B: h.tensor_tensor(out=t4[:], in0=B[:], in1=kre[:], op=ALU.mult), [B, kre], [t4])
                    yield
                    dv(lambda h: h.tensor_add(out=k2re[:], in0=t1[:], in1=t2[:]), [t1, t2], [k2re])
                    yield
                    dv(lambda h: h.tensor_sub(out=k2im[:], in0=t3[:], in1=t4[:]), [t3, t4], [k2im])
                    yield
                    yield
                    rb = s["r"][:, sb:sb + 1].to_broadcast([128, TS])
                    yield
                    if sp == 0:
                        i_re, i_im = 0.0, 0.0
                        rd_i = []
                    else:
                        se = send[:, sb, :]
                        dv(lambda h, se=se, A=A: h.tensor_tensor(out=se[:, 2:3], in0=A[:, 1:2], in1=se[:, 0:1], op=ALU.mult), [A, send], [send])
                        dv(lambda h, se=se, B=B: h.tensor_tensor(out=se[:, 3:4], in0=B[:, 1:2], in1=se[:, 1:2], op=ALU.mult), [B, send], [send])
                        dv(lambda h, se=se: h.tensor_sub(out=se[:, 2:3], in0=se[:, 2:3], in1=se[:, 3:4]), [send], [send])
                        dv(lambda h, se=se, A=A: h.tensor_tensor(out=se[:, 3:4], in0=A[:, 1:2], in1=se[:, 1:2], op=ALU.mult), [A, send], [send])
                        dv(lambda h, se=se, B=B: h.scalar_tensor_tensor(out=se[:, 3:4], in0=B[:, 1:2], scalar=se[:, 0:1], in1=se[:, 3:4],
                                                                       op0=ALU.mult, op1=ALU.add), [B, send], [send])
                        i_re, i_im = se[:, 2:3], se[:, 3:4]
                        rd_i = [send]
                    yield
                    dv(lambda h, rb=rb, i_re=i_re: h.tensor_tensor_scan(out=kre[:], data0=rb, data1=k2re[:], initial=i_re,
                                                                        op0=ALU.mult, op1=ALU.add), [s["r"], k2re, kre] + rd_i, [kre])
                    yield
                    dv(lambda h, rb=rb, i_im=i_im: h.tensor_tensor_scan(out=kim[:], data0=rb, data1=k2im[:], initial=i_im,
                                                                        op0=ALU.mult, op1=ALU.add), [s["r"], k2im, kim] + rd_i, [kim])
                    yield
                    dv(lambda h, A=A: h.tensor_tensor(out=t1[:], in0=A[:], in1=kre[:], op=ALU.mult), [A, kre], [t1])
                    yield
                    P.op("dve", lambda h, B=B: h.tensor_tensor(out=t2[:], in0=B[:], in1=kim[:], op=ALU.mult),
                         reads=[B.res, kim.res], writes=[t2.res])
                    yield
                    P.op("dve", lambda h, A=A: h.tensor_tensor(out=t3[:], in0=A[:], in1=kim[:], op=ALU.mult),
                         reads=[A.res, kim.res], writes=[t3.res])
                    yield
                    dv(lambda h, B=B: h.tensor_tensor(out=t4[:], in0=B[:], in1=kre[:], op=ALU.mult), [B, kre], [t4])
                    yield
                    dv(lambda h: h.tensor_sub(out=sre[:], in0=t1[:], in1=t2[:]), [t1, t2], [sre])
                    yield
                    dv(lambda h: h.tensor_add(out=sim[:], in0=t3[:], in1=t4[:]), [t3, t4], [sim])
                    yield
                    if NSP > 1:
                        dv(lambda h, sb=sb: h.tensor_sub(out=send[:, sb, 0:1], in0=t1[:, TS - 1:TS], in1=t2[:, TS - 1:TS]), [t1, t2], [send])
                        dv(lambda h, sb=sb: h.tensor_add(out=send[:, sb, 1:2], in0=t3[:, TS - 1:TS], in1=t4[:, TS - 1:TS]), [t3, t4], [send])
                    yield
                    for tt in range(TS // 512):
                        sl = slice(tt * 512, (tt + 1) * 512)
                        mm_group(yb_[tt], ps_ap(yb_[tt]), [(Cre[:, sb, :], sre[:, sl]), (Cim[:, sb, :], sim[:, sl])],
                                 [Cre.res, Cim.res, sre.res, sim.res], start=(j == 0), stop=(j == 3))
                    yield
                for jp in (0, 2):
                    gens = [it_gen(jp), it_gen(jp + 1)]
                    live = list(gens)
                    while live:
                        for g_ in list(live):
                            try:
                                next(g_)
                            except StopIteration:
                                live.remove(g_)
                for tt in range(TS // 512):
                    sl = slice(tt * 512, (tt + 1) * 512)
                    y_ = yv.next(); w_ = y2.next(); z_ = zo.next()
                    b = yb_[tt]
                    P.op("dve", lambda h, b=b, y_=y_, sl=sl, fb=fb: h.scalar_tensor_tensor(
                        out=y_[:], in0=uall[:, sl], scalar=sv[:, 0, fb:fb + 1], in1=ps_ap(b), op0=ALU.mult, op1=ALU.add),
                        reads=[uall.res, sv.res, banks[b]], writes=[y_.res])
                    P.op("dve", lambda h, y_=y_, w_=w_: h.tensor_tensor(out=w_[:], in0=y_[:], in1=y_[:], op=ALU.mult),
                         reads=[y_.res], writes=[w_.res])
                    P.op("dve", lambda h, w_=w_: h.tensor_scalar(out=w_[:], in0=w_[:], scalar1=0.044715, scalar2=1.0,
                                                                  op0=ALU.mult, op1=ALU.add), reads=[w_.res], writes=[w_.res])
                    P.op("dve", lambda h, y_=y_, w_=w_: h.tensor_tensor(out=w_[:], in0=w_[:], in1=y_[:], op=ALU.mult),
                         reads=[y_.res, w_.res], writes=[w_.res])
                    P.op("act", lambda h, w_=w_: h.activation(out=w_[:], in_=w_[:], func=AF.Sigmoid, scale=1.5957691216057308),
                         reads=[w_.res], writes=[w_.res])
                    dv(lambda h, y_=y_, w_=w_, z_=z_: h.tensor_tensor(out=z_[:], in0=y_[:], in1=w_[:], op=ALU.mult), [y_, w_], [z_])
                    P.dma("sp", zT[fb * 128:(fb + 1) * 128, t0 + tt * 512:t0 + (tt + 1) * 512], z_[:], reads=[z_.res])
        P.barrier()

    def make_epi_win(l):
        def epi(bl, tag, c0, rows, tok0, aux):
            b = bl[0]
            if tag == "ua":
                evac_store(b, rows, 512, None, 1.0, BF16, hU[c0 - O_UA:c0 - O_UA + rows, tok0:tok0 + 512], aux)
            elif tag == "qb":
                evac_store(b, rows, 512, None, 128.0 ** -0.5, BF16, hQB[c0 - O_QB:c0 - O_QB + rows, tok0:tok0 + 512], aux)
            elif tag == "kb":
                evac_store(b, rows, 512, None, 1.0, BF16, hKB[c0 - O_KB:c0 - O_KB + rows, tok0:tok0 + 512], aux)
            elif tag == "gl":
                evac_store(b, rows, 512, None, 1.0, F32, hGL[0:16, tok0:tok0 + 512], aux, eng="act")
            elif tag == "gb":
                evac_store(b, rows, 512, AF.Silu, 1.0, BF16, hGB[c0 - O_GB:c0 - O_GB + rows, tok0:tok0 + 512], aux)
            elif tag == "qc":
                evac_store(b, rows, 512, AF.Silu, 1.0, BF16, hQC[c0 - O_QC:c0 - O_QC + rows, tok0:tok0 + 512], aux)
            elif tag == "fc":
                evac_store(b, rows, 512, None, 1.0, F32, hFC[c0 - O_FC:c0 - O_FC + rows, tok0:tok0 + 512], aux)
            elif tag == "gc":
                evac_store(b, rows, 512, AF.Sigmoid, 1.0, BF16, hGC[c0 - O_GC:c0 - O_GC + rows, tok0:tok0 + 512], aux)
            elif tag == "mg":
                evac_store(b, rows, 512, AF.Sigmoid, 1.0, BF16, hMG[c0 - O_MG:c0 - O_MG + rows, tok0:tok0 + 512], aux)
            elif tag == "vb":
                evac_store(b, 128, rows, None, 1.0, BF16, hVB[tok0:tok0 + 128, c0 - O_VB:c0 - O_VB + rows], aux)
            elif tag == "ic":
                evac_store(b, 128, rows, None, 1.0, BF16, hIC[tok0:tok0 + 128, c0 - O_IC:c0 - O_IC + rows], aux, eng="act")
        return epi

    def seg(off, width, tag):
        return [(off + i * 512, min(512, width - i * 512), tag) for i in range((width + 511) // 512)]

    def layer(l, last):
        TPh = min(2048, T)
        fm_cols = (seg(O_UA, 1024, "ua") + seg(O_QB, 512, "qb") + seg(O_KB, 512, "kb") + seg(O_GL, 16, "gl") +
                   seg(O_GB, 1024, "gb") + seg(O_QC, 1024, "qc") + seg(O_FC, 1024, "fc") + seg(O_GC, 1024, "gc") +
                   seg(O_MG, 3 * D, "mg"))
        gemm_phase(xT, D, w_in[l], fm_cols, "fm", make_epi_win(l), TPh)
        gemm_phase(xT, D, w_in[l], seg(O_VB, 1024, "vb") + seg(O_IC, 1024, "ic"), "tm", make_epi_win(l), TPh)
        s5_phase(l)
        gla_like(l, "gla")
        ar.reset()
        gla_like_h(l)
        def epi_glu(bl, tag, c0, rows, tok0, aux):
            b = bl[0]
            sf = aux["sf"].next(); g = aux["gb"].next(); so = aux["sb"].next()
            fbk, r0 = c0 // 128, c0 % 128
            P.op("act", lambda h: h.activation(out=sf[0:rows, :], in_=ps_ap(b, rows), func=AF.Sigmoid,
                                               bias=glu_b[:, fbk:fbk + 1]), reads=[banks[b], glu_b.res], writes=[sf.res])
            P.dma("act", g[0:rows, 0, :], zT[c0:c0 + rows, tok0:tok0 + 512], writes=[g.res])
            P.op("dve", lambda h: h.tensor_tensor(out=so[0:rows, :], in0=sf[0:rows, :], in1=g[0:rows, 0, :], op=ALU.mult),
                 reads=[sf.res, g.res], writes=[so.res])
            P.dma("sp", yT[c0:c0 + rows, tok0:tok0 + 512], so[0:rows, :], reads=[so.res])
        glu_b = glu_holder["t"]
        P.dma("sp", glu_b[:], s5v[l, :, 1, :], writes=[glu_b.res])
        gemm_phase(zT, W, w_glu[l], seg(0, W, "glu"), "fm", epi_glu, TPh)
        def epi_up(bl, tag, c0, rows, tok0, aux):
            g = aux["gb"].next(); xf = aux["xf"].next(); so = aux["sb"].next()
            P.dma("act", g[:], hMG[:, tok0:tok0 + 512].rearrange("(b n) t -> n b t", b=3)[c0:c0 + 128], writes=[g.res])
            for i in range(3):
                P.op("dve", lambda h, i=i: h.tensor_tensor(out=xf[:, i, :], in0=ps_ap(bl[i]), in1=g[:, i, :], op=ALU.mult),
                     reads=[banks[bl[i]], g.res], writes=[xf.res])
            P.op("dve", lambda h: h.tensor_add(out=xf[:, 0, :], in0=xf[:, 0, :], in1=xf[:, 1, :]), reads=[xf.res], writes=[xf.res])
            P.op("dve", lambda h: h.tensor_add(out=so[:], in0=xf[:, 0, :], in1=xf[:, 2, :]), reads=[xf.res], writes=[so.res])
            P.dma("sp", mT[c0:c0 + 128, tok0:tok0 + 512], so[:], reads=[so.res])
        gemm_phase(yT, 3 * W, w_up[l], seg(0, D, "up"), "fm", epi_up, min(1024, T), kgroups=[(0, 8), (8, 16), (16, 24)])
        def make_epi_res(xsrc):
            def epi(bl, tag, c0, width, tok0, aux):
                b = bl[0]
                xf = aux["xf"].next(); sf = aux["sf"].next()
                P.dma("act", xf[:, 0, 0:width], xsrc[tok0:tok0 + 128, c0:c0 + width], writes=[xf.res])
                P.op("dve", lambda h: h.scalar_tensor_tensor(out=sf[:, 0:width], in0=xf[:, 0, 0:width], scalar=float(ALPHA),
                                                             in1=ps_ap(b, 128, width), op0=ALU.mult, op1=ALU.add),
                     reads=[xf.res, banks[b]], writes=[sf.res])
                P.dma("sp", zF[tok0:tok0 + 128, c0:c0 + width], sf[:, 0:width], reads=[sf.res])
            return epi
        xsrc = x_in if l == 0 else xF
        gemm_phase(mT, D, w_out[l], seg(0, D, "o"), "tm", make_epi_res(xsrc), TPh)
        ln_phase(zF, True, l, 0, xF, None)
        def epi_m1(bl, tag, c0, rows, tok0, aux):
            b = bl[0]
            sf = aux["sf"].next(); so = aux["sb"].next()
            P.op("act", lambda h: h.activation(out=sf[:], in_=ps_ap(b), func=AF.Relu), reads=[banks[b]], writes=[sf.res])
            P.op("act", lambda h: h.activation(out=so[:], in_=sf[:], func=AF.Square), reads=[sf.res], writes=[so.res])
            P.dma("sp", hT[c0:c0 + 128, tok0:tok0 + 512], so[:], reads=[so.res])
        gemm_phase(xT, D, w_m1[l], seg(0, HID, "m1"), "fm", epi_m1, TPh)
        gemm_phase(hT, HID, w_m2[l], seg(0, D, "m2"), "tm", make_epi_res(xF), 1024, kbp=8)
        ln_phase(zF, True, l, 1, xF, out if last else None)

    glu_holder = {}

    def gla_like_h(l):
        gla_like(l, "hgrn")

    glu_holder["t"] = ar.alloc([128, 8], F32)
    oml = ar.alloc([128, 8], F32)
    ar_small["oml"] = oml
    ar.base = ar.off

    c_setup()
    ln_phase(x_in, False, 0, 0, None, None)
    for l in range(L):
        P.op("dve", lambda h, l=l: h.tensor_scalar(out=oml[:], in0=lb_all[:, :, l], scalar1=-1.0, scalar2=1.0,
                                                   op0=ALU.mult, op1=ALU.add), reads=[lb_all.res], writes=[oml.res])
        layer(l, l == L - 1)
    P.stopped = False
    P._barrier()
    P.emit()
    st.close()
    return nc


def prep_weights(inp, L):
    f = np.float32
    g = {}
    g["w_in"] = np.ascontiguousarray(inp["w_in"][:L], dtype=f)
    g["w_glu"] = np.ascontiguousarray(inp["s5_w_glu"][:L], dtype=f)
    g["w_up"] = np.ascontiguousarray(inp["w_up"][:L], dtype=f).reshape(L, 3 * W, D)
    g["w_out"] = np.ascontiguousarray(inp["w_out"][:L], dtype=f)
    g["w_m1"] = np.ascontiguousarray(inp["w_mlp_in"][:L], dtype=f)
    g["w_m2"] = np.ascontiguousarray(inp["w_mlp_out"][:L], dtype=f)
    g["lnp"] = np.ascontiguousarray(np.stack([inp["ln1_g"][:L], inp["ln1_b"][:L], inp["ln2_g"][:L], inp["ln2_b"][:L]], axis=1), dtype=f)
    lam_re = np.asarray(inp["s5_lam_re"][:L], f).reshape(L, 32, 128).transpose(0, 2, 1)
    lam_im = np.asarray(inp["s5_lam_im"][:L], f).reshape(L, 32, 128).transpose(0, 2, 1)
    ldt = np.repeat(np.asarray(inp["s5_log_dt"][:L], f)[:, :, None], 64, axis=2).reshape(L, 32, 128).transpose(0, 2, 1)
    g["s5p"] = np.ascontiguousarray(np.stack([lam_re, lam_im, ldt], axis=2), dtype=f)
    bpad = np.zeros((L, 2, 128, 32, 128), f)
    cpad = np.zeros((L, 2, 128, 32, 128), f)
    for ri, (bk, ck) in enumerate((("s5_b_re", "s5_c_re"), ("s5_b_im", "s5_c_im"))):
        Bm = np.asarray(inp[bk][:L], f)
        Cm = np.asarray(inp[ck][:L], f)
        for sb in range(32):
            for gi in range(2):
                gidx = 2 * sb + gi
                r0 = (gidx % 8) * 16
                bpad[:, ri, r0:r0 + 16, sb, gi * 64:(gi + 1) * 64] = Bm[:, gidx].transpose(0, 2, 1)
                cpad[:, ri, gi * 64:(gi + 1) * 64, sb, r0:r0 + 16] = Cm[:, gidx].transpose(0, 2, 1)
    g["s5b"] = bpad
    g["s5c"] = cpad
    dsk = np.asarray(inp["s5_d"][:L], f).reshape(L, 8, 128).transpose(0, 2, 1)
    bgl = np.asarray(inp["s5_b_glu"][:L], f).reshape(L, 8, 128).transpose(0, 2, 1)
    g["s5v"] = np.ascontiguousarray(np.stack([dsk, bgl], axis=2), dtype=f)
    g["glaw"] = np.ascontiguousarray(inp["gla_w_gate"][:L], dtype=f)
    bg = np.asarray(inp["gla_b_gate"][:L], f).reshape(L, 4, 128).transpose(0, 2, 1)
    nw = np.asarray(inp["gla_norm_w"][:L], f).reshape(L, 2, 128).transpose(0, 2, 1)
    g["glav"] = np.ascontiguousarray(np.concatenate([bg, nw], axis=2), dtype=f)
    g["hlb"] = np.ascontiguousarray(np.asarray(inp["hgrn_lb_logits"], f).reshape(DEPTH, 8, 128).transpose(2, 1, 0), dtype=f)
    g["hnw"] = np.ascontiguousarray(np.asarray(inp["hgrn_norm_w"][:L], f).T, dtype=f)
    return g


_CACHE = {}


def kernel(**inputs):
    x = np.asarray(inputs["x"], np.float32)
    Bn, T, _ = x.shape
    key = (T, DEPTH)
    if key not in _CACHE:
        _CACHE[key] = build(T, DEPTH)
    nc = _CACHE[key]
    wts = prep_weights(inputs, DEPTH)
    in_maps = []
    for b in range(Bn):
        m = dict(wts)
        m["x"] = np.ascontiguousarray(x[b])
        in_maps.append(m)
    res = run_bass_kernel_spmd(nc, in_maps, core_ids=list(range(Bn)))
    return np.stack([np.asarray(r["out"], np.float32) for r in res.results], axis=0)
```
